# Optimizing a Trainium2 kernel written in Bass

```python
import jax, jax.numpy as jnp
from jax import lax
import numpy as np

D_MODEL = 2048
BATCH = 4
SEQ = 4096
DEPTH = 2

GRID_W = 64
CTX_LEN = 256
EPS = 1e-6
N_BRANCH = 4
BRANCH_W = D_MODEL // 2
ATT_HEAD_DIM = 128
ATT_HEADS = BRANCH_W // ATT_HEAD_DIM
ATT_KV_HEADS = ATT_HEADS // 4
ATT_GROUP = ATT_HEADS // ATT_KV_HEADS
ATT_SCALE = ATT_HEAD_DIM ** -0.5
ROPE_THETA = 10000.0
Q_BLOCK = 128
MLSTM_HEADS = 4
MLSTM_HEAD_DIM = BRANCH_W // MLSTM_HEADS
MLSTM_CHUNK = 64
M_INIT = -1e30
CONV_WIDTH = 3
POOL_WINDOWS = (2, 4, 8, 16)
POOL_GROUP = BRANCH_W // len(POOL_WINDOWS)

SPLITS = (
    ATT_HEADS * ATT_HEAD_DIM, ATT_KV_HEADS * ATT_HEAD_DIM, ATT_KV_HEADS * ATT_HEAD_DIM, BRANCH_W,
    BRANCH_W, BRANCH_W, BRANCH_W, BRANCH_W, BRANCH_W, 4 * MLSTM_HEADS,
    BRANCH_W, BRANCH_W, BRANCH_W, BRANCH_W,
    BRANCH_W, BRANCH_W,
    N_BRANCH * D_MODEL,
)
N_IN = sum(SPLITS)

kernel_name = "hybrid_parallel_gated_dit_block"


def split_cols(y):
    idx = np.cumsum(SPLITS)[:-1].tolist()
    return jnp.split(y, idx, axis=-1)


def rms_norm(x, gain):
    xf = x.astype(jnp.float32)
    y = xf * lax.rsqrt(jnp.mean(xf * xf, axis=-1, keepdims=True) + EPS)
    return (y * gain.astype(jnp.float32)).astype(x.dtype)


def heads(a, n, d):
    return a.reshape(a.shape[:2] + (n, d))


def axial_rope(x):
    T = x.shape[1]
    n_rows = T // GRID_W
    row = jnp.broadcast_to(jnp.arange(n_rows, dtype=jnp.float32)[:, None], (n_rows, GRID_W)).reshape(-1)
    col = jnp.broadcast_to(jnp.arange(GRID_W, dtype=jnp.float32)[None, :], (n_rows, GRID_W)).reshape(-1)
    half = ATT_HEAD_DIM // 2
    quarter = half // 2
    freq = ROPE_THETA ** (-jnp.arange(quarter, dtype=jnp.float32) / quarter)

    def rot(xa, pos):
        ang = pos[:, None] * freq[None, :]
        cos = jnp.cos(ang)[None, :, None, :]
        sin = jnp.sin(ang)[None, :, None, :]
        x1, x2 = xa[..., :quarter], xa[..., quarter:]
        return jnp.concatenate([x1 * cos - x2 * sin, x2 * cos + x1 * sin], axis=-1)

    xf = x.astype(jnp.float32)
    out = jnp.concatenate([rot(xf[..., :half], row), rot(xf[..., half:], col)], axis=-1)
    return out.astype(x.dtype)


def gqa_attend(q, k, v):
    s = jnp.einsum('bqhgd,bkhd->bhgqk', q, k).astype(jnp.float32) * ATT_SCALE
    p = jax.nn.softmax(s, axis=-1).astype(v.dtype)
    return jnp.einsum('bhgqk,bkhd->bqhgd', p, v)


def latent_attention(q, k, v, k_ctx, v_ctx):
    B, T = q.shape[:2]
    k_all = jnp.concatenate([k, k_ctx], axis=1)
    v_all = jnp.concatenate([v, v_ctx], axis=1)
    nb = T // Q_BLOCK
    qb = q.reshape(B, nb, Q_BLOCK, ATT_KV_HEADS, ATT_GROUP, ATT_HEAD_DIM).swapaxes(0, 1)
    ob = lax.map(lambda qblk: gqa_attend(qblk, k_all, v_all), qb)
    return ob.swapaxes(0, 1).reshape(B, T, ATT_HEADS * ATT_HEAD_DIM)


def context_attention(q, k, v):
    B, L = q.shape[:2]
    o = gqa_attend(q.reshape(B, L, ATT_KV_HEADS, ATT_GROUP, ATT_HEAD_DIM), k, v)
    return o.reshape(B, L, ATT_HEADS * ATT_HEAD_DIM)


def attn_qk(q, k, q_gain, k_gain):
    q = rms_norm(heads(q, ATT_HEADS, ATT_HEAD_DIM), q_gain)
    k = rms_norm(heads(k, ATT_KV_HEADS, ATT_HEAD_DIM), k_gain)
    return q, k


def mlstm_scan(q, k, v, log_i, log_f, state):
    B, H, T, _ = q.shape
    L = MLSTM_CHUNK
    nc = T // L

    def to_chunks(a):
        return jnp.moveaxis(a.reshape(a.shape[:2] + (nc, L) + a.shape[3:]), 2, 0)

    xs = tuple(to_chunks(a) for a in (q, k, v, log_i, log_f))
    lower = jnp.tril(jnp.ones((L, L), dtype=bool))

    def step(carry, inp):
        C, n, m = carry
        qc, kc, vc, ic, fc = inp
        b = jnp.cumsum(fc, axis=-1)
        log_d = jnp.where(lower, b[..., :, None] - b[..., None, :] + ic[..., None, :], -jnp.inf)
        log_inter = b + m[..., None]
        m_t = jnp.maximum(log_inter, jnp.max(log_d, axis=-1))
        d = jnp.exp(log_d - m_t[..., None])
        inter = jnp.exp(log_inter - m_t)
        s = jnp.einsum('bhtd,bhsd->bhts', qc, kc) * d
        num = inter[..., None] * jnp.einsum('bhtd,bhde->bhte', qc, C) + jnp.einsum('bhts,bhse->bhte', s, vc)
        den = inter * jnp.einsum('bhtd,bhd->bht', qc, n) + jnp.sum(s, axis=-1)
        h = num / jnp.maximum(jnp.abs(den), jnp.exp(-m_t))[..., None]
        b_last = b[..., -1]
        log_w = b_last[..., None] - b + ic
        m_new = jnp.maximum(b_last + m, jnp.max(log_w, axis=-1))
        w = jnp.exp(log_w - m_new[..., None])
        decay = jnp.exp(b_last + m - m_new)
        C_new = decay[..., None, None] * C + jnp.einsum('bhs,bhsd,bhse->bhde', w, kc, vc)
        n_new = decay[..., None] * n + jnp.einsum('bhs,bhsd->bhd', w, kc)
        return (C_new, n_new, m_new), h

    state, hs = lax.scan(step, state, xs)
    h = jnp.moveaxis(hs, 0, 2).reshape(B, H, T, -1)
    return h, state


def mlstm_inputs(q, k, v, g, gate_bias):
    B, T = q.shape[:2]
    def hd(a):
        return heads(a, MLSTM_HEADS, MLSTM_HEAD_DIM).transpose(0, 2, 1, 3).astype(jnp.float32)
    qh, kh, vh = hd(q), hd(k) * (MLSTM_HEAD_DIM ** -0.5), hd(v)
    gates = (g + gate_bias.reshape(-1)).astype(jnp.float32).reshape(B, T, 4, MLSTM_HEADS).transpose(2, 0, 3, 1)
    return qh, kh, vh, gates


def flip_t(a):
    return jnp.flip(a, axis=2)


def mlstm_bidir(qh, kh, vh, gates, state_f, state_b):
    h_f, st_f = mlstm_scan(qh, kh, vh, gates[0], jax.nn.log_sigmoid(gates[1]), state_f)
    h_b, st_b = mlstm_scan(flip_t(qh), flip_t(kh), flip_t(vh), flip_t(gates[2]),
                           flip_t(jax.nn.log_sigmoid(gates[3])), state_b)
    return h_f + flip_t(h_b), st_f, st_b


def mlstm_empty_state(B):
    return (jnp.zeros((B, MLSTM_HEADS, MLSTM_HEAD_DIM, MLSTM_HEAD_DIM), jnp.float32),
            jnp.zeros((B, MLSTM_HEADS, MLSTM_HEAD_DIM), jnp.float32),
            jnp.full((B, MLSTM_HEADS), M_INIT, jnp.float32))


def mlstm_out(h, o, gain):
    B, _, T, _ = h.shape
    hn = rms_norm(h.transpose(0, 2, 1, 3), jnp.ones((MLSTM_HEAD_DIM,), jnp.float32)).reshape(B, T, BRANCH_W)
    return (hn * gain.astype(jnp.float32) * jax.nn.sigmoid(o.astype(jnp.float32))).astype(o.dtype)


def short_conv(u, bg, cg, w):
    a = jnp.pad(cg * u, ((0, 0), (1, 1), (0, 0)))
    y = w[0] * a[:, :-2] + w[1] * a[:, 1:-1] + w[2] * a[:, 2:]
    return bg * y


def multiscale_pool(u, pool_w, pool_scale):
    B, T, _ = u.shape
    uf = u.astype(jnp.float32)
    csum = jnp.concatenate([jnp.zeros((B, 1, BRANCH_W), jnp.float32), jnp.cumsum(uf, axis=1)], axis=1)
    t = jnp.arange(T)
    outs = []
    for g, w in enumerate(POOL_WINDOWS):
        lo = jnp.clip(t - w // 2, 0, T)
        hi = jnp.clip(t + w - w // 2, 0, T)
        sl = slice(g * POOL_GROUP, (g + 1) * POOL_GROUP)
        cs = csum[..., sl]
        mean = (cs[:, hi] - cs[:, lo]) / (hi - lo).astype(jnp.float32)[None, :, None]
        outs.append((mean - uf[..., sl]).astype(u.dtype) @ pool_w[g])
    return jnp.concatenate(outs, axis=-1) * pool_scale


def merge_branches(branches, gate_cols, w_branch, w_out):
    gates = jnp.split(jax.nn.sigmoid(gate_cols.astype(jnp.float32)).astype(gate_cols.dtype), N_BRANCH, axis=-1)
    acc = gates[0] * (branches[0] @ w_branch[0])
    for i in range(1, N_BRANCH):
        acc = acc + gates[i] * (branches[i] @ w_branch[i])
    return acc @ w_out


def hybrid_layer(x, xc, c, c_ctx, norm_gain, w_mod, b_mod, w_in, q_gain, k_gain, gate_bias, m_gain,
                 conv_w, pool_w, pool_scale, w_branch, w_out, update_ctx):
    B = x.shape[0]
    mod_l = jax.nn.silu(c) @ w_mod + b_mod
    mod_c = jax.nn.silu(c_ctx) @ w_mod + b_mod
    sh_l, sc_l, gt_l = jnp.split(mod_l[:, None, :], 3, axis=-1)
    sh_c, sc_c, gt_c = jnp.split(mod_c, 3, axis=-1)
    h_l = rms_norm(x, norm_gain) * (1 + sc_l) + sh_l
    h_c = rms_norm(xc, norm_gain) * (1 + sc_c) + sh_c
    (aq_l, ak_l, av_l, az_l, mq_l, mk_l, mv_l, mo_l, mz_l, mg_l,
     cu_l, cb_l, cc_l, cz_l, pu_l, pz_l, gm_l) = split_cols(h_l @ w_in)
    (aq_c, ak_c, av_c, az_c, mq_c, mk_c, mv_c, mo_c, mz_c, mg_c,
     cu_c, cb_c, cc_c, cz_c, pu_c, pz_c, gm_c) = split_cols(h_c @ w_in)

    q_l, k_l = attn_qk(aq_l, ak_l, q_gain, k_gain)
    q_l, k_l = axial_rope(q_l), axial_rope(k_l)
    q_c, k_c = attn_qk(aq_c, ak_c, q_gain, k_gain)
    v_l = heads(av_l, ATT_KV_HEADS, ATT_HEAD_DIM)
    v_c = heads(av_c, ATT_KV_HEADS, ATT_HEAD_DIM)
    ya_l = latent_attention(q_l, k_l, v_l, k_c, v_c)

    mq, mk, mv, mg = mlstm_inputs(mq_c, mk_c, mv_c, mg_c, gate_bias)
    hm_c, st_f, st_b = mlstm_bidir(mq, mk, mv, mg, mlstm_empty_state(B), mlstm_empty_state(B))
    mq, mk, mv, mg = mlstm_inputs(mq_l, mk_l, mv_l, mg_l, gate_bias)
    hm_l, _, _ = mlstm_bidir(mq, mk, mv, mg, st_f, st_b)
    yb_l = mlstm_out(hm_l, mo_l, m_gain)

    yc_l = short_conv(cu_l, cb_l, cc_l, conv_w)
    yd_l = multiscale_pool(pu_l, pool_w, pool_scale)

    out_l = merge_branches([ya_l * jax.nn.silu(az_l), yb_l * jax.nn.silu(mz_l),
                            yc_l * jax.nn.silu(cz_l), yd_l * jax.nn.silu(pz_l)], gm_l, w_branch, w_out)
    x = x + gt_l * out_l

    if update_ctx:
        ya_c = context_attention(q_c, k_c, v_c)
        yb_c = mlstm_out(hm_c, mo_c, m_gain)
        yc_c = short_conv(cu_c, cb_c, cc_c, conv_w)
        yd_c = multiscale_pool(pu_c, pool_w, pool_scale)
        out_c = merge_branches([ya_c * jax.nn.silu(az_c), yb_c * jax.nn.silu(mz_c),
                                yc_c * jax.nn.silu(cz_c), yd_c * jax.nn.silu(pz_c)], gm_c, w_branch, w_out)
        xc = xc + gt_c * out_c
    return x, xc


def setup_inputs(seed: int = 0) -> dict:
    key = jax.random.key(seed)
    ks = jax.random.split(key, 20)
    f32 = jnp.float32

    def nrm(k, shape, scale):
        return jax.random.normal(k, shape, f32) * scale

    fbias = jnp.array([0.0, 1.0, 0.0, 1.0], f32)[:, None] * jnp.linspace(3.0, 6.0, MLSTM_HEADS, dtype=f32)[None, :]
    return {
        "x": nrm(ks[0], (BATCH, SEQ, D_MODEL), 1.0),
        "c": nrm(ks[1], (BATCH, D_MODEL), 1.0),
        "ctx": nrm(ks[2], (BATCH, CTX_LEN, D_MODEL), 1.0),
        "c_ctx": nrm(ks[3], (D_MODEL,), 1.0),
        "norm_gain": 1.0 + nrm(ks[4], (DEPTH, D_MODEL), 0.05),
        "w_mod": nrm(ks[5], (DEPTH, D_MODEL, 3 * D_MODEL), D_MODEL ** -0.5),
        "b_mod": nrm(ks[6], (DEPTH, 3 * D_MODEL), 0.02),
        "w_in": nrm(ks[7], (DEPTH, D_MODEL, N_IN), D_MODEL ** -0.5),
        "q_norm_gain": 1.0 + nrm(ks[8], (DEPTH, ATT_HEAD_DIM), 0.05),
        "k_norm_gain": 1.0 + nrm(ks[9], (DEPTH, ATT_HEAD_DIM), 0.05),
        "mlstm_gate_bias": fbias[None] + nrm(ks[10], (DEPTH, 4, MLSTM_HEADS), 0.1),
        "mlstm_norm_gain": 1.0 + nrm(ks[11], (DEPTH, BRANCH_W), 0.05),
        "conv_w": nrm(ks[12], (DEPTH, CONV_WIDTH, BRANCH_W), CONV_WIDTH ** -0.5),
        "pool_w": nrm(ks[13], (DEPTH, len(POOL_WINDOWS), POOL_GROUP, POOL_GROUP), POOL_GROUP ** -0.5),
        "pool_scale": 1.0 + nrm(ks[14], (DEPTH, BRANCH_W), 0.1),
        "w_branch": nrm(ks[15], (DEPTH, N_BRANCH, BRANCH_W, D_MODEL), BRANCH_W ** -0.5),
        "w_out": nrm(ks[16], (DEPTH, D_MODEL, D_MODEL), D_MODEL ** -0.5),
        "final_norm_gain": 1.0 + nrm(ks[17], (D_MODEL,), 0.05),
    }


def reference(x, c, ctx, c_ctx, norm_gain, w_mod, b_mod, w_in, q_norm_gain, k_norm_gain, mlstm_gate_bias,
              mlstm_norm_gain, conv_w, pool_w, pool_scale, w_branch, w_out, final_norm_gain):
    for l in range(DEPTH):
        x, ctx = hybrid_layer(x, ctx, c, c_ctx, norm_gain[l], w_mod[l], b_mod[l], w_in[l], q_norm_gain[l],
                              k_norm_gain[l], mlstm_gate_bias[l], mlstm_norm_gain[l], conv_w[l], pool_w[l],
                              pool_scale[l], w_branch[l], w_out[l], update_ctx=(l < DEPTH - 1))
    return rms_norm(x, final_norm_gain)
```

```python
import contextlib
import numpy as np
import concourse.bass as bass
import concourse.mybir as mybir
from concourse.bass_utils import run_bass_kernel_spmd

F32 = mybir.dt.float32
BF16 = mybir.dt.bfloat16
AF = mybir.ActivationFunctionType
ALU = mybir.AluOpType
AX = mybir.AxisListType

NT = 4352
NTI = 34
NL = 4096
NC = 256
D = 2048
KC = 16
EPS = 1e-6
BLKS = [(0, 256)] + [(256 + 512 * j, 512) for j in range(8)]
DEPTH = 2
WCOL = dict(aq=0, ak=1024, av=1280, az=1536, mq=2560, mk=3584, mv=4608, mo=5632, mz=6656, mg=7680,
            cu=7696, cb=8720, cc=9744, cz=10768, pu=11792, pz=12816, gm=13840)
N_IN = 22032
FROW = dict(az=0, mz=1024, cz=2048, pz=3072, mo=4096, gm=5120, mq=13312, mk=14336, cu=15360, cb=16384,
            cc=17408, pu=18432)
NF = 19456
FSEG = [("az", 1024, "silu"), ("mz", 1024, "silu"), ("cz", 1024, "silu"), ("pz", 1024, "silu"),
        ("mo", 1024, "sig"), ("gm", 8192, "sig"),
        ("mq", 1024, "copy"), ("mk", 1024, "copy16"), ("cu", 1024, "copy"), ("cb", 1024, "copy"),
        ("cc", 1024, "copy"), ("pu", 1024, "copy")]


class Tok:
    __slots__ = ("q", "sem", "val", "rec")

    def __init__(self, q, rec):
        self.q = q
        self.rec = rec
        self.sem = None
        self.val = None


class Buf:
    __slots__ = ("name", "w", "r", "excl", "wf")

    def __init__(self, name="", excl=False):
        self.name = name
        self.w = {}
        self.wf = {}
        self.r = {}
        self.excl = excl


class Rec:
    __slots__ = ("fn", "deps", "signal", "tok", "dma", "dsem")

    def __init__(self, fn, deps, dma):
        self.fn = fn
        self.deps = deps
        self.signal = False
        self.tok = None
        self.dma = dma
        self.dsem = None


class Prog:
    NDMA = 8

    def __init__(self, nc):
        self.nc = nc
        self.eng = {"pe": nc.tensor, "act": nc.scalar, "dve": nc.vector, "pool": nc.gpsimd, "sp": nc.sync}
        self.q = {k: [] for k in self.eng}
        self.dma_n = {k: 0 for k in self.eng}
        self.dma_last = {k: [None] * self.NDMA for k in self.eng}
        self.last = {k: None for k in self.eng}
        self.fence = {k: [] for k in self.eng}

    def barrier(self):
        toks = []
        for q in self.eng:
            if self.last[q] is not None:
                toks.append(self.last[q])
            toks.extend(t for t in self.dma_last[q] if t is not None)
        for q in self.eng:
            self.fence[q] = list(toks)

    def op(self, q, fn, reads=(), writes=(), dma=False, part=False):
        deps = []
        if self.fence[q]:
            deps.extend(self.fence[q])
            self.fence[q] = []
        for b in reads:
            deps.extend(b.w.values())
            if b.excl:
                deps.extend(t for kk, t in b.r.items() if kk[0] != q)
        for b in writes:
            deps.extend(b.r.values())
            if not part:
                deps.extend(b.w.values())
            else:
                deps.extend(b.wf.values())
        rec = Rec(fn, deps, dma)
        tok = Tok(q, rec)
        rec.tok = tok
        if dma:
            n = self.dma_n[q]
            self.dma_n[q] = n + 1
            slot = n % self.NDMA
            prev = self.dma_last[q][slot]
            if prev is not None:
                rec.deps.append(prev)
            self.dma_last[q][slot] = tok
            rec.dsem = slot
        else:
            self.last[q] = tok
        self.q[q].append(rec)
        key = (q, rec.dsem)
        for b in reads:
            b.r[key] = tok
        for b in writes:
            if part:
                b.w[key] = tok
            else:
                b.w = {key: tok}
                b.wf = {key: tok}
                b.r = {}
        return tok

    def emit(self, sems, dsems):
        for q, recs in self.q.items():
            for rec in recs:
                for t in rec.deps:
                    if t.rec.dma:
                        continue
                    if t.q == q and q == "pe":
                        continue
                    t.rec.signal = True
        for q, recs in self.q.items():
            cnt = 0
            dcnt = [0] * self.NDMA
            for rec in recs:
                if rec.dma:
                    dcnt[rec.dsem] += 16
                    rec.tok.sem = dsems[q][rec.dsem]
                    rec.tok.val = dcnt[rec.dsem]
                elif rec.signal:
                    cnt += 1
                    rec.tok.sem = sems[q]
                    rec.tok.val = cnt
        self.stats = {}
        with self.nc.Block() as block:
            def mk(q):
                def body(e):
                    seen = {}
                    nw = 0
                    for rec in self.q[q]:
                        need = {}
                        for t in rec.deps:
                            if t.sem is None or (t.q == q and q == "pe" and not t.rec.dma):
                                continue
                            k = id(t.sem)
                            if seen.get(k, 0) >= t.val:
                                continue
                            if k not in need or need[k][1] < t.val:
                                need[k] = (t.sem, t.val)
                        for k, (s, v) in need.items():
                            e.wait_ge(s, v)
                            seen[k] = v
                            nw += 1
                        ins = rec.fn(e)
                        if rec.dma:
                            ins.then_inc(rec.tok.sem, 16)
                        elif rec.signal:
                            ins.then_inc(rec.tok.sem, 1)
                    for t in self.dma_last[q]:
                        if t is not None:
                            e.wait_ge(t.sem, t.val)
                    self.stats[q] = (len(self.q[q]), nw)
                return body
            block.tensor(mk("pe"))
            block.scalar(mk("act"))
            block.vector(mk("dve"))
            block.gpsimd(mk("pool"))
            block.sync(mk("sp"))


def C(name, *a, **kw):
    return lambda e: getattr(e, name)(*a, **kw)


class Arena:
    def __init__(self, ap, nbytes):
        self.ap = ap
        self.cap = nbytes
        self.top = 0

    def mark(self):
        return self.top

    def reset(self, m):
        self.top = m

    def alloc(self, shape, dt, name=""):
        esz = 2 if dt == BF16 else 4
        n = int(np.prod(shape[1:]))
        nb = (n * esz + 31) // 32 * 32
        off = self.top
        self.top += nb
        assert self.top <= self.cap, f"arena overflow {self.top} > {self.cap} at {name}"
        v = self.ap[0:shape[0], off // 4: off // 4 + (n * esz + 3) // 4]
        if dt != F32:
            v = v.bitcast(dt)[:, 0:n]
        if len(shape) == 3:
            v = v.rearrange("p (a b) -> p a b", b=shape[2])
        elif len(shape) == 4:
            v = v.rearrange("p (a b c) -> p a b c", b=shape[2], c=shape[3])
        return v, Buf(name)


class K:
    pass


def build_program(debug=(), stop_after=None, nlayers=DEPTH):
    nc = bass.Bass("TRN2", target_bir_lowering=False)
    P = Prog(nc)
    k = K()
    k.nc, k.P = nc, P
    k.stop = stop_after

    def din(name, shape, dt=F32):
        return nc.dram_tensor(name, list(shape), dt, kind="ExternalInput").ap()

    def dscr(name, shape, dt, out=False):
        return nc.dram_tensor(name, list(shape), dt, kind=("ExternalOutput" if (out or name in debug) else "Internal")).ap()

    I = {}
    I["x"] = din("x", [NL, D])
    I["ctx"] = din("ctx", [NC, D])
    I["c2T"] = din("c2T", [128, KC, 2])
    I["ngcol"] = din("ngcol", [DEPTH, 128, KC])
    I["w_mod"] = din("w_mod", [DEPTH, D, 3 * D])
    I["b_mod"] = din("b_mod", [DEPTH, 3 * D])
    I["w_in"] = din("w_in", [DEPTH, D, N_IN])
    I["qg"] = din("qg", [DEPTH, 128])
    I["kg"] = din("kg", [DEPTH, 128])
    I["gbias"] = din("gbias", [DEPTH, 64, 2])
    I["mgain"] = din("mgain", [DEPTH, 1024])
    I["convw"] = din("convw", [DEPTH, 128, 8, 3])
    I["pool_w"] = din("pool_w", [DEPTH, 4, 256, 256])
    I["pscale"] = din("pscale", [DEPTH, 128, 8])
    I["wbt"] = din("wbt", [DEPTH, 16, 128, 4 * 8 * 128])
    I["w_out"] = din("w_out", [DEPTH, D, D])
    I["fgain"] = din("fgain", [D])
    I["ident"] = din("ident", [128, 128])
    I["masks"] = din("masks", [2, 128, 128])
    I["rope"] = din("rope", [NTI, 128, 2, 256])
    I["sel"] = din("sel", [64, 8, 128])
    I["rcnt"] = din("rcnt", [4, NT])
    k.I = I
    S = {}
    S["modd"] = dscr("modd", [DEPTH, 2, 3 * D], F32)
    S["F"] = dscr("Fs", [NF, NT], BF16)
    S["QT"] = dscr("QT", [1024, NT], BF16)
    S["KT"] = dscr("KT", [256, NT], BF16)
    S["Vt"] = dscr("Vt", [NT, 256], BF16)
    S["MKt"] = dscr("MKt", [NT, 1024], BF16)
    S["MVt"] = dscr("MVt", [NT, 1024], BF16)
    S["HF"] = dscr("HF", [2, NT, 1024], BF16)
    S["Y"] = dscr("Y", [4, 1024, NT], BF16)
    S["X1"] = dscr("X1", [NT, D], F32)
    S["ACC"] = dscr("ACC", [D, NT], BF16)
    S["out"] = dscr("out", [NL, D], F32, out=True)
    k.S = S
    k.SB = {n: Buf(n) for n in S}

    with contextlib.ExitStack() as es:
        ARENA_BYTES = 206 * 1024
        arena_t = es.enter_context(nc.sbuf_tensor("arena", [128, ARENA_BYTES // 4], F32))
        A = Arena(arena_t, ARENA_BYTES)
        k.A = A
        k.pb = [es.enter_context(nc.psum_tensor(f"pb{i}", [128, 512], F32))[:, :] for i in range(8)]
        k.pbB = [Buf(f"pb{i}", excl=True) for i in range(8)]
        sems = {q: es.enter_context(nc.semaphore("s_" + q)) for q in P.eng}
        dsems = {q: [es.enter_context(nc.semaphore(f"d_{q}{i}")) for i in range(P.NDMA)] for q in P.eng}

        k.idf, k.idfB = A.alloc([128, 128], F32, "idf")
        k.idb, k.idbB = A.alloc([128, 128], BF16, "idb")
        k.onesb, k.onesbB = A.alloc([128, 128], BF16, "onesb")
        k.modcol, k.modcolB = A.alloc([128, DEPTH, 48, 2], F32, "modcol")
        k.gcol, k.gcolB = A.alloc([128, DEPTH, KC, 2], F32, "gcol")
        k.Gtok, k.GtokB = A.alloc([128, NTI, 2, 36], F32, "Gtok")
        P.op("pool", C("memset", k.Gtok, 0.0), writes=[k.GtokB])
        P.op("sp", C("dma_start", out=k.idf, in_=I["ident"]), writes=[k.idfB], dma=True)
        P.op("pool", C("dma_start", out=k.idb, in_=I["ident"]), writes=[k.idbB], dma=True)
        P.op("dve", C("memset", k.onesb, 1.0), writes=[k.onesbB])

        phase_mod(k)
        for l in range(nlayers if stop_after != ("mod", 0) else 0):
            last = (l == DEPTH - 1)
            phase_norm_inproj(k, l)
            if stop_after in (("inproj", l), ("norm", l), ("tokmaj", l)):
                break
            phase_attn(k, l)
            if stop_after == ("attn", l):
                break
            phase_mlstm(k, l)
            if stop_after == ("mlstm", l):
                break
            phase_conv_pool(k, l)
            if stop_after == ("convpool", l):
                break
            phase_merge_out(k, l)
        P.emit(sems, dsems)
    return nc


def phase_mod(k):
    P, A, I, S = k.P, k.A, k.I, k.S
    m0 = A.mark()
    cT, cTB = A.alloc([128, KC, 2], F32, "cT")
    scT, scTB = A.alloc([128, KC, 2], F32, "scT")
    ngc, ngcB = A.alloc([128, DEPTH, KC], F32, "ngc")
    wm = [A.alloc([128, 3072], F32, f"wm{i}") for i in range(3)]
    mo, moB = A.alloc([2, 6144], F32, "mo")
    bm, bmB = A.alloc([2, 6144], F32, "bm")
    tmp, tmpB = A.alloc([128, KC, 2], F32, "tmpg")
    P.op("sp", C("dma_start", out=cT, in_=I["c2T"]), writes=[cTB], dma=True)
    P.op("sp", C("dma_start", out=ngc, in_=I["ngcol"].rearrange("l p k -> p l k")), writes=[ngcB], dma=True)
    P.op("act", C("activation", out=scT, in_=cT, func=AF.Silu), reads=[cTB], writes=[scTB])
    n = 0
    for l in range(DEPTH):
        P.op("sp", C("dma_start", out=bm, in_=I["b_mod"][l, :].partition_broadcast(2)), writes=[bmB], dma=True)
        for half in range(2):
            for kc in range(KC):
                w, wB = wm[n % 3]
                n += 1
                P.op("sp", C("dma_start",
                    out=w, in_=I["w_mod"][l, kc * 128:(kc + 1) * 128, half * 3072:(half + 1) * 3072]), writes=[wB], dma=True)
                for j in range(6):
                    P.op("pe", C("matmul", k.pb[j][0:2, :], lhsT=scT[:, kc, :], rhs=w[:, j * 512:(j + 1) * 512],
                                                                start=(kc == 0), stop=(kc == KC - 1)),
                         reads=[scTB, wB], writes=[k.pbB[j]])
            for j in range(6):
                c0 = half * 3072 + j * 512
                P.op("dve", C("tensor_tensor", out=mo[:, c0:c0 + 512], in0=k.pb[j][0:2, :], in1=bm[:, c0:c0 + 512], op=ALU.add),
                     reads=[k.pbB[j], bmB], writes=[moB], part=True)
        P.op("sp", C("dma_start", out=S["modd"][l], in_=mo), reads=[moB], writes=[k.SB["modd"]], dma=True, part=True)
        for j in range(48):
            P.op("pe", C("transpose", k.pb[6][:, 2 * j:2 * j + 2], mo[0:2, j * 128:(j + 1) * 128], k.idf[0:2, 0:2]),
                 reads=[moB, k.idfB], writes=[k.pbB[6]])
        P.op("dve", C("tensor_copy", out=k.modcol[:, l], in_=k.pb[6][:, 0:96].rearrange("p (j r) -> p j r", r=2)),
             reads=[k.pbB[6]], writes=[k.modcolB], part=True)
        P.op("dve", C("tensor_scalar", out=tmp, in0=k.modcol[:, l, 16:32, :], scalar1=1.0, scalar2=None, op0=ALU.add),
             reads=[k.modcolB], writes=[tmpB])
        P.op("dve", C("tensor_tensor", out=k.gcol[:, l], in0=tmp, in1=ngc[:, l, :].unsqueeze(2).to_broadcast([128, KC, 2]), op=ALU.mult),
             reads=[tmpB, ngcB], writes=[k.gcolB], part=True)
    P.barrier()
    A.reset(m0)


def src_tile(k, l, i):
    if l == 0:
        if i < 2:
            return k.I["ctx"][i * 128:(i + 1) * 128, :]
        return k.I["x"][(i - 2) * 128:(i - 1) * 128, :]
    return k.S["X1"][i * 128:(i + 1) * 128, :]


def phase_norm_inproj(k, l):
    P, A, I, S = k.P, k.A, k.I, k.S
    m0 = A.mark()
    hT, hTB = A.alloc([128, KC, NT], BF16, "hT")
    m1 = A.mark()
    xt = [A.alloc([128, D], F32, f"xt{i}") for i in range(2)]
    xn = [A.alloc([128, D], BF16, f"xn{i}") for i in range(2)]
    junk, junkB = A.alloc([128, D], BF16, "junk")
    ss, ssB = A.alloc([128, NTI], F32, "ss")
    rs, rsB = A.alloc([128, NTI], F32, "rs")
    x1B = [k.SB["X1"]] if l > 0 else []
    for i in range(NTI):
        r = 1 if i < 2 else 0
        x_, xB = xt[i % 2]
        n_, nB = xn[i % 2]
        P.op("sp", C("dma_start", out=x_, in_=src_tile(k, l, i)), reads=x1B, writes=[xB], dma=True)
        P.op("act", C("activation", out=junk, in_=x_, func=AF.Square, accum_out=ss[:, i:i + 1]),
             reads=[xB], writes=[junkB, ssB])
        P.op("dve", C("tensor_scalar", out=rs[:, i:i + 1], in0=ss[:, i:i + 1], scalar1=1.0 / D, scalar2=EPS, op0=ALU.mult, op1=ALU.add),
             reads=[ssB], writes=[rsB])
        P.op("act", C("activation", out=rs[:, i:i + 1], in_=rs[:, i:i + 1], func=AF.Sqrt), reads=[rsB], writes=[rsB])
        P.op("dve", C("reciprocal", out=rs[:, i:i + 1], in_=rs[:, i:i + 1]), reads=[rsB], writes=[rsB])
        P.op("dve", C("tensor_scalar", out=n_, in0=x_, scalar1=rs[:, i:i + 1], scalar2=None, op0=ALU.mult),
             reads=[xB, rsB], writes=[nB])
        import os
        KD = os.environ.get("KDBG", "")
        for half in range(2):
            if KD == "A":
                break
            bi = (2 * i + half) % 4
            pbf = k.pb[bi][:, :].bitcast(BF16)
            for j in range(8):
                kc = half * 8 + j
                P.op("pe", C("transpose", pbf[:, j * 128:(j + 1) * 128], n_[:, kc * 128:(kc + 1) * 128], k.idb),
                     reads=[nB, k.idbB], writes=[k.pbB[bi]])
            for j in range(8):
                kc = half * 8 + j
                if half == 0 or KD == "B":
                    P.op("dve", C("tensor_scalar",
                        out=hT[:, kc, i * 128:(i + 1) * 128], in0=pbf[:, j * 128:(j + 1) * 128],
                        scalar1=k.gcol[:, l, kc, r:r + 1], scalar2=k.modcol[:, l, kc, r:r + 1], op0=ALU.mult, op1=ALU.add),
                        reads=[k.pbB[bi], k.gcolB, k.modcolB], writes=[hTB], part=True)
                else:
                    P.op("act", C("activation",
                        out=hT[:, kc, i * 128:(i + 1) * 128], in_=pbf[:, j * 128:(j + 1) * 128], func=AF.Identity,
                        bias=k.modcol[:, l, kc, r:r + 1], scale=k.gcol[:, l, kc, r:r + 1]),
                        reads=[k.pbB[bi], k.gcolB, k.modcolB], writes=[hTB], part=True)
    P.barrier()
    A.reset(m1)
    if k.stop == ("norm", l):
        A.reset(m0)
        return
    W = [A.alloc([128, KC, 512], BF16, f"W{i}") for i in range(2)]
    m2 = A.mark()
    t1, t1B = A.alloc([128, 512], F32, "t1")
    t2, t2B = A.alloc([128, 512], F32, "t2")
    t3, t3B = A.alloc([128, 512], F32, "t3")
    ta, taB = A.alloc([128, 512], F32, "ta")
    tb, tbB = A.alloc([128, 512], F32, "tb")
    qf, qfB = A.alloc([128, 512], BF16, "qf")
    ssq, ssqB = A.alloc([128, 4], F32, "ssq")
    rop = [A.alloc([128, 2, 8, 32], F32, f"rope{i}") for i in range(2)]
    qgb, qgbB = A.alloc([128, 128], F32, "qgb")
    kgb, kgbB = A.alloc([128, 128], F32, "kgb")
    qst = [A.alloc([128, 4, 512], BF16, f"qst{i}") for i in range(2)]
    vst = [A.alloc([128, 512], BF16, f"vst{i}") for i in range(2)]
    P.op("sp", C("dma_start", out=qgb, in_=I["qg"][l, :].partition_broadcast(128)), writes=[qgbB], dma=True)
    P.op("sp", C("dma_start", out=kgb, in_=I["kg"][l, :].partition_broadcast(128)), writes=[kgbB], dma=True)
    P.op("dve", C("tensor_scalar", out=qgb, in0=qgb, scalar1=float(128 ** -0.5), scalar2=None, op0=ALU.mult), reads=[qgbB], writes=[qgbB])
    wsrc = I["w_in"][l].rearrange("(kc p) c -> p kc c", p=128)
    groups = [("q", WCOL["aq"], 0), ("q", WCOL["aq"] + 512, 1), ("kv", WCOL["ak"], 0),
              ("mk", WCOL["mk"], 0), ("mk", WCOL["mk"] + 512, 1), ("mv", WCOL["mv"], 0), ("mv", WCOL["mv"] + 512, 1),
              ("mg", WCOL["mg"], 0)]
    nload = [0]

    def load_w(c0, ncol):
        w, wB = W[nload[0] % 2]
        nload[0] += 1
        P.op("pool", C("dma_start", out=w[:, :, 0:ncol], in_=wsrc[:, :, c0:c0 + ncol]), writes=[wB], dma=True)
        return w, wB

    def qk_post(ps, psB, nh, gb, gbB, rp, rpB, out, outB):
        Wd = nh * 128
        g = nh * 2
        P.op("act", C("activation", out=t1[:, 0:Wd], in_=ps[:, 0:Wd], func=AF.Square), reads=[psB], writes=[t1B])
        P.op("dve", C("tensor_reduce", out=ssq[:, 0:nh], in_=t1[:, 0:Wd].rearrange("p (h d) -> p h d", d=128), axis=AX.X, op=ALU.add),
             reads=[t1B], writes=[ssqB])
        P.op("dve", C("tensor_scalar", out=ssq[:, 0:nh], in0=ssq[:, 0:nh], scalar1=1.0 / 128, scalar2=EPS, op0=ALU.mult, op1=ALU.add),
             reads=[ssqB], writes=[ssqB])
        P.op("act", C("activation", out=ssq[:, 0:nh], in_=ssq[:, 0:nh], func=AF.Sqrt), reads=[ssqB], writes=[ssqB])
        P.op("dve", C("reciprocal", out=ssq[:, 0:nh], in_=ssq[:, 0:nh]), reads=[ssqB], writes=[ssqB])
        P.op("dve", C("tensor_tensor", out=t2[:, 0:Wd].rearrange("p (h d) -> p h d", d=128), in0=ps[:, 0:Wd].rearrange("p (h d) -> p h d", d=128),
                                              in1=ssq[:, 0:nh].unsqueeze(2).to_broadcast([128, nh, 128]), op=ALU.mult),
             reads=[psB, ssqB], writes=[t2B])
        P.op("pool", C("tensor_tensor", out=t3[:, 0:Wd].rearrange("p (h d) -> p h d", d=128), in0=t2[:, 0:Wd].rearrange("p (h d) -> p h d", d=128),
                                               in1=gb.unsqueeze(1).to_broadcast([128, nh, 128]), op=ALU.mult),
             reads=[t2B, gbB], writes=[t3B])
        t3v = t3[:, 0:Wd].rearrange("p (g x j) -> p g x j", x=2, j=32)
        tav = ta[:, 0:Wd].rearrange("p (g x j) -> p g x j", x=2, j=32)
        tbv = tb[:, 0:Wd].rearrange("p (g x j) -> p g x j", x=2, j=32)
        ov = out.rearrange("p (g x j) -> p g x j", x=2, j=32)
        P.op("pool", C("tensor_tensor", out=tav, in0=t3v, in1=rp[:, 0, 0:g, :].unsqueeze(2).to_broadcast([128, g, 2, 32]), op=ALU.mult),
             reads=[t3B, rpB], writes=[taB])
        P.op("pool", C("tensor_tensor", out=tbv[:, :, 0, :], in0=t3v[:, :, 1, :], in1=rp[:, 1, 0:g, :], op=ALU.mult),
             reads=[t3B, rpB], writes=[tbB], part=True)
        P.op("pool", C("tensor_tensor", out=tbv[:, :, 1, :], in0=t3v[:, :, 0, :], in1=rp[:, 1, 0:g, :], op=ALU.mult),
             reads=[t3B, rpB], writes=[tbB], part=True)
        P.op("dve", C("tensor_tensor", out=ov[:, :, 0, :], in0=tav[:, :, 0, :], in1=tbv[:, :, 0, :], op=ALU.subtract),
             reads=[taB, tbB], writes=[outB], part=True)
        P.op("dve", C("tensor_tensor", out=ov[:, :, 1, :], in0=tav[:, :, 1, :], in1=tbv[:, :, 1, :], op=ALU.add),
             reads=[taB, tbB], writes=[outB], part=True)

    nps = [0]
    cur = load_w(groups[0][1], 512)
    for gi, (kind, c0, sub) in enumerate(groups):
        w, wB = cur
        if gi + 1 < len(groups):
            nk, nc0, _ = groups[gi + 1]
            cur = load_w(nc0, 16 if nk == "mg" else 512)
        ncol = 16 if kind == "mg" else 512
        for i in range(NTI):
            bi = nps[0] % 4
            nps[0] += 1
            ps, psB = k.pb[bi], k.pbB[bi]
            if kind in ("q", "kv"):
                rp, rpB = rop[i % 2]
                P.op("sp", C("dma_start", out=rp, in_=I["rope"][i].rearrange("p a (g j) -> p a g j", j=32)), writes=[rpB], dma=True)
            for kc in range(KC):
                P.op("pe", C("matmul", ps[:, 0:ncol], lhsT=hT[:, kc, i * 128:(i + 1) * 128], rhs=w[:, kc, 0:ncol],
                                                                               start=(kc == 0), stop=(kc == KC - 1)),
                     reads=[hTB, wB], writes=[psB])
            if kind == "q":
                qk_post(ps, psB, 4, qgb, qgbB, rp, rpB, qf, qfB)
                grp = 0 if i < 2 else 1 + (i - 2) // 4
                pos = i if i < 2 else (i - 2) % 4
                st, stB = qst[grp % 2]
                tbi = 4 + (i % 2)
                pbf = k.pb[tbi][:, :].bitcast(BF16)
                for h in range(4):
                    P.op("pe", C("transpose", pbf[:, h * 128:(h + 1) * 128], qf[:, h * 128:(h + 1) * 128], k.idb),
                         reads=[qfB, k.idbB], writes=[k.pbB[tbi]])
                P.op("act", C("activation", out=st[:, :, pos * 128:(pos + 1) * 128],
                                                                            in_=pbf[:, 0:512].rearrange("p (h t) -> p h t", t=128), func=AF.Copy),
                     reads=[k.pbB[tbi]], writes=[stB], part=True)
                done = (i == 1) or (i >= 2 and pos == 3)
                if done:
                    t0 = 0 if i < 2 else 256 + ((i - 2) // 4) * 512
                    n = 256 if i < 2 else 512
                    P.op("sp", C("dma_start",
                        out=S["QT"].rearrange("(h d) t -> d h t", d=128)[:, sub * 4:(sub + 1) * 4, t0:t0 + n], in_=st[:, :, 0:n]),
                        reads=[stB], writes=[k.SB["QT"]], dma=True, part=True)
            elif kind == "kv":
                qk_post(ps, psB, 2, kgb, kgbB, rp, rpB, qf[:, 0:256], qfB)
                grp = 0 if i < 2 else 1 + (i - 2) // 4
                pos = i if i < 2 else (i - 2) % 4
                st, stB = qst[grp % 2]
                tbi = 4 + (i % 2)
                pbf = k.pb[tbi][:, :].bitcast(BF16)
                for h in range(2):
                    P.op("pe", C("transpose", pbf[:, h * 128:(h + 1) * 128], qf[:, h * 128:(h + 1) * 128], k.idb),
                         reads=[qfB, k.idbB], writes=[k.pbB[tbi]])
                P.op("act", C("activation", out=st[:, 0:2, pos * 128:(pos + 1) * 128],
                                                                            in_=pbf[:, 0:256].rearrange("p (h t) -> p h t", t=128), func=AF.Copy),
                     reads=[k.pbB[tbi]], writes=[stB], part=True)
                done = (i == 1) or (i >= 2 and pos == 3)
                if done:
                    t0 = 0 if i < 2 else 256 + ((i - 2) // 4) * 512
                    n = 256 if i < 2 else 512
                    P.op("sp", C("dma_start",
                        out=S["KT"].rearrange("(h d) t -> d h t", d=128)[:, :, t0:t0 + n], in_=st[:, 0:2, 0:n]),
                        reads=[stB], writes=[k.SB["KT"]], dma=True, part=True)
                v_, vB = vst[i % 2]
                P.op("act", C("activation", out=v_[:, 0:256], in_=ps[:, 256:512], func=AF.Copy), reads=[psB], writes=[vB])
                P.op("sp", C("dma_start", out=S["Vt"][i * 128:(i + 1) * 128, :], in_=v_[:, 0:256]),
                     reads=[vB], writes=[k.SB["Vt"]], dma=True, part=True)
            elif kind in ("mk", "mv"):
                v_, vB = vst[i % 2]
                sc = 0.0625 if kind == "mk" else 1.0
                P.op("act", C("activation", out=v_, in_=ps, func=AF.Copy, scale=sc), reads=[psB], writes=[vB])
                dst = S["MKt"] if kind == "mk" else S["MVt"]
                dB = k.SB["MKt"] if kind == "mk" else k.SB["MVt"]
                P.op("sp", C("dma_start", out=dst[i * 128:(i + 1) * 128, sub * 512:(sub + 1) * 512], in_=v_),
                     reads=[vB], writes=[dB], dma=True, part=True)
            else:
                for gi in range(4):
                    P.op("dve", C("tensor_copy", out=k.Gtok[:, i, gi % 2, (gi // 2) * 32:(gi // 2) * 32 + 4], in_=ps[:, gi * 4:gi * 4 + 4]),
                         reads=[psB], writes=[k.GtokB], part=True)
    P.barrier()
    A.reset(m2)
    if k.stop == ("tokmaj", l):
        A.reset(m0)
        return
    stg = [A.alloc([128, NT], BF16, f"stg{i}") for i in range(2)]
    glist = []
    for name, ncols, fn in FSEG:
        for g in range(ncols // 512):
            glist.append((name, WCOL[name] + g * 512, FROW[name] + g * 512, fn))
    cur = load_w(glist[0][1], 512)
    nst = 0
    for gi, (name, c0, r0, fn) in enumerate(glist):
        w, wB = cur
        if gi + 1 < len(glist):
            cur = load_w(glist[gi + 1][1], 512)
        for j in range(4):
            st, stB = stg[nst % 2]
            nst += 1
            for (t0, n) in BLKS:
                bi = nps[0] % 4
                nps[0] += 1
                ps, psB = k.pb[bi], k.pbB[bi]
                for kc in range(KC):
                    P.op("pe", C("matmul", ps[:, 0:n], lhsT=w[:, kc, j * 128:(j + 1) * 128], rhs=hT[:, kc, t0:t0 + n],
                                                                                    start=(kc == 0), stop=(kc == KC - 1)),
                         reads=[hTB, wB], writes=[psB])
                if fn == "silu":
                    P.op("act", C("activation", out=st[:, t0:t0 + n], in_=ps[:, 0:n], func=AF.Silu),
                         reads=[psB], writes=[stB], part=True)
                elif fn == "sig":
                    P.op("act", C("activation", out=st[:, t0:t0 + n], in_=ps[:, 0:n], func=AF.Sigmoid),
                         reads=[psB], writes=[stB], part=True)
                elif fn == "copy16":
                    P.op("dve", C("tensor_scalar", out=st[:, t0:t0 + n], in0=ps[:, 0:n], scalar1=0.0625, scalar2=None, op0=ALU.mult),
                         reads=[psB], writes=[stB], part=True)
                else:
                    P.op("dve", C("tensor_copy", out=st[:, t0:t0 + n], in_=ps[:, 0:n]),
                         reads=[psB], writes=[stB], part=True)
            P.op("sp", C("dma_start", out=S["F"][r0 + j * 128:r0 + (j + 1) * 128, :], in_=st),
                 reads=[stB], writes=[k.SB["F"]], dma=True, part=True)
    P.barrier()
    A.reset(m0)


def phase_attn(k, l):
    P, A, I, S = k.P, k.A, k.I, k.S
    m0 = A.mark()
    KTs, KTB = A.alloc([128, 2, NT], BF16, "KTs")
    Vs, VB = A.alloc([128, NTI, 256], BF16, "Vs")
    Qb = [A.alloc([128, 8, 512], BF16, f"Qb{i}") for i in range(2)]
    AZ = [A.alloc([128, 8, 512], BF16, f"AZ{i}") for i in range(2)]
    PT = [A.alloc([128, 512], BF16, f"PT{i}") for i in range(3)]
    rec, recB = A.alloc([128, 512], F32, "rec")
    t4, t4B = A.alloc([128, 512], F32, "t4")
    ost = [A.alloc([128, 8, 512], BF16, f"ost{i}") for i in range(2)]
    P.op("sp", C("dma_start", out=KTs, in_=S["KT"].rearrange("(h d) t -> d h t", d=128)), reads=[k.SB["KT"]], writes=[KTB], dma=True)
    P.op("sp", C("dma_start", out=Vs, in_=S["Vt"].rearrange("(i p) c -> p i c", p=128)), reads=[k.SB["Vt"]], writes=[VB], dma=True)
    blocks = []
    if l < DEPTH - 1:
        blocks.append((0, 256, [0, 1]))
    for j in range(8):
        blocks.append((256 + 512 * j, 512, list(range(NTI))))
    QTv = S["QT"].rearrange("(h d) t -> d h t", d=128)
    AZv = S["F"][FROW["az"]:FROW["az"] + 1024, :].rearrange("(h d) t -> d h t", d=128)
    Yv = S["Y"][0].rearrange("(h d) t -> d h t", d=128)
    ns = [0]
    npt = [0]
    for bi, (t0, n, keys) in enumerate(blocks):
        q_, qB = Qb[bi % 2]
        az_, azB = AZ[bi % 2]
        o_, oB = ost[bi % 2]
        P.op("sp", C("dma_start", out=q_[:, :, 0:n], in_=QTv[:, :, t0:t0 + n]), reads=[k.SB["QT"]], writes=[qB], dma=True)
        P.op("sp", C("dma_start", out=az_[:, :, 0:n], in_=AZv[:, :, t0:t0 + n]), reads=[k.SB["F"]], writes=[azB], dma=True)
        nk = len(keys)
        for h in range(8):
            kv = h // 4
            psO, psOB = k.pb[4 + (h % 2)], k.pbB[4 + (h % 2)]
            psD, psDB = k.pb[6 + (h % 2)], k.pbB[6 + (h % 2)]
            sbank = []

            def emitS(idx):
                kt = keys[idx]
                b = ns[0] % 2
                ns[0] += 1
                sbank.append(b)
                P.op("pe", C("matmul", k.pb[b][:, 0:n], lhsT=KTs[:, kv, kt * 128:(kt + 1) * 128], rhs=q_[:, h, 0:n], start=True, stop=True),
                     reads=[KTB, qB], writes=[k.pbB[b]])
            emitS(0)
            for idx in range(nk):
                if idx + 1 < nk:
                    emitS(idx + 1)
                kt = keys[idx]
                b = sbank[idx]
                p_, pB = PT[npt[0] % 3]
                npt[0] += 1
                P.op("act", C("activation", out=p_[:, 0:n], in_=k.pb[b][:, 0:n], func=AF.Exp), reads=[k.pbB[b]], writes=[pB])
                P.op("pe", C("matmul", psO[:, 0:n], lhsT=Vs[:, kt, kv * 128:(kv + 1) * 128], rhs=p_[:, 0:n],
                                                                    start=(idx == 0), stop=(idx == nk - 1)), reads=[VB, pB], writes=[psOB])
                P.op("pe", C("matmul", psD[:, 0:n], lhsT=k.onesb, rhs=p_[:, 0:n], start=(idx == 0), stop=(idx == nk - 1)),
                     reads=[k.onesbB, pB], writes=[psDB])
            P.op("dve", C("reciprocal", out=rec[:, 0:n], in_=psD[:, 0:n]), reads=[psDB], writes=[recB])
            P.op("dve", C("tensor_tensor", out=t4[:, 0:n], in0=psO[:, 0:n], in1=rec[:, 0:n], op=ALU.mult), reads=[psOB, recB], writes=[t4B])
            P.op("pool", C("tensor_tensor", out=o_[:, h, 0:n], in0=t4[:, 0:n], in1=az_[:, h, 0:n], op=ALU.mult),
                 reads=[t4B, azB], writes=[oB], part=True)
        P.op("sp", C("dma_start", out=Yv[:, :, t0:t0 + n], in_=o_[:, :, 0:n]), reads=[oB], writes=[k.SB["Y"]], dma=True, part=True)
    P.barrier()
    A.reset(m0)


def phase_mlstm(k, l):
    P, A, I, S = k.P, k.A, k.I, k.S
    last_layer = (l == DEPTH - 1)
    m0 = A.mark()
    WC, WCB = A.alloc([128, NTI, 16], F32, "WC")
    DECB, DECBB = A.alloc([128, 8, NTI], F32, "DECB")
    m1 = A.mark()
    GI, GIB = A.alloc([64, NT], F32, "GI")
    GF, GFB = A.alloc([64, NT], F32, "GF")
    ONE, ONEB = A.alloc([64, NT], F32, "ONE")
    BP, BPB = A.alloc([64, NT], F32, "BP")
    AP_, APB = A.alloc([64, NT], F32, "APr")
    MM, MMB = A.alloc([64, NT], F32, "MM")
    M2, M2B = A.alloc([64, NT], F32, "M2")
    WR, WRB = A.alloc([64, NT], F32, "WR")
    CL, CLB = A.alloc([64, NT], F32, "CL")
    gb, gbB = A.alloc([64, 2], F32, "gb")
    dec, decB = A.alloc([64, NTI], F32, "dec")
    sel, selB = A.alloc([64, 8, 128], F32, "sel")
    P.op("sp", C("dma_start", out=gb, in_=I["gbias"][l]), writes=[gbB], dma=True)
    P.op("sp", C("dma_start", out=sel, in_=I["sel"]), writes=[selB], dma=True)
    for t_, tB in ((GI, GIB), (GF, GFB), (dec, decB)):
        P.op("pool", C("memset", t_, 0.0), writes=[tB])
    P.op("pool", C("memset", ONE, 1.0), writes=[ONEB])
    R = (slice(0, 4), slice(32, 36))
    nb = 0
    for (t0, n) in BLKS:
        bt0 = (t0 - 256) if t0 >= 256 else 4096
        for gf, (dst, dstB) in enumerate(((GI, GIB), (GF, GFB))):
            bi = nb % 4
            nb += 1
            ps, psB = k.pb[bi], k.pbB[bi]
            for j in range(n // 128):
                i = t0 // 128 + j
                P.op("pe", C("transpose", ps[0:36, j * 128:(j + 1) * 128], k.Gtok[:, i, gf, :], k.idf),
                     reads=[k.GtokB, k.idfB], writes=[psB])
            P.op("act", C("activation", out=dst[0:4, t0:t0 + n], in_=ps[0:4, 0:n], func=AF.Identity,
                                                                               bias=gb[0:4, gf:gf + 1], scale=1.0),
                 reads=[psB, gbB], writes=[dstB], part=True)
            P.op("act", C("activation", out=dst[32:36, bt0:bt0 + n], in_=ps[32:36, 0:n], func=AF.Identity,
                                                                                 bias=gb[32:36, gf:gf + 1], scale=1.0),
                 reads=[psB, gbB], writes=[dstB], part=True)
    P.op("act", C("activation", out=GF[0:36, :], in_=GF[0:36, :], func=AF.Exp, scale=-1.0), reads=[GFB], writes=[GFB])
    P.op("act", C("activation", out=GF[0:36, :], in_=GF[0:36, :], func=AF.Ln, bias=1.0, scale=1.0), reads=[GFB], writes=[GFB])
    P.op("dve", C("tensor_tensor_scan", out=BP[0:36, :], data0=ONE[0:36, :], data1=GF[0:36, :], initial=0.0, op0=ALU.mult, op1=ALU.add),
         reads=[ONEB, GFB], writes=[BPB])
    P.op("dve", C("tensor_scalar", out=M2[32:36, :], in0=BP[32:36, :], scalar1=BP[32:36, NT - 1:NT], scalar2=-1.0, op0=ALU.subtract, op1=ALU.mult),
         reads=[BPB], writes=[M2B])
    P.op("dve", C("tensor_tensor", out=BP[32:36, :], in0=M2[32:36, :], in1=GF[32:36, :], op=ALU.add), reads=[M2B, GFB], writes=[BPB])
    P.op("dve", C("tensor_tensor", out=AP_[0:36, :], in0=GI[0:36, :], in1=BP[0:36, :], op=ALU.add), reads=[GIB, BPB], writes=[APB])
    P.op("dve", C("tensor_tensor_scan", out=MM[0:4, :], data0=ONE[0:4, :], data1=AP_[0:4, :], initial=-1e30, op0=ALU.mult, op1=ALU.max),
         reads=[ONEB, APB], writes=[MMB], part=True)
    src, srcB = AP_, APB
    bufs = [(M2, M2B), (MM, MMB)]
    sh = 1
    step = 0
    while sh < NT:
        dst, dstB = bufs[step % 2]
        P.op("dve", C("tensor_tensor", out=dst[32:36, 0:NT - sh], in0=src[32:36, 0:NT - sh], in1=src[32:36, sh:NT], op=ALU.max),
             reads=[srcB], writes=[dstB], part=True)
        P.op("pool", C("tensor_copy", out=dst[32:36, NT - sh:NT], in_=src[32:36, NT - sh:NT]),
             reads=[srcB], writes=[dstB], part=True)
        src, srcB = dst, dstB
        sh *= 2
        step += 1
    if src is not MM:
        P.op("dve", C("tensor_copy", out=MM[32:36, :], in_=src[32:36, :]), reads=[srcB], writes=[MMB], part=True)

    def v3(t_, r):
        return t_[r, :].rearrange("p (c t) -> p c t", t=128)
    for d, r in enumerate(R):
        li = 127 if d == 0 else 0
        mlast = v3(MM, r)[:, :, li:li + 1].to_broadcast([4, NTI, 128])
        P.op("dve", C("tensor_tensor", out=v3(WR, r), in0=v3(AP_, r), in1=mlast, op=ALU.subtract), reads=[APB, MMB], writes=[WRB], part=True)
        P.op("act", C("activation", out=WR[r, :], in_=WR[r, :], func=AF.Exp), reads=[WRB], writes=[WRB], part=True)
        P.op("dve", C("tensor_tensor", out=v3(CL, r), in0=v3(BP, r), in1=mlast, op=ALU.subtract), reads=[BPB, MMB], writes=[CLB], part=True)
        P.op("act", C("activation", out=CL[r, :], in_=CL[r, :], func=AF.Exp), reads=[CLB], writes=[CLB], part=True)
        ml2 = v3(MM, r)[:, :, li]
        if d == 0:
            P.op("dve", C("tensor_tensor", out=dec[r, 1:NTI], in0=ml2[:, 0:NTI - 1], in1=ml2[:, 1:NTI], op=ALU.subtract),
                 reads=[MMB], writes=[decB], part=True)
            P.op("act", C("activation", out=dec[r, 1:NTI], in_=dec[r, 1:NTI], func=AF.Exp), reads=[decB], writes=[decB], part=True)
        else:
            P.op("dve", C("tensor_tensor", out=dec[r, 0:NTI - 1], in0=ml2[:, 1:NTI], in1=ml2[:, 0:NTI - 1], op=ALU.subtract),
                 reads=[MMB], writes=[decB], part=True)
            P.op("act", C("activation", out=dec[r, 0:NTI - 1], in_=dec[r, 0:NTI - 1], func=AF.Exp), reads=[decB], writes=[decB], part=True)
    for q in range(8):
        P.op("pe", C("matmul", k.pb[0][:, q * NTI:(q + 1) * NTI], lhsT=sel[0:36, q, :], rhs=dec[0:36, :], start=True, stop=True),
             reads=[selB, decB], writes=[k.pbB[0]])
    P.op("dve", C("tensor_copy", out=DECB, in_=k.pb[0][:, 0:8 * NTI].rearrange("p (q c) -> p q c", c=NTI)), reads=[k.pbB[0]], writes=[DECBB])
    for half in range(2):
        tiles = list(range(half * 17, half * 17 + 17))
        ps, psB = k.pb[1 + half], k.pbB[1 + half]
        for jj, i in enumerate(tiles):
            fc = i * 128
            bc = (i - 2) * 128 if i >= 2 else 4096 + i * 128
            for qq, (src, srcB, r, c0) in enumerate(((WR, WRB, R[0], fc), (WR, WRB, R[1], bc), (CL, CLB, R[0], fc), (CL, CLB, R[1], bc))):
                P.op("pe", C("transpose", ps[:, jj * 16 + qq * 4: jj * 16 + qq * 4 + 4], src[r, c0:c0 + 128], k.idf[r, r]),
                     reads=[srcB, k.idfB], writes=[psB])
        P.op("dve", C("tensor_copy", out=WC[:, half * 17:half * 17 + 17, :], in_=ps[:, 0:17 * 16].rearrange("p (i q) -> p i q", q=16)),
             reads=[psB], writes=[WCB], part=True)
    P.barrier()
    A.reset(m1)
    msk, mskB = A.alloc([128, 2, 128], F32, "msk")
    mgb, mgbB = A.alloc([128, 1024], F32, "mgb")
    P.op("sp", C("dma_start", out=msk, in_=I["masks"].rearrange("m s t -> s m t")), writes=[mskB], dma=True)
    P.op("sp", C("dma_start", out=mgb, in_=I["mgain"][l, :].partition_broadcast(128)), writes=[mgbB], dma=True)
    Fq = S["F"][FROW["mq"]:FROW["mq"] + 1024, :].rearrange("(a p) t -> p a t", p=128)
    Fk = S["F"][FROW["mk"]:FROW["mk"] + 1024, :].rearrange("(a p) t -> p a t", p=128)
    Fo = S["F"][FROW["mo"]:FROW["mo"] + 1024, :].rearrange("(a p) t -> p a t", p=128)
    Fz = S["F"][FROW["mz"]:FROW["mz"] + 1024, :].rearrange("(a p) t -> p a t", p=128)
    Yb = S["Y"][1].rearrange("(a p) t -> p a t", p=128)
    HB_ = [[Buf(f"H{d}_{i}") for i in range(NTI)] for d in range(2)]
    BD = []
    for d in range(2):
        b = K()
        b.Cf, _ = A.alloc([128, 4, 2, 257], F32, f"Cf{d}")
        b.Ct, _ = A.alloc([128, 4, 2, 257], BF16, f"Ct{d}")
        b.CfBs = [Buf(f"Cf{d}{h}") for h in range(4)]
        b.CtBs = [Buf(f"Ct{d}{h}") for h in range(4)]
        b.qT = [A.alloc([128, 8, 128], BF16, f"qT{d}{i}") for i in range(2)]
        b.kT = [A.alloc([128, 8, 128], BF16, f"kT{d}{i}") for i in range(2)]
        b.ktk = [A.alloc([128, 1024], BF16, f"ktk{d}{i}") for i in range(2)]
        b.vtk = [A.alloc([128, 4, 257], BF16, f"vtk{d}{i}") for i in range(2)]
        b.moT = [A.alloc([128, 8, 128], BF16, f"moT{d}{i}") for i in range(2)]
        b.mzT = [A.alloc([128, 8, 128], BF16, f"mzT{d}{i}") for i in range(2)]
        b.hfl = [A.alloc([128, 1024], BF16, f"hfl{d}{i}") for i in range(2)]
        b.Sm = [A.alloc([128, 128], BF16, f"Sm{d}{i}") for i in range(2)]
        b.vw = [A.alloc([128, 257], BF16, f"vw{d}{i}") for i in range(2)]
        b.hst = [A.alloc([128, 4, 256], BF16, f"hst{d}{i}") for i in range(2)]
        b.hs, b.hsB = A.alloc([128, 4, 256], F32, f"hs{d}")
        b.hj, b.hjB = A.alloc([128, 1024], F32, f"hj{d}")
        b.hb, b.hbB = A.alloc([128, 1024], BF16, f"hb{d}")
        b.hss, b.hssB = A.alloc([128, 4], F32, f"hss{d}")
        b.dn = [A.alloc([128, 2], F32, f"dn{d}{h}") for h in range(4)]
        b.tT, b.tTB = A.alloc([128, 8, 128], F32, f"tT{d}")
        b.yst = [A.alloc([128, 8, 128], BF16, f"yst{d}{i}") for i in range(2)]
        b.cnt = dict(sm=0, vw=0, y=0)
        for v_, vB in b.vtk:
            P.op("pool", C("memset", v_, 1.0), writes=[vB])
        BD.append(b)
    orders = [list(range(NTI)), [1, 0] + list(range(NTI - 1, 1, -1))]

    def chunk(d, step, i):
        b = BD[d]
        is_ctx = i < 2
        need_out = not (is_ctx and last_layer)
        if is_ctx:
            first = (i == 0) if d == 0 else (i == 1)
        else:
            first = (i <= 17) if d == 0 else (i > 17)
        combine = need_out and not first
        cidx = i if d == 0 else ((i - 2) if i >= 2 else 32 + i)
        sl = slice(i * 128, (i + 1) * 128)
        q_, qB = b.qT[step % 2]
        k_, kB = b.kT[step % 2]
        kt_, ktB = b.ktk[step % 2]
        v_, vB = b.vtk[step % 2]
        P.op("sp", C("dma_start", out=q_, in_=Fq[:, :, sl]), reads=[k.SB["F"]], writes=[qB], dma=True)
        P.op("sp", C("dma_start", out=k_, in_=Fk[:, :, sl]), reads=[k.SB["F"]], writes=[kB], dma=True)
        P.op("sp", C("dma_start", out=kt_, in_=S["MKt"][sl, :]), reads=[k.SB["MKt"]], writes=[ktB], dma=True)
        P.op("sp", C("dma_start", out=v_[:, :, 0:256], in_=S["MVt"][sl, :].rearrange("t (h e) -> t h e", e=256)),
             reads=[k.SB["MVt"]], writes=[vB], dma=True, part=True)
        if combine:
            o_, oB = b.moT[step % 2]
            z_, zB = b.mzT[step % 2]
            f_, fB = b.hfl[step % 2]
            P.op("sp", C("dma_start", out=o_, in_=Fo[:, :, sl]), reads=[k.SB["F"]], writes=[oB], dma=True)
            P.op("sp", C("dma_start", out=z_, in_=Fz[:, :, sl]), reads=[k.SB["F"]], writes=[zB], dma=True)
            P.op("sp", C("dma_start", out=f_, in_=S["HF"][1 - d][sl, :]), reads=[HB_[1 - d][i]], writes=[fB], dma=True)
        h_, hB = b.hst[step % 2]
        bS, bP, bU = d, 2 + d, 4 + 2 * d
        psS, psSB = k.pb[bS], k.pbB[bS]
        psP, psPB = k.pb[bP], k.pbB[bP]
        for hh in range(4):
            qi = d * 4 + hh
            for dc in range(2):
                P.op("pe", C("matmul", psS[:, 0:128], lhsT=k_[:, hh * 2 + dc, :], rhs=q_[:, hh * 2 + dc, :], start=(dc == 0), stop=(dc == 1)),
                     reads=[kB, qB], writes=[psSB])
            sm_, smB = b.Sm[b.cnt["sm"] % 2]
            b.cnt["sm"] += 1
            P.op("dve", C("tensor_tensor", out=sm_, in0=psS[:, 0:128], in1=msk[:, d, :], op=ALU.mult), reads=[psSB, mskB], writes=[smB])
            vw_, vwB = b.vw[b.cnt["vw"] % 2]
            b.cnt["vw"] += 1
            P.op("pool", C("tensor_scalar", out=vw_, in0=v_[:, hh, :], scalar1=WC[:, i, qi:qi + 1], scalar2=None, op0=ALU.mult),
                 reads=[vB, WCB], writes=[vwB])
            if step > 0:
                P.op("act", C("activation", out=b.Ct[:, hh], in_=b.Cf[:, hh], func=AF.Copy, scale=DECB[:, qi, cidx:cidx + 1]),
                     reads=[b.CfBs[hh], DECBB], writes=[b.CtBs[hh]])
            P.op("pe", C("matmul", psP[:, 0:257], lhsT=sm_, rhs=vw_, start=True, stop=(step == 0)), reads=[smB, vwB], writes=[psPB])
            if step > 0:
                for dc in range(2):
                    P.op("pe", C("matmul", psP[:, 0:257], lhsT=q_[:, hh * 2 + dc, :], rhs=b.Ct[:, hh, dc, :], start=False, stop=(dc == 1)),
                         reads=[qB, b.CtBs[hh]], writes=[psPB])
            for dc in range(2):
                psU, psUB = k.pb[bU + dc], k.pbB[bU + dc]
                P.op("pe", C("matmul", psU[:, 0:257], lhsT=kt_[:, hh * 256 + dc * 128: hh * 256 + (dc + 1) * 128], rhs=vw_, start=True, stop=True),
                     reads=[ktB, vwB], writes=[psUB])
                if step == 0:
                    P.op("dve", C("tensor_copy", out=b.Cf[:, hh, dc, :], in_=psU[:, 0:257]), reads=[psUB], writes=[b.CfBs[hh]], part=(dc == 1))
                else:
                    P.op("dve", C("scalar_tensor_tensor", out=b.Cf[:, hh, dc, :], in0=b.Cf[:, hh, dc, :], scalar=DECB[:, qi, cidx:cidx + 1], in1=psU[:, 0:257],
                                  op0=ALU.mult, op1=ALU.add), reads=[psUB, b.CfBs[hh], DECBB], writes=[b.CfBs[hh]], part=(dc == 1))
            if need_out:
                dn, dnB = b.dn[hh]
                P.op("dve", C("tensor_scalar", out=dn[:, 1:2], in0=psP[:, 256:257], scalar1=WC[:, i, 8 + qi:9 + qi], scalar2=None, op0=ALU.max),
                     reads=[psPB, WCB], writes=[dnB])
                P.op("dve", C("scalar_tensor_tensor", out=dn[:, 0:1], in0=psP[:, 256:257], scalar=-1.0, in1=dn[:, 1:2], op0=ALU.mult, op1=ALU.max),
                     reads=[psPB, dnB], writes=[dnB])
                P.op("dve", C("reciprocal", out=dn[:, 1:2], in_=dn[:, 0:1]), reads=[dnB], writes=[dnB])
                if not combine:
                    P.op("act", C("activation", out=h_[:, hh, :], in_=psP[:, 0:256], func=AF.Copy, scale=dn[:, 1:2]),
                         reads=[psPB, dnB], writes=[hB], part=(hh > 0))
                else:
                    P.op("dve", C("scalar_tensor_tensor", out=b.hs[:, hh, :], in0=psP[:, 0:256], scalar=dn[:, 1:2],
                                  in1=f_[:, hh * 256:(hh + 1) * 256], op0=ALU.mult, op1=ALU.add),
                         reads=[psPB, dnB, fB], writes=[b.hsB], part=(hh > 0))
        if need_out and not combine:
            P.op("sp", C("dma_start", out=S["HF"][d][sl, :], in_=h_.rearrange("p h e -> p (h e)")), reads=[hB], writes=[HB_[d][i]], dma=True)
        if combine:
            hs2 = b.hs.rearrange("p h e -> p (h e)")
            P.op("act", C("activation", out=b.hj, in_=hs2, func=AF.Square), reads=[b.hsB], writes=[b.hjB])
            P.op("dve", C("tensor_reduce", out=b.hss, in_=b.hj.rearrange("p (h e) -> p h e", e=256), axis=AX.X, op=ALU.add), reads=[b.hjB], writes=[b.hssB])
            P.op("dve", C("tensor_scalar", out=b.hss, in0=b.hss, scalar1=1.0 / 256, scalar2=EPS, op0=ALU.mult, op1=ALU.add), reads=[b.hssB], writes=[b.hssB])
            P.op("act", C("activation", out=b.hss, in_=b.hss, func=AF.Sqrt), reads=[b.hssB], writes=[b.hssB])
            P.op("dve", C("reciprocal", out=b.hss, in_=b.hss), reads=[b.hssB], writes=[b.hssB])
            hn = b.hj.rearrange("p (h e) -> p h e", e=256)
            P.op("dve", C("tensor_tensor", out=hn, in0=b.hs, in1=b.hss.unsqueeze(2).to_broadcast([128, 4, 256]), op=ALU.mult), reads=[b.hsB, b.hssB, b.hjB], writes=[b.hjB])
            P.op("pool", C("tensor_tensor", out=b.hb, in0=b.hj, in1=mgb, op=ALU.mult), reads=[b.hjB, mgbB], writes=[b.hbB])
            pbf = psP.bitcast(BF16)
            for cc in range(8):
                P.op("pe", C("transpose", pbf[:, cc * 128:(cc + 1) * 128], b.hb[:, cc * 128:(cc + 1) * 128], k.idb),
                     reads=[b.hbB, k.idbB], writes=[psPB])
            P.op("dve", C("tensor_tensor", out=b.tT, in0=pbf.rearrange("p (a t) -> p a t", t=128), in1=o_, op=ALU.mult),
                 reads=[psPB, oB], writes=[b.tTB])
            y_, yB = b.yst[b.cnt["y"] % 2]
            b.cnt["y"] += 1
            P.op("pool", C("tensor_tensor", out=y_, in0=b.tT, in1=z_, op=ALU.mult), reads=[b.tTB, zB], writes=[yB])
            P.op("sp", C("dma_start", out=Yb[:, :, sl], in_=y_), reads=[yB], writes=[k.SB["Y"]], dma=True, part=True)

    for step in range(NTI):
        for d in range(2):
            chunk(d, step, orders[d][step])
    P.barrier()
    A.reset(m0)


def phase_conv_pool(k, l):
    P, A, I, S = k.P, k.A, k.I, k.S
    m0 = A.mark()
    cw, cwB = A.alloc([128, 8, 3], F32, "cw")
    P.op("sp", C("dma_start", out=cw, in_=I["convw"][l]), writes=[cwB], dma=True)
    inb = [[A.alloc([128, NT], BF16, f"cv{j}_{i}") for j in range(4)] for i in range(2)]
    ap_, apB = A.alloc([128, NT + 2], F32, "apad")
    y_, yB = A.alloc([128, NT], F32, "ycv")
    y2, y2B = A.alloc([128, NT], F32, "ycv2")
    ost = [A.alloc([128, NT], BF16, f"cvo{i}") for i in range(2)]
    P.op("pool", C("memset", ap_, 0.0), writes=[apB])
    names = ("cu", "cc", "cb", "cz")
    for cc in range(8):
        tl = inb[cc % 2]
        for j, nm in enumerate(names):
            t_, tB = tl[j]
            r0 = FROW[nm] + cc * 128
            P.op("sp", C("dma_start", out=t_, in_=S["F"][r0:r0 + 128, :]), reads=[k.SB["F"]], writes=[tB], dma=True)
        (cu, cuB), (cg, cgB), (cb, cbB), (cz, czB) = tl
        w0, w1, w2 = cw[:, cc, 0:1], cw[:, cc, 1:2], cw[:, cc, 2:3]
        P.op("pool", C("tensor_tensor", out=ap_[:, 1:NT + 1], in0=cu, in1=cg, op=ALU.mult), reads=[cuB, cgB], writes=[apB])
        P.op("dve", C("tensor_scalar", out=y_, in0=ap_[:, 1:NT + 1], scalar1=w1, scalar2=None, op0=ALU.mult), reads=[apB, cwB], writes=[yB])
        P.op("dve", C("scalar_tensor_tensor", out=y_, in0=ap_[:, 0:NT], scalar=w0, in1=y_, op0=ALU.mult, op1=ALU.add), reads=[apB, cwB, yB], writes=[yB])
        P.op("dve", C("scalar_tensor_tensor", out=y_, in0=ap_[:, 2:NT + 2], scalar=w2, in1=y_, op0=ALU.mult, op1=ALU.add), reads=[apB, cwB, yB], writes=[yB])
        P.op("dve", C("tensor_scalar", out=y_[:, 255:256], in0=ap_[:, 255:256], scalar1=w0, scalar2=None, op0=ALU.mult), reads=[apB, cwB, yB], writes=[yB])
        P.op("dve", C("scalar_tensor_tensor", out=y_[:, 255:256], in0=ap_[:, 256:257], scalar=w1, in1=y_[:, 255:256], op0=ALU.mult, op1=ALU.add), reads=[apB, cwB, yB], writes=[yB])
        P.op("dve", C("tensor_scalar", out=y_[:, 256:257], in0=ap_[:, 257:258], scalar1=w1, scalar2=None, op0=ALU.mult), reads=[apB, cwB, yB], writes=[yB])
        P.op("dve", C("scalar_tensor_tensor", out=y_[:, 256:257], in0=ap_[:, 258:259], scalar=w2, in1=y_[:, 256:257], op0=ALU.mult, op1=ALU.add), reads=[apB, cwB, yB], writes=[yB])
        P.op("pool", C("tensor_tensor", out=y2, in0=y_, in1=cb, op=ALU.mult), reads=[yB, cbB], writes=[y2B])
        o_, oB = ost[cc % 2]
        P.op("pool", C("tensor_tensor", out=o_, in0=y2, in1=cz, op=ALU.mult), reads=[y2B, czB], writes=[oB])
        P.op("sp", C("dma_start", out=S["Y"][2][cc * 128:(cc + 1) * 128, :], in_=o_), reads=[oB], writes=[k.SB["Y"]], dma=True, part=True)
    P.barrier()
    A.reset(m0)
    OC, OL = 8, 8 + 256 + 16
    PW = OL + NL + 16
    psc, pscB = A.alloc([128, 8], F32, "psc")
    P.op("sp", C("dma_start", out=psc, in_=I["pscale"][l]), writes=[pscB], dma=True)
    pub = [A.alloc([128, NT], BF16, f"pu{i}") for i in range(2)]
    pzb = [A.alloc([128, NT], BF16, f"pz{i}") for i in range(2)]
    up = [A.alloc([128, PW], F32, f"up{i}") for i in range(2)]
    sa, saB = A.alloc([128, PW], F32, "sa")
    sb_, sbB = A.alloc([128, PW], F32, "sb")
    rcb, rcbB = A.alloc([128, NT], F32, "rcb")
    dT = [A.alloc([128, NT], BF16, f"dT{i}") for i in range(2)]
    pw = [A.alloc([128, 2, 256], BF16, f"pw{i}") for i in range(2)]
    yo = [A.alloc([128, NT], BF16, f"ypo{i}") for i in range(2)]
    for u_, uB in up:
        P.op("pool", C("memset", u_, 0.0), writes=[uB])
    P.op("pool", C("memset", sa, 0.0), writes=[saB])
    P.op("pool", C("memset", sb_, 0.0), writes=[sbB])
    nps = 0
    for g, w in enumerate((2, 4, 8, 16)):
        P.op("sp", C("dma_start", out=rcb, in_=I["rcnt"][g, :].partition_broadcast(128)), writes=[rcbB], dma=True)
        pw_, pwB = pw[g % 2]
        P.op("pool", C("dma_start", out=pw_, in_=I["pool_w"][l, g].rearrange("(kc p) o -> p kc o", p=128)), writes=[pwB], dma=True)
        for kc2 in range(2):
            ct = g * 2 + kc2
            pu_, puB = pub[kc2]
            u_, uB = up[kc2]
            d_, dB = dT[kc2]
            P.op("sp", C("dma_start", out=pu_, in_=S["F"][FROW["pu"] + ct * 128:FROW["pu"] + (ct + 1) * 128, :]), reads=[k.SB["F"]], writes=[puB], dma=True)
            P.op("pool", C("tensor_copy", out=u_[:, OC:OC + NC], in_=pu_[:, 0:NC]), reads=[puB], writes=[uB], part=True)
            P.op("pool", C("tensor_copy", out=u_[:, OL:OL + NL], in_=pu_[:, NC:NT]), reads=[puB], writes=[uB], part=True)
            cur, curB = u_, uB
            m = 1
            pp = [(sa, saB), (sb_, sbB)]
            si = 0
            while m < w:
                nx, nxB = pp[si % 2]
                si += 1
                P.op("dve", C("tensor_tensor", out=nx[:, 0:PW - m], in0=cur[:, 0:PW - m], in1=cur[:, m:PW], op=ALU.add), reads=[curB], writes=[nxB])
                cur, curB = nx, nxB
                m *= 2
            hw_ = w // 2
            for (po, to, n) in ((OC, 0, NC), (OL, NC, NL)):
                P.op("dve", C("tensor_tensor", out=sa[:, po:po + n] if cur is not sa else sb_[:, po:po + n],
                                                                                        in0=cur[:, po - hw_:po - hw_ + n], in1=rcb[:, to:to + n], op=ALU.mult),
                     reads=[curB, rcbB], writes=[saB if cur is not sa else sbB])
                tmpb, tmpB = (sa, saB) if cur is not sa else (sb_, sbB)
                P.op("pool", C("tensor_tensor", out=d_[:, to:to + n], in0=tmpb[:, po:po + n], in1=u_[:, po:po + n], op=ALU.subtract),
                     reads=[tmpB, uB], writes=[dB], part=True)
        for oc in range(2):
            ct = g * 2 + oc
            pz_, pzB = pzb[oc]
            o_, oB = yo[oc]
            P.op("sp", C("dma_start", out=pz_, in_=S["F"][FROW["pz"] + ct * 128:FROW["pz"] + (ct + 1) * 128, :]), reads=[k.SB["F"]], writes=[pzB], dma=True)
            for (t0, n) in BLKS:
                bi = nps % 4
                nps += 1
                ps, psB = k.pb[bi], k.pbB[bi]
                for kc2 in range(2):
                    P.op("pe", C("matmul", ps[:, 0:n], lhsT=pw_[:, kc2, oc * 128:(oc + 1) * 128], rhs=dT[kc2][0][:, t0:t0 + n],
                                                                                          start=(kc2 == 0), stop=(kc2 == 1)), reads=[pwB, dT[kc2][1]], writes=[psB])
                P.op("dve", C("scalar_tensor_tensor", out=o_[:, t0:t0 + n], in0=ps[:, 0:n], scalar=psc[:, ct:ct + 1], in1=pz_[:, t0:t0 + n],
                                                                                                 op0=ALU.mult, op1=ALU.mult), reads=[psB, pscB, pzB], writes=[oB], part=True)
            P.op("sp", C("dma_start", out=S["Y"][3][ct * 128:(ct + 1) * 128, :], in_=o_), reads=[oB], writes=[k.SB["Y"]], dma=True, part=True)
    P.barrier()
    A.reset(m0)


def phase_merge_out(k, l):
    P, A, I, S = k.P, k.A, k.I, k.S
    last_layer = (l == DEPTH - 1)
    blocks = BLKS[1:] if last_layer else BLKS
    m0 = A.mark()
    wbr, _ = A.alloc([128, 16, 4 * 8 * 128], BF16, "wbr")
    wbrB = [Buf(f"wbr{ct}") for ct in range(16)]
    Yb, YbB = A.alloc([128, 4, 8, 512], BF16, "Yb")
    gm = [A.alloc([128, 4, 512], BF16, f"gm{i}") for i in range(2)]
    tm = [A.alloc([128, 512], F32, f"tm{i}") for i in range(4)]
    ast = [A.alloc([128, 512], BF16, f"ast{i}") for i in range(2)]
    for ct in range(16):
        P.op("pool", C("dma_start", out=wbr[:, ct, :], in_=I["wbt"][l, ct]), writes=[wbrB[ct]], dma=True)
    Yv = S["Y"].rearrange("b (kc p) t -> p b kc t", p=128)
    Gv = S["F"][FROW["gm"]:FROW["gm"] + 4 * D, :].rearrange("(b c p) t -> p b c t", p=128, c=16)
    nset = 0
    for (t0, n) in blocks:
        P.op("sp", C("dma_start", out=Yb[:, :, :, 0:n], in_=Yv[:, :, :, t0:t0 + n]), reads=[k.SB["Y"]], writes=[YbB], dma=True)
        for ct in range(16):
            g_, gB = gm[ct % 2]
            a_, aB = ast[ct % 2]
            w_ = wbr[:, ct, :].rearrange("p (b k c) -> p b k c", b=4, k=8)
            P.op("sp", C("dma_start", out=g_[:, :, 0:n], in_=Gv[:, :, ct, t0:t0 + n]), reads=[k.SB["F"]], writes=[gB], dma=True)
            base = 4 * (nset % 2)
            nset += 1
            for br in range(4):
                ps, psB = k.pb[base + br], k.pbB[base + br]
                for kc in range(8):
                    P.op("pe", C("matmul", ps[:, 0:n], lhsT=w_[:, br, kc, :], rhs=Yb[:, br, kc, 0:n], start=(kc == 0), stop=(kc == 7)),
                         reads=[wbrB[ct], YbB], writes=[psB])
                P.op("dve", C("tensor_tensor", out=tm[br][0][:, 0:n], in0=ps[:, 0:n], in1=g_[:, br, 0:n], op=ALU.mult),
                     reads=[psB, gB], writes=[tm[br][1]])
            P.op("pool", C("tensor_tensor", out=tm[0][0][:, 0:n], in0=tm[0][0][:, 0:n], in1=tm[1][0][:, 0:n], op=ALU.add), reads=[tm[0][1], tm[1][1]], writes=[tm[0][1]])
            P.op("pool", C("tensor_tensor", out=tm[2][0][:, 0:n], in0=tm[2][0][:, 0:n], in1=tm[3][0][:, 0:n], op=ALU.add), reads=[tm[2][1], tm[3][1]], writes=[tm[2][1]])
            P.op("pool", C("tensor_tensor", out=a_[:, 0:n], in0=tm[0][0][:, 0:n], in1=tm[2][0][:, 0:n], op=ALU.add), reads=[tm[0][1], tm[2][1]], writes=[aB])
            P.op("sp", C("dma_start", out=S["ACC"][ct * 128:(ct + 1) * 128, t0:t0 + n], in_=a_[:, 0:n]), reads=[aB], writes=[k.SB["ACC"]], dma=True, part=True)
    P.barrier()
    A.reset(m0)
    wo, _ = A.alloc([128, KC, D], BF16, "wo")
    woB = [Buf(f"wo{cg}") for cg in range(4)]
    accT = [A.alloc([128, 16, 512], BF16, f"accT{i}") for i in range(2)]
    xb = [A.alloc([128, 4, D], F32, f"xblk{i}") for i in range(2)]
    gtb = [A.alloc([128, D], F32, f"gtb{i}") for i in range(2)]
    t5, t5B = A.alloc([128, 512], F32, "t5")
    ss, ssB = A.alloc([128, 4], F32, "fss")
    junk, junkB = A.alloc([128, D], BF16, "junkf")
    wov = I["w_out"][l].rearrange("(kc p) c -> p kc c", p=128)
    for cg in range(4):
        P.op("pool", C("dma_start", out=wo[:, :, cg * 512:(cg + 1) * 512], in_=wov[:, :, cg * 512:(cg + 1) * 512]), writes=[woB[cg]], dma=True)
    if last_layer:
        fgb, fgbB = A.alloc([128, D], F32, "fgb")
        P.op("sp", C("dma_start", out=fgb, in_=I["fgain"].partition_broadcast(128)), writes=[fgbB], dma=True)
    for r in range(2):
        P.op("sp", C("dma_start", out=gtb[r][0], in_=S["modd"][l, r, 2 * D:3 * D].partition_broadcast(128)), reads=[k.SB["modd"]], writes=[gtb[r][1]], dma=True)
    Av = S["ACC"].rearrange("(kc p) t -> p kc t", p=128)
    x1B = [k.SB["X1"]] if l > 0 else []
    nps = 0
    for bi, (t0, n) in enumerate(blocks):
        r = 1 if t0 < 256 else 0
        nti = n // 128
        a_, aB = accT[bi % 2]
        xblk, xblkB = xb[bi % 2]
        P.op("sp", C("dma_start", out=a_[:, :, 0:n], in_=Av[:, :, t0:t0 + n]), reads=[k.SB["ACC"]], writes=[aB], dma=True)
        for ti in range(nti):
            i = t0 // 128 + ti
            P.op("sp", C("dma_start", out=xblk[:, ti, :], in_=src_tile(k, l, i)), reads=x1B, writes=[xblkB], dma=True, part=(ti > 0))
        for ti in range(nti):
            i = t0 // 128 + ti
            for cg in range(4):
                b_ = nps % 8
                nps += 1
                ps, psB = k.pb[b_], k.pbB[b_]
                for kc in range(KC):
                    P.op("pe", C("matmul", ps, lhsT=a_[:, kc, ti * 128:(ti + 1) * 128], rhs=wo[:, kc, cg * 512:(cg + 1) * 512], start=(kc == 0), stop=(kc == KC - 1)),
                         reads=[aB, woB[cg]], writes=[psB])
                P.op("dve", C("tensor_tensor", out=t5, in0=ps, in1=gtb[r][0][:, cg * 512:(cg + 1) * 512], op=ALU.mult), reads=[psB, gtb[r][1]], writes=[t5B])
                P.op("pool", C("tensor_tensor", out=xblk[:, ti, cg * 512:(cg + 1) * 512], in0=xblk[:, ti, cg * 512:(cg + 1) * 512], in1=t5, op=ALU.add),
                     reads=[t5B, xblkB], writes=[xblkB], part=True)
            if not last_layer:
                P.op("sp", C("dma_start", out=S["X1"][i * 128:(i + 1) * 128, :], in_=xblk[:, ti, :]), reads=[xblkB], writes=[k.SB["X1"]], dma=True, part=True)
            else:
                P.op("act", C("activation", out=junk, in_=xblk[:, ti, :], func=AF.Square, accum_out=ss[:, ti:ti + 1]), reads=[xblkB], writes=[junkB, ssB])
                P.op("dve", C("tensor_scalar", out=ss[:, ti:ti + 1], in0=ss[:, ti:ti + 1], scalar1=1.0 / D, scalar2=EPS, op0=ALU.mult, op1=ALU.add), reads=[ssB], writes=[ssB])
                P.op("act", C("activation", out=ss[:, ti:ti + 1], in_=ss[:, ti:ti + 1], func=AF.Sqrt), reads=[ssB], writes=[ssB])
                P.op("dve", C("reciprocal", out=ss[:, ti:ti + 1], in_=ss[:, ti:ti + 1]), reads=[ssB], writes=[ssB])
                P.op("dve", C("scalar_tensor_tensor", out=xblk[:, ti, :], in0=xblk[:, ti, :], scalar=ss[:, ti:ti + 1], in1=fgb, op0=ALU.mult, op1=ALU.mult),
                     reads=[xblkB, ssB, fgbB], writes=[xblkB], part=True)
                P.op("sp", C("dma_start", out=S["out"][(i - 2) * 128:(i - 1) * 128, :], in_=xblk[:, ti, :]), reads=[xblkB], writes=[k.SB["out"]], dma=True, part=True)
    P.barrier()
    A.reset(m0)


def host_constants():
    C = {}
    C["ident"] = np.eye(128, dtype=np.float32)
    s = np.arange(128)[:, None]
    t = np.arange(128)[None, :]
    C["masks"] = np.stack([(s <= t), (s >= t)]).astype(np.float32)
    freq = (10000.0 ** (-np.arange(32, dtype=np.float32) / 32)).astype(np.float32)
    rope = np.zeros((NTI, 128, 2, 8, 32), np.float32)
    rope[:, :, 0] = 1.0
    for i in range(2, NTI):
        tt = (i - 2) * 128 + np.arange(128)
        row = (tt // 64).astype(np.float32)
        col = (tt % 64).astype(np.float32)
        for half, pos in enumerate((row, col)):
            ang = (pos[:, None] * freq[None, :]).astype(np.float32)
            for h in range(4):
                rope[i, :, 0, h * 2 + half, :] = np.cos(ang)
                rope[i, :, 1, h * 2 + half, :] = np.sin(ang)
    C["rope"] = rope.reshape(NTI, 128, 2, 256)
    sel = np.zeros((64, 8, 128), np.float32)
    for d in range(2):
        for h in range(4):
            sel[d * 32 + h, d * 4 + h, :] = 1.0
    C["sel"] = sel
    rc = np.zeros((4, NT), np.float32)
    for g, w in enumerate((2, 4, 8, 16)):
        for (o, T) in ((0, NC), (NC, NL)):
            tt = np.arange(T)
            lo = np.clip(tt - w // 2, 0, T)
            hi = np.clip(tt + w - w // 2, 0, T)
            rc[g, o:o + T] = 1.0 / (hi - lo).astype(np.float32)
    C["rcnt"] = rc
    return C


_CACHE = {}
NCORES = 4


def make_in_maps(inputs):
    f = lambda a: np.ascontiguousarray(np.asarray(a, dtype=np.float32))
    x, c, ctx, c_ctx = f(inputs["x"]), f(inputs["c"]), f(inputs["ctx"]), f(inputs["c_ctx"])
    C = host_constants()
    shared = dict(C)
    shared["ngcol"] = f(inputs["norm_gain"]).reshape(DEPTH, KC, 128).transpose(0, 2, 1).copy()
    shared["w_mod"] = f(inputs["w_mod"])
    shared["b_mod"] = f(inputs["b_mod"])
    shared["w_in"] = f(inputs["w_in"])
    shared["qg"] = f(inputs["q_norm_gain"])
    shared["kg"] = f(inputs["k_norm_gain"])
    gb = f(inputs["mlstm_gate_bias"])
    gbias = np.zeros((DEPTH, 64, 2), np.float32)
    gbias[:, 0:4, 0] = gb[:, 0]
    gbias[:, 32:36, 0] = gb[:, 2]
    gbias[:, 0:4, 1] = gb[:, 1]
    gbias[:, 32:36, 1] = gb[:, 3]
    shared["gbias"] = gbias
    shared["mgain"] = f(inputs["mlstm_norm_gain"])
    shared["convw"] = f(inputs["conv_w"]).reshape(DEPTH, 3, 8, 128).transpose(0, 3, 2, 1).copy()
    shared["pool_w"] = f(inputs["pool_w"])
    shared["pscale"] = f(inputs["pool_scale"]).reshape(DEPTH, 8, 128).transpose(0, 2, 1).copy()
    wb = f(inputs["w_branch"])
    shared["wbt"] = wb.reshape(DEPTH, 4, 8, 128, 16, 128).transpose(0, 4, 3, 1, 2, 5).reshape(DEPTH, 16, 128, 4 * 8 * 128).copy()
    shared["w_out"] = f(inputs["w_out"])
    shared["fgain"] = f(inputs["final_norm_gain"])
    maps = []
    for core in range(NCORES):
        b = core % 4
        m = dict(shared)
        m["x"] = x[b]
        m["ctx"] = ctx[b]
        c2 = np.stack([c[b], c_ctx])
        m["c2T"] = c2.reshape(2, KC, 128).transpose(2, 1, 0).copy()
        maps.append(m)
    return maps


def kernel(**inputs):
    if "nc" not in _CACHE:
        _CACHE["nc"] = build_program()
    nc = _CACHE["nc"]
    maps = make_in_maps(inputs)
    res = run_bass_kernel_spmd(nc, maps, core_ids=list(range(NCORES)))
    out = np.stack([np.asarray(res.results[b]["out"]) for b in range(4)]).astype(np.float32)
    return out
```

```python
import contextlib
import numpy as np
import concourse.bass as bass
import concourse.mybir as mybir
from concourse.bass_utils import run_bass_kernel_spmd

F32 = mybir.dt.float32
BF16 = mybir.dt.bfloat16
AF = mybir.ActivationFunctionType
ALU = mybir.AluOpType
AX = mybir.AxisListType

NT = 4352
NTI = 34
NL = 4096
NC = 256
D = 2048
KC = 16
EPS = 1e-6
BLKS = [(0, 256)] + [(256 + 512 * j, 512) for j in range(8)]
DEPTH = 2
WCOL = dict(aq=0, ak=1024, av=1280, az=1536, mq=2560, mk=3584, mv=4608, mo=5632, mz=6656, mg=7680,
            cu=7696, cb=8720, cc=9744, cz=10768, pu=11792, pz=12816, gm=13840)
N_IN = 22032
FROW = dict(az=0, mz=1024, cz=2048, pz=3072, mo=4096, gm=5120, mq=13312, mk=14336, cu=15360, cb=16384,
            cc=17408, pu=18432)
NF = 19456
FSEG = [("az", 1024, "silu"), ("mz", 1024, "silu"), ("cz", 1024, "silu"), ("pz", 1024, "silu"),
        ("mo", 1024, "sig"), ("gm", 8192, "sig"),
        ("mq", 1024, "copy"), ("mk", 1024, "copy16"), ("cu", 1024, "copy"), ("cb", 1024, "copy"),
        ("cc", 1024, "copy"), ("pu", 1024, "copy")]


class Tok:
    __slots__ = ("q", "sem", "val", "rec")

    def __init__(self, q, rec):
        self.q = q
        self.rec = rec
        self.sem = None
        self.val = None


class Buf:
    __slots__ = ("name", "w", "r", "excl", "wf")

    def __init__(self, name="", excl=False):
        self.name = name
        self.w = {}
        self.wf = {}
        self.r = {}
        self.excl = excl


class Rec:
    __slots__ = ("fn", "deps", "signal", "tok", "dma", "dsem")

    def __init__(self, fn, deps, dma):
        self.fn = fn
        self.deps = deps
        self.signal = False
        self.tok = None
        self.dma = dma
        self.dsem = None


class Prog:
    NDMA = 8

    def __init__(self, nc):
        self.nc = nc
        self.eng = {"pe": nc.tensor, "act": nc.scalar, "dve": nc.vector, "pool": nc.gpsimd, "sp": nc.sync}
        self.q = {k: [] for k in self.eng}
        self.dma_n = {k: 0 for k in self.eng}
        self.dma_last = {k: [None] * self.NDMA for k in self.eng}
        self.last = {k: None for k in self.eng}
        self.fence = {k: [] for k in self.eng}

    def barrier(self):
        toks = []
        for q in self.eng:
            if self.last[q] is not None:
                toks.append(self.last[q])
            toks.extend(t for t in self.dma_last[q] if t is not None)
        for q in self.eng:
            self.fence[q] = list(toks)

    def op(self, q, fn, reads=(), writes=(), dma=False, part=False):
        deps = []
        if self.fence[q]:
            deps.extend(self.fence[q])
            self.fence[q] = []
        for b in reads:
            deps.extend(b.w.values())
            if b.excl:
                deps.extend(t for kk, t in b.r.items() if kk[0] != q)
        for b in writes:
            deps.extend(b.r.values())
            if not part:
                deps.extend(b.w.values())
            else:
                deps.extend(b.wf.values())
        rec = Rec(fn, deps, dma)
        tok = Tok(q, rec)
        rec.tok = tok
        if dma:
            n = self.dma_n[q]
            self.dma_n[q] = n + 1
            slot = n % self.NDMA
            prev = self.dma_last[q][slot]
            if prev is not None:
                rec.deps.append(prev)
            self.dma_last[q][slot] = tok
            rec.dsem = slot
        else:
            self.last[q] = tok
        self.q[q].append(rec)
        key = (q, rec.dsem)
        for b in reads:
            b.r[key] = tok
        for b in writes:
            if part:
                b.w[key] = tok
            else:
                b.w = {key: tok}
                b.wf = {key: tok}
                b.r = {}
        return tok

    def emit(self, sems, dsems):
        for q, recs in self.q.items():
            for rec in recs:
                for t in rec.deps:
                    if t.rec.dma:
                        continue
                    if t.q == q and q == "pe":
                        continue
                    t.rec.signal = True
        for q, recs in self.q.items():
            cnt = 0
            dcnt = [0] * self.NDMA
            for rec in recs:
                if rec.dma:
                    dcnt[rec.dsem] += 16
                    rec.tok.sem = dsems[q][rec.dsem]
                    rec.tok.val = dcnt[rec.dsem]
                elif rec.signal:
                    cnt += 1
                    rec.tok.sem = sems[q]
                    rec.tok.val = cnt
        self.stats = {}
        with self.nc.Block() as block:
            def mk(q):
                def body(e):
                    seen = {}
                    nw = 0
                    for rec in self.q[q]:
                        need = {}
                        for t in rec.deps:
                            if t.sem is None or (t.q == q and q == "pe" and not t.rec.dma):
                                continue
                            k = id(t.sem)
                            if seen.get(k, 0) >= t.val:
                                continue
                            if k not in need or need[k][1] < t.val:
                                need[k] = (t.sem, t.val)
                        for k, (s, v) in need.items():
                            e.wait_ge(s, v)
                            seen[k] = v
                            nw += 1
                        ins = rec.fn(e)
                        if rec.dma:
                            ins.then_inc(rec.tok.sem, 16)
                        elif rec.signal:
                            ins.then_inc(rec.tok.sem, 1)
                    for t in self.dma_last[q]:
                        if t is not None:
                            e.wait_ge(t.sem, t.val)
                    self.stats[q] = (len(self.q[q]), nw)
                return body
            block.tensor(mk("pe"))
            block.scalar(mk("act"))
            block.vector(mk("dve"))
            block.gpsimd(mk("pool"))
            block.sync(mk("sp"))


def C(name, *a, **kw):
    return lambda e: getattr(e, name)(*a, **kw)


class Arena:
    def __init__(self, ap, nbytes):
        self.ap = ap
        self.cap = nbytes
        self.top = 0

    def mark(self):
        return self.top

    def reset(self, m):
        self.top = m

    def alloc(self, shape, dt, name=""):
        esz = 2 if dt == BF16 else 4
        n = int(np.prod(shape[1:]))
        nb = (n * esz + 31) // 32 * 32
        off = self.top
        self.top += nb
        assert self.top <= self.cap, f"arena overflow {self.top} > {self.cap} at {name}"
        v = self.ap[0:shape[0], off // 4: off // 4 + (n * esz + 3) // 4]
        if dt != F32:
            v = v.bitcast(dt)[:, 0:n]
        if len(shape) == 3:
            v = v.rearrange("p (a b) -> p a b", b=shape[2])
        elif len(shape) == 4:
            v = v.rearrange("p (a b c) -> p a b c", b=shape[2], c=shape[3])
        return v, Buf(name)


class K:
    pass


def build_program(debug=(), stop_after=None, nlayers=DEPTH):
    nc = bass.Bass("TRN2", target_bir_lowering=False)
    P = Prog(nc)
    k = K()
    k.nc, k.P = nc, P
    k.stop = stop_after

    def din(name, shape, dt=F32):
        return nc.dram_tensor(name, list(shape), dt, kind="ExternalInput").ap()

    def dscr(name, shape, dt, out=False):
        return nc.dram_tensor(name, list(shape), dt, kind=("ExternalOutput" if (out or name in debug) else "Internal")).ap()

    I = {}
    I["x"] = din("x", [NL, D])
    I["ctx"] = din("ctx", [NC, D])
    I["c2T"] = din("c2T", [128, KC, 2])
    I["ngcol"] = din("ngcol", [DEPTH, 128, KC])
    I["w_mod"] = din("w_mod", [DEPTH, D, 3 * D])
    I["b_mod"] = din("b_mod", [DEPTH, 3 * D])
    I["w_in"] = din("w_in", [DEPTH, D, N_IN])
    I["qg"] = din("qg", [DEPTH, 128])
    I["kg"] = din("kg", [DEPTH, 128])
    I["gbias"] = din("gbias", [DEPTH, 64, 2])
    I["mgain"] = din("mgain", [DEPTH, 1024])
    I["convw"] = din("convw", [DEPTH, 128, 8, 3])
    I["pool_w"] = din("pool_w", [DEPTH, 4, 256, 256])
    I["pscale"] = din("pscale", [DEPTH, 128, 8])
    I["wbt"] = din("wbt", [DEPTH, 16, 128, 4 * 8 * 128])
    I["w_out"] = din("w_out", [DEPTH, D, D])
    I["fgain"] = din("fgain", [D])
    I["ident"] = din("ident", [128, 128])
    I["masks"] = din("masks", [2, 128, 128])
    I["rope"] = din("rope", [NTI, 128, 2, 256])
    I["sel"] = din("sel", [64, 8, 128])
    I["rcnt"] = din("rcnt", [4, NT])
    k.I = I
    S = {}
    S["modd"] = dscr("modd", [DEPTH, 2, 3 * D], F32)
    S["F"] = dscr("Fs", [NF, NT], BF16)
    S["QT"] = dscr("QT", [1024, NT], BF16)
    S["KT"] = dscr("KT", [256, NT], BF16)
    S["Vt"] = dscr("Vt", [NT, 256], BF16)
    S["MKt"] = dscr("MKt", [NT, 1024], BF16)
    S["MVt"] = dscr("MVt", [NT, 1024], BF16)
    S["HF"] = dscr("HF", [2, NT, 1024], BF16)
    S["Y"] = dscr("Y", [4, 1024, NT], BF16)
    S["X1"] = dscr("X1", [NT, D], F32)
    S["ACC"] = dscr("ACC", [D, NT], BF16)
    S["out"] = dscr("out", [NL, D], F32, out=True)
    k.S = S
    k.SB = {n: Buf(n) for n in S}

    with contextlib.ExitStack() as es:
        ARENA_BYTES = 206 * 1024
        arena_t = es.enter_context(nc.sbuf_tensor("arena", [128, ARENA_BYTES // 4], F32))
        A = Arena(arena_t, ARENA_BYTES)
        k.A = A
        k.pb2 = [es.enter_context(nc.psum_tensor(f"pbb{i}", [128, 1024], F32))[:, :] for i in range(4)]
        k.pb = [k.pb2[i // 2][:, (i % 2) * 512:(i % 2 + 1) * 512] for i in range(8)]
        k.pbB = [Buf(f"pb{i}", excl=True) for i in range(8)]
        sems = {q: es.enter_context(nc.semaphore("s_" + q)) for q in P.eng}
        dsems = {q: [es.enter_context(nc.semaphore(f"d_{q}{i}")) for i in range(P.NDMA)] for q in P.eng}

        k.idf, k.idfB = A.alloc([128, 128], F32, "idf")
        k.idb, k.idbB = A.alloc([128, 128], BF16, "idb")
        k.onesb, k.onesbB = A.alloc([128, 128], BF16, "onesb")
        k.modcol, k.modcolB = A.alloc([128, DEPTH, 48, 2], F32, "modcol")
        k.gcol, k.gcolB = A.alloc([128, DEPTH, KC, 2], F32, "gcol")
        k.Gtok, k.GtokB = A.alloc([128, NTI, 2, 36], F32, "Gtok")
        P.op("pool", C("memset", k.Gtok, 0.0), writes=[k.GtokB])
        P.op("sp", C("dma_start", out=k.idf, in_=I["ident"]), writes=[k.idfB], dma=True)
        P.op("pool", C("dma_start", out=k.idb, in_=I["ident"]), writes=[k.idbB], dma=True)
        P.op("dve", C("memset", k.onesb, 1.0), writes=[k.onesbB])

        phase_mod(k)
        for l in range(nlayers if stop_after != ("mod", 0) else 0):
            last = (l == DEPTH - 1)
            phase_norm_inproj(k, l)
            if stop_after in (("inproj", l), ("norm", l), ("tokmaj", l)):
                break
            phase_attn(k, l)
            if stop_after == ("attn", l):
                break
            phase_mlstm(k, l)
            if stop_after == ("mlstm", l):
                break
            phase_conv_pool(k, l)
            if stop_after == ("convpool", l):
                break
            phase_merge_out(k, l)
        P.emit(sems, dsems)
    return nc


def phase_mod(k):
    P, A, I, S = k.P, k.A, k.I, k.S
    m0 = A.mark()
    cT, cTB = A.alloc([128, KC, 2], F32, "cT")
    scT, scTB = A.alloc([128, KC, 2], F32, "scT")
    ngc, ngcB = A.alloc([128, DEPTH, KC], F32, "ngc")
    wm = [A.alloc([128, 3072], F32, f"wm{i}") for i in range(3)]
    mo, moB = A.alloc([2, 6144], F32, "mo")
    bm, bmB = A.alloc([2, 6144], F32, "bm")
    tmp, tmpB = A.alloc([128, KC, 2], F32, "tmpg")
    P.op("sp", C("dma_start", out=cT, in_=I["c2T"]), writes=[cTB], dma=True)
    P.op("sp", C("dma_start", out=ngc, in_=I["ngcol"].rearrange("l p k -> p l k")), writes=[ngcB], dma=True)
    P.op("act", C("activation", out=scT, in_=cT, func=AF.Silu), reads=[cTB], writes=[scTB])
    n = 0
    for l in range(DEPTH):
        P.op("sp", C("dma_start", out=bm, in_=I["b_mod"][l, :].partition_broadcast(2)), writes=[bmB], dma=True)
        for half in range(2):
            for kc in range(KC):
                w, wB = wm[n % 3]
                n += 1
                P.op("sp", C("dma_start",
                    out=w, in_=I["w_mod"][l, kc * 128:(kc + 1) * 128, half * 3072:(half + 1) * 3072]), writes=[wB], dma=True)
                for j in range(6):
                    P.op("pe", C("matmul", k.pb[j][0:2, :], lhsT=scT[:, kc, :], rhs=w[:, j * 512:(j + 1) * 512],
                                                                start=(kc == 0), stop=(kc == KC - 1)),
                         reads=[scTB, wB], writes=[k.pbB[j]])
            for j in range(6):
                c0 = half * 3072 + j * 512
                P.op("dve", C("tensor_tensor", out=mo[:, c0:c0 + 512], in0=k.pb[j][0:2, :], in1=bm[:, c0:c0 + 512], op=ALU.add),
                     reads=[k.pbB[j], bmB], writes=[moB], part=True)
        P.op("sp", C("dma_start", out=S["modd"][l], in_=mo), reads=[moB], writes=[k.SB["modd"]], dma=True, part=True)
        for j in range(48):
            P.op("pe", C("transpose", k.pb[6][:, 2 * j:2 * j + 2], mo[0:2, j * 128:(j + 1) * 128], k.idf[0:2, 0:2]),
                 reads=[moB, k.idfB], writes=[k.pbB[6]])
        P.op("dve", C("tensor_copy", out=k.modcol[:, l], in_=k.pb[6][:, 0:96].rearrange("p (j r) -> p j r", r=2)),
             reads=[k.pbB[6]], writes=[k.modcolB], part=True)
        P.op("dve", C("tensor_scalar", out=tmp, in0=k.modcol[:, l, 16:32, :], scalar1=1.0, scalar2=None, op0=ALU.add),
             reads=[k.modcolB], writes=[tmpB])
        P.op("dve", C("tensor_tensor", out=k.gcol[:, l], in0=tmp, in1=ngc[:, l, :].unsqueeze(2).to_broadcast([128, KC, 2]), op=ALU.mult),
             reads=[tmpB, ngcB], writes=[k.gcolB], part=True)
    P.barrier()
    A.reset(m0)


def src_tile(k, l, i):
    if l == 0:
        if i < 2:
            return k.I["ctx"][i * 128:(i + 1) * 128, :]
        return k.I["x"][(i - 2) * 128:(i - 1) * 128, :]
    return k.S["X1"][i * 128:(i + 1) * 128, :]


def phase_norm_inproj(k, l):
    P, A, I, S = k.P, k.A, k.I, k.S
    m0 = A.mark()
    hT, hTB = A.alloc([128, KC, NT], BF16, "hT")
    m1 = A.mark()
    xt = [A.alloc([128, D], F32, f"xt{i}") for i in range(2)]
    xn = [A.alloc([128, D], BF16, f"xn{i}") for i in range(2)]
    junk, junkB = A.alloc([128, D], BF16, "junk")
    ss, ssB = A.alloc([128, NTI], F32, "ss")
    rs, rsB = A.alloc([128, NTI], F32, "rs")
    x1B = [k.SB["X1"]] if l > 0 else []
    for i in range(NTI):
        r = 1 if i < 2 else 0
        x_, xB = xt[i % 2]
        n_, nB = xn[i % 2]
        P.op("sp", C("dma_start", out=x_, in_=src_tile(k, l, i)), reads=x1B, writes=[xB], dma=True)
        P.op("act", C("activation", out=junk, in_=x_, func=AF.Square, accum_out=ss[:, i:i + 1]),
             reads=[xB], writes=[junkB, ssB])
        P.op("dve", C("tensor_scalar", out=rs[:, i:i + 1], in0=ss[:, i:i + 1], scalar1=1.0 / D, scalar2=EPS, op0=ALU.mult, op1=ALU.add),
             reads=[ssB], writes=[rsB])
        P.op("act", C("activation", out=rs[:, i:i + 1], in_=rs[:, i:i + 1], func=AF.Sqrt), reads=[rsB], writes=[rsB])
        P.op("dve", C("reciprocal", out=rs[:, i:i + 1], in_=rs[:, i:i + 1]), reads=[rsB], writes=[rsB])
        P.op("dve", C("tensor_scalar", out=n_, in0=x_, scalar1=rs[:, i:i + 1], scalar2=None, op0=ALU.mult),
             reads=[xB, rsB], writes=[nB])
        import os
        KD = os.environ.get("KDBG", "")
        for half in range(2):
            if KD == "A":
                break
            bi = (2 * i + half) % 4
            pbf = k.pb[bi][:, :].bitcast(BF16)
            for j in range(8):
                kc = half * 8 + j
                P.op("pe", C("transpose", pbf[:, j * 128:(j + 1) * 128], n_[:, kc * 128:(kc + 1) * 128], k.idb),
                     reads=[nB, k.idbB], writes=[k.pbB[bi]])
            for j in range(8):
                kc = half * 8 + j
                if half == 0 or KD == "B":
                    P.op("dve", C("tensor_scalar",
                        out=hT[:, kc, i * 128:(i + 1) * 128], in0=pbf[:, j * 128:(j + 1) * 128],
                        scalar1=k.gcol[:, l, kc, r:r + 1], scalar2=k.modcol[:, l, kc, r:r + 1], op0=ALU.mult, op1=ALU.add),
                        reads=[k.pbB[bi], k.gcolB, k.modcolB], writes=[hTB], part=True)
                else:
                    P.op("act", C("activation",
                        out=hT[:, kc, i * 128:(i + 1) * 128], in_=pbf[:, j * 128:(j + 1) * 128], func=AF.Identity,
                        bias=k.modcol[:, l, kc, r:r + 1], scale=k.gcol[:, l, kc, r:r + 1]),
                        reads=[k.pbB[bi], k.gcolB, k.modcolB], writes=[hTB], part=True)
    P.barrier()
    A.reset(m1)
    if k.stop == ("norm", l):
        A.reset(m0)
        return
    W = [A.alloc([128, KC, 512], BF16, f"W{i}") for i in range(2)]
    m2 = A.mark()
    t1, t1B = A.alloc([128, 512], F32, "t1")
    t2, t2B = A.alloc([128, 512], F32, "t2")
    t3, t3B = A.alloc([128, 512], F32, "t3")
    ta, taB = A.alloc([128, 512], F32, "ta")
    tb, tbB = A.alloc([128, 512], F32, "tb")
    qf, qfB = A.alloc([128, 512], BF16, "qf")
    ssq, ssqB = A.alloc([128, 4], F32, "ssq")
    rop = [A.alloc([128, 2, 8, 32], F32, f"rope{i}") for i in range(2)]
    qgb, qgbB = A.alloc([128, 128], F32, "qgb")
    kgb, kgbB = A.alloc([128, 128], F32, "kgb")
    qst = [A.alloc([128, 4, 512], BF16, f"qst{i}") for i in range(2)]
    vst = [A.alloc([128, 512], BF16, f"vst{i}") for i in range(2)]
    P.op("sp", C("dma_start", out=qgb, in_=I["qg"][l, :].partition_broadcast(128)), writes=[qgbB], dma=True)
    P.op("sp", C("dma_start", out=kgb, in_=I["kg"][l, :].partition_broadcast(128)), writes=[kgbB], dma=True)
    P.op("dve", C("tensor_scalar", out=qgb, in0=qgb, scalar1=float(128 ** -0.5), scalar2=None, op0=ALU.mult), reads=[qgbB], writes=[qgbB])
    wsrc = I["w_in"][l].rearrange("(kc p) c -> p kc c", p=128)
    groups = [("q", WCOL["aq"], 0), ("q", WCOL["aq"] + 512, 1), ("kv", WCOL["ak"], 0),
              ("mk", WCOL["mk"], 0), ("mk", WCOL["mk"] + 512, 1), ("mv", WCOL["mv"], 0), ("mv", WCOL["mv"] + 512, 1),
              ("mg", WCOL["mg"], 0)]
    nload = [0]

    def load_w(c0, ncol):
        w, wB = W[nload[0] % 2]
        nload[0] += 1
        P.op("pool", C("dma_start", out=w[:, :, 0:ncol], in_=wsrc[:, :, c0:c0 + ncol]), writes=[wB], dma=True)
        return w, wB

    def qk_post(ps, psB, nh, gb, gbB, rp, rpB, out, outB):
        Wd = nh * 128
        g = nh * 2
        P.op("act", C("activation", out=t1[:, 0:Wd], in_=ps[:, 0:Wd], func=AF.Square), reads=[psB], writes=[t1B])
        P.op("dve", C("tensor_reduce", out=ssq[:, 0:nh], in_=t1[:, 0:Wd].rearrange("p (h d) -> p h d", d=128), axis=AX.X, op=ALU.add),
             reads=[t1B], writes=[ssqB])
        P.op("dve", C("tensor_scalar", out=ssq[:, 0:nh], in0=ssq[:, 0:nh], scalar1=1.0 / 128, scalar2=EPS, op0=ALU.mult, op1=ALU.add),
             reads=[ssqB], writes=[ssqB])
        P.op("act", C("activation", out=ssq[:, 0:nh], in_=ssq[:, 0:nh], func=AF.Sqrt), reads=[ssqB], writes=[ssqB])
        P.op("dve", C("reciprocal", out=ssq[:, 0:nh], in_=ssq[:, 0:nh]), reads=[ssqB], writes=[ssqB])
        P.op("dve", C("tensor_tensor", out=t2[:, 0:Wd].rearrange("p (h d) -> p h d", d=128), in0=ps[:, 0:Wd].rearrange("p (h d) -> p h d", d=128),
                                              in1=ssq[:, 0:nh].unsqueeze(2).to_broadcast([128, nh, 128]), op=ALU.mult),
             reads=[psB, ssqB], writes=[t2B])
        P.op("pool", C("tensor_tensor", out=t3[:, 0:Wd].rearrange("p (h d) -> p h d", d=128), in0=t2[:, 0:Wd].rearrange("p (h d) -> p h d", d=128),
                                               in1=gb.unsqueeze(1).to_broadcast([128, nh, 128]), op=ALU.mult),
             reads=[t2B, gbB], writes=[t3B])
        t3v = t3[:, 0:Wd].rearrange("p (g x j) -> p g x j", x=2, j=32)
        tav = ta[:, 0:Wd].rearrange("p (g x j) -> p g x j", x=2, j=32)
        tbv = tb[:, 0:Wd].rearrange("p (g x j) -> p g x j", x=2, j=32)
        ov = out.rearrange("p (g x j) -> p g x j", x=2, j=32)
        P.op("pool", C("tensor_tensor", out=tav, in0=t3v, in1=rp[:, 0, 0:g, :].unsqueeze(2).to_broadcast([128, g, 2, 32]), op=ALU.mult),
             reads=[t3B, rpB], writes=[taB])
        P.op("pool", C("tensor_tensor", out=tbv[:, :, 0, :], in0=t3v[:, :, 1, :], in1=rp[:, 1, 0:g, :], op=ALU.mult),
             reads=[t3B, rpB], writes=[tbB], part=True)
        P.op("pool", C("tensor_tensor", out=tbv[:, :, 1, :], in0=t3v[:, :, 0, :], in1=rp[:, 1, 0:g, :], op=ALU.mult),
             reads=[t3B, rpB], writes=[tbB], part=True)
        P.op("dve", C("tensor_tensor", out=ov[:, :, 0, :], in0=tav[:, :, 0, :], in1=tbv[:, :, 0, :], op=ALU.subtract),
             reads=[taB, tbB], writes=[outB], part=True)
        P.op("dve", C("tensor_tensor", out=ov[:, :, 1, :], in0=tav[:, :, 1, :], in1=tbv[:, :, 1, :], op=ALU.add),
             reads=[taB, tbB], writes=[outB], part=True)

    nps = [0]
    cur = load_w(groups[0][1], 512)
    for gi, (kind, c0, sub) in enumerate(groups):
        w, wB = cur
        if gi + 1 < len(groups):
            nk, nc0, _ = groups[gi + 1]
            cur = load_w(nc0, 16 if nk == "mg" else 512)
        ncol = 16 if kind == "mg" else 512
        for i in range(NTI):
            bi = nps[0] % 4
            nps[0] += 1
            ps, psB = k.pb[bi], k.pbB[bi]
            if kind in ("q", "kv"):
                rp, rpB = rop[i % 2]
                P.op("sp", C("dma_start", out=rp, in_=I["rope"][i].rearrange("p a (g j) -> p a g j", j=32)), writes=[rpB], dma=True)
            for kc in range(KC):
                P.op("pe", C("matmul", ps[:, 0:ncol], lhsT=hT[:, kc, i * 128:(i + 1) * 128], rhs=w[:, kc, 0:ncol],
                                                                               start=(kc == 0), stop=(kc == KC - 1)),
                     reads=[hTB, wB], writes=[psB])
            if kind == "q":
                qk_post(ps, psB, 4, qgb, qgbB, rp, rpB, qf, qfB)
                grp = 0 if i < 2 else 1 + (i - 2) // 4
                pos = i if i < 2 else (i - 2) % 4
                st, stB = qst[grp % 2]
                tbi = 4 + (i % 2)
                pbf = k.pb[tbi][:, :].bitcast(BF16)
                for h in range(4):
                    P.op("pe", C("transpose", pbf[:, h * 128:(h + 1) * 128], qf[:, h * 128:(h + 1) * 128], k.idb),
                         reads=[qfB, k.idbB], writes=[k.pbB[tbi]])
                P.op("act", C("activation", out=st[:, :, pos * 128:(pos + 1) * 128],
                                                                            in_=pbf[:, 0:512].rearrange("p (h t) -> p h t", t=128), func=AF.Copy),
                     reads=[k.pbB[tbi]], writes=[stB], part=True)
                done = (i == 1) or (i >= 2 and pos == 3)
                if done:
                    t0 = 0 if i < 2 else 256 + ((i - 2) // 4) * 512
                    n = 256 if i < 2 else 512
                    P.op("sp", C("dma_start",
                        out=S["QT"].rearrange("(h d) t -> d h t", d=128)[:, sub * 4:(sub + 1) * 4, t0:t0 + n], in_=st[:, :, 0:n]),
                        reads=[stB], writes=[k.SB["QT"]], dma=True, part=True)
            elif kind == "kv":
                qk_post(ps, psB, 2, kgb, kgbB, rp, rpB, qf[:, 0:256], qfB)
                grp = 0 if i < 2 else 1 + (i - 2) // 4
                pos = i if i < 2 else (i - 2) % 4
                st, stB = qst[grp % 2]
                tbi = 4 + (i % 2)
                pbf = k.pb[tbi][:, :].bitcast(BF16)
                for h in range(2):
                    P.op("pe", C("transpose", pbf[:, h * 128:(h + 1) * 128], qf[:, h * 128:(h + 1) * 128], k.idb),
                         reads=[qfB, k.idbB], writes=[k.pbB[tbi]])
                P.op("act", C("activation", out=st[:, 0:2, pos * 128:(pos + 1) * 128],
                                                                            in_=pbf[:, 0:256].rearrange("p (h t) -> p h t", t=128), func=AF.Copy),
                     reads=[k.pbB[tbi]], writes=[stB], part=True)
                done = (i == 1) or (i >= 2 and pos == 3)
                if done:
                    t0 = 0 if i < 2 else 256 + ((i - 2) // 4) * 512
                    n = 256 if i < 2 else 512
                    P.op("sp", C("dma_start",
                        out=S["KT"].rearrange("(h d) t -> d h t", d=128)[:, :, t0:t0 + n], in_=st[:, 0:2, 0:n]),
                        reads=[stB], writes=[k.SB["KT"]], dma=True, part=True)
                v_, vB = vst[i % 2]
                P.op("act", C("activation", out=v_[:, 0:256], in_=ps[:, 256:512], func=AF.Copy), reads=[psB], writes=[vB])
                P.op("sp", C("dma_start", out=S["Vt"][i * 128:(i + 1) * 128, :], in_=v_[:, 0:256]),
                     reads=[vB], writes=[k.SB["Vt"]], dma=True, part=True)
            elif kind in ("mk", "mv"):
                v_, vB = vst[i % 2]
                sc = 0.0625 if kind == "mk" else 1.0
                P.op("act", C("activation", out=v_, in_=ps, func=AF.Copy, scale=sc), reads=[psB], writes=[vB])
                dst = S["MKt"] if kind == "mk" else S["MVt"]
                dB = k.SB["MKt"] if kind == "mk" else k.SB["MVt"]
                P.op("sp", C("dma_start", out=dst[i * 128:(i + 1) * 128, sub * 512:(sub + 1) * 512], in_=v_),
                     reads=[vB], writes=[dB], dma=True, part=True)
            else:
                for gi in range(4):
                    P.op("dve", C("tensor_copy", out=k.Gtok[:, i, gi % 2, (gi // 2) * 32:(gi // 2) * 32 + 4], in_=ps[:, gi * 4:gi * 4 + 4]),
                         reads=[psB], writes=[k.GtokB], part=True)
    P.barrier()
    A.reset(m2)
    if k.stop == ("tokmaj", l):
        A.reset(m0)
        return
    stg = [A.alloc([128, NT], BF16, f"stg{i}") for i in range(2)]
    glist = []
    for name, ncols, fn in FSEG:
        for g in range(ncols // 512):
            glist.append((name, WCOL[name] + g * 512, FROW[name] + g * 512, fn))
    cur = load_w(glist[0][1], 512)
    nst = 0
    for gi, (name, c0, r0, fn) in enumerate(glist):
        w, wB = cur
        if gi + 1 < len(glist):
            cur = load_w(glist[gi + 1][1], 512)
        for j in range(4):
            st, stB = stg[nst % 2]
            nst += 1
            for (t0, n) in BLKS:
                bi = nps[0] % 4
                nps[0] += 1
                ps, psB = k.pb[bi], k.pbB[bi]
                for kc in range(KC):
                    P.op("pe", C("matmul", ps[:, 0:n], lhsT=w[:, kc, j * 128:(j + 1) * 128], rhs=hT[:, kc, t0:t0 + n],
                                                                                    start=(kc == 0), stop=(kc == KC - 1)),
                         reads=[hTB, wB], writes=[psB])
                if fn == "silu":
                    P.op("act", C("activation", out=st[:, t0:t0 + n], in_=ps[:, 0:n], func=AF.Silu),
                         reads=[psB], writes=[stB], part=True)
                elif fn == "sig":
                    P.op("act", C("activation", out=st[:, t0:t0 + n], in_=ps[:, 0:n], func=AF.Sigmoid),
                         reads=[psB], writes=[stB], part=True)
                elif fn == "copy16":
                    P.op("dve", C("tensor_scalar", out=st[:, t0:t0 + n], in0=ps[:, 0:n], scalar1=0.0625, scalar2=None, op0=ALU.mult),
                         reads=[psB], writes=[stB], part=True)
                else:
                    P.op("dve", C("tensor_copy", out=st[:, t0:t0 + n], in_=ps[:, 0:n]),
                         reads=[psB], writes=[stB], part=True)
            P.op("sp", C("dma_start", out=S["F"][r0 + j * 128:r0 + (j + 1) * 128, :], in_=st),
                 reads=[stB], writes=[k.SB["F"]], dma=True, part=True)
    P.barrier()
    A.reset(m0)


def phase_attn(k, l):
    P, A, I, S = k.P, k.A, k.I, k.S
    m0 = A.mark()
    KTs, KTB = A.alloc([128, 2, NT], BF16, "KTs")
    Vs, VB = A.alloc([128, NTI, 256], BF16, "Vs")
    Qb = [A.alloc([128, 8, 512], BF16, f"Qb{i}") for i in range(2)]
    AZ = [A.alloc([128, 8, 512], BF16, f"AZ{i}") for i in range(2)]
    PT = [A.alloc([128, 2, 512], BF16, f"PT{i}") for i in range(3)]
    rec, recB = A.alloc([128, 512], F32, "rec")
    t4, t4B = A.alloc([128, 512], F32, "t4")
    ost = [A.alloc([128, 8, 512], BF16, f"ost{i}") for i in range(2)]
    P.op("sp", C("dma_start", out=KTs, in_=S["KT"].rearrange("(h d) t -> d h t", d=128)), reads=[k.SB["KT"]], writes=[KTB], dma=True)
    P.op("sp", C("dma_start", out=Vs, in_=S["Vt"].rearrange("(i p) c -> p i c", p=128)), reads=[k.SB["Vt"]], writes=[VB], dma=True)
    blocks = []
    if l < DEPTH - 1:
        blocks.append((0, 256, [0, 1]))
    for j in range(8):
        blocks.append((256 + 512 * j, 512, list(range(NTI))))
    QTv = S["QT"].rearrange("(h d) t -> d h t", d=128)
    AZv = S["F"][FROW["az"]:FROW["az"] + 1024, :].rearrange("(h d) t -> d h t", d=128)
    Yv = S["Y"][0].rearrange("(h d) t -> d h t", d=128)
    ns = [0]
    npt = [0]
    for bi, (t0, n, keys) in enumerate(blocks):
        q_, qB = Qb[bi % 2]
        az_, azB = AZ[bi % 2]
        o_, oB = ost[bi % 2]
        P.op("sp", C("dma_start", out=q_[:, :, 0:n], in_=QTv[:, :, t0:t0 + n]), reads=[k.SB["QT"]], writes=[qB], dma=True)
        P.op("sp", C("dma_start", out=az_[:, :, 0:n], in_=AZv[:, :, t0:t0 + n]), reads=[k.SB["F"]], writes=[azB], dma=True)
        nk = len(keys)
        npair = nk // 2
        for h in range(8):
            kv = h // 4
            psO, psOB = k.pb[4 + (h % 2)], k.pbB[4 + (h % 2)]
            psD, psDB = k.pb[6 + (h % 2)], k.pbB[6 + (h % 2)]
            spair = []

            def emitS(pi):
                pr = ns[0] % 2
                ns[0] += 1
                spair.append(pr)
                for j in range(2):
                    kt = keys[2 * pi + j]
                    bb = 2 * pr + j
                    P.op("pe", C("matmul", k.pb[bb][:, 0:n], lhsT=KTs[:, kv, kt * 128:(kt + 1) * 128], rhs=q_[:, h, 0:n], start=True, stop=True),
                         reads=[KTB, qB], writes=[k.pbB[bb]])
            emitS(0)
            for pi in range(npair):
                if pi + 1 < npair:
                    emitS(pi + 1)
                pr = spair[pi]
                p_, pB = PT[npt[0] % 3]
                npt[0] += 1
                sv = k.pb2[pr].rearrange("p (b c) -> p b c", b=2)[:, :, 0:n]
                P.op("act", C("activation", out=p_[:, :, 0:n], in_=sv, func=AF.Exp), reads=[k.pbB[2 * pr], k.pbB[2 * pr + 1]], writes=[pB])
                for j in range(2):
                    kt = keys[2 * pi + j]
                    idx = 2 * pi + j
                    P.op("pe", C("matmul", psO[:, 0:n], lhsT=Vs[:, kt, kv * 128:(kv + 1) * 128], rhs=p_[:, j, 0:n],
                                 start=(idx == 0), stop=(idx == nk - 1)), reads=[VB, pB], writes=[psOB])
                for j in range(2):
                    idx = 2 * pi + j
                    P.op("pe", C("matmul", psD[:, 0:n], lhsT=k.onesb, rhs=p_[:, j, 0:n], start=(idx == 0), stop=(idx == nk - 1)),
                         reads=[k.onesbB, pB], writes=[psDB])
            P.op("dve", C("reciprocal", out=rec[:, 0:n], in_=psD[:, 0:n]), reads=[psDB], writes=[recB])
            P.op("dve", C("tensor_tensor", out=t4[:, 0:n], in0=psO[:, 0:n], in1=rec[:, 0:n], op=ALU.mult), reads=[psOB, recB], writes=[t4B])
            P.op("pool", C("tensor_tensor", out=o_[:, h, 0:n], in0=t4[:, 0:n], in1=az_[:, h, 0:n], op=ALU.mult),
                 reads=[t4B, azB], writes=[oB], part=True)
        P.op("sp", C("dma_start", out=Yv[:, :, t0:t0 + n], in_=o_[:, :, 0:n]), reads=[oB], writes=[k.SB["Y"]], dma=True, part=True)
    P.barrier()
    A.reset(m0)


def phase_mlstm(k, l):
    P, A, I, S = k.P, k.A, k.I, k.S
    last_layer = (l == DEPTH - 1)
    m0 = A.mark()
    WC, WCB = A.alloc([128, NTI, 16], F32, "WC")
    DECB, DECBB = A.alloc([128, 8, NTI], F32, "DECB")
    m1 = A.mark()
    GI, GIB = A.alloc([64, NT], F32, "GI")
    GF, GFB = A.alloc([64, NT], F32, "GF")
    ONE, ONEB = A.alloc([64, NT], F32, "ONE")
    BP, BPB = A.alloc([64, NT], F32, "BP")
    AP_, APB = A.alloc([64, NT], F32, "APr")
    MM, MMB = A.alloc([64, NT], F32, "MM")
    M2, M2B = A.alloc([64, NT], F32, "M2")
    WR, WRB = A.alloc([64, NT], F32, "WR")
    CL, CLB = A.alloc([64, NT], F32, "CL")
    gb, gbB = A.alloc([64, 2], F32, "gb")
    dec, decB = A.alloc([64, NTI], F32, "dec")
    sel, selB = A.alloc([64, 8, 128], F32, "sel")
    P.op("sp", C("dma_start", out=gb, in_=I["gbias"][l]), writes=[gbB], dma=True)
    P.op("sp", C("dma_start", out=sel, in_=I["sel"]), writes=[selB], dma=True)
    for t_, tB in ((GI, GIB), (GF, GFB), (dec, decB)):
        P.op("pool", C("memset", t_, 0.0), writes=[tB])
    P.op("pool", C("memset", ONE, 1.0), writes=[ONEB])
    R = (slice(0, 4), slice(32, 36))
    nb = 0
    for (t0, n) in BLKS:
        bt0 = (t0 - 256) if t0 >= 256 else 4096
        for gf, (dst, dstB) in enumerate(((GI, GIB), (GF, GFB))):
            bi = nb % 4
            nb += 1
            ps, psB = k.pb[bi], k.pbB[bi]
            for j in range(n // 128):
                i = t0 // 128 + j
                P.op("pe", C("transpose", ps[0:36, j * 128:(j + 1) * 128], k.Gtok[:, i, gf, :], k.idf),
                     reads=[k.GtokB, k.idfB], writes=[psB])
            P.op("act", C("activation", out=dst[0:4, t0:t0 + n], in_=ps[0:4, 0:n], func=AF.Identity,
                                                                               bias=gb[0:4, gf:gf + 1], scale=1.0),
                 reads=[psB, gbB], writes=[dstB], part=True)
            P.op("act", C("activation", out=dst[32:36, bt0:bt0 + n], in_=ps[32:36, 0:n], func=AF.Identity,
                                                                                 bias=gb[32:36, gf:gf + 1], scale=1.0),
                 reads=[psB, gbB], writes=[dstB], part=True)
    P.op("act", C("activation", out=GF[0:36, :], in_=GF[0:36, :], func=AF.Exp, scale=-1.0), reads=[GFB], writes=[GFB])
    P.op("act", C("activation", out=GF[0:36, :], in_=GF[0:36, :], func=AF.Ln, bias=1.0, scale=1.0), reads=[GFB], writes=[GFB])
    P.op("dve", C("tensor_tensor_scan", out=BP[0:36, :], data0=ONE[0:36, :], data1=GF[0:36, :], initial=0.0, op0=ALU.mult, op1=ALU.add),
         reads=[ONEB, GFB], writes=[BPB])
    P.op("dve", C("tensor_scalar", out=M2[32:36, :], in0=BP[32:36, :], scalar1=BP[32:36, NT - 1:NT], scalar2=-1.0, op0=ALU.subtract, op1=ALU.mult),
         reads=[BPB], writes=[M2B])
    P.op("dve", C("tensor_tensor", out=BP[32:36, :], in0=M2[32:36, :], in1=GF[32:36, :], op=ALU.add), reads=[M2B, GFB], writes=[BPB])
    P.op("dve", C("tensor_tensor", out=AP_[0:36, :], in0=GI[0:36, :], in1=BP[0:36, :], op=ALU.add), reads=[GIB, BPB], writes=[APB])
    P.op("dve", C("tensor_tensor_scan", out=MM[0:4, :], data0=ONE[0:4, :], data1=AP_[0:4, :], initial=-1e30, op0=ALU.mult, op1=ALU.max),
         reads=[ONEB, APB], writes=[MMB], part=True)
    src, srcB = AP_, APB
    bufs = [(M2, M2B), (MM, MMB)]
    sh = 1
    step = 0
    while sh < NT:
        dst, dstB = bufs[step % 2]
        P.op("dve", C("tensor_tensor", out=dst[32:36, 0:NT - sh], in0=src[32:36, 0:NT - sh], in1=src[32:36, sh:NT], op=ALU.max),
             reads=[srcB], writes=[dstB], part=True)
        P.op("pool", C("tensor_copy", out=dst[32:36, NT - sh:NT], in_=src[32:36, NT - sh:NT]),
             reads=[srcB], writes=[dstB], part=True)
        src, srcB = dst, dstB
        sh *= 2
        step += 1
    if src is not MM:
        P.op("dve", C("tensor_copy", out=MM[32:36, :], in_=src[32:36, :]), reads=[srcB], writes=[MMB], part=True)

    def v3(t_, r):
        return t_[r, :].rearrange("p (c t) -> p c t", t=128)
    for d, r in enumerate(R):
        li = 127 if d == 0 else 0
        mlast = v3(MM, r)[:, :, li:li + 1].to_broadcast([4, NTI, 128])
        P.op("dve", C("tensor_tensor", out=v3(WR, r), in0=v3(AP_, r), in1=mlast, op=ALU.subtract), reads=[APB, MMB], writes=[WRB], part=True)
        P.op("act", C("activation", out=WR[r, :], in_=WR[r, :], func=AF.Exp), reads=[WRB], writes=[WRB], part=True)
        P.op("dve", C("tensor_tensor", out=v3(CL, r), in0=v3(BP, r), in1=mlast, op=ALU.subtract), reads=[BPB, MMB], writes=[CLB], part=True)
        P.op("act", C("activation", out=CL[r, :], in_=CL[r, :], func=AF.Exp), reads=[CLB], writes=[CLB], part=True)
        ml2 = v3(MM, r)[:, :, li]
        if d == 0:
            P.op("dve", C("tensor_tensor", out=dec[r, 1:NTI], in0=ml2[:, 0:NTI - 1], in1=ml2[:, 1:NTI], op=ALU.subtract),
                 reads=[MMB], writes=[decB], part=True)
            P.op("act", C("activation", out=dec[r, 1:NTI], in_=dec[r, 1:NTI], func=AF.Exp), reads=[decB], writes=[decB], part=True)
        else:
            P.op("dve", C("tensor_tensor", out=dec[r, 0:NTI - 1], in0=ml2[:, 1:NTI], in1=ml2[:, 0:NTI - 1], op=ALU.subtract),
                 reads=[MMB], writes=[decB], part=True)
            P.op("act", C("activation", out=dec[r, 0:NTI - 1], in_=dec[r, 0:NTI - 1], func=AF.Exp), reads=[decB], writes=[decB], part=True)
    for q in range(8):
        P.op("pe", C("matmul", k.pb[0][:, q * NTI:(q + 1) * NTI], lhsT=sel[0:36, q, :], rhs=dec[0:36, :], start=True, stop=True),
             reads=[selB, decB], writes=[k.pbB[0]])
    P.op("dve", C("tensor_copy", out=DECB, in_=k.pb[0][:, 0:8 * NTI].rearrange("p (q c) -> p q c", c=NTI)), reads=[k.pbB[0]], writes=[DECBB])
    for half in range(2):
        tiles = list(range(half * 17, half * 17 + 17))
        ps, psB = k.pb[1 + half], k.pbB[1 + half]
        for jj, i in enumerate(tiles):
            fc = i * 128
            bc = (i - 2) * 128 if i >= 2 else 4096 + i * 128
            for qq, (src, srcB, r, c0) in enumerate(((WR, WRB, R[0], fc), (WR, WRB, R[1], bc), (CL, CLB, R[0], fc), (CL, CLB, R[1], bc))):
                P.op("pe", C("transpose", ps[:, jj * 16 + qq * 4: jj * 16 + qq * 4 + 4], src[r, c0:c0 + 128], k.idf[r, r]),
                     reads=[srcB, k.idfB], writes=[psB])
        P.op("dve", C("tensor_copy", out=WC[:, half * 17:half * 17 + 17, :], in_=ps[:, 0:17 * 16].rearrange("p (i q) -> p i q", q=16)),
             reads=[psB], writes=[WCB], part=True)
    P.barrier()
    A.reset(m1)
    msk, mskB = A.alloc([128, 2, 128], F32, "msk")
    mgb, mgbB = A.alloc([128, 1024], F32, "mgb")
    P.op("sp", C("dma_start", out=msk, in_=I["masks"].rearrange("m s t -> s m t")), writes=[mskB], dma=True)
    P.op("sp", C("dma_start", out=mgb, in_=I["mgain"][l, :].partition_broadcast(128)), writes=[mgbB], dma=True)
    Fq = S["F"][FROW["mq"]:FROW["mq"] + 1024, :].rearrange("(a p) t -> p a t", p=128)
    Fk = S["F"][FROW["mk"]:FROW["mk"] + 1024, :].rearrange("(a p) t -> p a t", p=128)
    Fo = S["F"][FROW["mo"]:FROW["mo"] + 1024, :].rearrange("(a p) t -> p a t", p=128)
    Fz = S["F"][FROW["mz"]:FROW["mz"] + 1024, :].rearrange("(a p) t -> p a t", p=128)
    Yb = S["Y"][1].rearrange("(a p) t -> p a t", p=128)
    HB_ = [[Buf(f"H{d}_{i}") for i in range(NTI)] for d in range(2)]
    BD = []
    for d in range(2):
        b = K()
        b.Cf, _ = A.alloc([128, 4, 2, 257], F32, f"Cf{d}")
        b.Ct, _ = A.alloc([128, 4, 2, 257], BF16, f"Ct{d}")
        b.CfBs = [Buf(f"Cf{d}{h}") for h in range(4)]
        b.CtBs = [Buf(f"Ct{d}{h}") for h in range(4)]
        b.qT = [A.alloc([128, 8, 128], BF16, f"qT{d}{i}") for i in range(2)]
        b.kT = [A.alloc([128, 8, 128], BF16, f"kT{d}{i}") for i in range(2)]
        b.ktk = [A.alloc([128, 1024], BF16, f"ktk{d}{i}") for i in range(2)]
        b.vtk = [A.alloc([128, 4, 257], BF16, f"vtk{d}{i}") for i in range(2)]
        b.moT = [A.alloc([128, 8, 128], BF16, f"moT{d}{i}") for i in range(2)]
        b.mzT = [A.alloc([128, 8, 128], BF16, f"mzT{d}{i}") for i in range(2)]
        b.hfl = [A.alloc([128, 1024], BF16, f"hfl{d}{i}") for i in range(2)]
        b.Sm = [A.alloc([128, 128], BF16, f"Sm{d}{i}") for i in range(2)]
        b.vw = [A.alloc([128, 257], BF16, f"vw{d}{i}") for i in range(2)]
        b.hst = [A.alloc([128, 4, 256], BF16, f"hst{d}{i}") for i in range(2)]
        b.hs, b.hsB = A.alloc([128, 4, 256], F32, f"hs{d}")
        b.hj, b.hjB = A.alloc([128, 1024], F32, f"hj{d}")
        b.hb, b.hbB = A.alloc([128, 1024], BF16, f"hb{d}")
        b.hss, b.hssB = A.alloc([128, 4], F32, f"hss{d}")
        b.dn = [A.alloc([128, 2], F32, f"dn{d}{h}") for h in range(4)]
        b.tT, b.tTB = A.alloc([128, 8, 128], F32, f"tT{d}")
        b.yst = [A.alloc([128, 8, 128], BF16, f"yst{d}{i}") for i in range(2)]
        b.cnt = dict(sm=0, vw=0, y=0)
        for v_, vB in b.vtk:
            P.op("pool", C("memset", v_, 1.0), writes=[vB])
        BD.append(b)
    orders = [list(range(NTI)), [1, 0] + list(range(NTI - 1, 1, -1))]

    def chunk(d, step, i):
        b = BD[d]
        is_ctx = i < 2
        need_out = not (is_ctx and last_layer)
        if is_ctx:
            first = (i == 0) if d == 0 else (i == 1)
        else:
            first = (i <= 17) if d == 0 else (i > 17)
        combine = need_out and not first
        cidx = i if d == 0 else ((i - 2) if i >= 2 else 32 + i)
        sl = slice(i * 128, (i + 1) * 128)
        q_, qB = b.qT[step % 2]
        k_, kB = b.kT[step % 2]
        kt_, ktB = b.ktk[step % 2]
        v_, vB = b.vtk[step % 2]
        P.op("sp", C("dma_start", out=q_, in_=Fq[:, :, sl]), reads=[k.SB["F"]], writes=[qB], dma=True)
        P.op("sp", C("dma_start", out=k_, in_=Fk[:, :, sl]), reads=[k.SB["F"]], writes=[kB], dma=True)
        P.op("sp", C("dma_start", out=kt_, in_=S["MKt"][sl, :]), reads=[k.SB["MKt"]], writes=[ktB], dma=True)
        P.op("sp", C("dma_start", out=v_[:, :, 0:256], in_=S["MVt"][sl, :].rearrange("t (h e) -> t h e", e=256)),
             reads=[k.SB["MVt"]], writes=[vB], dma=True, part=True)
        if combine:
            o_, oB = b.moT[step % 2]
            z_, zB = b.mzT[step % 2]
            f_, fB = b.hfl[step % 2]
            P.op("sp", C("dma_start", out=o_, in_=Fo[:, :, sl]), reads=[k.SB["F"]], writes=[oB], dma=True)
            P.op("sp", C("dma_start", out=z_, in_=Fz[:, :, sl]), reads=[k.SB["F"]], writes=[zB], dma=True)
            P.op("sp", C("dma_start", out=f_, in_=S["HF"][1 - d][sl, :]), reads=[HB_[1 - d][i]], writes=[fB], dma=True)
        yield
        h_, hB = b.hst[step % 2]
        bS, bP, bU = d, 2 + d, 4 + 2 * d
        psS, psSB = k.pb[bS], k.pbB[bS]
        psP, psPB = k.pb[bP], k.pbB[bP]
        for hh in range(4):
            qi = d * 4 + hh
            for dc in range(2):
                P.op("pe", C("matmul", psS[:, 0:128], lhsT=k_[:, hh * 2 + dc, :], rhs=q_[:, hh * 2 + dc, :], start=(dc == 0), stop=(dc == 1)),
                     reads=[kB, qB], writes=[psSB])
            yield
            sm_, smB = b.Sm[b.cnt["sm"] % 2]
            b.cnt["sm"] += 1
            P.op("dve", C("tensor_tensor", out=sm_, in0=psS[:, 0:128], in1=msk[:, d, :], op=ALU.mult), reads=[psSB, mskB], writes=[smB])
            vw_, vwB = b.vw[b.cnt["vw"] % 2]
            b.cnt["vw"] += 1
            P.op("act", C("activation", out=vw_, in_=v_[:, hh, :], func=AF.Copy, scale=WC[:, i, qi:qi + 1]),
                 reads=[vB, WCB], writes=[vwB])
            if step > 0:
                P.op("act", C("activation", out=b.Ct[:, hh], in_=b.Cf[:, hh], func=AF.Copy, scale=DECB[:, qi, cidx:cidx + 1]),
                     reads=[b.CfBs[hh], DECBB], writes=[b.CtBs[hh]])
            yield
            P.op("pe", C("matmul", psP[:, 0:257], lhsT=sm_, rhs=vw_, start=True, stop=(step == 0)), reads=[smB, vwB], writes=[psPB])
            if step > 0:
                for dc in range(2):
                    P.op("pe", C("matmul", psP[:, 0:257], lhsT=q_[:, hh * 2 + dc, :], rhs=b.Ct[:, hh, dc, :], start=False, stop=(dc == 1)),
                         reads=[qB, b.CtBs[hh]], writes=[psPB])
            for dc in range(2):
                psU, psUB = k.pb[bU + dc], k.pbB[bU + dc]
                P.op("pe", C("matmul", psU[:, 0:257], lhsT=kt_[:, hh * 256 + dc * 128: hh * 256 + (dc + 1) * 128], rhs=vw_, start=True, stop=True),
                     reads=[ktB, vwB], writes=[psUB])
                if step == 0:
                    P.op("dve", C("tensor_copy", out=b.Cf[:, hh, dc, :], in_=psU[:, 0:257]), reads=[psUB], writes=[b.CfBs[hh]], part=(dc == 1))
                else:
                    P.op("dve", C("scalar_tensor_tensor", out=b.Cf[:, hh, dc, :], in0=b.Cf[:, hh, dc, :], scalar=DECB[:, qi, cidx:cidx + 1], in1=psU[:, 0:257],
                                  op0=ALU.mult, op1=ALU.add), reads=[psUB, b.CfBs[hh], DECBB], writes=[b.CfBs[hh]], part=(dc == 1))
            yield
            if need_out:
                dn, dnB = b.dn[hh]
                P.op("dve", C("tensor_scalar", out=dn[:, 1:2], in0=psP[:, 256:257], scalar1=WC[:, i, 8 + qi:9 + qi], scalar2=None, op0=ALU.max),
                     reads=[psPB, WCB], writes=[dnB])
                P.op("dve", C("scalar_tensor_tensor", out=dn[:, 0:1], in0=psP[:, 256:257], scalar=-1.0, in1=dn[:, 1:2], op0=ALU.mult, op1=ALU.max),
                     reads=[psPB, dnB], writes=[dnB])
                P.op("dve", C("reciprocal", out=dn[:, 1:2], in_=dn[:, 0:1]), reads=[dnB], writes=[dnB])
                if not combine:
                    P.op("act", C("activation", out=h_[:, hh, :], in_=psP[:, 0:256], func=AF.Copy, scale=dn[:, 1:2]),
                         reads=[psPB, dnB], writes=[hB], part=(hh > 0))
                else:
                    P.op("dve", C("scalar_tensor_tensor", out=b.hs[:, hh, :], in0=psP[:, 0:256], scalar=dn[:, 1:2],
                                  in1=f_[:, hh * 256:(hh + 1) * 256], op0=ALU.mult, op1=ALU.add),
                         reads=[psPB, dnB, fB], writes=[b.hsB], part=(hh > 0))
        if need_out and not combine:
            P.op("sp", C("dma_start", out=S["HF"][d][sl, :], in_=h_.rearrange("p h e -> p (h e)")), reads=[hB], writes=[HB_[d][i]], dma=True)
        yield
        if combine:
            hs2 = b.hs.rearrange("p h e -> p (h e)")
            P.op("act", C("activation", out=b.hj, in_=hs2, func=AF.Square), reads=[b.hsB], writes=[b.hjB])
            P.op("dve", C("tensor_reduce", out=b.hss, in_=b.hj.rearrange("p (h e) -> p h e", e=256), axis=AX.X, op=ALU.add), reads=[b.hjB], writes=[b.hssB])
            P.op("dve", C("tensor_scalar", out=b.hss, in0=b.hss, scalar1=1.0 / 256, scalar2=EPS, op0=ALU.mult, op1=ALU.add), reads=[b.hssB], writes=[b.hssB])
            P.op("act", C("activation", out=b.hss, in_=b.hss, func=AF.Sqrt), reads=[b.hssB], writes=[b.hssB])
            P.op("dve", C("reciprocal", out=b.hss, in_=b.hss), reads=[b.hssB], writes=[b.hssB])
            hn = b.hj.rearrange("p (h e) -> p h e", e=256)
            P.op("dve", C("tensor_tensor", out=hn, in0=b.hs, in1=b.hss.unsqueeze(2).to_broadcast([128, 4, 256]), op=ALU.mult), reads=[b.hsB, b.hssB, b.hjB], writes=[b.hjB])
            P.op("pool", C("tensor_tensor", out=b.hb, in0=b.hj, in1=mgb, op=ALU.mult), reads=[b.hjB, mgbB], writes=[b.hbB])
            yield
            pbf = psP.bitcast(BF16)
            for cc in range(8):
                P.op("pe", C("transpose", pbf[:, cc * 128:(cc + 1) * 128], b.hb[:, cc * 128:(cc + 1) * 128], k.idb),
                     reads=[b.hbB, k.idbB], writes=[psPB])
            P.op("dve", C("tensor_tensor", out=b.tT, in0=pbf.rearrange("p (a t) -> p a t", t=128), in1=o_, op=ALU.mult),
                 reads=[psPB, oB], writes=[b.tTB])
            y_, yB = b.yst[b.cnt["y"] % 2]
            b.cnt["y"] += 1
            P.op("pool", C("tensor_tensor", out=y_, in0=b.tT, in1=z_, op=ALU.mult), reads=[b.tTB, zB], writes=[yB])
            P.op("sp", C("dma_start", out=Yb[:, :, sl], in_=y_), reads=[yB], writes=[k.SB["Y"]], dma=True, part=True)

    for step in range(NTI):
        gens = [chunk(d, step, orders[d][step]) for d in range(2)]
        while gens:
            for g in list(gens):
                try:
                    next(g)
                except StopIteration:
                    gens.remove(g)
    P.barrier()
    A.reset(m0)


def phase_conv_pool(k, l):
    P, A, I, S = k.P, k.A, k.I, k.S
    m0 = A.mark()
    cw, cwB = A.alloc([128, 8, 3], F32, "cw")
    P.op("sp", C("dma_start", out=cw, in_=I["convw"][l]), writes=[cwB], dma=True)
    inb = [[A.alloc([128, NT], BF16, f"cv{j}_{i}") for j in range(4)] for i in range(2)]
    ap_, apB = A.alloc([128, NT + 2], F32, "apad")
    y_, yB = A.alloc([128, NT], F32, "ycv")
    y2, y2B = A.alloc([128, NT], F32, "ycv2")
    ost = [A.alloc([128, NT], BF16, f"cvo{i}") for i in range(2)]
    P.op("pool", C("memset", ap_, 0.0), writes=[apB])
    names = ("cu", "cc", "cb", "cz")
    for cc in range(8):
        tl = inb[cc % 2]
        for j, nm in enumerate(names):
            t_, tB = tl[j]
            r0 = FROW[nm] + cc * 128
            P.op("sp", C("dma_start", out=t_, in_=S["F"][r0:r0 + 128, :]), reads=[k.SB["F"]], writes=[tB], dma=True)
        (cu, cuB), (cg, cgB), (cb, cbB), (cz, czB) = tl
        w0, w1, w2 = cw[:, cc, 0:1], cw[:, cc, 1:2], cw[:, cc, 2:3]
        P.op("pool", C("tensor_tensor", out=ap_[:, 1:NT + 1], in0=cu, in1=cg, op=ALU.mult), reads=[cuB, cgB], writes=[apB])
        P.op("dve", C("tensor_scalar", out=y_, in0=ap_[:, 1:NT + 1], scalar1=w1, scalar2=None, op0=ALU.mult), reads=[apB, cwB], writes=[yB])
        P.op("dve", C("scalar_tensor_tensor", out=y_, in0=ap_[:, 0:NT], scalar=w0, in1=y_, op0=ALU.mult, op1=ALU.add), reads=[apB, cwB, yB], writes=[yB])
        P.op("dve", C("scalar_tensor_tensor", out=y_, in0=ap_[:, 2:NT + 2], scalar=w2, in1=y_, op0=ALU.mult, op1=ALU.add), reads=[apB, cwB, yB], writes=[yB])
        P.op("dve", C("tensor_scalar", out=y_[:, 255:256], in0=ap_[:, 255:256], scalar1=w0, scalar2=None, op0=ALU.mult), reads=[apB, cwB, yB], writes=[yB])
        P.op("dve", C("scalar_tensor_tensor", out=y_[:, 255:256], in0=ap_[:, 256:257], scalar=w1, in1=y_[:, 255:256], op0=ALU.mult, op1=ALU.add), reads=[apB, cwB, yB], writes=[yB])
        P.op("dve", C("tensor_scalar", out=y_[:, 256:257], in0=ap_[:, 257:258], scalar1=w1, scalar2=None, op0=ALU.mult), reads=[apB, cwB, yB], writes=[yB])
        P.op("dve", C("scalar_tensor_tensor", out=y_[:, 256:257], in0=ap_[:, 258:259], scalar=w2, in1=y_[:, 256:257], op0=ALU.mult, op1=ALU.add), reads=[apB, cwB, yB], writes=[yB])
        P.op("pool", C("tensor_tensor", out=y2, in0=y_, in1=cb, op=ALU.mult), reads=[yB, cbB], writes=[y2B])
        o_, oB = ost[cc % 2]
        P.op("pool", C("tensor_tensor", out=o_, in0=y2, in1=cz, op=ALU.mult), reads=[y2B, czB], writes=[oB])
        P.op("sp", C("dma_start", out=S["Y"][2][cc * 128:(cc + 1) * 128, :], in_=o_), reads=[oB], writes=[k.SB["Y"]], dma=True, part=True)
    P.barrier()
    A.reset(m0)
    OC, OL = 8, 8 + 256 + 16
    PW = OL + NL + 16
    psc, pscB = A.alloc([128, 8], F32, "psc")
    P.op("sp", C("dma_start", out=psc, in_=I["pscale"][l]), writes=[pscB], dma=True)
    pub = [A.alloc([128, NT], BF16, f"pu{i}") for i in range(2)]
    pzb = [A.alloc([128, NT], BF16, f"pz{i}") for i in range(2)]
    up = [A.alloc([128, PW], F32, f"up{i}") for i in range(2)]
    sa, saB = A.alloc([128, PW], F32, "sa")
    sb_, sbB = A.alloc([128, PW], F32, "sb")
    rcb, rcbB = A.alloc([128, NT], F32, "rcb")
    dT = [A.alloc([128, NT], BF16, f"dT{i}") for i in range(2)]
    pw = [A.alloc([128, 2, 256], BF16, f"pw{i}") for i in range(2)]
    yo = [A.alloc([128, NT], BF16, f"ypo{i}") for i in range(2)]
    for u_, uB in up:
        P.op("pool", C("memset", u_, 0.0), writes=[uB])
    P.op("pool", C("memset", sa, 0.0), writes=[saB])
    P.op("pool", C("memset", sb_, 0.0), writes=[sbB])
    nps = 0
    for g, w in enumerate((2, 4, 8, 16)):
        P.op("sp", C("dma_start", out=rcb, in_=I["rcnt"][g, :].partition_broadcast(128)), writes=[rcbB], dma=True)
        pw_, pwB = pw[g % 2]
        P.op("pool", C("dma_start", out=pw_, in_=I["pool_w"][l, g].rearrange("(kc p) o -> p kc o", p=128)), writes=[pwB], dma=True)
        for kc2 in range(2):
            ct = g * 2 + kc2
            pu_, puB = pub[kc2]
            u_, uB = up[kc2]
            d_, dB = dT[kc2]
            P.op("sp", C("dma_start", out=pu_, in_=S["F"][FROW["pu"] + ct * 128:FROW["pu"] + (ct + 1) * 128, :]), reads=[k.SB["F"]], writes=[puB], dma=True)
            P.op("pool", C("tensor_copy", out=u_[:, OC:OC + NC], in_=pu_[:, 0:NC]), reads=[puB], writes=[uB], part=True)
            P.op("pool", C("tensor_copy", out=u_[:, OL:OL + NL], in_=pu_[:, NC:NT]), reads=[puB], writes=[uB], part=True)
            cur, curB = u_, uB
            m = 1
            pp = [(sa, saB), (sb_, sbB)]
            si = 0
            while m < w:
                nx, nxB = pp[si % 2]
                si += 1
                P.op("dve", C("tensor_tensor", out=nx[:, 0:PW - m], in0=cur[:, 0:PW - m], in1=cur[:, m:PW], op=ALU.add), reads=[curB], writes=[nxB])
                cur, curB = nx, nxB
                m *= 2
            hw_ = w // 2
            for (po, to, n) in ((OC, 0, NC), (OL, NC, NL)):
                P.op("dve", C("tensor_tensor", out=sa[:, po:po + n] if cur is not sa else sb_[:, po:po + n],
                                                                                        in0=cur[:, po - hw_:po - hw_ + n], in1=rcb[:, to:to + n], op=ALU.mult),
                     reads=[curB, rcbB], writes=[saB if cur is not sa else sbB])
                tmpb, tmpB = (sa, saB) if cur is not sa else (sb_, sbB)
                P.op("pool", C("tensor_tensor", out=d_[:, to:to + n], in0=tmpb[:, po:po + n], in1=u_[:, po:po + n], op=ALU.subtract),
                     reads=[tmpB, uB], writes=[dB], part=True)
        for oc in range(2):
            ct = g * 2 + oc
            pz_, pzB = pzb[oc]
            o_, oB = yo[oc]
            P.op("sp", C("dma_start", out=pz_, in_=S["F"][FROW["pz"] + ct * 128:FROW["pz"] + (ct + 1) * 128, :]), reads=[k.SB["F"]], writes=[pzB], dma=True)
            for (t0, n) in BLKS:
                bi = nps % 4
                nps += 1
                ps, psB = k.pb[bi], k.pbB[bi]
                for kc2 in range(2):
                    P.op("pe", C("matmul", ps[:, 0:n], lhsT=pw_[:, kc2, oc * 128:(oc + 1) * 128], rhs=dT[kc2][0][:, t0:t0 + n],
                                                                                          start=(kc2 == 0), stop=(kc2 == 1)), reads=[pwB, dT[kc2][1]], writes=[psB])
                P.op("dve", C("scalar_tensor_tensor", out=o_[:, t0:t0 + n], in0=ps[:, 0:n], scalar=psc[:, ct:ct + 1], in1=pz_[:, t0:t0 + n],
                                                                                                 op0=ALU.mult, op1=ALU.mult), reads=[psB, pscB, pzB], writes=[oB], part=True)
            P.op("sp", C("dma_start", out=S["Y"][3][ct * 128:(ct + 1) * 128, :], in_=o_), reads=[oB], writes=[k.SB["Y"]], dma=True, part=True)
    P.barrier()
    A.reset(m0)


def phase_merge_out(k, l):
    P, A, I, S = k.P, k.A, k.I, k.S
    last_layer = (l == DEPTH - 1)
    blocks = BLKS[1:] if last_layer else BLKS
    m0 = A.mark()
    wbr, _ = A.alloc([128, 16, 4 * 8 * 128], BF16, "wbr")
    wbrB = [Buf(f"wbr{ct}") for ct in range(16)]
    Yb, YbB = A.alloc([128, 4, 8, 512], BF16, "Yb")
    gm = [A.alloc([128, 4, 512], BF16, f"gm{i}") for i in range(2)]
    tm = [A.alloc([128, 512], F32, f"tm{i}") for i in range(4)]
    ast = [A.alloc([128, 512], BF16, f"ast{i}") for i in range(2)]
    for ct in range(16):
        P.op("pool", C("dma_start", out=wbr[:, ct, :], in_=I["wbt"][l, ct]), writes=[wbrB[ct]], dma=True)
    Yv = S["Y"].rearrange("b (kc p) t -> p b kc t", p=128)
    Gv = S["F"][FROW["gm"]:FROW["gm"] + 4 * D, :].rearrange("(b c p) t -> p b c t", p=128, c=16)
    nset = 0
    for (t0, n) in blocks:
        P.op("sp", C("dma_start", out=Yb[:, :, :, 0:n], in_=Yv[:, :, :, t0:t0 + n]), reads=[k.SB["Y"]], writes=[YbB], dma=True)
        for ct in range(16):
            g_, gB = gm[ct % 2]
            a_, aB = ast[ct % 2]
            w_ = wbr[:, ct, :].rearrange("p (b k c) -> p b k c", b=4, k=8)
            P.op("sp", C("dma_start", out=g_[:, :, 0:n], in_=Gv[:, :, ct, t0:t0 + n]), reads=[k.SB["F"]], writes=[gB], dma=True)
            base = 4 * (nset % 2)
            nset += 1
            for br in range(4):
                ps, psB = k.pb[base + br], k.pbB[base + br]
                for kc in range(8):
                    P.op("pe", C("matmul", ps[:, 0:n], lhsT=w_[:, br, kc, :], rhs=Yb[:, br, kc, 0:n], start=(kc == 0), stop=(kc == 7)),
                         reads=[wbrB[ct], YbB], writes=[psB])
                P.op("dve", C("tensor_tensor", out=tm[br][0][:, 0:n], in0=ps[:, 0:n], in1=g_[:, br, 0:n], op=ALU.mult),
                     reads=[psB, gB], writes=[tm[br][1]])
            P.op("pool", C("tensor_tensor", out=tm[0][0][:, 0:n], in0=tm[0][0][:, 0:n], in1=tm[1][0][:, 0:n], op=ALU.add), reads=[tm[0][1], tm[1][1]], writes=[tm[0][1]])
            P.op("pool", C("tensor_tensor", out=tm[2][0][:, 0:n], in0=tm[2][0][:, 0:n], in1=tm[3][0][:, 0:n], op=ALU.add), reads=[tm[2][1], tm[3][1]], writes=[tm[2][1]])
            P.op("pool", C("tensor_tensor", out=a_[:, 0:n], in0=tm[0][0][:, 0:n], in1=tm[2][0][:, 0:n], op=ALU.add), reads=[tm[0][1], tm[2][1]], writes=[aB])
            P.op("sp", C("dma_start", out=S["ACC"][ct * 128:(ct + 1) * 128, t0:t0 + n], in_=a_[:, 0:n]), reads=[aB], writes=[k.SB["ACC"]], dma=True, part=True)
    P.barrier()
    A.reset(m0)
    wo, _ = A.alloc([128, KC, D], BF16, "wo")
    woB = [Buf(f"wo{cg}") for cg in range(4)]
    accT = [A.alloc([128, 16, 512], BF16, f"accT{i}") for i in range(2)]
    xb = [A.alloc([128, 4, D], F32, f"xblk{i}") for i in range(2)]
    gtb = [A.alloc([128, D], F32, f"gtb{i}") for i in range(2)]
    t5, t5B = A.alloc([128, 512], F32, "t5")
    ss, ssB = A.alloc([128, 4], F32, "fss")
    junk, junkB = A.alloc([128, D], BF16, "junkf")
    wov = I["w_out"][l].rearrange("(kc p) c -> p kc c", p=128)
    for cg in range(4):
        P.op("pool", C("dma_start", out=wo[:, :, cg * 512:(cg + 1) * 512], in_=wov[:, :, cg * 512:(cg + 1) * 512]), writes=[woB[cg]], dma=True)
    if last_layer:
        fgb, fgbB = A.alloc([128, D], F32, "fgb")
        P.op("sp", C("dma_start", out=fgb, in_=I["fgain"].partition_broadcast(128)), writes=[fgbB], dma=True)
    for r in range(2):
        P.op("sp", C("dma_start", out=gtb[r][0], in_=S["modd"][l, r, 2 * D:3 * D].partition_broadcast(128)), reads=[k.SB["modd"]], writes=[gtb[r][1]], dma=True)
    Av = S["ACC"].rearrange("(kc p) t -> p kc t", p=128)
    x1B = [k.SB["X1"]] if l > 0 else []
    nps = 0
    for bi, (t0, n) in enumerate(blocks):
        r = 1 if t0 < 256 else 0
        nti = n // 128
        a_, aB = accT[bi % 2]
        xblk, xblkB = xb[bi % 2]
        P.op("sp", C("dma_start", out=a_[:, :, 0:n], in_=Av[:, :, t0:t0 + n]), reads=[k.SB["ACC"]], writes=[aB], dma=True)
        for ti in range(nti):
            i = t0 // 128 + ti
            P.op("sp", C("dma_start", out=xblk[:, ti, :], in_=src_tile(k, l, i)), reads=x1B, writes=[xblkB], dma=True, part=(ti > 0))
        for ti in range(nti):
            i = t0 // 128 + ti
            for cg in range(4):
                b_ = nps % 8
                nps += 1
                ps, psB = k.pb[b_], k.pbB[b_]
                for kc in range(KC):
                    P.op("pe", C("matmul", ps, lhsT=a_[:, kc, ti * 128:(ti + 1) * 128], rhs=wo[:, kc, cg * 512:(cg + 1) * 512], start=(kc == 0), stop=(kc == KC - 1)),
                         reads=[aB, woB[cg]], writes=[psB])
                P.op("dve", C("tensor_tensor", out=t5, in0=ps, in1=gtb[r][0][:, cg * 512:(cg + 1) * 512], op=ALU.mult), reads=[psB, gtb[r][1]], writes=[t5B])
                P.op("pool", C("tensor_tensor", out=xblk[:, ti, cg * 512:(cg + 1) * 512], in0=xblk[:, ti, cg * 512:(cg + 1) * 512], in1=t5, op=ALU.add),
                     reads=[t5B, xblkB], writes=[xblkB], part=True)
            if not last_layer:
                P.op("sp", C("dma_start", out=S["X1"][i * 128:(i + 1) * 128, :], in_=xblk[:, ti, :]), reads=[xblkB], writes=[k.SB["X1"]], dma=True, part=True)
            else:
                P.op("act", C("activation", out=junk, in_=xblk[:, ti, :], func=AF.Square, accum_out=ss[:, ti:ti + 1]), reads=[xblkB], writes=[junkB, ssB])
                P.op("dve", C("tensor_scalar", out=ss[:, ti:ti + 1], in0=ss[:, ti:ti + 1], scalar1=1.0 / D, scalar2=EPS, op0=ALU.mult, op1=ALU.add), reads=[ssB], writes=[ssB])
                P.op("act", C("activation", out=ss[:, ti:ti + 1], in_=ss[:, ti:ti + 1], func=AF.Sqrt), reads=[ssB], writes=[ssB])
                P.op("dve", C("reciprocal", out=ss[:, ti:ti + 1], in_=ss[:, ti:ti + 1]), reads=[ssB], writes=[ssB])
                P.op("dve", C("scalar_tensor_tensor", out=xblk[:, ti, :], in0=xblk[:, ti, :], scalar=ss[:, ti:ti + 1], in1=fgb, op0=ALU.mult, op1=ALU.mult),
                     reads=[xblkB, ssB, fgbB], writes=[xblkB], part=True)
                P.op("sp", C("dma_start", out=S["out"][(i - 2) * 128:(i - 1) * 128, :], in_=xblk[:, ti, :]), reads=[xblkB], writes=[k.SB["out"]], dma=True, part=True)
    P.barrier()
    A.reset(m0)


def host_constants():
    C = {}
    C["ident"] = np.eye(128, dtype=np.float32)
    s = np.arange(128)[:, None]
    t = np.arange(128)[None, :]
    C["masks"] = np.stack([(s <= t), (s >= t)]).astype(np.float32)
    freq = (10000.0 ** (-np.arange(32, dtype=np.float32) / 32)).astype(np.float32)
    rope = np.zeros((NTI, 128, 2, 8, 32), np.float32)
    rope[:, :, 0] = 1.0
    for i in range(2, NTI):
        tt = (i - 2) * 128 + np.arange(128)
        row = (tt // 64).astype(np.float32)
        col = (tt % 64).astype(np.float32)
        for half, pos in enumerate((row, col)):
            ang = (pos[:, None] * freq[None, :]).astype(np.float32)
            for h in range(4):
                rope[i, :, 0, h * 2 + half, :] = np.cos(ang)
                rope[i, :, 1, h * 2 + half, :] = np.sin(ang)
    C["rope"] = rope.reshape(NTI, 128, 2, 256)
    sel = np.zeros((64, 8, 128), np.float32)
    for d in range(2):
        for h in range(4):
            sel[d * 32 + h, d * 4 + h, :] = 1.0
    C["sel"] = sel
    rc = np.zeros((4, NT), np.float32)
    for g, w in enumerate((2, 4, 8, 16)):
        for (o, T) in ((0, NC), (NC, NL)):
            tt = np.arange(T)
            lo = np.clip(tt - w // 2, 0, T)
            hi = np.clip(tt + w - w // 2, 0, T)
            rc[g, o:o + T] = 1.0 / (hi - lo).astype(np.float32)
    C["rcnt"] = rc
    return C


_CACHE = {}
NCORES = 4


def make_in_maps(inputs):
    f = lambda a: np.ascontiguousarray(np.asarray(a, dtype=np.float32))
    x, c, ctx, c_ctx = f(inputs["x"]), f(inputs["c"]), f(inputs["ctx"]), f(inputs["c_ctx"])
    C = host_constants()
    shared = dict(C)
    shared["ngcol"] = f(inputs["norm_gain"]).reshape(DEPTH, KC, 128).transpose(0, 2, 1).copy()
    shared["w_mod"] = f(inputs["w_mod"])
    shared["b_mod"] = f(inputs["b_mod"])
    shared["w_in"] = f(inputs["w_in"])
    shared["qg"] = f(inputs["q_norm_gain"])
    shared["kg"] = f(inputs["k_norm_gain"])
    gb = f(inputs["mlstm_gate_bias"])
    gbias = np.zeros((DEPTH, 64, 2), np.float32)
    gbias[:, 0:4, 0] = gb[:, 0]
    gbias[:, 32:36, 0] = gb[:, 2]
    gbias[:, 0:4, 1] = gb[:, 1]
    gbias[:, 32:36, 1] = gb[:, 3]
    shared["gbias"] = gbias
    shared["mgain"] = f(inputs["mlstm_norm_gain"])
    shared["convw"] = f(inputs["conv_w"]).reshape(DEPTH, 3, 8, 128).transpose(0, 3, 2, 1).copy()
    shared["pool_w"] = f(inputs["pool_w"])
    shared["pscale"] = f(inputs["pool_scale"]).reshape(DEPTH, 8, 128).transpose(0, 2, 1).copy()
    wb = f(inputs["w_branch"])
    shared["wbt"] = wb.reshape(DEPTH, 4, 8, 128, 16, 128).transpose(0, 4, 3, 1, 2, 5).reshape(DEPTH, 16, 128, 4 * 8 * 128).copy()
    shared["w_out"] = f(inputs["w_out"])
    shared["fgain"] = f(inputs["final_norm_gain"])
    maps = []
    for core in range(NCORES):
        b = core % 4
        m = dict(shared)
        m["x"] = x[b]
        m["ctx"] = ctx[b]
        c2 = np.stack([c[b], c_ctx])
        m["c2T"] = c2.reshape(2, KC, 128).transpose(2, 1, 0).copy()
        maps.append(m)
    return maps


def kernel(**inputs):
    if "nc" not in _CACHE:
        _CACHE["nc"] = build_program()
    nc = _CACHE["nc"]
    maps = make_in_maps(inputs)
    res = run_bass_kernel_spmd(nc, maps, core_ids=list(range(NCORES)))
    out = np.stack([np.asarray(res.results[b]["out"]) for b in range(4)]).astype(np.float32)
    return out
```

```python
import contextlib
import numpy as np
import concourse.bass as bass
import concourse.mybir as mybir
from concourse.bass_utils import run_bass_kernel_spmd

F32 = mybir.dt.float32
BF16 = mybir.dt.bfloat16
AF = mybir.ActivationFunctionType
ALU = mybir.AluOpType
AX = mybir.AxisListType

NT = 4352
NTI = 34
NL = 4096
NC = 256
D = 2048
KC = 16
EPS = 1e-6
BLKS = [(0, 256)] + [(256 + 512 * j, 512) for j in range(8)]
DEPTH = 2
WCOL = dict(aq=0, ak=1024, av=1280, az=1536, mq=2560, mk=3584, mv=4608, mo=5632, mz=6656, mg=7680,
            cu=7696, cb=8720, cc=9744, cz=10768, pu=11792, pz=12816, gm=13840)
N_IN = 22032
FROW = dict(az=0, mz=1024, cz=2048, pz=3072, mo=4096, gm=5120, mq=13312, mk=14336, cu=15360, cb=16384,
            cc=17408, pu=18432)
NF = 19456
FSEG = [("az", 1024, "silu"), ("mz", 1024, "silu"), ("cz", 1024, "silu"), ("pz", 1024, "silu"),
        ("mo", 1024, "sig"), ("gm", 8192, "sig"),
        ("mq", 1024, "copy"), ("mk", 1024, "copy16"), ("cu", 1024, "copy"), ("cb", 1024, "copy"),
        ("cc", 1024, "copy"), ("pu", 1024, "copy")]


class Tok:
    __slots__ = ("q", "sem", "val", "rec")

    def __init__(self, q, rec):
        self.q = q
        self.rec = rec
        self.sem = None
        self.val = None


class Buf:
    __slots__ = ("name", "w", "r", "excl", "wf")

    def __init__(self, name="", excl=False):
        self.name = name
        self.w = {}
        self.wf = {}
        self.r = {}
        self.excl = excl


class Rec:
    __slots__ = ("fn", "deps", "signal", "tok", "dma", "dsem")

    def __init__(self, fn, deps, dma):
        self.fn = fn
        self.deps = deps
        self.signal = False
        self.tok = None
        self.dma = dma
        self.dsem = None


class Prog:
    NDMA = 8

    def __init__(self, nc):
        self.nc = nc
        self.eng = {"pe": nc.tensor, "act": nc.scalar, "dve": nc.vector, "pool": nc.gpsimd, "sp": nc.sync}
        self.q = {k: [] for k in self.eng}
        self.dma_n = {k: 0 for k in self.eng}
        self.dma_last = {k: [None] * self.NDMA for k in self.eng}
        self.last = {k: None for k in self.eng}
        self.fence = {k: [] for k in self.eng}

    def barrier(self):
        toks = []
        for q in self.eng:
            if self.last[q] is not None:
                toks.append(self.last[q])
            toks.extend(t for t in self.dma_last[q] if t is not None)
        for q in self.eng:
            self.fence[q] = list(toks)

    def op(self, q, fn, reads=(), writes=(), dma=False, part=False):
        deps = []
        if self.fence[q]:
            deps.extend(self.fence[q])
            self.fence[q] = []
        for b in reads:
            deps.extend(b.w.values())
            if b.excl:
                deps.extend(t for kk, t in b.r.items() if kk[0] != q)
        for b in writes:
            deps.extend(b.r.values())
            if not part:
                deps.extend(b.w.values())
            else:
                deps.extend(b.wf.values())
        rec = Rec(fn, deps, dma)
        tok = Tok(q, rec)
        rec.tok = tok
        if dma:
            n = self.dma_n[q]
            self.dma_n[q] = n + 1
            slot = n % self.NDMA
            prev = self.dma_last[q][slot]
            if prev is not None:
                rec.deps.append(prev)
            self.dma_last[q][slot] = tok
            rec.dsem = slot
        else:
            self.last[q] = tok
        self.q[q].append(rec)
        key = (q, rec.dsem)
        for b in reads:
            b.r[key] = tok
        for b in writes:
            if part:
                b.w[key] = tok
            else:
                b.w = {key: tok}
                b.wf = {key: tok}
                b.r = {}
        return tok

    def emit(self, sems, dsems):
        for q, recs in self.q.items():
            for rec in recs:
                for t in rec.deps:
                    if t.rec.dma:
                        continue
                    if t.q == q and q == "pe":
                        continue
                    t.rec.signal = True
        for q, recs in self.q.items():
            cnt = 0
            dcnt = [0] * self.NDMA
            for rec in recs:
                if rec.dma:
                    dcnt[rec.dsem] += 16
                    rec.tok.sem = dsems[q][rec.dsem]
                    rec.tok.val = dcnt[rec.dsem]
                elif rec.signal:
                    cnt += 1
                    rec.tok.sem = sems[q]
                    rec.tok.val = cnt
        self.stats = {}
        with self.nc.Block() as block:
            def mk(q):
                def body(e):
                    seen = {}
                    nw = 0
                    for rec in self.q[q]:
                        need = {}
                        for t in rec.deps:
                            if t.sem is None or (t.q == q and q == "pe" and not t.rec.dma):
                                continue
                            k = id(t.sem)
                            if seen.get(k, 0) >= t.val:
                                continue
                            if k not in need or need[k][1] < t.val:
                                need[k] = (t.sem, t.val)
                        for k, (s, v) in need.items():
                            e.wait_ge(s, v)
                            seen[k] = v
                            nw += 1
                        ins = rec.fn(e)
                        if rec.dma:
                            ins.then_inc(rec.tok.sem, 16)
                        elif rec.signal:
                            ins.then_inc(rec.tok.sem, 1)
                    for t in self.dma_last[q]:
                        if t is not None:
                            e.wait_ge(t.sem, t.val)
                    self.stats[q] = (len(self.q[q]), nw)
                return body
            block.tensor(mk("pe"))
            block.scalar(mk("act"))
            block.vector(mk("dve"))
            block.gpsimd(mk("pool"))
            block.sync(mk("sp"))


def C(name, *a, **kw):
    return lambda e: getattr(e, name)(*a, **kw)


class Arena:
    def __init__(self, ap, nbytes):
        self.ap = ap
        self.cap = nbytes
        self.top = 0

    def mark(self):
        return self.top

    def reset(self, m):
        self.top = m

    def alloc(self, shape, dt, name=""):
        esz = 2 if dt == BF16 else 4
        n = int(np.prod(shape[1:]))
        nb = (n * esz + 31) // 32 * 32
        off = self.top
        self.top += nb
        assert self.top <= self.cap, f"arena overflow {self.top} > {self.cap} at {name}"
        v = self.ap[0:shape[0], off // 4: off // 4 + (n * esz + 3) // 4]
        if dt != F32:
            v = v.bitcast(dt)[:, 0:n]
        if len(shape) == 3:
            v = v.rearrange("p (a b) -> p a b", b=shape[2])
        elif len(shape) == 4:
            v = v.rearrange("p (a b c) -> p a b c", b=shape[2], c=shape[3])
        return v, Buf(name)


class K:
    pass


def build_program(debug=(), stop_after=None, nlayers=DEPTH):
    nc = bass.Bass("TRN2", target_bir_lowering=False)
    P = Prog(nc)
    k = K()
    k.nc, k.P = nc, P
    k.stop = stop_after

    def din(name, shape, dt=F32):
        return nc.dram_tensor(name, list(shape), dt, kind="ExternalInput").ap()

    def dscr(name, shape, dt, out=False):
        return nc.dram_tensor(name, list(shape), dt, kind=("ExternalOutput" if (out or name in debug) else "Internal")).ap()

    I = {}
    I["x"] = din("x", [NL, D])
    I["ctx"] = din("ctx", [NC, D])
    I["c2T"] = din("c2T", [128, KC, 2])
    I["ngcol"] = din("ngcol", [DEPTH, 128, KC])
    I["w_mod"] = din("w_mod", [DEPTH, D, 3 * D])
    I["b_mod"] = din("b_mod", [DEPTH, 3 * D])
    I["w_in"] = din("w_in", [DEPTH, D, N_IN])
    I["qg"] = din("qg", [DEPTH, 128])
    I["kg"] = din("kg", [DEPTH, 128])
    I["gbias"] = din("gbias", [DEPTH, 64, 2])
    I["mgain"] = din("mgain", [DEPTH, 1024])
    I["convw"] = din("convw", [DEPTH, 128, 8, 3])
    I["pool_w"] = din("pool_w", [DEPTH, 4, 256, 256])
    I["pscale"] = din("pscale", [DEPTH, 128, 8])
    I["wbt"] = din("wbt", [DEPTH, 16, 128, 4 * 8 * 128])
    I["w_out"] = din("w_out", [DEPTH, D, D])
    I["fgain"] = din("fgain", [D])
    I["ident"] = din("ident", [128, 128])
    I["masks"] = din("masks", [2, 128, 128])
    I["rope"] = din("rope", [NTI, 128, 2, 256])
    I["sel"] = din("sel", [64, 8, 128])
    I["rcnt"] = din("rcnt", [4, NT])
    k.I = I
    S = {}
    S["modd"] = dscr("modd", [DEPTH, 2, 3 * D], F32)
    S["F"] = dscr("Fs", [NF, NT], BF16)
    S["QT"] = dscr("QT", [1024, NT], BF16)
    S["KT"] = dscr("KT", [256, NT], BF16)
    S["Vt"] = dscr("Vt", [NT, 256], BF16)
    S["MKt"] = dscr("MKt", [NT, 1024], BF16)
    S["MVt"] = dscr("MVt", [NT, 1024], BF16)
    S["HF"] = dscr("HF", [2, NT, 1024], BF16)
    S["Y"] = dscr("Y", [4, 1024, NT], BF16)
    S["X1"] = dscr("X1", [NT, D], F32)
    S["ACC"] = dscr("ACC", [D, NT], BF16)
    S["out"] = dscr("out", [NL, D], F32, out=True)
    k.S = S
    k.SB = {n: Buf(n) for n in S}

    with contextlib.ExitStack() as es:
        ARENA_BYTES = 206 * 1024
        arena_t = es.enter_context(nc.sbuf_tensor("arena", [128, ARENA_BYTES // 4], F32))
        A = Arena(arena_t, ARENA_BYTES)
        k.A = A
        k.pb2 = [es.enter_context(nc.psum_tensor(f"pbb{i}", [128, 1024], F32))[:, :] for i in range(4)]
        k.pb = [k.pb2[i // 2][:, (i % 2) * 512:(i % 2 + 1) * 512] for i in range(8)]
        k.pbB = [Buf(f"pb{i}", excl=True) for i in range(8)]
        sems = {q: es.enter_context(nc.semaphore("s_" + q)) for q in P.eng}
        dsems = {q: [es.enter_context(nc.semaphore(f"d_{q}{i}")) for i in range(P.NDMA)] for q in P.eng}

        k.idf, k.idfB = A.alloc([128, 128], F32, "idf")
        k.idb, k.idbB = A.alloc([128, 128], BF16, "idb")
        k.onesb, k.onesbB = A.alloc([128, 128], BF16, "onesb")
        k.modcol, k.modcolB = A.alloc([128, DEPTH, 48, 2], F32, "modcol")
        k.gcol, k.gcolB = A.alloc([128, DEPTH, KC, 2], F32, "gcol")
        k.Gtok, k.GtokB = A.alloc([128, NTI, 2, 36], F32, "Gtok")
        P.op("pool", C("memset", k.Gtok, 0.0), writes=[k.GtokB])
        P.op("sp", C("dma_start", out=k.idf, in_=I["ident"]), writes=[k.idfB], dma=True)
        P.op("pool", C("dma_start", out=k.idb, in_=I["ident"]), writes=[k.idbB], dma=True)
        P.op("dve", C("memset", k.onesb, 1.0), writes=[k.onesbB])

        phase_mod(k)
        for l in range(nlayers if stop_after != ("mod", 0) else 0):
            last = (l == DEPTH - 1)
            phase_norm_inproj(k, l)
            if stop_after in (("inproj", l), ("norm", l), ("tokmaj", l)):
                break
            phase_attn(k, l)
            if stop_after == ("attn", l):
                break
            phase_mlstm(k, l)
            if stop_after == ("mlstm", l):
                break
            phase_conv_pool(k, l)
            if stop_after == ("convpool", l):
                break
            phase_merge_out(k, l)
        P.emit(sems, dsems)
    return nc


def phase_mod(k):
    P, A, I, S = k.P, k.A, k.I, k.S
    m0 = A.mark()
    cT, cTB = A.alloc([128, KC, 2], F32, "cT")
    scT, scTB = A.alloc([128, KC, 2], F32, "scT")
    ngc, ngcB = A.alloc([128, DEPTH, KC], F32, "ngc")
    wm = [A.alloc([128, 3072], F32, f"wm{i}") for i in range(3)]
    mo, moB = A.alloc([2, 6144], F32, "mo")
    bm, bmB = A.alloc([2, 6144], F32, "bm")
    tmp, tmpB = A.alloc([128, KC, 2], F32, "tmpg")
    P.op("sp", C("dma_start", out=cT, in_=I["c2T"]), writes=[cTB], dma=True)
    P.op("sp", C("dma_start", out=ngc, in_=I["ngcol"].rearrange("l p k -> p l k")), writes=[ngcB], dma=True)
    P.op("act", C("activation", out=scT, in_=cT, func=AF.Silu), reads=[cTB], writes=[scTB])
    n = 0
    for l in range(DEPTH):
        P.op("sp", C("dma_start", out=bm, in_=I["b_mod"][l, :].partition_broadcast(2)), writes=[bmB], dma=True)
        for half in range(2):
            for kc in range(KC):
                w, wB = wm[n % 3]
                n += 1
                P.op("sp", C("dma_start",
                    out=w, in_=I["w_mod"][l, kc * 128:(kc + 1) * 128, half * 3072:(half + 1) * 3072]), writes=[wB], dma=True)
                for j in range(6):
                    P.op("pe", C("matmul", k.pb[j][0:2, :], lhsT=scT[:, kc, :], rhs=w[:, j * 512:(j + 1) * 512],
                                                                start=(kc == 0), stop=(kc == KC - 1)),
                         reads=[scTB, wB], writes=[k.pbB[j]])
            for j in range(6):
                c0 = half * 3072 + j * 512
                P.op("dve", C("tensor_tensor", out=mo[:, c0:c0 + 512], in0=k.pb[j][0:2, :], in1=bm[:, c0:c0 + 512], op=ALU.add),
                     reads=[k.pbB[j], bmB], writes=[moB], part=True)
        P.op("sp", C("dma_start", out=S["modd"][l], in_=mo), reads=[moB], writes=[k.SB["modd"]], dma=True, part=True)
        for j in range(48):
            P.op("pe", C("transpose", k.pb[6][:, 2 * j:2 * j + 2], mo[0:2, j * 128:(j + 1) * 128], k.idf[0:2, 0:2]),
                 reads=[moB, k.idfB], writes=[k.pbB[6]])
        P.op("dve", C("tensor_copy", out=k.modcol[:, l], in_=k.pb[6][:, 0:96].rearrange("p (j r) -> p j r", r=2)),
             reads=[k.pbB[6]], writes=[k.modcolB], part=True)
        P.op("dve", C("tensor_scalar", out=tmp, in0=k.modcol[:, l, 16:32, :], scalar1=1.0, scalar2=None, op0=ALU.add),
             reads=[k.modcolB], writes=[tmpB])
        P.op("dve", C("tensor_tensor", out=k.gcol[:, l], in0=tmp, in1=ngc[:, l, :].unsqueeze(2).to_broadcast([128, KC, 2]), op=ALU.mult),
             reads=[tmpB, ngcB], writes=[k.gcolB], part=True)
    P.barrier()
    A.reset(m0)


def src_tile(k, l, i):
    if l == 0:
        if i < 2:
            return k.I["ctx"][i * 128:(i + 1) * 128, :]
        return k.I["x"][(i - 2) * 128:(i - 1) * 128, :]
    return k.S["X1"][i * 128:(i + 1) * 128, :]


def phase_norm_inproj(k, l):
    P, A, I, S = k.P, k.A, k.I, k.S
    m0 = A.mark()
    hT, hTB = A.alloc([128, KC, NT], BF16, "hT")
    m1 = A.mark()
    xt = [A.alloc([128, D], F32, f"xt{i}") for i in range(2)]
    xn = [A.alloc([128, D], BF16, f"xn{i}") for i in range(2)]
    junk, junkB = A.alloc([128, D], BF16, "junk")
    ss, ssB = A.alloc([128, NTI], F32, "ss")
    rs, rsB = A.alloc([128, NTI], F32, "rs")
    x1B = [k.SB["X1"]] if l > 0 else []
    for i in range(NTI):
        r = 1 if i < 2 else 0
        x_, xB = xt[i % 2]
        n_, nB = xn[i % 2]
        P.op("sp", C("dma_start", out=x_, in_=src_tile(k, l, i)), reads=x1B, writes=[xB], dma=True)
        P.op("act", C("activation", out=junk, in_=x_, func=AF.Square, accum_out=ss[:, i:i + 1]),
             reads=[xB], writes=[junkB, ssB])
        P.op("dve", C("tensor_scalar", out=rs[:, i:i + 1], in0=ss[:, i:i + 1], scalar1=1.0 / D, scalar2=EPS, op0=ALU.mult, op1=ALU.add),
             reads=[ssB], writes=[rsB])
        P.op("act", C("activation", out=rs[:, i:i + 1], in_=rs[:, i:i + 1], func=AF.Sqrt), reads=[rsB], writes=[rsB])
        P.op("dve", C("reciprocal", out=rs[:, i:i + 1], in_=rs[:, i:i + 1]), reads=[rsB], writes=[rsB])
        P.op("dve", C("tensor_scalar", out=n_, in0=x_, scalar1=rs[:, i:i + 1], scalar2=None, op0=ALU.mult),
             reads=[xB, rsB], writes=[nB])
        import os
        KD = os.environ.get("KDBG", "")
        for half in range(2):
            if KD == "A":
                break
            bi = (2 * i + half) % 4
            pbf = k.pb[bi][:, :].bitcast(BF16)
            for j in range(8):
                kc = half * 8 + j
                P.op("pe", C("transpose", pbf[:, j * 128:(j + 1) * 128], n_[:, kc * 128:(kc + 1) * 128], k.idb),
                     reads=[nB, k.idbB], writes=[k.pbB[bi]])
            for j in range(8):
                kc = half * 8 + j
                if half == 0 or KD == "B":
                    P.op("dve", C("tensor_scalar",
                        out=hT[:, kc, i * 128:(i + 1) * 128], in0=pbf[:, j * 128:(j + 1) * 128],
                        scalar1=k.gcol[:, l, kc, r:r + 1], scalar2=k.modcol[:, l, kc, r:r + 1], op0=ALU.mult, op1=ALU.add),
                        reads=[k.pbB[bi], k.gcolB, k.modcolB], writes=[hTB], part=True)
                else:
                    P.op("act", C("activation",
                        out=hT[:, kc, i * 128:(i + 1) * 128], in_=pbf[:, j * 128:(j + 1) * 128], func=AF.Identity,
                        bias=k.modcol[:, l, kc, r:r + 1], scale=k.gcol[:, l, kc, r:r + 1]),
                        reads=[k.pbB[bi], k.gcolB, k.modcolB], writes=[hTB], part=True)
    P.barrier()
    A.reset(m1)
    if k.stop == ("norm", l):
        A.reset(m0)
        return
    W = [A.alloc([128, KC, 512], BF16, f"W{i}") for i in range(2)]
    m2 = A.mark()
    t1, t1B = A.alloc([128, 512], F32, "t1")
    t2, t2B = A.alloc([128, 512], F32, "t2")
    t3, t3B = A.alloc([128, 512], F32, "t3")
    ta, taB = A.alloc([128, 512], F32, "ta")
    tb, tbB = A.alloc([128, 512], F32, "tb")
    qf, qfB = A.alloc([128, 512], BF16, "qf")
    ssq, ssqB = A.alloc([128, 4], F32, "ssq")
    rop = [A.alloc([128, 2, 8, 32], F32, f"rope{i}") for i in range(2)]
    qgb, qgbB = A.alloc([128, 128], F32, "qgb")
    kgb, kgbB = A.alloc([128, 128], F32, "kgb")
    qst = [A.alloc([128, 4, 512], BF16, f"qst{i}") for i in range(2)]
    vst = [A.alloc([128, 512], BF16, f"vst{i}") for i in range(2)]
    P.op("sp", C("dma_start", out=qgb, in_=I["qg"][l, :].partition_broadcast(128)), writes=[qgbB], dma=True)
    P.op("sp", C("dma_start", out=kgb, in_=I["kg"][l, :].partition_broadcast(128)), writes=[kgbB], dma=True)
    P.op("dve", C("tensor_scalar", out=qgb, in0=qgb, scalar1=float(128 ** -0.5), scalar2=None, op0=ALU.mult), reads=[qgbB], writes=[qgbB])
    wsrc = I["w_in"][l].rearrange("(kc p) c -> p kc c", p=128)
    groups = [("q", WCOL["aq"], 0), ("q", WCOL["aq"] + 512, 1), ("kv", WCOL["ak"], 0),
              ("mk", WCOL["mk"], 0), ("mk", WCOL["mk"] + 512, 1), ("mv", WCOL["mv"], 0), ("mv", WCOL["mv"] + 512, 1),
              ("mg", WCOL["mg"], 0)]
    nload = [0]

    def load_w(c0, ncol):
        w, wB = W[nload[0] % 2]
        nload[0] += 1
        P.op("pool", C("dma_start", out=w[:, :, 0:ncol], in_=wsrc[:, :, c0:c0 + ncol]), writes=[wB], dma=True)
        return w, wB

    def qk_post(ps, psB, nh, gb, gbB, rp, rpB, out, outB):
        Wd = nh * 128
        g = nh * 2
        P.op("act", C("activation", out=t1[:, 0:Wd], in_=ps[:, 0:Wd], func=AF.Square), reads=[psB], writes=[t1B])
        P.op("dve", C("tensor_reduce", out=ssq[:, 0:nh], in_=t1[:, 0:Wd].rearrange("p (h d) -> p h d", d=128), axis=AX.X, op=ALU.add),
             reads=[t1B], writes=[ssqB])
        P.op("dve", C("tensor_scalar", out=ssq[:, 0:nh], in0=ssq[:, 0:nh], scalar1=1.0 / 128, scalar2=EPS, op0=ALU.mult, op1=ALU.add),
             reads=[ssqB], writes=[ssqB])
        P.op("act", C("activation", out=ssq[:, 0:nh], in_=ssq[:, 0:nh], func=AF.Sqrt), reads=[ssqB], writes=[ssqB])
        P.op("dve", C("reciprocal", out=ssq[:, 0:nh], in_=ssq[:, 0:nh]), reads=[ssqB], writes=[ssqB])
        P.op("dve", C("tensor_tensor", out=t2[:, 0:Wd].rearrange("p (h d) -> p h d", d=128), in0=ps[:, 0:Wd].rearrange("p (h d) -> p h d", d=128),
                                              in1=ssq[:, 0:nh].unsqueeze(2).to_broadcast([128, nh, 128]), op=ALU.mult),
             reads=[psB, ssqB], writes=[t2B])
        P.op("pool", C("tensor_tensor", out=t3[:, 0:Wd].rearrange("p (h d) -> p h d", d=128), in0=t2[:, 0:Wd].rearrange("p (h d) -> p h d", d=128),
                                               in1=gb.unsqueeze(1).to_broadcast([128, nh, 128]), op=ALU.mult),
             reads=[t2B, gbB], writes=[t3B])
        t3v = t3[:, 0:Wd].rearrange("p (g x j) -> p g x j", x=2, j=32)
        tav = ta[:, 0:Wd].rearrange("p (g x j) -> p g x j", x=2, j=32)
        tbv = tb[:, 0:Wd].rearrange("p (g x j) -> p g x j", x=2, j=32)
        ov = out.rearrange("p (g x j) -> p g x j", x=2, j=32)
        P.op("pool", C("tensor_tensor", out=tav, in0=t3v, in1=rp[:, 0, 0:g, :].unsqueeze(2).to_broadcast([128, g, 2, 32]), op=ALU.mult),
             reads=[t3B, rpB], writes=[taB])
        P.op("pool", C("tensor_tensor", out=tbv[:, :, 0, :], in0=t3v[:, :, 1, :], in1=rp[:, 1, 0:g, :], op=ALU.mult),
             reads=[t3B, rpB], writes=[tbB], part=True)
        P.op("pool", C("tensor_tensor", out=tbv[:, :, 1, :], in0=t3v[:, :, 0, :], in1=rp[:, 1, 0:g, :], op=ALU.mult),
             reads=[t3B, rpB], writes=[tbB], part=True)
        P.op("dve", C("tensor_tensor", out=ov[:, :, 0, :], in0=tav[:, :, 0, :], in1=tbv[:, :, 0, :], op=ALU.subtract),
             reads=[taB, tbB], writes=[outB], part=True)
        P.op("dve", C("tensor_tensor", out=ov[:, :, 1, :], in0=tav[:, :, 1, :], in1=tbv[:, :, 1, :], op=ALU.add),
             reads=[taB, tbB], writes=[outB], part=True)

    nps = [0]
    cur = load_w(groups[0][1], 512)
    for gi, (kind, c0, sub) in enumerate(groups):
        w, wB = cur
        if gi + 1 < len(groups):
            nk, nc0, _ = groups[gi + 1]
            cur = load_w(nc0, 16 if nk == "mg" else 512)
        ncol = 16 if kind == "mg" else 512
        for i in range(NTI):
            bi = nps[0] % 4
            nps[0] += 1
            ps, psB = k.pb[bi], k.pbB[bi]
            if kind in ("q", "kv"):
                rp, rpB = rop[i % 2]
                P.op("sp", C("dma_start", out=rp, in_=I["rope"][i].rearrange("p a (g j) -> p a g j", j=32)), writes=[rpB], dma=True)
            for kc in range(KC):
                P.op("pe", C("matmul", ps[:, 0:ncol], lhsT=hT[:, kc, i * 128:(i + 1) * 128], rhs=w[:, kc, 0:ncol],
                                                                               start=(kc == 0), stop=(kc == KC - 1)),
                     reads=[hTB, wB], writes=[psB])
            if kind == "q":
                qk_post(ps, psB, 4, qgb, qgbB, rp, rpB, qf, qfB)
                grp = 0 if i < 2 else 1 + (i - 2) // 4
                pos = i if i < 2 else (i - 2) % 4
                st, stB = qst[grp % 2]
                tbi = 4 + (i % 2)
                pbf = k.pb[tbi][:, :].bitcast(BF16)
                for h in range(4):
                    P.op("pe", C("transpose", pbf[:, h * 128:(h + 1) * 128], qf[:, h * 128:(h + 1) * 128], k.idb),
                         reads=[qfB, k.idbB], writes=[k.pbB[tbi]])
                P.op("act", C("activation", out=st[:, :, pos * 128:(pos + 1) * 128],
                                                                            in_=pbf[:, 0:512].rearrange("p (h t) -> p h t", t=128), func=AF.Copy),
                     reads=[k.pbB[tbi]], writes=[stB], part=True)
                done = (i == 1) or (i >= 2 and pos == 3)
                if done:
                    t0 = 0 if i < 2 else 256 + ((i - 2) // 4) * 512
                    n = 256 if i < 2 else 512
                    P.op("sp", C("dma_start",
                        out=S["QT"].rearrange("(h d) t -> d h t", d=128)[:, sub * 4:(sub + 1) * 4, t0:t0 + n], in_=st[:, :, 0:n]),
                        reads=[stB], writes=[k.SB["QT"]], dma=True, part=True)
            elif kind == "kv":
                qk_post(ps, psB, 2, kgb, kgbB, rp, rpB, qf[:, 0:256], qfB)
                grp = 0 if i < 2 else 1 + (i - 2) // 4
                pos = i if i < 2 else (i - 2) % 4
                st, stB = qst[grp % 2]
                tbi = 4 + (i % 2)
                pbf = k.pb[tbi][:, :].bitcast(BF16)
                for h in range(2):
                    P.op("pe", C("transpose", pbf[:, h * 128:(h + 1) * 128], qf[:, h * 128:(h + 1) * 128], k.idb),
                         reads=[qfB, k.idbB], writes=[k.pbB[tbi]])
                P.op("act", C("activation", out=st[:, 0:2, pos * 128:(pos + 1) * 128],
                                                                            in_=pbf[:, 0:256].rearrange("p (h t) -> p h t", t=128), func=AF.Copy),
                     reads=[k.pbB[tbi]], writes=[stB], part=True)
                done = (i == 1) or (i >= 2 and pos == 3)
                if done:
                    t0 = 0 if i < 2 else 256 + ((i - 2) // 4) * 512
                    n = 256 if i < 2 else 512
                    P.op("sp", C("dma_start",
                        out=S["KT"].rearrange("(h d) t -> d h t", d=128)[:, :, t0:t0 + n], in_=st[:, 0:2, 0:n]),
                        reads=[stB], writes=[k.SB["KT"]], dma=True, part=True)
                v_, vB = vst[i % 2]
                P.op("act", C("activation", out=v_[:, 0:256], in_=ps[:, 256:512], func=AF.Copy), reads=[psB], writes=[vB])
                P.op("sp", C("dma_start", out=S["Vt"][i * 128:(i + 1) * 128, :], in_=v_[:, 0:256]),
                     reads=[vB], writes=[k.SB["Vt"]], dma=True, part=True)
            elif kind in ("mk", "mv"):
                v_, vB = vst[i % 2]
                sc = 0.0625 if kind == "mk" else 1.0
                P.op("act", C("activation", out=v_, in_=ps, func=AF.Copy, scale=sc), reads=[psB], writes=[vB])
                dst = S["MKt"] if kind == "mk" else S["MVt"]
                dB = k.SB["MKt"] if kind == "mk" else k.SB["MVt"]
                P.op("sp", C("dma_start", out=dst[i * 128:(i + 1) * 128, sub * 512:(sub + 1) * 512], in_=v_),
                     reads=[vB], writes=[dB], dma=True, part=True)
            else:
                for gi in range(4):
                    P.op("dve", C("tensor_copy", out=k.Gtok[:, i, gi % 2, (gi // 2) * 32:(gi // 2) * 32 + 4], in_=ps[:, gi * 4:gi * 4 + 4]),
                         reads=[psB], writes=[k.GtokB], part=True)
    P.barrier()
    A.reset(m2)
    if k.stop == ("tokmaj", l):
        A.reset(m0)
        return
    stg = [A.alloc([128, NT], BF16, f"stg{i}") for i in range(2)]
    for st_, stB_ in stg:
        P.op("pool", C("memset", st_, 0.0), writes=[stB_])
    glist = []
    for name, ncols, fn in FSEG:
        for g in range(ncols // 512):
            glist.append((name, WCOL[name] + g * 512, FROW[name] + g * 512, fn))
    cur = load_w(glist[0][1], 512)
    nst = 0
    for gi, (name, c0, r0, fn) in enumerate(glist):
        w, wB = cur
        if gi + 1 < len(glist):
            cur = load_w(glist[gi + 1][1], 512)
        for j in range(4):
            st, stB = stg[nst % 2]
            nst += 1
            fblks = BLKS[1:] if (l == DEPTH - 1 and name != "mk") else BLKS
            for (t0, n) in fblks:
                bi = nps[0] % 4
                nps[0] += 1
                ps, psB = k.pb[bi], k.pbB[bi]
                for kc in range(KC):
                    P.op("pe", C("matmul", ps[:, 0:n], lhsT=w[:, kc, j * 128:(j + 1) * 128], rhs=hT[:, kc, t0:t0 + n],
                                                                                    start=(kc == 0), stop=(kc == KC - 1)),
                         reads=[hTB, wB], writes=[psB])
                if fn == "silu":
                    P.op("act", C("activation", out=st[:, t0:t0 + n], in_=ps[:, 0:n], func=AF.Silu),
                         reads=[psB], writes=[stB], part=True)
                elif fn == "sig":
                    P.op("act", C("activation", out=st[:, t0:t0 + n], in_=ps[:, 0:n], func=AF.Sigmoid),
                         reads=[psB], writes=[stB], part=True)
                elif fn == "copy16":
                    P.op("dve", C("tensor_scalar", out=st[:, t0:t0 + n], in0=ps[:, 0:n], scalar1=0.0625, scalar2=None, op0=ALU.mult),
                         reads=[psB], writes=[stB], part=True)
                else:
                    P.op("dve", C("tensor_copy", out=st[:, t0:t0 + n], in_=ps[:, 0:n]),
                         reads=[psB], writes=[stB], part=True)
            P.op("sp", C("dma_start", out=S["F"][r0 + j * 128:r0 + (j + 1) * 128, :], in_=st),
                 reads=[stB], writes=[k.SB["F"]], dma=True, part=True)
    P.barrier()
    A.reset(m0)


def phase_attn(k, l):
    P, A, I, S = k.P, k.A, k.I, k.S
    m0 = A.mark()
    KTs, KTB = A.alloc([128, 2, NT], BF16, "KTs")
    Vs, VB = A.alloc([128, NTI, 256], BF16, "Vs")
    Qb = [A.alloc([128, 8, 512], BF16, f"Qb{i}") for i in range(2)]
    AZ = [A.alloc([128, 8, 512], BF16, f"AZ{i}") for i in range(2)]
    PT = [A.alloc([128, 2, 512], BF16, f"PT{i}") for i in range(3)]
    rec, recB = A.alloc([128, 512], F32, "rec")
    t4, t4B = A.alloc([128, 512], F32, "t4")
    ost = [A.alloc([128, 8, 512], BF16, f"ost{i}") for i in range(2)]
    dacc, daccB = A.alloc([128, 2, 512], F32, "dacc")
    onesf, onesfB = A.alloc([128, 128], F32, "onesf")
    P.op("pool", C("memset", onesf, 1.0), writes=[onesfB])
    P.op("sp", C("dma_start", out=KTs, in_=S["KT"].rearrange("(h d) t -> d h t", d=128)), reads=[k.SB["KT"]], writes=[KTB], dma=True)
    P.op("sp", C("dma_start", out=Vs, in_=S["Vt"].rearrange("(i p) c -> p i c", p=128)), reads=[k.SB["Vt"]], writes=[VB], dma=True)
    blocks = []
    if l < DEPTH - 1:
        blocks.append((0, 256, [0, 1]))
    for j in range(8):
        blocks.append((256 + 512 * j, 512, list(range(NTI))))
    QTv = S["QT"].rearrange("(h d) t -> d h t", d=128)
    AZv = S["F"][FROW["az"]:FROW["az"] + 1024, :].rearrange("(h d) t -> d h t", d=128)
    Yv = S["Y"][0].rearrange("(h d) t -> d h t", d=128)
    ns = [0]
    npt = [0]
    for bi, (t0, n, keys) in enumerate(blocks):
        q_, qB = Qb[bi % 2]
        az_, azB = AZ[bi % 2]
        o_, oB = ost[bi % 2]
        P.op("sp", C("dma_start", out=q_[:, :, 0:n], in_=QTv[:, :, t0:t0 + n]), reads=[k.SB["QT"]], writes=[qB], dma=True)
        P.op("sp", C("dma_start", out=az_[:, :, 0:n], in_=AZv[:, :, t0:t0 + n]), reads=[k.SB["F"]], writes=[azB], dma=True)
        nk = len(keys)
        npair = nk // 2
        for h in range(8):
            kv = h // 4
            psO, psOB = k.pb[4 + (h % 2)], k.pbB[4 + (h % 2)]
            psD, psDB = k.pb[6 + (h % 2)], k.pbB[6 + (h % 2)]
            spair = []

            def emitS(pi):
                pr = ns[0] % 2
                ns[0] += 1
                spair.append(pr)
                for j in range(2):
                    kt = keys[2 * pi + j]
                    bb = 2 * pr + j
                    P.op("pe", C("matmul", k.pb[bb][:, 0:n], lhsT=KTs[:, kv, kt * 128:(kt + 1) * 128], rhs=q_[:, h, 0:n], start=True, stop=True),
                         reads=[KTB, qB], writes=[k.pbB[bb]])
            emitS(0)
            for pi in range(npair):
                if pi + 1 < npair:
                    emitS(pi + 1)
                pr = spair[pi]
                p_, pB = PT[npt[0] % 3]
                npt[0] += 1
                sv = k.pb2[pr].rearrange("p (b c) -> p b c", b=2)[:, :, 0:n]
                P.op("act", C("activation", out=p_[:, :, 0:n], in_=sv, func=AF.Exp), reads=[k.pbB[2 * pr], k.pbB[2 * pr + 1]], writes=[pB])
                for j in range(2):
                    kt = keys[2 * pi + j]
                    idx = 2 * pi + j
                    P.op("pe", C("matmul", psO[:, 0:n], lhsT=Vs[:, kt, kv * 128:(kv + 1) * 128], rhs=p_[:, j, 0:n],
                                 start=(idx == 0), stop=(idx == nk - 1)), reads=[VB, pB], writes=[psOB])
                if pi == 0:
                    P.op("dve", C("tensor_copy", out=dacc[:, :, 0:n], in_=p_[:, :, 0:n]), reads=[pB], writes=[daccB])
                else:
                    P.op("dve", C("tensor_tensor", out=dacc[:, :, 0:n], in0=dacc[:, :, 0:n], in1=p_[:, :, 0:n], op=ALU.add), reads=[pB, daccB], writes=[daccB])
            for j in range(2):
                P.op("pe", C("matmul", psD[:, 0:n], lhsT=onesf, rhs=dacc[:, j, 0:n], start=(j == 0), stop=(j == 1)), reads=[onesfB, daccB], writes=[psDB])
            P.op("dve", C("reciprocal", out=rec[:, 0:n], in_=psD[:, 0:n]), reads=[psDB], writes=[recB])
            P.op("dve", C("tensor_tensor", out=t4[:, 0:n], in0=psO[:, 0:n], in1=rec[:, 0:n], op=ALU.mult), reads=[psOB, recB], writes=[t4B])
            P.op("pool", C("tensor_tensor", out=o_[:, h, 0:n], in0=t4[:, 0:n], in1=az_[:, h, 0:n], op=ALU.mult),
                 reads=[t4B, azB], writes=[oB], part=True)
        P.op("sp", C("dma_start", out=Yv[:, :, t0:t0 + n], in_=o_[:, :, 0:n]), reads=[oB], writes=[k.SB["Y"]], dma=True, part=True)
    P.barrier()
    A.reset(m0)


def phase_mlstm(k, l):
    P, A, I, S = k.P, k.A, k.I, k.S
    last_layer = (l == DEPTH - 1)
    m0 = A.mark()
    WC, WCB = A.alloc([128, NTI, 16], F32, "WC")
    DECB, DECBB = A.alloc([128, 8, NTI], F32, "DECB")
    m1 = A.mark()
    GI, GIB = A.alloc([64, NT], F32, "GI")
    GF, GFB = A.alloc([64, NT], F32, "GF")
    ONE, ONEB = A.alloc([64, NT], F32, "ONE")
    BP, BPB = A.alloc([64, NT], F32, "BP")
    AP_, APB = A.alloc([64, NT], F32, "APr")
    MM, MMB = A.alloc([64, NT], F32, "MM")
    M2, M2B = A.alloc([64, NT], F32, "M2")
    WR, WRB = A.alloc([64, NT], F32, "WR")
    CL, CLB = A.alloc([64, NT], F32, "CL")
    gb, gbB = A.alloc([64, 2], F32, "gb")
    dec, decB = A.alloc([64, NTI], F32, "dec")
    sel, selB = A.alloc([64, 8, 128], F32, "sel")
    P.op("sp", C("dma_start", out=gb, in_=I["gbias"][l]), writes=[gbB], dma=True)
    P.op("sp", C("dma_start", out=sel, in_=I["sel"]), writes=[selB], dma=True)
    for t_, tB in ((GI, GIB), (GF, GFB), (dec, decB)):
        P.op("pool", C("memset", t_, 0.0), writes=[tB])
    P.op("pool", C("memset", ONE, 1.0), writes=[ONEB])
    R = (slice(0, 4), slice(32, 36))
    nb = 0
    for (t0, n) in BLKS:
        bt0 = (t0 - 256) if t0 >= 256 else 4096
        for gf, (dst, dstB) in enumerate(((GI, GIB), (GF, GFB))):
            bi = nb % 4
            nb += 1
            ps, psB = k.pb[bi], k.pbB[bi]
            for j in range(n // 128):
                i = t0 // 128 + j
                P.op("pe", C("transpose", ps[0:36, j * 128:(j + 1) * 128], k.Gtok[:, i, gf, :], k.idf),
                     reads=[k.GtokB, k.idfB], writes=[psB])
            P.op("act", C("activation", out=dst[0:4, t0:t0 + n], in_=ps[0:4, 0:n], func=AF.Identity,
                                                                               bias=gb[0:4, gf:gf + 1], scale=1.0),
                 reads=[psB, gbB], writes=[dstB], part=True)
            P.op("act", C("activation", out=dst[32:36, bt0:bt0 + n], in_=ps[32:36, 0:n], func=AF.Identity,
                                                                                 bias=gb[32:36, gf:gf + 1], scale=1.0),
                 reads=[psB, gbB], writes=[dstB], part=True)
    P.op("act", C("activation", out=GF[0:36, :], in_=GF[0:36, :], func=AF.Exp, scale=-1.0), reads=[GFB], writes=[GFB])
    P.op("act", C("activation", out=GF[0:36, :], in_=GF[0:36, :], func=AF.Ln, bias=1.0, scale=1.0), reads=[GFB], writes=[GFB])
    P.op("dve", C("tensor_tensor_scan", out=BP[0:36, :], data0=ONE[0:36, :], data1=GF[0:36, :], initial=0.0, op0=ALU.mult, op1=ALU.add),
         reads=[ONEB, GFB], writes=[BPB])
    P.op("dve", C("tensor_scalar", out=M2[32:36, :], in0=BP[32:36, :], scalar1=BP[32:36, NT - 1:NT], scalar2=-1.0, op0=ALU.subtract, op1=ALU.mult),
         reads=[BPB], writes=[M2B])
    P.op("dve", C("tensor_tensor", out=BP[32:36, :], in0=M2[32:36, :], in1=GF[32:36, :], op=ALU.add), reads=[M2B, GFB], writes=[BPB])
    P.op("dve", C("tensor_tensor", out=AP_[0:36, :], in0=GI[0:36, :], in1=BP[0:36, :], op=ALU.add), reads=[GIB, BPB], writes=[APB])
    P.op("dve", C("tensor_tensor_scan", out=MM[0:4, :], data0=ONE[0:4, :], data1=AP_[0:4, :], initial=-1e30, op0=ALU.mult, op1=ALU.max),
         reads=[ONEB, APB], writes=[MMB], part=True)
    src, srcB = AP_, APB
    bufs = [(M2, M2B), (MM, MMB)]
    sh = 1
    step = 0
    while sh < NT:
        dst, dstB = bufs[step % 2]
        P.op("dve", C("tensor_tensor", out=dst[32:36, 0:NT - sh], in0=src[32:36, 0:NT - sh], in1=src[32:36, sh:NT], op=ALU.max),
             reads=[srcB], writes=[dstB], part=True)
        P.op("pool", C("tensor_copy", out=dst[32:36, NT - sh:NT], in_=src[32:36, NT - sh:NT]),
             reads=[srcB], writes=[dstB], part=True)
        src, srcB = dst, dstB
        sh *= 2
        step += 1
    if src is not MM:
        P.op("dve", C("tensor_copy", out=MM[32:36, :], in_=src[32:36, :]), reads=[srcB], writes=[MMB], part=True)

    def v3(t_, r):
        return t_[r, :].rearrange("p (c t) -> p c t", t=128)
    for d, r in enumerate(R):
        li = 127 if d == 0 else 0
        mlast = v3(MM, r)[:, :, li:li + 1].to_broadcast([4, NTI, 128])
        P.op("dve", C("tensor_tensor", out=v3(WR, r), in0=v3(AP_, r), in1=mlast, op=ALU.subtract), reads=[APB, MMB], writes=[WRB], part=True)
        P.op("act", C("activation", out=WR[r, :], in_=WR[r, :], func=AF.Exp), reads=[WRB], writes=[WRB], part=True)
        P.op("dve", C("tensor_tensor", out=v3(CL, r), in0=v3(BP, r), in1=mlast, op=ALU.subtract), reads=[BPB, MMB], writes=[CLB], part=True)
        P.op("act", C("activation", out=CL[r, :], in_=CL[r, :], func=AF.Exp), reads=[CLB], writes=[CLB], part=True)
        ml2 = v3(MM, r)[:, :, li]
        if d == 0:
            P.op("dve", C("tensor_tensor", out=dec[r, 1:NTI], in0=ml2[:, 0:NTI - 1], in1=ml2[:, 1:NTI], op=ALU.subtract),
                 reads=[MMB], writes=[decB], part=True)
            P.op("act", C("activation", out=dec[r, 1:NTI], in_=dec[r, 1:NTI], func=AF.Exp), reads=[decB], writes=[decB], part=True)
        else:
            P.op("dve", C("tensor_tensor", out=dec[r, 0:NTI - 1], in0=ml2[:, 1:NTI], in1=ml2[:, 0:NTI - 1], op=ALU.subtract),
                 reads=[MMB], writes=[decB], part=True)
            P.op("act", C("activation", out=dec[r, 0:NTI - 1], in_=dec[r, 0:NTI - 1], func=AF.Exp), reads=[decB], writes=[decB], part=True)
    for q in range(8):
        P.op("pe", C("matmul", k.pb[0][:, q * NTI:(q + 1) * NTI], lhsT=sel[0:36, q, :], rhs=dec[0:36, :], start=True, stop=True),
             reads=[selB, decB], writes=[k.pbB[0]])
    P.op("dve", C("tensor_copy", out=DECB, in_=k.pb[0][:, 0:8 * NTI].rearrange("p (q c) -> p q c", c=NTI)), reads=[k.pbB[0]], writes=[DECBB])
    for half in range(2):
        tiles = list(range(half * 17, half * 17 + 17))
        ps, psB = k.pb[1 + half], k.pbB[1 + half]
        for jj, i in enumerate(tiles):
            fc = i * 128
            bc = (i - 2) * 128 if i >= 2 else 4096 + i * 128
            for qq, (src, srcB, r, c0) in enumerate(((WR, WRB, R[0], fc), (WR, WRB, R[1], bc), (CL, CLB, R[0], fc), (CL, CLB, R[1], bc))):
                P.op("pe", C("transpose", ps[:, jj * 16 + qq * 4: jj * 16 + qq * 4 + 4], src[r, c0:c0 + 128], k.idf[r, r]),
                     reads=[srcB, k.idfB], writes=[psB])
        P.op("dve", C("tensor_copy", out=WC[:, half * 17:half * 17 + 17, :], in_=ps[:, 0:17 * 16].rearrange("p (i q) -> p i q", q=16)),
             reads=[psB], writes=[WCB], part=True)
    P.barrier()
    A.reset(m1)
    msk, mskB = A.alloc([128, 2, 128], F32, "msk")
    mgb, mgbB = A.alloc([128, 1024], F32, "mgb")
    P.op("sp", C("dma_start", out=msk, in_=I["masks"].rearrange("m s t -> s m t")), writes=[mskB], dma=True)
    P.op("sp", C("dma_start", out=mgb, in_=I["mgain"][l, :].partition_broadcast(128)), writes=[mgbB], dma=True)
    Fq = S["F"][FROW["mq"]:FROW["mq"] + 1024, :].rearrange("(a p) t -> p a t", p=128)
    Fk = S["F"][FROW["mk"]:FROW["mk"] + 1024, :].rearrange("(a p) t -> p a t", p=128)
    Fo = S["F"][FROW["mo"]:FROW["mo"] + 1024, :].rearrange("(a p) t -> p a t", p=128)
    Fz = S["F"][FROW["mz"]:FROW["mz"] + 1024, :].rearrange("(a p) t -> p a t", p=128)
    Yb = S["Y"][1].rearrange("(a p) t -> p a t", p=128)
    HB_ = [[Buf(f"H{d}_{i}") for i in range(NTI)] for d in range(2)]
    BD = []
    for d in range(2):
        b = K()
        b.Cf, _ = A.alloc([128, 4, 2, 257], F32, f"Cf{d}")
        b.Ct, _ = A.alloc([128, 4, 2, 257], BF16, f"Ct{d}")
        b.CfBs = [Buf(f"Cf{d}{h}") for h in range(4)]
        b.CtBs = [Buf(f"Ct{d}{h}") for h in range(4)]
        b.qT = [A.alloc([128, 8, 128], BF16, f"qT{d}{i}") for i in range(2)]
        b.kT = [A.alloc([128, 8, 128], BF16, f"kT{d}{i}") for i in range(2)]
        b.ktk = [A.alloc([128, 1024], BF16, f"ktk{d}{i}") for i in range(2)]
        b.vtk = [A.alloc([128, 4, 257], BF16, f"vtk{d}{i}") for i in range(2)]
        b.moT = [A.alloc([128, 8, 128], BF16, f"moT{d}{i}") for i in range(2)]
        b.mzT = [A.alloc([128, 8, 128], BF16, f"mzT{d}{i}") for i in range(2)]
        b.hfl = [A.alloc([128, 1024], BF16, f"hfl{d}{i}") for i in range(2)]
        b.Sm = [A.alloc([128, 128], BF16, f"Sm{d}{i}") for i in range(2)]
        b.vw = [A.alloc([128, 257], BF16, f"vw{d}{i}") for i in range(2)]
        b.hst = [A.alloc([128, 4, 256], BF16, f"hst{d}{i}") for i in range(2)]
        b.hs, b.hsB = A.alloc([128, 4, 256], F32, f"hs{d}")
        b.hj, b.hjB = A.alloc([128, 1024], F32, f"hj{d}")
        b.hb, b.hbB = A.alloc([128, 1024], BF16, f"hb{d}")
        b.hss, b.hssB = A.alloc([128, 4], F32, f"hss{d}")
        b.dn = [A.alloc([128, 2], F32, f"dn{d}{h}") for h in range(4)]
        b.tT, b.tTB = A.alloc([128, 8, 128], F32, f"tT{d}")
        b.yst = [A.alloc([128, 8, 128], BF16, f"yst{d}{i}") for i in range(2)]
        b.cnt = dict(sm=0, vw=0, y=0)
        for v_, vB in b.vtk:
            P.op("pool", C("memset", v_, 1.0), writes=[vB])
        BD.append(b)
    orders = [list(range(NTI)), [1, 0] + list(range(NTI - 1, 1, -1))]

    def chunk(d, step, i):
        b = BD[d]
        is_ctx = i < 2
        need_out = not (is_ctx and last_layer)
        if is_ctx:
            first = (i == 0) if d == 0 else (i == 1)
        else:
            first = (i <= 17) if d == 0 else (i > 17)
        combine = need_out and not first
        cidx = i if d == 0 else ((i - 2) if i >= 2 else 32 + i)
        sl = slice(i * 128, (i + 1) * 128)
        q_, qB = b.qT[step % 2]
        k_, kB = b.kT[step % 2]
        kt_, ktB = b.ktk[step % 2]
        v_, vB = b.vtk[step % 2]
        P.op("sp", C("dma_start", out=q_, in_=Fq[:, :, sl]), reads=[k.SB["F"]], writes=[qB], dma=True)
        P.op("sp", C("dma_start", out=k_, in_=Fk[:, :, sl]), reads=[k.SB["F"]], writes=[kB], dma=True)
        P.op("sp", C("dma_start", out=kt_, in_=S["MKt"][sl, :]), reads=[k.SB["MKt"]], writes=[ktB], dma=True)
        P.op("sp", C("dma_start", out=v_[:, :, 0:256], in_=S["MVt"][sl, :].rearrange("t (h e) -> t h e", e=256)),
             reads=[k.SB["MVt"]], writes=[vB], dma=True, part=True)
        if combine:
            o_, oB = b.moT[step % 2]
            z_, zB = b.mzT[step % 2]
            f_, fB = b.hfl[step % 2]
            P.op("sp", C("dma_start", out=o_, in_=Fo[:, :, sl]), reads=[k.SB["F"]], writes=[oB], dma=True)
            P.op("sp", C("dma_start", out=z_, in_=Fz[:, :, sl]), reads=[k.SB["F"]], writes=[zB], dma=True)
            P.op("sp", C("dma_start", out=f_, in_=S["HF"][1 - d][sl, :]), reads=[HB_[1 - d][i]], writes=[fB], dma=True)
        yield
        h_, hB = b.hst[step % 2]
        bS, bP, bU = d, 2 + d, 4 + 2 * d
        psS, psSB = k.pb[bS], k.pbB[bS]
        psP, psPB = k.pb[bP], k.pbB[bP]
        for hh in range(4):
            qi = d * 4 + hh
            for dc in range(2):
                P.op("pe", C("matmul", psS[:, 0:128], lhsT=k_[:, hh * 2 + dc, :], rhs=q_[:, hh * 2 + dc, :], start=(dc == 0), stop=(dc == 1)),
                     reads=[kB, qB], writes=[psSB])
            yield
            sm_, smB = b.Sm[b.cnt["sm"] % 2]
            b.cnt["sm"] += 1
            P.op("dve", C("tensor_tensor", out=sm_, in0=psS[:, 0:128], in1=msk[:, d, :], op=ALU.mult), reads=[psSB, mskB], writes=[smB])
            vw_, vwB = b.vw[b.cnt["vw"] % 2]
            b.cnt["vw"] += 1
            P.op("act", C("activation", out=vw_, in_=v_[:, hh, :], func=AF.Copy, scale=WC[:, i, qi:qi + 1]),
                 reads=[vB, WCB], writes=[vwB])
            if step > 0:
                P.op("act", C("activation", out=b.Ct[:, hh], in_=b.Cf[:, hh], func=AF.Copy, scale=DECB[:, qi, cidx:cidx + 1]),
                     reads=[b.CfBs[hh], DECBB], writes=[b.CtBs[hh]])
            yield
            P.op("pe", C("matmul", psP[:, 0:257], lhsT=sm_, rhs=vw_, start=True, stop=(step == 0)), reads=[smB, vwB], writes=[psPB])
            if step > 0:
                for dc in range(2):
                    P.op("pe", C("matmul", psP[:, 0:257], lhsT=q_[:, hh * 2 + dc, :], rhs=b.Ct[:, hh, dc, :], start=False, stop=(dc == 1)),
                         reads=[qB, b.CtBs[hh]], writes=[psPB])
            for dc in range(2):
                psU, psUB = k.pb[bU + dc], k.pbB[bU + dc]
                P.op("pe", C("matmul", psU[:, 0:257], lhsT=kt_[:, hh * 256 + dc * 128: hh * 256 + (dc + 1) * 128], rhs=vw_, start=True, stop=True),
                     reads=[ktB, vwB], writes=[psUB])
                if step == 0:
                    P.op("dve", C("tensor_copy", out=b.Cf[:, hh, dc, :], in_=psU[:, 0:257]), reads=[psUB], writes=[b.CfBs[hh]], part=(dc == 1))
                else:
                    P.op("dve", C("scalar_tensor_tensor", out=b.Cf[:, hh, dc, :], in0=b.Cf[:, hh, dc, :], scalar=DECB[:, qi, cidx:cidx + 1], in1=psU[:, 0:257],
                                  op0=ALU.mult, op1=ALU.add), reads=[psUB, b.CfBs[hh], DECBB], writes=[b.CfBs[hh]], part=(dc == 1))
            yield
            if need_out:
                dn, dnB = b.dn[hh]
                P.op("dve", C("tensor_scalar", out=dn[:, 1:2], in0=psP[:, 256:257], scalar1=WC[:, i, 8 + qi:9 + qi], scalar2=None, op0=ALU.max),
                     reads=[psPB, WCB], writes=[dnB])
                P.op("dve", C("scalar_tensor_tensor", out=dn[:, 0:1], in0=psP[:, 256:257], scalar=-1.0, in1=dn[:, 1:2], op0=ALU.mult, op1=ALU.max),
                     reads=[psPB, dnB], writes=[dnB])
                P.op("dve", C("reciprocal", out=dn[:, 1:2], in_=dn[:, 0:1]), reads=[dnB], writes=[dnB])
                if not combine:
                    P.op("act", C("activation", out=h_[:, hh, :], in_=psP[:, 0:256], func=AF.Copy, scale=dn[:, 1:2]),
                         reads=[psPB, dnB], writes=[hB], part=(hh > 0))
                else:
                    P.op("dve", C("scalar_tensor_tensor", out=b.hs[:, hh, :], in0=psP[:, 0:256], scalar=dn[:, 1:2],
                                  in1=f_[:, hh * 256:(hh + 1) * 256], op0=ALU.mult, op1=ALU.add),
                         reads=[psPB, dnB, fB], writes=[b.hsB], part=(hh > 0))
        if need_out and not combine:
            P.op("sp", C("dma_start", out=S["HF"][d][sl, :], in_=h_.rearrange("p h e -> p (h e)")), reads=[hB], writes=[HB_[d][i]], dma=True)
        yield
        if combine:
            hs2 = b.hs.rearrange("p h e -> p (h e)")
            P.op("act", C("activation", out=b.hj, in_=hs2, func=AF.Square), reads=[b.hsB], writes=[b.hjB])
            P.op("dve", C("tensor_reduce", out=b.hss, in_=b.hj.rearrange("p (h e) -> p h e", e=256), axis=AX.X, op=ALU.add), reads=[b.hjB], writes=[b.hssB])
            P.op("dve", C("tensor_scalar", out=b.hss, in0=b.hss, scalar1=1.0 / 256, scalar2=EPS, op0=ALU.mult, op1=ALU.add), reads=[b.hssB], writes=[b.hssB])
            P.op("act", C("activation", out=b.hss, in_=b.hss, func=AF.Sqrt), reads=[b.hssB], writes=[b.hssB])
            P.op("dve", C("reciprocal", out=b.hss, in_=b.hss), reads=[b.hssB], writes=[b.hssB])
            hn = b.hj.rearrange("p (h e) -> p h e", e=256)
            P.op("dve", C("tensor_tensor", out=hn, in0=b.hs, in1=b.hss.unsqueeze(2).to_broadcast([128, 4, 256]), op=ALU.mult), reads=[b.hsB, b.hssB, b.hjB], writes=[b.hjB])
            P.op("pool", C("tensor_tensor", out=b.hb, in0=b.hj, in1=mgb, op=ALU.mult), reads=[b.hjB, mgbB], writes=[b.hbB])
            yield
            pbf = psP.bitcast(BF16)
            for cc in range(8):
                P.op("pe", C("transpose", pbf[:, cc * 128:(cc + 1) * 128], b.hb[:, cc * 128:(cc + 1) * 128], k.idb),
                     reads=[b.hbB, k.idbB], writes=[psPB])
            P.op("dve", C("tensor_tensor", out=b.tT, in0=pbf.rearrange("p (a t) -> p a t", t=128), in1=o_, op=ALU.mult),
                 reads=[psPB, oB], writes=[b.tTB])
            y_, yB = b.yst[b.cnt["y"] % 2]
            b.cnt["y"] += 1
            P.op("pool", C("tensor_tensor", out=y_, in0=b.tT, in1=z_, op=ALU.mult), reads=[b.tTB, zB], writes=[yB])
            P.op("sp", C("dma_start", out=Yb[:, :, sl], in_=y_), reads=[yB], writes=[k.SB["Y"]], dma=True, part=True)

    for step in range(NTI):
        gens = [chunk(d, step, orders[d][step]) for d in range(2)]
        while gens:
            for g in list(gens):
                try:
                    next(g)
                except StopIteration:
                    gens.remove(g)
    P.barrier()
    A.reset(m0)


def phase_conv_pool(k, l):
    P, A, I, S = k.P, k.A, k.I, k.S
    m0 = A.mark()
    cw, cwB = A.alloc([128, 8, 3], F32, "cw")
    P.op("sp", C("dma_start", out=cw, in_=I["convw"][l]), writes=[cwB], dma=True)
    inb = [[A.alloc([128, NT], BF16, f"cv{j}_{i}") for j in range(4)] for i in range(2)]
    ap_, apB = A.alloc([128, NT + 2], F32, "apad")
    y_, yB = A.alloc([128, NT], F32, "ycv")
    y2, y2B = A.alloc([128, NT], F32, "ycv2")
    ost = [A.alloc([128, NT], BF16, f"cvo{i}") for i in range(2)]
    P.op("pool", C("memset", ap_, 0.0), writes=[apB])
    names = ("cu", "cc", "cb", "cz")
    for cc in range(8):
        tl = inb[cc % 2]
        for j, nm in enumerate(names):
            t_, tB = tl[j]
            r0 = FROW[nm] + cc * 128
            P.op("sp", C("dma_start", out=t_, in_=S["F"][r0:r0 + 128, :]), reads=[k.SB["F"]], writes=[tB], dma=True)
        (cu, cuB), (cg, cgB), (cb, cbB), (cz, czB) = tl
        w0, w1, w2 = cw[:, cc, 0:1], cw[:, cc, 1:2], cw[:, cc, 2:3]
        P.op("pool", C("tensor_tensor", out=ap_[:, 1:NT + 1], in0=cu, in1=cg, op=ALU.mult), reads=[cuB, cgB], writes=[apB])
        P.op("dve", C("tensor_scalar", out=y_, in0=ap_[:, 1:NT + 1], scalar1=w1, scalar2=None, op0=ALU.mult), reads=[apB, cwB], writes=[yB])
        P.op("dve", C("scalar_tensor_tensor", out=y_, in0=ap_[:, 0:NT], scalar=w0, in1=y_, op0=ALU.mult, op1=ALU.add), reads=[apB, cwB, yB], writes=[yB])
        P.op("dve", C("scalar_tensor_tensor", out=y_, in0=ap_[:, 2:NT + 2], scalar=w2, in1=y_, op0=ALU.mult, op1=ALU.add), reads=[apB, cwB, yB], writes=[yB])
        P.op("dve", C("tensor_scalar", out=y_[:, 255:256], in0=ap_[:, 255:256], scalar1=w0, scalar2=None, op0=ALU.mult), reads=[apB, cwB, yB], writes=[yB])
        P.op("dve", C("scalar_tensor_tensor", out=y_[:, 255:256], in0=ap_[:, 256:257], scalar=w1, in1=y_[:, 255:256], op0=ALU.mult, op1=ALU.add), reads=[apB, cwB, yB], writes=[yB])
        P.op("dve", C("tensor_scalar", out=y_[:, 256:257], in0=ap_[:, 257:258], scalar1=w1, scalar2=None, op0=ALU.mult), reads=[apB, cwB, yB], writes=[yB])
        P.op("dve", C("scalar_tensor_tensor", out=y_[:, 256:257], in0=ap_[:, 258:259], scalar=w2, in1=y_[:, 256:257], op0=ALU.mult, op1=ALU.add), reads=[apB, cwB, yB], writes=[yB])
        P.op("pool", C("tensor_tensor", out=y2, in0=y_, in1=cb, op=ALU.mult), reads=[yB, cbB], writes=[y2B])
        o_, oB = ost[cc % 2]
        P.op("pool", C("tensor_tensor", out=o_, in0=y2, in1=cz, op=ALU.mult), reads=[y2B, czB], writes=[oB])
        P.op("sp", C("dma_start", out=S["Y"][2][cc * 128:(cc + 1) * 128, :], in_=o_), reads=[oB], writes=[k.SB["Y"]], dma=True, part=True)
    P.barrier()
    A.reset(m0)
    OC, OL = 8, 8 + 256 + 16
    PW = OL + NL + 16
    psc, pscB = A.alloc([128, 8], F32, "psc")
    P.op("sp", C("dma_start", out=psc, in_=I["pscale"][l]), writes=[pscB], dma=True)
    pub = [A.alloc([128, NT], BF16, f"pu{i}") for i in range(2)]
    pzb = [A.alloc([128, NT], BF16, f"pz{i}") for i in range(2)]
    up = [A.alloc([128, PW], F32, f"up{i}") for i in range(2)]
    sa, saB = A.alloc([128, PW], F32, "sa")
    sb_, sbB = A.alloc([128, PW], F32, "sb")
    rcb, rcbB = A.alloc([128, NT], F32, "rcb")
    dT = [A.alloc([128, NT], BF16, f"dT{i}") for i in range(2)]
    pw = [A.alloc([128, 2, 256], BF16, f"pw{i}") for i in range(2)]
    yo = [A.alloc([128, NT], BF16, f"ypo{i}") for i in range(2)]
    for u_, uB in up:
        P.op("pool", C("memset", u_, 0.0), writes=[uB])
    P.op("pool", C("memset", sa, 0.0), writes=[saB])
    P.op("pool", C("memset", sb_, 0.0), writes=[sbB])
    nps = 0
    for g, w in enumerate((2, 4, 8, 16)):
        P.op("sp", C("dma_start", out=rcb, in_=I["rcnt"][g, :].partition_broadcast(128)), writes=[rcbB], dma=True)
        pw_, pwB = pw[g % 2]
        P.op("pool", C("dma_start", out=pw_, in_=I["pool_w"][l, g].rearrange("(kc p) o -> p kc o", p=128)), writes=[pwB], dma=True)
        for kc2 in range(2):
            ct = g * 2 + kc2
            pu_, puB = pub[kc2]
            u_, uB = up[kc2]
            d_, dB = dT[kc2]
            P.op("sp", C("dma_start", out=pu_, in_=S["F"][FROW["pu"] + ct * 128:FROW["pu"] + (ct + 1) * 128, :]), reads=[k.SB["F"]], writes=[puB], dma=True)
            P.op("pool", C("tensor_copy", out=u_[:, OC:OC + NC], in_=pu_[:, 0:NC]), reads=[puB], writes=[uB], part=True)
            P.op("pool", C("tensor_copy", out=u_[:, OL:OL + NL], in_=pu_[:, NC:NT]), reads=[puB], writes=[uB], part=True)
            cur, curB = u_, uB
            m = 1
            pp = [(sa, saB), (sb_, sbB)]
            si = 0
            while m < w:
                nx, nxB = pp[si % 2]
                si += 1
                P.op("dve", C("tensor_tensor", out=nx[:, 0:PW - m], in0=cur[:, 0:PW - m], in1=cur[:, m:PW], op=ALU.add), reads=[curB], writes=[nxB])
                cur, curB = nx, nxB
                m *= 2
            hw_ = w // 2
            for (po, to, n) in ((OC, 0, NC), (OL, NC, NL)):
                P.op("dve", C("tensor_tensor", out=sa[:, po:po + n] if cur is not sa else sb_[:, po:po + n],
                                                                                        in0=cur[:, po - hw_:po - hw_ + n], in1=rcb[:, to:to + n], op=ALU.mult),
                     reads=[curB, rcbB], writes=[saB if cur is not sa else sbB])
                tmpb, tmpB = (sa, saB) if cur is not sa else (sb_, sbB)
                P.op("pool", C("tensor_tensor", out=d_[:, to:to + n], in0=tmpb[:, po:po + n], in1=u_[:, po:po + n], op=ALU.subtract),
                     reads=[tmpB, uB], writes=[dB], part=True)
        for oc in range(2):
            ct = g * 2 + oc
            pz_, pzB = pzb[oc]
            o_, oB = yo[oc]
            P.op("sp", C("dma_start", out=pz_, in_=S["F"][FROW["pz"] + ct * 128:FROW["pz"] + (ct + 1) * 128, :]), reads=[k.SB["F"]], writes=[pzB], dma=True)
            for (t0, n) in BLKS:
                bi = nps % 4
                nps += 1
                ps, psB = k.pb[bi], k.pbB[bi]
                for kc2 in range(2):
                    P.op("pe", C("matmul", ps[:, 0:n], lhsT=pw_[:, kc2, oc * 128:(oc + 1) * 128], rhs=dT[kc2][0][:, t0:t0 + n],
                                                                                          start=(kc2 == 0), stop=(kc2 == 1)), reads=[pwB, dT[kc2][1]], writes=[psB])
                P.op("dve", C("scalar_tensor_tensor", out=o_[:, t0:t0 + n], in0=ps[:, 0:n], scalar=psc[:, ct:ct + 1], in1=pz_[:, t0:t0 + n],
                                                                                                 op0=ALU.mult, op1=ALU.mult), reads=[psB, pscB, pzB], writes=[oB], part=True)
            P.op("sp", C("dma_start", out=S["Y"][3][ct * 128:(ct + 1) * 128, :], in_=o_), reads=[oB], writes=[k.SB["Y"]], dma=True, part=True)
    P.barrier()
    A.reset(m0)


def phase_merge_out(k, l):
    P, A, I, S = k.P, k.A, k.I, k.S
    last_layer = (l == DEPTH - 1)
    blocks = BLKS[1:] if last_layer else BLKS
    m0 = A.mark()
    wbr, _ = A.alloc([128, 16, 4 * 8 * 128], BF16, "wbr")
    wbrB = [Buf(f"wbr{ct}") for ct in range(16)]
    Yb, YbB = A.alloc([128, 4, 8, 512], BF16, "Yb")
    gm = [A.alloc([128, 4, 512], BF16, f"gm{i}") for i in range(2)]
    tm = [A.alloc([128, 512], F32, f"tm{i}") for i in range(4)]
    ast = [A.alloc([128, 512], BF16, f"ast{i}") for i in range(2)]
    for ct in range(16):
        P.op("pool", C("dma_start", out=wbr[:, ct, :], in_=I["wbt"][l, ct]), writes=[wbrB[ct]], dma=True)
    Yv = S["Y"].rearrange("b (kc p) t -> p b kc t", p=128)
    Gv = S["F"][FROW["gm"]:FROW["gm"] + 4 * D, :].rearrange("(b c p) t -> p b c t", p=128, c=16)
    nset = 0
    for (t0, n) in blocks:
        P.op("sp", C("dma_start", out=Yb[:, :, :, 0:n], in_=Yv[:, :, :, t0:t0 + n]), reads=[k.SB["Y"]], writes=[YbB], dma=True)
        for ct in range(16):
            g_, gB = gm[ct % 2]
            a_, aB = ast[ct % 2]
            w_ = wbr[:, ct, :].rearrange("p (b k c) -> p b k c", b=4, k=8)
            P.op("sp", C("dma_start", out=g_[:, :, 0:n], in_=Gv[:, :, ct, t0:t0 + n]), reads=[k.SB["F"]], writes=[gB], dma=True)
            base = 4 * (nset % 2)
            nset += 1
            for br in range(4):
                ps, psB = k.pb[base + br], k.pbB[base + br]
                for kc in range(8):
                    P.op("pe", C("matmul", ps[:, 0:n], lhsT=w_[:, br, kc, :], rhs=Yb[:, br, kc, 0:n], start=(kc == 0), stop=(kc == 7)),
                         reads=[wbrB[ct], YbB], writes=[psB])
                P.op("dve", C("tensor_tensor", out=tm[br][0][:, 0:n], in0=ps[:, 0:n], in1=g_[:, br, 0:n], op=ALU.mult),
                     reads=[psB, gB], writes=[tm[br][1]])
            P.op("pool", C("tensor_tensor", out=tm[0][0][:, 0:n], in0=tm[0][0][:, 0:n], in1=tm[1][0][:, 0:n], op=ALU.add), reads=[tm[0][1], tm[1][1]], writes=[tm[0][1]])
            P.op("pool", C("tensor_tensor", out=tm[2][0][:, 0:n], in0=tm[2][0][:, 0:n], in1=tm[3][0][:, 0:n], op=ALU.add), reads=[tm[2][1], tm[3][1]], writes=[tm[2][1]])
            P.op("pool", C("tensor_tensor", out=a_[:, 0:n], in0=tm[0][0][:, 0:n], in1=tm[2][0][:, 0:n], op=ALU.add), reads=[tm[0][1], tm[2][1]], writes=[aB])
            P.op("sp", C("dma_start", out=S["ACC"][ct * 128:(ct + 1) * 128, t0:t0 + n], in_=a_[:, 0:n]), reads=[aB], writes=[k.SB["ACC"]], dma=True, part=True)
    P.barrier()
    A.reset(m0)
    wo, _ = A.alloc([128, KC, D], BF16, "wo")
    woB = [Buf(f"wo{cg}") for cg in range(4)]
    accT = [A.alloc([128, 16, 512], BF16, f"accT{i}") for i in range(2)]
    xb = [A.alloc([128, 4, D], F32, f"xblk{i}") for i in range(2)]
    gtb = [A.alloc([128, D], F32, f"gtb{i}") for i in range(2)]
    t5, t5B = A.alloc([128, 512], F32, "t5")
    ss, ssB = A.alloc([128, 4], F32, "fss")
    junk, junkB = A.alloc([128, D], BF16, "junkf")
    wov = I["w_out"][l].rearrange("(kc p) c -> p kc c", p=128)
    for cg in range(4):
        P.op("pool", C("dma_start", out=wo[:, :, cg * 512:(cg + 1) * 512], in_=wov[:, :, cg * 512:(cg + 1) * 512]), writes=[woB[cg]], dma=True)
    if last_layer:
        fgb, fgbB = A.alloc([128, D], F32, "fgb")
        P.op("sp", C("dma_start", out=fgb, in_=I["fgain"].partition_broadcast(128)), writes=[fgbB], dma=True)
    for r in range(2):
        P.op("sp", C("dma_start", out=gtb[r][0], in_=S["modd"][l, r, 2 * D:3 * D].partition_broadcast(128)), reads=[k.SB["modd"]], writes=[gtb[r][1]], dma=True)
    Av = S["ACC"].rearrange("(kc p) t -> p kc t", p=128)
    x1B = [k.SB["X1"]] if l > 0 else []
    nps = 0
    for bi, (t0, n) in enumerate(blocks):
        r = 1 if t0 < 256 else 0
        nti = n // 128
        a_, aB = accT[bi % 2]
        xblk, xblkB = xb[bi % 2]
        P.op("sp", C("dma_start", out=a_[:, :, 0:n], in_=Av[:, :, t0:t0 + n]), reads=[k.SB["ACC"]], writes=[aB], dma=True)
        for ti in range(nti):
            i = t0 // 128 + ti
            P.op("sp", C("dma_start", out=xblk[:, ti, :], in_=src_tile(k, l, i)), reads=x1B, writes=[xblkB], dma=True, part=(ti > 0))
        for ti in range(nti):
            i = t0 // 128 + ti
            for cg in range(4):
                b_ = nps % 8
                nps += 1
                ps, psB = k.pb[b_], k.pbB[b_]
                for kc in range(KC):
                    P.op("pe", C("matmul", ps, lhsT=a_[:, kc, ti * 128:(ti + 1) * 128], rhs=wo[:, kc, cg * 512:(cg + 1) * 512], start=(kc == 0), stop=(kc == KC - 1)),
                         reads=[aB, woB[cg]], writes=[psB])
                P.op("dve", C("tensor_tensor", out=t5, in0=ps, in1=gtb[r][0][:, cg * 512:(cg + 1) * 512], op=ALU.mult), reads=[psB, gtb[r][1]], writes=[t5B])
                P.op("pool", C("tensor_tensor", out=xblk[:, ti, cg * 512:(cg + 1) * 512], in0=xblk[:, ti, cg * 512:(cg + 1) * 512], in1=t5, op=ALU.add),
                     reads=[t5B, xblkB], writes=[xblkB], part=True)
            if not last_layer:
                P.op("sp", C("dma_start", out=S["X1"][i * 128:(i + 1) * 128, :], in_=xblk[:, ti, :]), reads=[xblkB], writes=[k.SB["X1"]], dma=True, part=True)
            else:
                P.op("act", C("activation", out=junk, in_=xblk[:, ti, :], func=AF.Square, accum_out=ss[:, ti:ti + 1]), reads=[xblkB], writes=[junkB, ssB])
                P.op("dve", C("tensor_scalar", out=ss[:, ti:ti + 1], in0=ss[:, ti:ti + 1], scalar1=1.0 / D, scalar2=EPS, op0=ALU.mult, op1=ALU.add), reads=[ssB], writes=[ssB])
                P.op("act", C("activation", out=ss[:, ti:ti + 1], in_=ss[:, ti:ti + 1], func=AF.Sqrt), reads=[ssB], writes=[ssB])
                P.op("dve", C("reciprocal", out=ss[:, ti:ti + 1], in_=ss[:, ti:ti + 1]), reads=[ssB], writes=[ssB])
                P.op("dve", C("scalar_tensor_tensor", out=xblk[:, ti, :], in0=xblk[:, ti, :], scalar=ss[:, ti:ti + 1], in1=fgb, op0=ALU.mult, op1=ALU.mult),
                     reads=[xblkB, ssB, fgbB], writes=[xblkB], part=True)
                P.op("sp", C("dma_start", out=S["out"][(i - 2) * 128:(i - 1) * 128, :], in_=xblk[:, ti, :]), reads=[xblkB], writes=[k.SB["out"]], dma=True, part=True)
    P.barrier()
    A.reset(m0)


def host_constants():
    C = {}
    C["ident"] = np.eye(128, dtype=np.float32)
    s = np.arange(128)[:, None]
    t = np.arange(128)[None, :]
    C["masks"] = np.stack([(s <= t), (s >= t)]).astype(np.float32)
    freq = (10000.0 ** (-np.arange(32, dtype=np.float32) / 32)).astype(np.float32)
    rope = np.zeros((NTI, 128, 2, 8, 32), np.float32)
    rope[:, :, 0] = 1.0
    for i in range(2, NTI):
        tt = (i - 2) * 128 + np.arange(128)
        row = (tt // 64).astype(np.float32)
        col = (tt % 64).astype(np.float32)
        for half, pos in enumerate((row, col)):
            ang = (pos[:, None] * freq[None, :]).astype(np.float32)
            for h in range(4):
                rope[i, :, 0, h * 2 + half, :] = np.cos(ang)
                rope[i, :, 1, h * 2 + half, :] = np.sin(ang)
    C["rope"] = rope.reshape(NTI, 128, 2, 256)
    sel = np.zeros((64, 8, 128), np.float32)
    for d in range(2):
        for h in range(4):
            sel[d * 32 + h, d * 4 + h, :] = 1.0
    C["sel"] = sel
    rc = np.zeros((4, NT), np.float32)
    for g, w in enumerate((2, 4, 8, 16)):
        for (o, T) in ((0, NC), (NC, NL)):
            tt = np.arange(T)
            lo = np.clip(tt - w // 2, 0, T)
            hi = np.clip(tt + w - w // 2, 0, T)
            rc[g, o:o + T] = 1.0 / (hi - lo).astype(np.float32)
    C["rcnt"] = rc
    return C


_CACHE = {}
NCORES = 4


def make_in_maps(inputs):
    f = lambda a: np.ascontiguousarray(np.asarray(a, dtype=np.float32))
    x, c, ctx, c_ctx = f(inputs["x"]), f(inputs["c"]), f(inputs["ctx"]), f(inputs["c_ctx"])
    C = host_constants()
    shared = dict(C)
    shared["ngcol"] = f(inputs["norm_gain"]).reshape(DEPTH, KC, 128).transpose(0, 2, 1).copy()
    shared["w_mod"] = f(inputs["w_mod"])
    shared["b_mod"] = f(inputs["b_mod"])
    shared["w_in"] = f(inputs["w_in"])
    shared["qg"] = f(inputs["q_norm_gain"])
    shared["kg"] = f(inputs["k_norm_gain"])
    gb = f(inputs["mlstm_gate_bias"])
    gbias = np.zeros((DEPTH, 64, 2), np.float32)
    gbias[:, 0:4, 0] = gb[:, 0]
    gbias[:, 32:36, 0] = gb[:, 2]
    gbias[:, 0:4, 1] = gb[:, 1]
    gbias[:, 32:36, 1] = gb[:, 3]
    shared["gbias"] = gbias
    shared["mgain"] = f(inputs["mlstm_norm_gain"])
    shared["convw"] = f(inputs["conv_w"]).reshape(DEPTH, 3, 8, 128).transpose(0, 3, 2, 1).copy()
    shared["pool_w"] = f(inputs["pool_w"])
    shared["pscale"] = f(inputs["pool_scale"]).reshape(DEPTH, 8, 128).transpose(0, 2, 1).copy()
    wb = f(inputs["w_branch"])
    shared["wbt"] = wb.reshape(DEPTH, 4, 8, 128, 16, 128).transpose(0, 4, 3, 1, 2, 5).reshape(DEPTH, 16, 128, 4 * 8 * 128).copy()
    shared["w_out"] = f(inputs["w_out"])
    shared["fgain"] = f(inputs["final_norm_gain"])
    maps = []
    for core in range(NCORES):
        b = core % 4
        m = dict(shared)
        m["x"] = x[b]
        m["ctx"] = ctx[b]
        c2 = np.stack([c[b], c_ctx])
        m["c2T"] = c2.reshape(2, KC, 128).transpose(2, 1, 0).copy()
        maps.append(m)
    return maps


def kernel(**inputs):
    if "nc" not in _CACHE:
        _CACHE["nc"] = build_program()
    nc = _CACHE["nc"]
    maps = make_in_maps(inputs)
    res = run_bass_kernel_spmd(nc, maps, core_ids=list(range(NCORES)))
    out = np.stack([np.asarray(res.results[b]["out"]) for b in range(4)]).astype(np.float32)
    return out
```

```python
import contextlib
import numpy as np
import concourse.bass as bass
import concourse.mybir as mybir
from concourse.bass_utils import run_bass_kernel_spmd

F32 = mybir.dt.float32
BF16 = mybir.dt.bfloat16
AF = mybir.ActivationFunctionType
ALU = mybir.AluOpType
AX = mybir.AxisListType

NT = 4352
NTI = 34
NL = 4096
NC = 256
D = 2048
KC = 16
EPS = 1e-6
BLKS = [(0, 256)] + [(256 + 512 * j, 512) for j in range(8)]
DEPTH = 2
WCOL = dict(aq=0, ak=1024, av=1280, az=1536, mq=2560, mk=3584, mv=4608, mo=5632, mz=6656, mg=7680,
            cu=7696, cb=8720, cc=9744, cz=10768, pu=11792, pz=12816, gm=13840)
N_IN = 22032
FROW = dict(az=0, mz=1024, cz=2048, pz=3072, mo=4096, gm=5120, mq=13312, mk=14336, cu=15360, cb=16384,
            cc=17408, pu=18432)
NF = 19456
FSEG = [("az", 1024, "silu"), ("mz", 1024, "silu"), ("cz", 1024, "silu"), ("pz", 1024, "silu"),
        ("mo", 1024, "sig"), ("gm", 8192, "sig"),
        ("mq", 1024, "copy"), ("mk", 1024, "copy16"), ("cu", 1024, "copy"), ("cb", 1024, "copy"),
        ("cc", 1024, "copy"), ("pu", 1024, "copy")]


class Tok:
    __slots__ = ("q", "sem", "val", "rec")

    def __init__(self, q, rec):
        self.q = q
        self.rec = rec
        self.sem = None
        self.val = None


class Buf:
    __slots__ = ("name", "w", "r", "excl", "wf")

    def __init__(self, name="", excl=False):
        self.name = name
        self.w = {}
        self.wf = {}
        self.r = {}
        self.excl = excl


class Rec:
    __slots__ = ("fn", "deps", "signal", "tok", "dma", "dsem")

    def __init__(self, fn, deps, dma):
        self.fn = fn
        self.deps = deps
        self.signal = False
        self.tok = None
        self.dma = dma
        self.dsem = None


class Prog:
    NDMA = 8

    def __init__(self, nc):
        self.nc = nc
        self.eng = {"pe": nc.tensor, "act": nc.scalar, "dve": nc.vector, "pool": nc.gpsimd, "sp": nc.sync}
        self.q = {k: [] for k in self.eng}
        self.dma_n = {k: 0 for k in self.eng}
        self.dma_last = {k: [None] * self.NDMA for k in self.eng}
        self.last = {k: None for k in self.eng}
        self.fence = {k: [] for k in self.eng}

    def barrier(self):
        toks = []
        for q in self.eng:
            if self.last[q] is not None:
                toks.append(self.last[q])
            toks.extend(t for t in self.dma_last[q] if t is not None)
        for q in self.eng:
            self.fence[q] = list(toks)

    def op(self, q, fn, reads=(), writes=(), dma=False, part=False):
        deps = []
        if self.fence[q]:
            deps.extend(self.fence[q])
            self.fence[q] = []
        for b in reads:
            deps.extend(b.w.values())
            if b.excl:
                deps.extend(t for kk, t in b.r.items() if kk[0] != q)
        for b in writes:
            deps.extend(b.r.values())
            if not part:
                deps.extend(b.w.values())
            else:
                deps.extend(b.wf.values())
        rec = Rec(fn, deps, dma)
        tok = Tok(q, rec)
        rec.tok = tok
        if dma:
            n = self.dma_n[q]
            self.dma_n[q] = n + 1
            slot = n % self.NDMA
            prev = self.dma_last[q][slot]
            if prev is not None:
                rec.deps.append(prev)
            self.dma_last[q][slot] = tok
            rec.dsem = slot
        else:
            self.last[q] = tok
        self.q[q].append(rec)
        key = (q, rec.dsem)
        for b in reads:
            b.r[key] = tok
        for b in writes:
            if part:
                b.w[key] = tok
            else:
                b.w = {key: tok}
                b.wf = {key: tok}
                b.r = {}
        return tok

    def emit(self, sems, dsems):
        for q, recs in self.q.items():
            for rec in recs:
                for t in rec.deps:
                    if t.rec.dma:
                        continue
                    if t.q == q and q == "pe":
                        continue
                    t.rec.signal = True
        for q, recs in self.q.items():
            cnt = 0
            dcnt = [0] * self.NDMA
            for rec in recs:
                if rec.dma:
                    dcnt[rec.dsem] += 16
                    rec.tok.sem = dsems[q][rec.dsem]
                    rec.tok.val = dcnt[rec.dsem]
                elif rec.signal:
                    cnt += 1
                    rec.tok.sem = sems[q]
                    rec.tok.val = cnt
        self.stats = {}
        with self.nc.Block() as block:
            def mk(q):
                def body(e):
                    seen = {}
                    nw = 0
                    for rec in self.q[q]:
                        need = {}
                        for t in rec.deps:
                            if t.sem is None or (t.q == q and q == "pe" and not t.rec.dma):
                                continue
                            k = id(t.sem)
                            if seen.get(k, 0) >= t.val:
                                continue
                            if k not in need or need[k][1] < t.val:
                                need[k] = (t.sem, t.val)
                        for k, (s, v) in need.items():
                            e.wait_ge(s, v)
                            seen[k] = v
                            nw += 1
                        ins = rec.fn(e)
                        if rec.dma:
                            ins.then_inc(rec.tok.sem, 16)
                        elif rec.signal:
                            ins.then_inc(rec.tok.sem, 1)
                    for t in self.dma_last[q]:
                        if t is not None:
                            e.wait_ge(t.sem, t.val)
                    self.stats[q] = (len(self.q[q]), nw)
                return body
            block.tensor(mk("pe"))
            block.scalar(mk("act"))
            block.vector(mk("dve"))
            block.gpsimd(mk("pool"))
            block.sync(mk("sp"))


def C(name, *a, **kw):
    return lambda e: getattr(e, name)(*a, **kw)


class Arena:
    def __init__(self, ap, nbytes):
        self.ap = ap
        self.cap = nbytes
        self.top = 0

    def mark(self):
        return self.top

    def reset(self, m):
        self.top = m

    def alloc(self, shape, dt, name=""):
        esz = 2 if dt == BF16 else 4
        n = int(np.prod(shape[1:]))
        nb = (n * esz + 31) // 32 * 32
        off = self.top
        self.top += nb
        assert self.top <= self.cap, f"arena overflow {self.top} > {self.cap} at {name}"
        v = self.ap[0:shape[0], off // 4: off // 4 + (n * esz + 3) // 4]
        if dt != F32:
            v = v.bitcast(dt)[:, 0:n]
        if len(shape) == 3:
            v = v.rearrange("p (a b) -> p a b", b=shape[2])
        elif len(shape) == 4:
            v = v.rearrange("p (a b c) -> p a b c", b=shape[2], c=shape[3])
        return v, Buf(name)


class K:
    pass


def build_program(debug=(), stop_after=None, nlayers=DEPTH):
    nc = bass.Bass("TRN2", target_bir_lowering=False)
    P = Prog(nc)
    k = K()
    k.nc, k.P = nc, P
    k.stop = stop_after

    def din(name, shape, dt=F32):
        return nc.dram_tensor(name, list(shape), dt, kind="ExternalInput").ap()

    def dscr(name, shape, dt, out=False):
        return nc.dram_tensor(name, list(shape), dt, kind=("ExternalOutput" if (out or name in debug) else "Internal")).ap()

    I = {}
    I["x"] = din("x", [NL, D])
    I["ctx"] = din("ctx", [NC, D])
    I["c2T"] = din("c2T", [128, KC, 2])
    I["ngcol"] = din("ngcol", [DEPTH, 128, KC])
    I["w_mod"] = din("w_mod", [DEPTH, D, 3 * D])
    I["b_mod"] = din("b_mod", [DEPTH, 3 * D])
    I["w_in"] = din("w_in", [DEPTH, D, N_IN])
    I["qg"] = din("qg", [DEPTH, 128])
    I["kg"] = din("kg", [DEPTH, 128])
    I["gbias"] = din("gbias", [DEPTH, 64, 2])
    I["mgain"] = din("mgain", [DEPTH, 1024])
    I["convw"] = din("convw", [DEPTH, 128, 8, 3])
    I["pool_w"] = din("pool_w", [DEPTH, 4, 256, 256])
    I["pscale"] = din("pscale", [DEPTH, 128, 8])
    I["wbt"] = din("wbt", [DEPTH, 16, 128, 4 * 8 * 128])
    I["w_out"] = din("w_out", [DEPTH, D, D])
    I["fgain"] = din("fgain", [D])
    I["ident"] = din("ident", [128, 128])
    I["masks"] = din("masks", [2, 128, 128])
    I["rope"] = din("rope", [NTI, 128, 2, 256])
    I["sel"] = din("sel", [64, 8, 128])
    I["rcnt"] = din("rcnt", [4, NT])
    k.I = I
    S = {}
    S["modd"] = dscr("modd", [DEPTH, 2, 3 * D], F32)
    S["F"] = dscr("Fs", [NF, NT], BF16)
    S["QT"] = dscr("QT", [1024, NT], BF16)
    S["KT"] = dscr("KT", [256, NT], BF16)
    S["Vt"] = dscr("Vt", [NT, 256], BF16)
    S["MKt"] = dscr("MKt", [NT, 1024], BF16)
    S["MVt"] = dscr("MVt", [NT, 1024], BF16)
    S["HF"] = dscr("HF", [2, NT, 1024], BF16)
    S["Y"] = dscr("Y", [4, 1024, NT], BF16)
    S["X1"] = dscr("X1", [NT, D], F32)
    S["ACC"] = dscr("ACC", [D, NT], BF16)
    S["GT"] = dscr("GT", [NTI, 128, 72], F32)
    S["out"] = dscr("out", [NL, D], F32, out=True)
    k.S = S
    k.SB = {n: Buf(n) for n in S}

    with contextlib.ExitStack() as es:
        ARENA_BYTES = 206 * 1024
        arena_t = es.enter_context(nc.sbuf_tensor("arena", [128, ARENA_BYTES // 4], F32))
        A = Arena(arena_t, ARENA_BYTES)
        k.A = A
        k.pb2 = [es.enter_context(nc.psum_tensor(f"pbb{i}", [128, 1024], F32))[:, :] for i in range(4)]
        k.pb = [k.pb2[i // 2][:, (i % 2) * 512:(i % 2 + 1) * 512] for i in range(8)]
        k.pbB = [Buf(f"pb{i}", excl=True) for i in range(8)]
        sems = {q: es.enter_context(nc.semaphore("s_" + q)) for q in P.eng}
        dsems = {q: [es.enter_context(nc.semaphore(f"d_{q}{i}")) for i in range(P.NDMA)] for q in P.eng}

        k.idf, k.idfB = A.alloc([128, 128], F32, "idf")
        k.idb, k.idbB = A.alloc([128, 128], BF16, "idb")
        k.onesb, k.onesbB = A.alloc([128, 128], BF16, "onesb")
        k.modcol, k.modcolB = A.alloc([128, DEPTH, 48, 2], F32, "modcol")
        k.gcol, k.gcolB = A.alloc([128, DEPTH, KC, 2], F32, "gcol")
        P.op("sp", C("dma_start", out=k.idf, in_=I["ident"]), writes=[k.idfB], dma=True)
        P.op("pool", C("dma_start", out=k.idb, in_=I["ident"]), writes=[k.idbB], dma=True)
        P.op("dve", C("memset", k.onesb, 1.0), writes=[k.onesbB])

        phase_mod(k)
        for l in range(nlayers if stop_after != ("mod", 0) else 0):
            last = (l == DEPTH - 1)
            phase_norm_inproj(k, l)
            if stop_after in (("inproj", l), ("norm", l), ("tokmaj", l)):
                break
            phase_attn(k, l)
            if stop_after == ("attn", l):
                break
            phase_mlstm(k, l)
            if stop_after == ("mlstm", l):
                break
            phase_conv_pool(k, l)
            if stop_after == ("convpool", l):
                break
            phase_merge_out(k, l)
        P.emit(sems, dsems)
    return nc


def phase_mod(k):
    P, A, I, S = k.P, k.A, k.I, k.S
    m0 = A.mark()
    cT, cTB = A.alloc([128, KC, 2], F32, "cT")
    scT, scTB = A.alloc([128, KC, 2], F32, "scT")
    ngc, ngcB = A.alloc([128, DEPTH, KC], F32, "ngc")
    wm = [A.alloc([128, 3072], F32, f"wm{i}") for i in range(3)]
    mo, moB = A.alloc([2, 6144], F32, "mo")
    bm, bmB = A.alloc([2, 6144], F32, "bm")
    tmp, tmpB = A.alloc([128, KC, 2], F32, "tmpg")
    P.op("sp", C("dma_start", out=cT, in_=I["c2T"]), writes=[cTB], dma=True)
    P.op("sp", C("dma_start", out=ngc, in_=I["ngcol"].rearrange("l p k -> p l k")), writes=[ngcB], dma=True)
    P.op("act", C("activation", out=scT, in_=cT, func=AF.Silu), reads=[cTB], writes=[scTB])
    n = 0
    for l in range(DEPTH):
        P.op("sp", C("dma_start", out=bm, in_=I["b_mod"][l, :].partition_broadcast(2)), writes=[bmB], dma=True)
        for half in range(2):
            for kc in range(KC):
                w, wB = wm[n % 3]
                n += 1
                P.op("sp", C("dma_start",
                    out=w, in_=I["w_mod"][l, kc * 128:(kc + 1) * 128, half * 3072:(half + 1) * 3072]), writes=[wB], dma=True)
                for j in range(6):
                    P.op("pe", C("matmul", k.pb[j][0:2, :], lhsT=scT[:, kc, :], rhs=w[:, j * 512:(j + 1) * 512],
                                                                start=(kc == 0), stop=(kc == KC - 1)),
                         reads=[scTB, wB], writes=[k.pbB[j]])
            for j in range(6):
                c0 = half * 3072 + j * 512
                P.op("dve", C("tensor_tensor", out=mo[:, c0:c0 + 512], in0=k.pb[j][0:2, :], in1=bm[:, c0:c0 + 512], op=ALU.add),
                     reads=[k.pbB[j], bmB], writes=[moB], part=True)
        P.op("sp", C("dma_start", out=S["modd"][l], in_=mo), reads=[moB], writes=[k.SB["modd"]], dma=True, part=True)
        for j in range(48):
            P.op("pe", C("transpose", k.pb[6][:, 2 * j:2 * j + 2], mo[0:2, j * 128:(j + 1) * 128], k.idf[0:2, 0:2]),
                 reads=[moB, k.idfB], writes=[k.pbB[6]])
        P.op("dve", C("tensor_copy", out=k.modcol[:, l], in_=k.pb[6][:, 0:96].rearrange("p (j r) -> p j r", r=2)),
             reads=[k.pbB[6]], writes=[k.modcolB], part=True)
        P.op("dve", C("tensor_scalar", out=tmp, in0=k.modcol[:, l, 16:32, :], scalar1=1.0, scalar2=None, op0=ALU.add),
             reads=[k.modcolB], writes=[tmpB])
        P.op("dve", C("tensor_tensor", out=k.gcol[:, l], in0=tmp, in1=ngc[:, l, :].unsqueeze(2).to_broadcast([128, KC, 2]), op=ALU.mult),
             reads=[tmpB, ngcB], writes=[k.gcolB], part=True)
    P.barrier()
    A.reset(m0)


def src_tile(k, l, i):
    if l == 0:
        if i < 2:
            return k.I["ctx"][i * 128:(i + 1) * 128, :]
        return k.I["x"][(i - 2) * 128:(i - 1) * 128, :]
    return k.S["X1"][i * 128:(i + 1) * 128, :]


def phase_norm_inproj(k, l):
    P, A, I, S = k.P, k.A, k.I, k.S
    m0 = A.mark()
    hT, hTB = A.alloc([128, KC, NT], BF16, "hT")
    m1 = A.mark()
    xt = [A.alloc([128, D], F32, f"xt{i}") for i in range(2)]
    xn = [A.alloc([128, D], BF16, f"xn{i}") for i in range(2)]
    junk, junkB = A.alloc([128, D], BF16, "junk")
    ss, ssB = A.alloc([128, NTI], F32, "ss")
    rs, rsB = A.alloc([128, NTI], F32, "rs")
    x1B = [k.SB["X1"]] if l > 0 else []
    for i in range(NTI):
        r = 1 if i < 2 else 0
        x_, xB = xt[i % 2]
        n_, nB = xn[i % 2]
        P.op("sp", C("dma_start", out=x_, in_=src_tile(k, l, i)), reads=x1B, writes=[xB], dma=True)
        P.op("act", C("activation", out=junk, in_=x_, func=AF.Square, accum_out=ss[:, i:i + 1]),
             reads=[xB], writes=[junkB, ssB])
        P.op("dve", C("tensor_scalar", out=rs[:, i:i + 1], in0=ss[:, i:i + 1], scalar1=1.0 / D, scalar2=EPS, op0=ALU.mult, op1=ALU.add),
             reads=[ssB], writes=[rsB])
        P.op("act", C("activation", out=rs[:, i:i + 1], in_=rs[:, i:i + 1], func=AF.Sqrt), reads=[rsB], writes=[rsB])
        P.op("dve", C("reciprocal", out=rs[:, i:i + 1], in_=rs[:, i:i + 1]), reads=[rsB], writes=[rsB])
        P.op("dve", C("tensor_scalar", out=n_, in0=x_, scalar1=rs[:, i:i + 1], scalar2=None, op0=ALU.mult),
             reads=[xB, rsB], writes=[nB])
        import os
        KD = os.environ.get("KDBG", "")
        for half in range(2):
            if KD == "A":
                break
            bi = (2 * i + half) % 4
            pbf = k.pb[bi][:, :].bitcast(BF16)
            for j in range(8):
                kc = half * 8 + j
                P.op("pe", C("transpose", pbf[:, j * 128:(j + 1) * 128], n_[:, kc * 128:(kc + 1) * 128], k.idb),
                     reads=[nB, k.idbB], writes=[k.pbB[bi]])
            for j in range(8):
                kc = half * 8 + j
                if half == 0 or KD == "B":
                    P.op("dve", C("tensor_scalar",
                        out=hT[:, kc, i * 128:(i + 1) * 128], in0=pbf[:, j * 128:(j + 1) * 128],
                        scalar1=k.gcol[:, l, kc, r:r + 1], scalar2=k.modcol[:, l, kc, r:r + 1], op0=ALU.mult, op1=ALU.add),
                        reads=[k.pbB[bi], k.gcolB, k.modcolB], writes=[hTB], part=True)
                else:
                    P.op("act", C("activation",
                        out=hT[:, kc, i * 128:(i + 1) * 128], in_=pbf[:, j * 128:(j + 1) * 128], func=AF.Identity,
                        bias=k.modcol[:, l, kc, r:r + 1], scale=k.gcol[:, l, kc, r:r + 1]),
                        reads=[k.pbB[bi], k.gcolB, k.modcolB], writes=[hTB], part=True)
    P.barrier()
    A.reset(m1)
    if k.stop == ("norm", l):
        A.reset(m0)
        return
    W = [A.alloc([128, KC, 512], BF16, f"W{i}") for i in range(2)]
    m2 = A.mark()
    TS = []
    for ts_i in range(2):
        TS.append(dict(t1=A.alloc([128, 512], F32, f"t1{ts_i}"), t2=A.alloc([128, 512], F32, f"t2{ts_i}"), t3=A.alloc([128, 512], F32, f"t3{ts_i}"),
                       ta=A.alloc([128, 512], F32, f"ta{ts_i}"), tb=A.alloc([128, 512], F32, f"tb{ts_i}"), qf=A.alloc([128, 512], BF16, f"qf{ts_i}"),
                       ssq=A.alloc([128, 4], F32, f"ssq{ts_i}")))
    gts = [A.alloc([128, 2, 36], F32, f"gts{i}") for i in range(2)]
    for g_, gB_ in gts:
        P.op("pool", C("memset", g_, 0.0), writes=[gB_])
    rop = [A.alloc([128, 2, 8, 32], F32, f"rope{i}") for i in range(2)]
    qgb, qgbB = A.alloc([128, 128], F32, "qgb")
    kgb, kgbB = A.alloc([128, 128], F32, "kgb")
    qst = [A.alloc([128, 4, 256], BF16, f"qst{i}") for i in range(2)]
    vst = [A.alloc([128, 512], BF16, f"vst{i}") for i in range(2)]
    P.op("sp", C("dma_start", out=qgb, in_=I["qg"][l, :].partition_broadcast(128)), writes=[qgbB], dma=True)
    P.op("sp", C("dma_start", out=kgb, in_=I["kg"][l, :].partition_broadcast(128)), writes=[kgbB], dma=True)
    P.op("dve", C("tensor_scalar", out=qgb, in0=qgb, scalar1=float(128 ** -0.5), scalar2=None, op0=ALU.mult), reads=[qgbB], writes=[qgbB])
    wsrc = I["w_in"][l].rearrange("(kc p) c -> p kc c", p=128)
    groups = [("q", WCOL["aq"], 0), ("q", WCOL["aq"] + 512, 1), ("kv", WCOL["ak"], 0),
              ("mk", WCOL["mk"], 0), ("mk", WCOL["mk"] + 512, 1), ("mv", WCOL["mv"], 0), ("mv", WCOL["mv"] + 512, 1),
              ("mg", WCOL["mg"], 0)]
    nload = [0]

    def load_w(c0, ncol):
        w, wB = W[nload[0] % 2]
        nload[0] += 1
        P.op("pool", C("dma_start", out=w[:, :, 0:ncol], in_=wsrc[:, :, c0:c0 + ncol]), writes=[wB], dma=True)
        return w, wB

    def qk_post(ps, psB, nh, gb, gbB, rp, rpB, out, outB, T):
        Wd = nh * 128
        g = nh * 2
        (t1, t1B), (t2, t2B), (t3, t3B), (ta, taB), (tb, tbB), (ssq, ssqB) = T["t1"], T["t2"], T["t3"], T["ta"], T["tb"], T["ssq"]
        P.op("act", C("activation", out=t1[:, 0:Wd], in_=ps[:, 0:Wd], func=AF.Square), reads=[psB], writes=[t1B])
        P.op("dve", C("tensor_reduce", out=ssq[:, 0:nh], in_=t1[:, 0:Wd].rearrange("p (h d) -> p h d", d=128), axis=AX.X, op=ALU.add),
             reads=[t1B], writes=[ssqB])
        P.op("dve", C("tensor_scalar", out=ssq[:, 0:nh], in0=ssq[:, 0:nh], scalar1=1.0 / 128, scalar2=EPS, op0=ALU.mult, op1=ALU.add),
             reads=[ssqB], writes=[ssqB])
        P.op("act", C("activation", out=ssq[:, 0:nh], in_=ssq[:, 0:nh], func=AF.Sqrt), reads=[ssqB], writes=[ssqB])
        P.op("dve", C("reciprocal", out=ssq[:, 0:nh], in_=ssq[:, 0:nh]), reads=[ssqB], writes=[ssqB])
        P.op("dve", C("tensor_tensor", out=t2[:, 0:Wd].rearrange("p (h d) -> p h d", d=128), in0=ps[:, 0:Wd].rearrange("p (h d) -> p h d", d=128),
                                              in1=ssq[:, 0:nh].unsqueeze(2).to_broadcast([128, nh, 128]), op=ALU.mult),
             reads=[psB, ssqB], writes=[t2B])
        P.op("pool", C("tensor_tensor", out=t3[:, 0:Wd].rearrange("p (h d) -> p h d", d=128), in0=t2[:, 0:Wd].rearrange("p (h d) -> p h d", d=128),
                                               in1=gb.unsqueeze(1).to_broadcast([128, nh, 128]), op=ALU.mult),
             reads=[t2B, gbB], writes=[t3B])
        t3v = t3[:, 0:Wd].rearrange("p (g x j) -> p g x j", x=2, j=32)
        tav = ta[:, 0:Wd].rearrange("p (g x j) -> p g x j", x=2, j=32)
        tbv = tb[:, 0:Wd].rearrange("p (g x j) -> p g x j", x=2, j=32)
        ov = out.rearrange("p (g x j) -> p g x j", x=2, j=32)
        P.op("pool", C("tensor_tensor", out=tav, in0=t3v, in1=rp[:, 0, 0:g, :].unsqueeze(2).to_broadcast([128, g, 2, 32]), op=ALU.mult),
             reads=[t3B, rpB], writes=[taB])
        P.op("pool", C("tensor_tensor", out=tbv[:, :, 0, :], in0=t3v[:, :, 1, :], in1=rp[:, 1, 0:g, :], op=ALU.mult),
             reads=[t3B, rpB], writes=[tbB], part=True)
        P.op("pool", C("tensor_tensor", out=tbv[:, :, 1, :], in0=t3v[:, :, 0, :], in1=rp[:, 1, 0:g, :], op=ALU.mult),
             reads=[t3B, rpB], writes=[tbB], part=True)
        P.op("dve", C("tensor_tensor", out=ov[:, :, 0, :], in0=tav[:, :, 0, :], in1=tbv[:, :, 0, :], op=ALU.subtract),
             reads=[taB, tbB], writes=[outB], part=True)
        P.op("dve", C("tensor_tensor", out=ov[:, :, 1, :], in0=tav[:, :, 1, :], in1=tbv[:, :, 1, :], op=ALU.add),
             reads=[taB, tbB], writes=[outB], part=True)

    nps = [0]
    cur = load_w(groups[0][1], 512)
    for gi, (kind, c0, sub) in enumerate(groups):
        w, wB = cur
        if gi + 1 < len(groups):
            nk, nc0, _ = groups[gi + 1]
            cur = load_w(nc0, 16 if nk == "mg" else 512)
        ncol = 16 if kind == "mg" else 512
        for i in range(NTI):
            bi = nps[0] % 4
            nps[0] += 1
            ps, psB = k.pb[bi], k.pbB[bi]
            if kind in ("q", "kv"):
                rp, rpB = rop[i % 2]
                P.op("sp", C("dma_start", out=rp, in_=I["rope"][i].rearrange("p a (g j) -> p a g j", j=32)), writes=[rpB], dma=True)
            for kc in range(KC):
                P.op("pe", C("matmul", ps[:, 0:ncol], lhsT=hT[:, kc, i * 128:(i + 1) * 128], rhs=w[:, kc, 0:ncol],
                                                                               start=(kc == 0), stop=(kc == KC - 1)),
                     reads=[hTB, wB], writes=[psB])
            if kind == "q":
                T = TS[i % 2]
                qf, qfB = T["qf"]
                qk_post(ps, psB, 4, qgb, qgbB, rp, rpB, qf, qfB, T)
                grp = i // 2
                pos = i % 2
                st, stB = qst[grp % 2]
                tbi = 4 + (i % 2)
                pbf = k.pb[tbi][:, :].bitcast(BF16)
                for h in range(4):
                    P.op("pe", C("transpose", pbf[:, h * 128:(h + 1) * 128], qf[:, h * 128:(h + 1) * 128], k.idb),
                         reads=[qfB, k.idbB], writes=[k.pbB[tbi]])
                P.op("act", C("activation", out=st[:, :, pos * 128:(pos + 1) * 128],
                                                                            in_=pbf[:, 0:512].rearrange("p (h t) -> p h t", t=128), func=AF.Copy),
                     reads=[k.pbB[tbi]], writes=[stB], part=True)
                done = (pos == 1)
                if done:
                    t0 = grp * 256
                    n = 256
                    P.op("sp", C("dma_start",
                        out=S["QT"].rearrange("(h d) t -> d h t", d=128)[:, sub * 4:(sub + 1) * 4, t0:t0 + n], in_=st[:, :, 0:n]),
                        reads=[stB], writes=[k.SB["QT"]], dma=True, part=True)
            elif kind == "kv":
                T = TS[i % 2]
                qf, qfB = T["qf"]
                qk_post(ps, psB, 2, kgb, kgbB, rp, rpB, qf[:, 0:256], qfB, T)
                grp = i // 2
                pos = i % 2
                st, stB = qst[grp % 2]
                tbi = 4 + (i % 2)
                pbf = k.pb[tbi][:, :].bitcast(BF16)
                for h in range(2):
                    P.op("pe", C("transpose", pbf[:, h * 128:(h + 1) * 128], qf[:, h * 128:(h + 1) * 128], k.idb),
                         reads=[qfB, k.idbB], writes=[k.pbB[tbi]])
                P.op("act", C("activation", out=st[:, 0:2, pos * 128:(pos + 1) * 128],
                                                                            in_=pbf[:, 0:256].rearrange("p (h t) -> p h t", t=128), func=AF.Copy),
                     reads=[k.pbB[tbi]], writes=[stB], part=True)
                done = (pos == 1)
                if done:
                    t0 = grp * 256
                    n = 256
                    P.op("sp", C("dma_start",
                        out=S["KT"].rearrange("(h d) t -> d h t", d=128)[:, :, t0:t0 + n], in_=st[:, 0:2, 0:n]),
                        reads=[stB], writes=[k.SB["KT"]], dma=True, part=True)
                v_, vB = vst[i % 2]
                P.op("act", C("activation", out=v_[:, 0:256], in_=ps[:, 256:512], func=AF.Copy), reads=[psB], writes=[vB])
                P.op("sp", C("dma_start", out=S["Vt"][i * 128:(i + 1) * 128, :], in_=v_[:, 0:256]),
                     reads=[vB], writes=[k.SB["Vt"]], dma=True, part=True)
            elif kind in ("mk", "mv"):
                v_, vB = vst[i % 2]
                sc = 0.0625 if kind == "mk" else 1.0
                P.op("act", C("activation", out=v_, in_=ps, func=AF.Copy, scale=sc), reads=[psB], writes=[vB])
                dst = S["MKt"] if kind == "mk" else S["MVt"]
                dB = k.SB["MKt"] if kind == "mk" else k.SB["MVt"]
                P.op("sp", C("dma_start", out=dst[i * 128:(i + 1) * 128, sub * 512:(sub + 1) * 512], in_=v_),
                     reads=[vB], writes=[dB], dma=True, part=True)
            else:
                g_, gB_ = gts[i % 2]
                for gi in range(4):
                    P.op("dve", C("tensor_copy", out=g_[:, gi % 2, (gi // 2) * 32:(gi // 2) * 32 + 4], in_=ps[:, gi * 4:gi * 4 + 4]),
                         reads=[psB], writes=[gB_], part=(gi > 0))
                P.op("sp", C("dma_start", out=S["GT"][i], in_=g_.rearrange("p a b -> p (a b)")), reads=[gB_], writes=[k.SB["GT"]], dma=True, part=True)
    P.barrier()
    A.reset(m2)
    if k.stop == ("tokmaj", l):
        A.reset(m0)
        return
    stg = [A.alloc([128, NT], BF16, f"stg{i}") for i in range(2)]
    for st_, stB_ in stg:
        P.op("pool", C("memset", st_, 0.0), writes=[stB_])
    glist = []
    for name, ncols, fn in FSEG:
        for g in range(ncols // 512):
            glist.append((name, WCOL[name] + g * 512, FROW[name] + g * 512, fn))
    cur = load_w(glist[0][1], 512)
    nst = 0
    for gi, (name, c0, r0, fn) in enumerate(glist):
        w, wB = cur
        if gi + 1 < len(glist):
            cur = load_w(glist[gi + 1][1], 512)
        for j in range(4):
            st, stB = stg[nst % 2]
            nst += 1
            fblks = BLKS[1:] if (l == DEPTH - 1 and name != "mk") else BLKS
            for (t0, n) in fblks:
                bi = nps[0] % 4
                nps[0] += 1
                ps, psB = k.pb[bi], k.pbB[bi]
                for kc in range(KC):
                    P.op("pe", C("matmul", ps[:, 0:n], lhsT=w[:, kc, j * 128:(j + 1) * 128], rhs=hT[:, kc, t0:t0 + n],
                                                                                    start=(kc == 0), stop=(kc == KC - 1)),
                         reads=[hTB, wB], writes=[psB])
                if fn == "silu":
                    P.op("act", C("activation", out=st[:, t0:t0 + n], in_=ps[:, 0:n], func=AF.Silu),
                         reads=[psB], writes=[stB], part=True)
                elif fn == "sig":
                    P.op("act", C("activation", out=st[:, t0:t0 + n], in_=ps[:, 0:n], func=AF.Sigmoid),
                         reads=[psB], writes=[stB], part=True)
                elif fn == "copy16":
                    P.op("dve", C("tensor_scalar", out=st[:, t0:t0 + n], in0=ps[:, 0:n], scalar1=0.0625, scalar2=None, op0=ALU.mult),
                         reads=[psB], writes=[stB], part=True)
                else:
                    P.op("dve", C("tensor_copy", out=st[:, t0:t0 + n], in_=ps[:, 0:n]),
                         reads=[psB], writes=[stB], part=True)
            P.op("sp", C("dma_start", out=S["F"][r0 + j * 128:r0 + (j + 1) * 128, :], in_=st),
                 reads=[stB], writes=[k.SB["F"]], dma=True, part=True)
    P.barrier()
    A.reset(m0)


def phase_attn(k, l):
    P, A, I, S = k.P, k.A, k.I, k.S
    m0 = A.mark()
    KTs, KTB = A.alloc([128, 2, NT], BF16, "KTs")
    Vs, VB = A.alloc([128, NTI, 256], BF16, "Vs")
    Qb = [A.alloc([128, 8, 512], BF16, f"Qb{i}") for i in range(2)]
    AZ = [A.alloc([128, 8, 512], BF16, f"AZ{i}") for i in range(2)]
    PT = [A.alloc([128, 2, 512], BF16, f"PT{i}") for i in range(3)]
    rec, recB = A.alloc([128, 512], F32, "rec")
    t4, t4B = A.alloc([128, 512], F32, "t4")
    ost = [A.alloc([128, 8, 512], BF16, f"ost{i}") for i in range(2)]
    P.op("sp", C("dma_start", out=KTs, in_=S["KT"].rearrange("(h d) t -> d h t", d=128)), reads=[k.SB["KT"]], writes=[KTB], dma=True)
    P.op("sp", C("dma_start", out=Vs, in_=S["Vt"].rearrange("(i p) c -> p i c", p=128)), reads=[k.SB["Vt"]], writes=[VB], dma=True)
    blocks = []
    if l < DEPTH - 1:
        blocks.append((0, 256, [0, 1]))
    for j in range(8):
        blocks.append((256 + 512 * j, 512, list(range(NTI))))
    QTv = S["QT"].rearrange("(h d) t -> d h t", d=128)
    AZv = S["F"][FROW["az"]:FROW["az"] + 1024, :].rearrange("(h d) t -> d h t", d=128)
    Yv = S["Y"][0].rearrange("(h d) t -> d h t", d=128)
    ns = [0]
    npt = [0]
    for bi, (t0, n, keys) in enumerate(blocks):
        q_, qB = Qb[bi % 2]
        az_, azB = AZ[bi % 2]
        o_, oB = ost[bi % 2]
        P.op("sp", C("dma_start", out=q_[:, :, 0:n], in_=QTv[:, :, t0:t0 + n]), reads=[k.SB["QT"]], writes=[qB], dma=True)
        P.op("sp", C("dma_start", out=az_[:, :, 0:n], in_=AZv[:, :, t0:t0 + n]), reads=[k.SB["F"]], writes=[azB], dma=True)
        nk = len(keys)
        npair = nk // 2
        for h in range(8):
            kv = h // 4
            psO, psOB = k.pb[4 + (h % 2)], k.pbB[4 + (h % 2)]
            psD, psDB = k.pb[6 + (h % 2)], k.pbB[6 + (h % 2)]
            spair = []

            def emitS(pi):
                pr = ns[0] % 2
                ns[0] += 1
                spair.append(pr)
                for j in range(2):
                    kt = keys[2 * pi + j]
                    bb = 2 * pr + j
                    P.op("pe", C("matmul", k.pb[bb][:, 0:n], lhsT=KTs[:, kv, kt * 128:(kt + 1) * 128], rhs=q_[:, h, 0:n], start=True, stop=True),
                         reads=[KTB, qB], writes=[k.pbB[bb]])
            emitS(0)
            for pi in range(npair):
                if pi + 1 < npair:
                    emitS(pi + 1)
                pr = spair[pi]
                p_, pB = PT[npt[0] % 3]
                npt[0] += 1
                sv = k.pb2[pr].rearrange("p (b c) -> p b c", b=2)[:, :, 0:n]
                P.op("act", C("activation", out=p_[:, :, 0:n], in_=sv, func=AF.Exp), reads=[k.pbB[2 * pr], k.pbB[2 * pr + 1]], writes=[pB])
                for j in range(2):
                    kt = keys[2 * pi + j]
                    idx = 2 * pi + j
                    P.op("pe", C("matmul", psO[:, 0:n], lhsT=Vs[:, kt, kv * 128:(kv + 1) * 128], rhs=p_[:, j, 0:n],
                                 start=(idx == 0), stop=(idx == nk - 1)), reads=[VB, pB], writes=[psOB])
                for j in range(2):
                    idx = 2 * pi + j
                    P.op("pe", C("matmul", psD[:, 0:n], lhsT=k.onesb, rhs=p_[:, j, 0:n], start=(idx == 0), stop=(idx == nk - 1)),
                         reads=[k.onesbB, pB], writes=[psDB])
            P.op("dve", C("reciprocal", out=rec[:, 0:n], in_=psD[:, 0:n]), reads=[psDB], writes=[recB])
            P.op("dve", C("tensor_tensor", out=t4[:, 0:n], in0=psO[:, 0:n], in1=rec[:, 0:n], op=ALU.mult), reads=[psOB, recB], writes=[t4B])
            P.op("pool", C("tensor_tensor", out=o_[:, h, 0:n], in0=t4[:, 0:n], in1=az_[:, h, 0:n], op=ALU.mult),
                 reads=[t4B, azB], writes=[oB], part=True)
        P.op("sp", C("dma_start", out=Yv[:, :, t0:t0 + n], in_=o_[:, :, 0:n]), reads=[oB], writes=[k.SB["Y"]], dma=True, part=True)
    P.barrier()
    A.reset(m0)


def phase_mlstm(k, l):
    P, A, I, S = k.P, k.A, k.I, k.S
    last_layer = (l == DEPTH - 1)
    m0 = A.mark()
    WC, WCB = A.alloc([128, NTI, 16], F32, "WC")
    DECB, DECBB = A.alloc([128, 8, NTI], F32, "DECB")
    m1 = A.mark()
    GI, GIB = A.alloc([64, NT], F32, "GI")
    GF, GFB = A.alloc([64, NT], F32, "GF")
    ONE, ONEB = A.alloc([64, NT], F32, "ONE")
    BP, BPB = A.alloc([64, NT], F32, "BP")
    AP_, APB = A.alloc([64, NT], F32, "APr")
    MM, MMB = A.alloc([64, NT], F32, "MM")
    M2, M2B = A.alloc([64, NT], F32, "M2")
    WR, WRB = A.alloc([64, NT], F32, "WR")
    CL, CLB = A.alloc([64, NT], F32, "CL")
    gb, gbB = A.alloc([64, 2], F32, "gb")
    Gtok, GtokB = A.alloc([128, NTI, 2, 36], F32, "Gtok")
    P.op("sp", C("dma_start", out=Gtok.rearrange("p i a b -> p i (a b)"), in_=S["GT"].rearrange("i p c -> p i c")), reads=[k.SB["GT"]], writes=[GtokB], dma=True)
    dec, decB = A.alloc([64, NTI], F32, "dec")
    sel, selB = A.alloc([64, 8, 128], F32, "sel")
    P.op("sp", C("dma_start", out=gb, in_=I["gbias"][l]), writes=[gbB], dma=True)
    P.op("sp", C("dma_start", out=sel, in_=I["sel"]), writes=[selB], dma=True)
    for t_, tB in ((GI, GIB), (GF, GFB), (dec, decB)):
        P.op("pool", C("memset", t_, 0.0), writes=[tB])
    P.op("pool", C("memset", ONE, 1.0), writes=[ONEB])
    R = (slice(0, 4), slice(32, 36))
    nb = 0
    for (t0, n) in BLKS:
        bt0 = (t0 - 256) if t0 >= 256 else 4096
        for gf, (dst, dstB) in enumerate(((GI, GIB), (GF, GFB))):
            bi = nb % 4
            nb += 1
            ps, psB = k.pb[bi], k.pbB[bi]
            for j in range(n // 128):
                i = t0 // 128 + j
                P.op("pe", C("transpose", ps[0:36, j * 128:(j + 1) * 128], Gtok[:, i, gf, :], k.idf),
                     reads=[GtokB, k.idfB], writes=[psB])
            P.op("act", C("activation", out=dst[0:4, t0:t0 + n], in_=ps[0:4, 0:n], func=AF.Identity,
                                                                               bias=gb[0:4, gf:gf + 1], scale=1.0),
                 reads=[psB, gbB], writes=[dstB], part=True)
            P.op("act", C("activation", out=dst[32:36, bt0:bt0 + n], in_=ps[32:36, 0:n], func=AF.Identity,
                                                                                 bias=gb[32:36, gf:gf + 1], scale=1.0),
                 reads=[psB, gbB], writes=[dstB], part=True)
    P.op("act", C("activation", out=GF[0:36, :], in_=GF[0:36, :], func=AF.Exp, scale=-1.0), reads=[GFB], writes=[GFB])
    P.op("act", C("activation", out=GF[0:36, :], in_=GF[0:36, :], func=AF.Ln, bias=1.0, scale=1.0), reads=[GFB], writes=[GFB])
    P.op("dve", C("tensor_tensor_scan", out=BP[0:36, :], data0=ONE[0:36, :], data1=GF[0:36, :], initial=0.0, op0=ALU.mult, op1=ALU.add),
         reads=[ONEB, GFB], writes=[BPB])
    P.op("dve", C("tensor_scalar", out=M2[32:36, :], in0=BP[32:36, :], scalar1=BP[32:36, NT - 1:NT], scalar2=-1.0, op0=ALU.subtract, op1=ALU.mult),
         reads=[BPB], writes=[M2B])
    P.op("dve", C("tensor_tensor", out=BP[32:36, :], in0=M2[32:36, :], in1=GF[32:36, :], op=ALU.add), reads=[M2B, GFB], writes=[BPB])
    P.op("dve", C("tensor_tensor", out=AP_[0:36, :], in0=GI[0:36, :], in1=BP[0:36, :], op=ALU.add), reads=[GIB, BPB], writes=[APB])
    P.op("dve", C("tensor_tensor_scan", out=MM[0:4, :], data0=ONE[0:4, :], data1=AP_[0:4, :], initial=-1e30, op0=ALU.mult, op1=ALU.max),
         reads=[ONEB, APB], writes=[MMB], part=True)
    src, srcB = AP_, APB
    bufs = [(M2, M2B), (MM, MMB)]
    sh = 1
    step = 0
    while sh < NT:
        dst, dstB = bufs[step % 2]
        P.op("dve", C("tensor_tensor", out=dst[32:36, 0:NT - sh], in0=src[32:36, 0:NT - sh], in1=src[32:36, sh:NT], op=ALU.max),
             reads=[srcB], writes=[dstB], part=True)
        P.op("pool", C("tensor_copy", out=dst[32:36, NT - sh:NT], in_=src[32:36, NT - sh:NT]),
             reads=[srcB], writes=[dstB], part=True)
        src, srcB = dst, dstB
        sh *= 2
        step += 1
    if src is not MM:
        P.op("dve", C("tensor_copy", out=MM[32:36, :], in_=src[32:36, :]), reads=[srcB], writes=[MMB], part=True)

    def v3(t_, r):
        return t_[r, :].rearrange("p (c t) -> p c t", t=128)
    for d, r in enumerate(R):
        li = 127 if d == 0 else 0
        mlast = v3(MM, r)[:, :, li:li + 1].to_broadcast([4, NTI, 128])
        P.op("dve", C("tensor_tensor", out=v3(WR, r), in0=v3(AP_, r), in1=mlast, op=ALU.subtract), reads=[APB, MMB], writes=[WRB], part=True)
        P.op("act", C("activation", out=WR[r, :], in_=WR[r, :], func=AF.Exp), reads=[WRB], writes=[WRB], part=True)
        P.op("dve", C("tensor_tensor", out=v3(CL, r), in0=v3(BP, r), in1=mlast, op=ALU.subtract), reads=[BPB, MMB], writes=[CLB], part=True)
        P.op("act", C("activation", out=CL[r, :], in_=CL[r, :], func=AF.Exp), reads=[CLB], writes=[CLB], part=True)
        ml2 = v3(MM, r)[:, :, li]
        if d == 0:
            P.op("dve", C("tensor_tensor", out=dec[r, 1:NTI], in0=ml2[:, 0:NTI - 1], in1=ml2[:, 1:NTI], op=ALU.subtract),
                 reads=[MMB], writes=[decB], part=True)
            P.op("act", C("activation", out=dec[r, 1:NTI], in_=dec[r, 1:NTI], func=AF.Exp), reads=[decB], writes=[decB], part=True)
        else:
            P.op("dve", C("tensor_tensor", out=dec[r, 0:NTI - 1], in0=ml2[:, 1:NTI], in1=ml2[:, 0:NTI - 1], op=ALU.subtract),
                 reads=[MMB], writes=[decB], part=True)
            P.op("act", C("activation", out=dec[r, 0:NTI - 1], in_=dec[r, 0:NTI - 1], func=AF.Exp), reads=[decB], writes=[decB], part=True)
    for q in range(8):
        P.op("pe", C("matmul", k.pb[0][:, q * NTI:(q + 1) * NTI], lhsT=sel[0:36, q, :], rhs=dec[0:36, :], start=True, stop=True),
             reads=[selB, decB], writes=[k.pbB[0]])
    P.op("dve", C("tensor_copy", out=DECB, in_=k.pb[0][:, 0:8 * NTI].rearrange("p (q c) -> p q c", c=NTI)), reads=[k.pbB[0]], writes=[DECBB])
    for half in range(2):
        tiles = list(range(half * 17, half * 17 + 17))
        ps, psB = k.pb[1 + half], k.pbB[1 + half]
        for jj, i in enumerate(tiles):
            fc = i * 128
            bc = (i - 2) * 128 if i >= 2 else 4096 + i * 128
            for qq, (src, srcB, r, c0) in enumerate(((WR, WRB, R[0], fc), (WR, WRB, R[1], bc), (CL, CLB, R[0], fc), (CL, CLB, R[1], bc))):
                P.op("pe", C("transpose", ps[:, jj * 16 + qq * 4: jj * 16 + qq * 4 + 4], src[r, c0:c0 + 128], k.idf[r, r]),
                     reads=[srcB, k.idfB], writes=[psB])
        P.op("dve", C("tensor_copy", out=WC[:, half * 17:half * 17 + 17, :], in_=ps[:, 0:17 * 16].rearrange("p (i q) -> p i q", q=16)),
             reads=[psB], writes=[WCB], part=True)
    P.barrier()
    A.reset(m1)
    msk, mskB = A.alloc([128, 2, 128], F32, "msk")
    mgb, mgbB = A.alloc([128, 1024], F32, "mgb")
    P.op("sp", C("dma_start", out=msk, in_=I["masks"].rearrange("m s t -> s m t")), writes=[mskB], dma=True)
    P.op("sp", C("dma_start", out=mgb, in_=I["mgain"][l, :].partition_broadcast(128)), writes=[mgbB], dma=True)
    Fq = S["F"][FROW["mq"]:FROW["mq"] + 1024, :].rearrange("(a p) t -> p a t", p=128)
    Fk = S["F"][FROW["mk"]:FROW["mk"] + 1024, :].rearrange("(a p) t -> p a t", p=128)
    Fo = S["F"][FROW["mo"]:FROW["mo"] + 1024, :].rearrange("(a p) t -> p a t", p=128)
    Fz = S["F"][FROW["mz"]:FROW["mz"] + 1024, :].rearrange("(a p) t -> p a t", p=128)
    Yb = S["Y"][1].rearrange("(a p) t -> p a t", p=128)
    HB_ = [[Buf(f"H{d}_{i}") for i in range(NTI)] for d in range(2)]
    BD = []
    for d in range(2):
        b = K()
        b.Cf, _ = A.alloc([128, 4, 2, 257], F32, f"Cf{d}")
        b.Ct, _ = A.alloc([128, 4, 2, 257], BF16, f"Ct{d}")
        b.CfBs = [Buf(f"Cf{d}{h}") for h in range(4)]
        b.CtBs = [Buf(f"Ct{d}{h}") for h in range(4)]
        b.qT = [A.alloc([128, 8, 128], BF16, f"qT{d}{i}") for i in range(2)]
        b.kT = [A.alloc([128, 8, 128], BF16, f"kT{d}{i}") for i in range(2)]
        b.ktk = [A.alloc([128, 1024], BF16, f"ktk{d}{i}") for i in range(2)]
        b.vtk = [A.alloc([128, 4, 257], BF16, f"vtk{d}{i}") for i in range(2)]
        b.moT = [A.alloc([128, 8, 128], BF16, f"moT{d}{i}") for i in range(2)]
        b.mzT = [A.alloc([128, 8, 128], BF16, f"mzT{d}{i}") for i in range(2)]
        b.hfl = [A.alloc([128, 1024], BF16, f"hfl{d}{i}") for i in range(2)]
        b.Sm = [A.alloc([128, 128], BF16, f"Sm{d}{i}") for i in range(2)]
        b.vw = [A.alloc([128, 257], BF16, f"vw{d}{i}") for i in range(2)]
        b.hst = [A.alloc([128, 4, 256], BF16, f"hst{d}{i}") for i in range(2)]
        b.hs, b.hsB = A.alloc([128, 4, 256], F32, f"hs{d}")
        b.hj, b.hjB = A.alloc([128, 1024], F32, f"hj{d}")
        b.hb, b.hbB = A.alloc([128, 1024], BF16, f"hb{d}")
        b.hss, b.hssB = A.alloc([128, 4], F32, f"hss{d}")
        b.dn = [A.alloc([128, 2], F32, f"dn{d}{h}") for h in range(4)]
        b.tT, b.tTB = A.alloc([128, 8, 128], F32, f"tT{d}")
        b.yst = [A.alloc([128, 8, 128], BF16, f"yst{d}{i}") for i in range(2)]
        b.cnt = dict(sm=0, vw=0, y=0)
        for v_, vB in b.vtk:
            P.op("pool", C("memset", v_, 1.0), writes=[vB])
        BD.append(b)
    orders = [list(range(NTI)), [1, 0] + list(range(NTI - 1, 1, -1))]

    def chunk(d, step, i):
        b = BD[d]
        is_ctx = i < 2
        need_out = not (is_ctx and last_layer)
        if is_ctx:
            first = (i == 0) if d == 0 else (i == 1)
        else:
            first = (i <= 17) if d == 0 else (i > 17)
        combine = need_out and not first
        cidx = i if d == 0 else ((i - 2) if i >= 2 else 32 + i)
        sl = slice(i * 128, (i + 1) * 128)
        q_, qB = b.qT[step % 2]
        k_, kB = b.kT[step % 2]
        kt_, ktB = b.ktk[step % 2]
        v_, vB = b.vtk[step % 2]
        P.op("sp", C("dma_start", out=q_, in_=Fq[:, :, sl]), reads=[k.SB["F"]], writes=[qB], dma=True)
        P.op("sp", C("dma_start", out=k_, in_=Fk[:, :, sl]), reads=[k.SB["F"]], writes=[kB], dma=True)
        P.op("sp", C("dma_start", out=kt_, in_=S["MKt"][sl, :]), reads=[k.SB["MKt"]], writes=[ktB], dma=True)
        P.op("sp", C("dma_start", out=v_[:, :, 0:256], in_=S["MVt"][sl, :].rearrange("t (h e) -> t h e", e=256)),
             reads=[k.SB["MVt"]], writes=[vB], dma=True, part=True)
        if combine:
            o_, oB = b.moT[step % 2]
            z_, zB = b.mzT[step % 2]
            f_, fB = b.hfl[step % 2]
            P.op("sp", C("dma_start", out=o_, in_=Fo[:, :, sl]), reads=[k.SB["F"]], writes=[oB], dma=True)
            P.op("sp", C("dma_start", out=z_, in_=Fz[:, :, sl]), reads=[k.SB["F"]], writes=[zB], dma=True)
            P.op("sp", C("dma_start", out=f_, in_=S["HF"][1 - d][sl, :]), reads=[HB_[1 - d][i]], writes=[fB], dma=True)
        yield
        h_, hB = b.hst[step % 2]
        bS, bP, bU = d, 2 + d, 4 + 2 * d
        psS, psSB = k.pb[bS], k.pbB[bS]
        psP, psPB = k.pb[bP], k.pbB[bP]
        for hh in range(4):
            qi = d * 4 + hh
            for dc in range(2):
                P.op("pe", C("matmul", psS[:, 0:128], lhsT=k_[:, hh * 2 + dc, :], rhs=q_[:, hh * 2 + dc, :], start=(dc == 0), stop=(dc == 1)),
                     reads=[kB, qB], writes=[psSB])
            yield
            sm_, smB = b.Sm[b.cnt["sm"] % 2]
            b.cnt["sm"] += 1
            P.op("dve", C("tensor_tensor", out=sm_, in0=psS[:, 0:128], in1=msk[:, d, :], op=ALU.mult), reads=[psSB, mskB], writes=[smB])
            vw_, vwB = b.vw[b.cnt["vw"] % 2]
            b.cnt["vw"] += 1
            P.op("act", C("activation", out=vw_, in_=v_[:, hh, :], func=AF.Copy, scale=WC[:, i, qi:qi + 1]),
                 reads=[vB, WCB], writes=[vwB])
            if step > 0:
                P.op("act", C("activation", out=b.Ct[:, hh], in_=b.Cf[:, hh], func=AF.Copy, scale=DECB[:, qi, cidx:cidx + 1]),
                     reads=[b.CfBs[hh], DECBB], writes=[b.CtBs[hh]])
            yield
            P.op("pe", C("matmul", psP[:, 0:257], lhsT=sm_, rhs=vw_, start=True, stop=(step == 0)), reads=[smB, vwB], writes=[psPB])
            if step > 0:
                for dc in range(2):
                    P.op("pe", C("matmul", psP[:, 0:257], lhsT=q_[:, hh * 2 + dc, :], rhs=b.Ct[:, hh, dc, :], start=False, stop=(dc == 1)),
                         reads=[qB, b.CtBs[hh]], writes=[psPB])
            for dc in range(2):
                psU, psUB = k.pb[bU + dc], k.pbB[bU + dc]
                P.op("pe", C("matmul", psU[:, 0:257], lhsT=kt_[:, hh * 256 + dc * 128: hh * 256 + (dc + 1) * 128], rhs=vw_, start=True, stop=True),
                     reads=[ktB, vwB], writes=[psUB])
                if step == 0:
                    P.op("dve", C("tensor_copy", out=b.Cf[:, hh, dc, :], in_=psU[:, 0:257]), reads=[psUB], writes=[b.CfBs[hh]], part=(dc == 1))
                else:
                    P.op("dve", C("scalar_tensor_tensor", out=b.Cf[:, hh, dc, :], in0=b.Cf[:, hh, dc, :], scalar=DECB[:, qi, cidx:cidx + 1], in1=psU[:, 0:257],
                                  op0=ALU.mult, op1=ALU.add), reads=[psUB, b.CfBs[hh], DECBB], writes=[b.CfBs[hh]], part=(dc == 1))
            yield
            if need_out:
                dn, dnB = b.dn[hh]
                P.op("dve", C("tensor_scalar", out=dn[:, 1:2], in0=psP[:, 256:257], scalar1=WC[:, i, 8 + qi:9 + qi], scalar2=None, op0=ALU.max),
                     reads=[psPB, WCB], writes=[dnB])
                P.op("dve", C("scalar_tensor_tensor", out=dn[:, 0:1], in0=psP[:, 256:257], scalar=-1.0, in1=dn[:, 1:2], op0=ALU.mult, op1=ALU.max),
                     reads=[psPB, dnB], writes=[dnB])
                P.op("dve", C("reciprocal", out=dn[:, 1:2], in_=dn[:, 0:1]), reads=[dnB], writes=[dnB])
                if not combine:
                    P.op("act", C("activation", out=h_[:, hh, :], in_=psP[:, 0:256], func=AF.Copy, scale=dn[:, 1:2]),
                         reads=[psPB, dnB], writes=[hB], part=(hh > 0))
                else:
                    P.op("dve", C("scalar_tensor_tensor", out=b.hs[:, hh, :], in0=psP[:, 0:256], scalar=dn[:, 1:2],
                                  in1=f_[:, hh * 256:(hh + 1) * 256], op0=ALU.mult, op1=ALU.add),
                         reads=[psPB, dnB, fB], writes=[b.hsB], part=(hh > 0))
        if need_out and not combine:
            P.op("sp", C("dma_start", out=S["HF"][d][sl, :], in_=h_.rearrange("p h e -> p (h e)")), reads=[hB], writes=[HB_[d][i]], dma=True)
        yield
        if combine:
            hs2 = b.hs.rearrange("p h e -> p (h e)")
            P.op("act", C("activation", out=b.hj, in_=hs2, func=AF.Square), reads=[b.hsB], writes=[b.hjB])
            P.op("dve", C("tensor_reduce", out=b.hss, in_=b.hj.rearrange("p (h e) -> p h e", e=256), axis=AX.X, op=ALU.add), reads=[b.hjB], writes=[b.hssB])
            P.op("dve", C("tensor_scalar", out=b.hss, in0=b.hss, scalar1=1.0 / 256, scalar2=EPS, op0=ALU.mult, op1=ALU.add), reads=[b.hssB], writes=[b.hssB])
            P.op("act", C("activation", out=b.hss, in_=b.hss, func=AF.Sqrt), reads=[b.hssB], writes=[b.hssB])
            P.op("dve", C("reciprocal", out=b.hss, in_=b.hss), reads=[b.hssB], writes=[b.hssB])
            hn = b.hj.rearrange("p (h e) -> p h e", e=256)
            P.op("dve", C("tensor_tensor", out=hn, in0=b.hs, in1=b.hss.unsqueeze(2).to_broadcast([128, 4, 256]), op=ALU.mult), reads=[b.hsB, b.hssB, b.hjB], writes=[b.hjB])
            P.op("pool", C("tensor_tensor", out=b.hb, in0=b.hj, in1=mgb, op=ALU.mult), reads=[b.hjB, mgbB], writes=[b.hbB])
            yield
            pbf = psP.bitcast(BF16)
            for cc in range(8):
                P.op("pe", C("transpose", pbf[:, cc * 128:(cc + 1) * 128], b.hb[:, cc * 128:(cc + 1) * 128], k.idb),
                     reads=[b.hbB, k.idbB], writes=[psPB])
            P.op("dve", C("tensor_tensor", out=b.tT, in0=pbf.rearrange("p (a t) -> p a t", t=128), in1=o_, op=ALU.mult),
                 reads=[psPB, oB], writes=[b.tTB])
            y_, yB = b.yst[b.cnt["y"] % 2]
            b.cnt["y"] += 1
            P.op("pool", C("tensor_tensor", out=y_, in0=b.tT, in1=z_, op=ALU.mult), reads=[b.tTB, zB], writes=[yB])
            P.op("sp", C("dma_start", out=Yb[:, :, sl], in_=y_), reads=[yB], writes=[k.SB["Y"]], dma=True, part=True)

    for step in range(NTI):
        gens = [chunk(d, step, orders[d][step]) for d in range(2)]
        while gens:
            for g in list(gens):
                try:
                    next(g)
                except StopIteration:
                    gens.remove(g)
    P.barrier()
    A.reset(m0)


def phase_conv_pool(k, l):
    P, A, I, S = k.P, k.A, k.I, k.S
    m0 = A.mark()
    cw, cwB = A.alloc([128, 8, 3], F32, "cw")
    P.op("sp", C("dma_start", out=cw, in_=I["convw"][l]), writes=[cwB], dma=True)
    inb = [[A.alloc([128, NT], BF16, f"cv{j}_{i}") for j in range(4)] for i in range(2)]
    ap_, apB = A.alloc([128, NT + 2], F32, "apad")
    y_, yB = A.alloc([128, NT], F32, "ycv")
    y2, y2B = A.alloc([128, NT], F32, "ycv2")
    ost = [A.alloc([128, NT], BF16, f"cvo{i}") for i in range(2)]
    P.op("pool", C("memset", ap_, 0.0), writes=[apB])
    names = ("cu", "cc", "cb", "cz")
    for cc in range(8):
        tl = inb[cc % 2]
        for j, nm in enumerate(names):
            t_, tB = tl[j]
            r0 = FROW[nm] + cc * 128
            P.op("sp", C("dma_start", out=t_, in_=S["F"][r0:r0 + 128, :]), reads=[k.SB["F"]], writes=[tB], dma=True)
        (cu, cuB), (cg, cgB), (cb, cbB), (cz, czB) = tl
        w0, w1, w2 = cw[:, cc, 0:1], cw[:, cc, 1:2], cw[:, cc, 2:3]
        P.op("pool", C("tensor_tensor", out=ap_[:, 1:NT + 1], in0=cu, in1=cg, op=ALU.mult), reads=[cuB, cgB], writes=[apB])
        P.op("dve", C("tensor_scalar", out=y_, in0=ap_[:, 1:NT + 1], scalar1=w1, scalar2=None, op0=ALU.mult), reads=[apB, cwB], writes=[yB])
        P.op("dve", C("scalar_tensor_tensor", out=y_, in0=ap_[:, 0:NT], scalar=w0, in1=y_, op0=ALU.mult, op1=ALU.add), reads=[apB, cwB, yB], writes=[yB])
        P.op("dve", C("scalar_tensor_tensor", out=y_, in0=ap_[:, 2:NT + 2], scalar=w2, in1=y_, op0=ALU.mult, op1=ALU.add), reads=[apB, cwB, yB], writes=[yB])
        P.op("dve", C("tensor_scalar", out=y_[:, 255:256], in0=ap_[:, 255:256], scalar1=w0, scalar2=None, op0=ALU.mult), reads=[apB, cwB, yB], writes=[yB])
        P.op("dve", C("scalar_tensor_tensor", out=y_[:, 255:256], in0=ap_[:, 256:257], scalar=w1, in1=y_[:, 255:256], op0=ALU.mult, op1=ALU.add), reads=[apB, cwB, yB], writes=[yB])
        P.op("dve", C("tensor_scalar", out=y_[:, 256:257], in0=ap_[:, 257:258], scalar1=w1, scalar2=None, op0=ALU.mult), reads=[apB, cwB, yB], writes=[yB])
        P.op("dve", C("scalar_tensor_tensor", out=y_[:, 256:257], in0=ap_[:, 258:259], scalar=w2, in1=y_[:, 256:257], op0=ALU.mult, op1=ALU.add), reads=[apB, cwB, yB], writes=[yB])
        P.op("pool", C("tensor_tensor", out=y2, in0=y_, in1=cb, op=ALU.mult), reads=[yB, cbB], writes=[y2B])
        o_, oB = ost[cc % 2]
        P.op("pool", C("tensor_tensor", out=o_, in0=y2, in1=cz, op=ALU.mult), reads=[y2B, czB], writes=[oB])
        P.op("sp", C("dma_start", out=S["Y"][2][cc * 128:(cc + 1) * 128, :], in_=o_), reads=[oB], writes=[k.SB["Y"]], dma=True, part=True)
    P.barrier()
    A.reset(m0)
    OC, OL = 8, 8 + 256 + 16
    PW = OL + NL + 16
    psc, pscB = A.alloc([128, 8], F32, "psc")
    P.op("sp", C("dma_start", out=psc, in_=I["pscale"][l]), writes=[pscB], dma=True)
    pub = [A.alloc([128, NT], BF16, f"pu{i}") for i in range(2)]
    pzb = [A.alloc([128, NT], BF16, f"pz{i}") for i in range(2)]
    up = [A.alloc([128, PW], F32, f"up{i}") for i in range(2)]
    sa, saB = A.alloc([128, PW], F32, "sa")
    sb_, sbB = A.alloc([128, PW], F32, "sb")
    rcb, rcbB = A.alloc([128, NT], F32, "rcb")
    dT = [A.alloc([128, NT], BF16, f"dT{i}") for i in range(2)]
    pw = [A.alloc([128, 2, 256], BF16, f"pw{i}") for i in range(2)]
    yo = [A.alloc([128, NT], BF16, f"ypo{i}") for i in range(2)]
    for u_, uB in up:
        P.op("pool", C("memset", u_, 0.0), writes=[uB])
    P.op("pool", C("memset", sa, 0.0), writes=[saB])
    P.op("pool", C("memset", sb_, 0.0), writes=[sbB])
    nps = 0
    for g, w in enumerate((2, 4, 8, 16)):
        P.op("sp", C("dma_start", out=rcb, in_=I["rcnt"][g, :].partition_broadcast(128)), writes=[rcbB], dma=True)
        pw_, pwB = pw[g % 2]
        P.op("pool", C("dma_start", out=pw_, in_=I["pool_w"][l, g].rearrange("(kc p) o -> p kc o", p=128)), writes=[pwB], dma=True)
        for kc2 in range(2):
            ct = g * 2 + kc2
            pu_, puB = pub[kc2]
            u_, uB = up[kc2]
            d_, dB = dT[kc2]
            P.op("sp", C("dma_start", out=pu_, in_=S["F"][FROW["pu"] + ct * 128:FROW["pu"] + (ct + 1) * 128, :]), reads=[k.SB["F"]], writes=[puB], dma=True)
            P.op("pool", C("tensor_copy", out=u_[:, OC:OC + NC], in_=pu_[:, 0:NC]), reads=[puB], writes=[uB], part=True)
            P.op("pool", C("tensor_copy", out=u_[:, OL:OL + NL], in_=pu_[:, NC:NT]), reads=[puB], writes=[uB], part=True)
            cur, curB = u_, uB
            m = 1
            pp = [(sa, saB), (sb_, sbB)]
            si = 0
            while m < w:
                nx, nxB = pp[si % 2]
                si += 1
                P.op("dve", C("tensor_tensor", out=nx[:, 0:PW - m], in0=cur[:, 0:PW - m], in1=cur[:, m:PW], op=ALU.add), reads=[curB], writes=[nxB])
                cur, curB = nx, nxB
                m *= 2
            hw_ = w // 2
            for (po, to, n) in ((OC, 0, NC), (OL, NC, NL)):
                P.op("dve", C("tensor_tensor", out=sa[:, po:po + n] if cur is not sa else sb_[:, po:po + n],
                                                                                        in0=cur[:, po - hw_:po - hw_ + n], in1=rcb[:, to:to + n], op=ALU.mult),
                     reads=[curB, rcbB], writes=[saB if cur is not sa else sbB])
                tmpb, tmpB = (sa, saB) if cur is not sa else (sb_, sbB)
                P.op("pool", C("tensor_tensor", out=d_[:, to:to + n], in0=tmpb[:, po:po + n], in1=u_[:, po:po + n], op=ALU.subtract),
                     reads=[tmpB, uB], writes=[dB], part=True)
        for oc in range(2):
            ct = g * 2 + oc
            pz_, pzB = pzb[oc]
            o_, oB = yo[oc]
            P.op("sp", C("dma_start", out=pz_, in_=S["F"][FROW["pz"] + ct * 128:FROW["pz"] + (ct + 1) * 128, :]), reads=[k.SB["F"]], writes=[pzB], dma=True)
            for (t0, n) in BLKS:
                bi = nps % 4
                nps += 1
                ps, psB = k.pb[bi], k.pbB[bi]
                for kc2 in range(2):
                    P.op("pe", C("matmul", ps[:, 0:n], lhsT=pw_[:, kc2, oc * 128:(oc + 1) * 128], rhs=dT[kc2][0][:, t0:t0 + n],
                                                                                          start=(kc2 == 0), stop=(kc2 == 1)), reads=[pwB, dT[kc2][1]], writes=[psB])
                P.op("dve", C("scalar_tensor_tensor", out=o_[:, t0:t0 + n], in0=ps[:, 0:n], scalar=psc[:, ct:ct + 1], in1=pz_[:, t0:t0 + n],
                                                                                                 op0=ALU.mult, op1=ALU.mult), reads=[psB, pscB, pzB], writes=[oB], part=True)
            P.op("sp", C("dma_start", out=S["Y"][3][ct * 128:(ct + 1) * 128, :], in_=o_), reads=[oB], writes=[k.SB["Y"]], dma=True, part=True)
    P.barrier()
    A.reset(m0)


def phase_merge_out(k, l):
    P, A, I, S = k.P, k.A, k.I, k.S
    last_layer = (l == DEPTH - 1)
    blocks = BLKS[1:] if last_layer else BLKS
    m0 = A.mark()
    wbr, _ = A.alloc([128, 16, 4 * 8 * 128], BF16, "wbr")
    wbrB = [Buf(f"wbr{ct}") for ct in range(16)]
    Yb, YbB = A.alloc([128, 4, 8, 512], BF16, "Yb")
    gm = [A.alloc([128, 4, 512], BF16, f"gm{i}") for i in range(2)]
    tm = [A.alloc([128, 512], F32, f"tm{i}") for i in range(4)]
    ast = [A.alloc([128, 512], BF16, f"ast{i}") for i in range(2)]
    for ct in range(16):
        P.op("pool", C("dma_start", out=wbr[:, ct, :], in_=I["wbt"][l, ct]), writes=[wbrB[ct]], dma=True)
    Yv = S["Y"].rearrange("b (kc p) t -> p b kc t", p=128)
    Gv = S["F"][FROW["gm"]:FROW["gm"] + 4 * D, :].rearrange("(b c p) t -> p b c t", p=128, c=16)
    nset = 0
    for (t0, n) in blocks:
        P.op("sp", C("dma_start", out=Yb[:, :, :, 0:n], in_=Yv[:, :, :, t0:t0 + n]), reads=[k.SB["Y"]], writes=[YbB], dma=True)
        for ct in range(16):
            g_, gB = gm[ct % 2]
            a_, aB = ast[ct % 2]
            w_ = wbr[:, ct, :].rearrange("p (b k c) -> p b k c", b=4, k=8)
            P.op("sp", C("dma_start", out=g_[:, :, 0:n], in_=Gv[:, :, ct, t0:t0 + n]), reads=[k.SB["F"]], writes=[gB], dma=True)
            base = 4 * (nset % 2)
            nset += 1
            for br in range(4):
                ps, psB = k.pb[base + br], k.pbB[base + br]
                for kc in range(8):
                    P.op("pe", C("matmul", ps[:, 0:n], lhsT=w_[:, br, kc, :], rhs=Yb[:, br, kc, 0:n], start=(kc == 0), stop=(kc == 7)),
                         reads=[wbrB[ct], YbB], writes=[psB])
                P.op("dve", C("tensor_tensor", out=tm[br][0][:, 0:n], in0=ps[:, 0:n], in1=g_[:, br, 0:n], op=ALU.mult),
                     reads=[psB, gB], writes=[tm[br][1]])
            P.op("pool", C("tensor_tensor", out=tm[0][0][:, 0:n], in0=tm[0][0][:, 0:n], in1=tm[1][0][:, 0:n], op=ALU.add), reads=[tm[0][1], tm[1][1]], writes=[tm[0][1]])
            P.op("pool", C("tensor_tensor", out=tm[2][0][:, 0:n], in0=tm[2][0][:, 0:n], in1=tm[3][0][:, 0:n], op=ALU.add), reads=[tm[2][1], tm[3][1]], writes=[tm[2][1]])
            P.op("pool", C("tensor_tensor", out=a_[:, 0:n], in0=tm[0][0][:, 0:n], in1=tm[2][0][:, 0:n], op=ALU.add), reads=[tm[0][1], tm[2][1]], writes=[aB])
            P.op("sp", C("dma_start", out=S["ACC"][ct * 128:(ct + 1) * 128, t0:t0 + n], in_=a_[:, 0:n]), reads=[aB], writes=[k.SB["ACC"]], dma=True, part=True)
    P.barrier()
    A.reset(m0)
    wo, _ = A.alloc([128, KC, D], BF16, "wo")
    woB = [Buf(f"wo{cg}") for cg in range(4)]
    accT = [A.alloc([128, 16, 512], BF16, f"accT{i}") for i in range(2)]
    xb = [A.alloc([128, 4, D], F32, f"xblk{i}") for i in range(2)]
    gtb = [A.alloc([128, D], F32, f"gtb{i}") for i in range(2)]
    t5, t5B = A.alloc([128, 512], F32, "t5")
    ss, ssB = A.alloc([128, 4], F32, "fss")
    junk, junkB = A.alloc([128, D], BF16, "junkf")
    wov = I["w_out"][l].rearrange("(kc p) c -> p kc c", p=128)
    for cg in range(4):
        P.op("pool", C("dma_start", out=wo[:, :, cg * 512:(cg + 1) * 512], in_=wov[:, :, cg * 512:(cg + 1) * 512]), writes=[woB[cg]], dma=True)
    if last_layer:
        fgb, fgbB = A.alloc([128, D], F32, "fgb")
        P.op("sp", C("dma_start", out=fgb, in_=I["fgain"].partition_broadcast(128)), writes=[fgbB], dma=True)
    for r in range(2):
        P.op("sp", C("dma_start", out=gtb[r][0], in_=S["modd"][l, r, 2 * D:3 * D].partition_broadcast(128)), reads=[k.SB["modd"]], writes=[gtb[r][1]], dma=True)
    Av = S["ACC"].rearrange("(kc p) t -> p kc t", p=128)
    x1B = [k.SB["X1"]] if l > 0 else []
    nps = 0
    for bi, (t0, n) in enumerate(blocks):
        r = 1 if t0 < 256 else 0
        nti = n // 128
        a_, aB = accT[bi % 2]
        xblk, xblkB = xb[bi % 2]
        P.op("sp", C("dma_start", out=a_[:, :, 0:n], in_=Av[:, :, t0:t0 + n]), reads=[k.SB["ACC"]], writes=[aB], dma=True)
        for ti in range(nti):
            i = t0 // 128 + ti
            P.op("sp", C("dma_start", out=xblk[:, ti, :], in_=src_tile(k, l, i)), reads=x1B, writes=[xblkB], dma=True, part=(ti > 0))
        for ti in range(nti):
            i = t0 // 128 + ti
            for cg in range(4):
                b_ = nps % 8
                nps += 1
                ps, psB = k.pb[b_], k.pbB[b_]
                for kc in range(KC):
                    P.op("pe", C("matmul", ps, lhsT=a_[:, kc, ti * 128:(ti + 1) * 128], rhs=wo[:, kc, cg * 512:(cg + 1) * 512], start=(kc == 0), stop=(kc == KC - 1)),
                         reads=[aB, woB[cg]], writes=[psB])
                P.op("dve", C("tensor_tensor", out=t5, in0=ps, in1=gtb[r][0][:, cg * 512:(cg + 1) * 512], op=ALU.mult), reads=[psB, gtb[r][1]], writes=[t5B])
                P.op("pool", C("tensor_tensor", out=xblk[:, ti, cg * 512:(cg + 1) * 512], in0=xblk[:, ti, cg * 512:(cg + 1) * 512], in1=t5, op=ALU.add),
                     reads=[t5B, xblkB], writes=[xblkB], part=True)
            if not last_layer:
                P.op("sp", C("dma_start", out=S["X1"][i * 128:(i + 1) * 128, :], in_=xblk[:, ti, :]), reads=[xblkB], writes=[k.SB["X1"]], dma=True, part=True)
            else:
                P.op("act", C("activation", out=junk, in_=xblk[:, ti, :], func=AF.Square, accum_out=ss[:, ti:ti + 1]), reads=[xblkB], writes=[junkB, ssB])
                P.op("dve", C("tensor_scalar", out=ss[:, ti:ti + 1], in0=ss[:, ti:ti + 1], scalar1=1.0 / D, scalar2=EPS, op0=ALU.mult, op1=ALU.add), reads=[ssB], writes=[ssB])
                P.op("act", C("activation", out=ss[:, ti:ti + 1], in_=ss[:, ti:ti + 1], func=AF.Sqrt), reads=[ssB], writes=[ssB])
                P.op("dve", C("reciprocal", out=ss[:, ti:ti + 1], in_=ss[:, ti:ti + 1]), reads=[ssB], writes=[ssB])
                P.op("dve", C("scalar_tensor_tensor", out=xblk[:, ti, :], in0=xblk[:, ti, :], scalar=ss[:, ti:ti + 1], in1=fgb, op0=ALU.mult, op1=ALU.mult),
                     reads=[xblkB, ssB, fgbB], writes=[xblkB], part=True)
                P.op("sp", C("dma_start", out=S["out"][(i - 2) * 128:(i - 1) * 128, :], in_=xblk[:, ti, :]), reads=[xblkB], writes=[k.SB["out"]], dma=True, part=True)
    P.barrier()
    A.reset(m0)


def host_constants():
    C = {}
    C["ident"] = np.eye(128, dtype=np.float32)
    s = np.arange(128)[:, None]
    t = np.arange(128)[None, :]
    C["masks"] = np.stack([(s <= t), (s >= t)]).astype(np.float32)
    freq = (10000.0 ** (-np.arange(32, dtype=np.float32) / 32)).astype(np.float32)
    rope = np.zeros((NTI, 128, 2, 8, 32), np.float32)
    rope[:, :, 0] = 1.0
    for i in range(2, NTI):
        tt = (i - 2) * 128 + np.arange(128)
        row = (tt // 64).astype(np.float32)
        col = (tt % 64).astype(np.float32)
        for half, pos in enumerate((row, col)):
            ang = (pos[:, None] * freq[None, :]).astype(np.float32)
            for h in range(4):
                rope[i, :, 0, h * 2 + half, :] = np.cos(ang)
                rope[i, :, 1, h * 2 + half, :] = np.sin(ang)
    C["rope"] = rope.reshape(NTI, 128, 2, 256)
    sel = np.zeros((64, 8, 128), np.float32)
    for d in range(2):
        for h in range(4):
            sel[d * 32 + h, d * 4 + h, :] = 1.0
    C["sel"] = sel
    rc = np.zeros((4, NT), np.float32)
    for g, w in enumerate((2, 4, 8, 16)):
        for (o, T) in ((0, NC), (NC, NL)):
            tt = np.arange(T)
            lo = np.clip(tt - w // 2, 0, T)
            hi = np.clip(tt + w - w // 2, 0, T)
            rc[g, o:o + T] = 1.0 / (hi - lo).astype(np.float32)
    C["rcnt"] = rc
    return C


_CACHE = {}
NCORES = 4


def make_in_maps(inputs):
    f = lambda a: np.ascontiguousarray(np.asarray(a, dtype=np.float32))
    x, c, ctx, c_ctx = f(inputs["x"]), f(inputs["c"]), f(inputs["ctx"]), f(inputs["c_ctx"])
    C = host_constants()
    shared = dict(C)
    shared["ngcol"] = f(inputs["norm_gain"]).reshape(DEPTH, KC, 128).transpose(0, 2, 1).copy()
    shared["w_mod"] = f(inputs["w_mod"])
    shared["b_mod"] = f(inputs["b_mod"])
    shared["w_in"] = f(inputs["w_in"])
    shared["qg"] = f(inputs["q_norm_gain"])
    shared["kg"] = f(inputs["k_norm_gain"])
    gb = f(inputs["mlstm_gate_bias"])
    gbias = np.zeros((DEPTH, 64, 2), np.float32)
    gbias[:, 0:4, 0] = gb[:, 0]
    gbias[:, 32:36, 0] = gb[:, 2]
    gbias[:, 0:4, 1] = gb[:, 1]
    gbias[:, 32:36, 1] = gb[:, 3]
    shared["gbias"] = gbias
    shared["mgain"] = f(inputs["mlstm_norm_gain"])
    shared["convw"] = f(inputs["conv_w"]).reshape(DEPTH, 3, 8, 128).transpose(0, 3, 2, 1).copy()
    shared["pool_w"] = f(inputs["pool_w"])
    shared["pscale"] = f(inputs["pool_scale"]).reshape(DEPTH, 8, 128).transpose(0, 2, 1).copy()
    wb = f(inputs["w_branch"])
    shared["wbt"] = wb.reshape(DEPTH, 4, 8, 128, 16, 128).transpose(0, 4, 3, 1, 2, 5).reshape(DEPTH, 16, 128, 4 * 8 * 128).copy()
    shared["w_out"] = f(inputs["w_out"])
    shared["fgain"] = f(inputs["final_norm_gain"])
    maps = []
    for core in range(NCORES):
        b = core % 4
        m = dict(shared)
        m["x"] = x[b]
        m["ctx"] = ctx[b]
        c2 = np.stack([c[b], c_ctx])
        m["c2T"] = c2.reshape(2, KC, 128).transpose(2, 1, 0).copy()
        maps.append(m)
    return maps


def kernel(**inputs):
    if "nc" not in _CACHE:
        _CACHE["nc"] = build_program()
    nc = _CACHE["nc"]
    maps = make_in_maps(inputs)
    res = run_bass_kernel_spmd(nc, maps, core_ids=list(range(NCORES)))
    out = np.stack([np.asarray(res.results[b]["out"]) for b in range(4)]).astype(np.float32)
    return out
```

```python
import contextlib
import numpy as np
import concourse.bass as bass
import concourse.mybir as mybir
from concourse.bass_utils import run_bass_kernel_spmd

F32 = mybir.dt.float32
BF16 = mybir.dt.bfloat16
AF = mybir.ActivationFunctionType
ALU = mybir.AluOpType
AX = mybir.AxisListType

NT = 4352
NTI = 34
NL = 4096
NC = 256
D = 2048
KC = 16
EPS = 1e-6
BLKS = [(0, 256)] + [(256 + 512 * j, 512) for j in range(8)]
DEPTH = 2
WCOL = dict(aq=0, ak=1024, av=1280, az=1536, mq=2560, mk=3584, mv=4608, mo=5632, mz=6656, mg=7680,
            cu=7696, cb=8720, cc=9744, cz=10768, pu=11792, pz=12816, gm=13840)
N_IN = 22032
FROW = dict(az=0, mz=1024, cz=2048, pz=3072, mo=4096, gm=5120, mq=13312, mk=14336, cu=15360, cb=16384,
            cc=17408, pu=18432)
NF = 19456
FSEG = [("az", 1024, "silu"), ("mz", 1024, "silu"), ("cz", 1024, "silu"), ("pz", 1024, "silu"),
        ("mo", 1024, "sig"), ("gm", 8192, "sig"),
        ("mq", 1024, "copy"), ("mk", 1024, "copy16"), ("cu", 1024, "copy"), ("cb", 1024, "copy"),
        ("cc", 1024, "copy"), ("pu", 1024, "copy")]


class Tok:
    __slots__ = ("q", "sem", "val", "rec")

    def __init__(self, q, rec):
        self.q = q
        self.rec = rec
        self.sem = None
        self.val = None


class Buf:
    __slots__ = ("name", "w", "r", "excl", "wf")

    def __init__(self, name="", excl=False):
        self.name = name
        self.w = {}
        self.wf = {}
        self.r = {}
        self.excl = excl


class Rec:
    __slots__ = ("fn", "deps", "signal", "tok", "dma", "dsem")

    def __init__(self, fn, deps, dma):
        self.fn = fn
        self.deps = deps
        self.signal = False
        self.tok = None
        self.dma = dma
        self.dsem = None


class Prog:
    NDMA = 8

    def __init__(self, nc):
        self.nc = nc
        self.eng = {"pe": nc.tensor, "act": nc.scalar, "dve": nc.vector, "pool": nc.gpsimd, "sp": nc.sync}
        self.q = {k: [] for k in self.eng}
        self.dma_n = {k: 0 for k in self.eng}
        self.dma_last = {k: [None] * self.NDMA for k in self.eng}
        self.last = {k: None for k in self.eng}
        self.fence = {k: [] for k in self.eng}
        self.deferred = []

    def barrier(self):
        self.flush()
        toks = []
        for q in self.eng:
            if self.last[q] is not None:
                toks.append(self.last[q])
            toks.extend(t for t in self.dma_last[q] if t is not None)
        for q in self.eng:
            self.fence[q] = list(toks)

    def flush(self):
        d, self.deferred = self.deferred, []
        for (q, fn, reads, writes, dma, part) in d:
            self.op(q, fn, reads, writes, dma, part)

    def op(self, q, fn, reads=(), writes=(), dma=False, part=False, defer=False):
        if defer:
            self.deferred.append((q, fn, list(reads), list(writes), dma, part))
            return None
        if self.deferred:
            for (_, _, dr, dw, _, _) in self.deferred:
                if any(b in dr for b in writes) or any(b in dw for b in reads):
                    self.flush()
                    break
        deps = []
        if self.fence[q]:
            deps.extend(self.fence[q])
            self.fence[q] = []
        for b in reads:
            deps.extend(b.w.values())
            if b.excl:
                deps.extend(t for kk, t in b.r.items() if kk[0] != q)
        for b in writes:
            deps.extend(b.r.values())
            if not part:
                deps.extend(b.w.values())
            else:
                deps.extend(b.wf.values())
        rec = Rec(fn, deps, dma)
        tok = Tok(q, rec)
        rec.tok = tok
        if dma:
            n = self.dma_n[q]
            self.dma_n[q] = n + 1
            slot = n % self.NDMA
            prev = self.dma_last[q][slot]
            if prev is not None:
                rec.deps.append(prev)
            self.dma_last[q][slot] = tok
            rec.dsem = slot
        else:
            self.last[q] = tok
        self.q[q].append(rec)
        key = (q, rec.dsem)
        for b in reads:
            b.r[key] = tok
        for b in writes:
            if part:
                b.w[key] = tok
            else:
                b.w = {key: tok}
                b.wf = {key: tok}
                b.r = {}
        return tok

    def emit(self, sems, dsems):
        self.flush()
        for q, recs in self.q.items():
            for rec in recs:
                for t in rec.deps:
                    if t.rec.dma:
                        continue
                    if t.q == q and q == "pe":
                        continue
                    t.rec.signal = True
        for q, recs in self.q.items():
            cnt = 0
            dcnt = [0] * self.NDMA
            for rec in recs:
                if rec.dma:
                    dcnt[rec.dsem] += 16
                    rec.tok.sem = dsems[q][rec.dsem]
                    rec.tok.val = dcnt[rec.dsem]
                elif rec.signal:
                    cnt += 1
                    rec.tok.sem = sems[q]
                    rec.tok.val = cnt
        self.stats = {}
        with self.nc.Block() as block:
            def mk(q):
                def body(e):
                    seen = {}
                    nw = 0
                    for rec in self.q[q]:
                        need = {}
                        for t in rec.deps:
                            if t.sem is None or (t.q == q and q == "pe" and not t.rec.dma):
                                continue
                            k = id(t.sem)
                            if seen.get(k, 0) >= t.val:
                                continue
                            if k not in need or need[k][1] < t.val:
                                need[k] = (t.sem, t.val)
                        for k, (s, v) in need.items():
                            e.wait_ge(s, v)
                            seen[k] = v
                            nw += 1
                        ins = rec.fn(e)
                        if rec.dma:
                            ins.then_inc(rec.tok.sem, 16)
                        elif rec.signal:
                            ins.then_inc(rec.tok.sem, 1)
                    for t in self.dma_last[q]:
                        if t is not None:
                            e.wait_ge(t.sem, t.val)
                    self.stats[q] = (len(self.q[q]), nw)
                return body
            block.tensor(mk("pe"))
            block.scalar(mk("act"))
            block.vector(mk("dve"))
            block.gpsimd(mk("pool"))
            block.sync(mk("sp"))


def C(name, *a, **kw):
    return lambda e: getattr(e, name)(*a, **kw)


class Arena:
    def __init__(self, ap, nbytes):
        self.ap = ap
        self.cap = nbytes
        self.top = 0

    def mark(self):
        return self.top

    def reset(self, m):
        self.top = m

    def alloc(self, shape, dt, name=""):
        esz = 2 if dt == BF16 else 4
        n = int(np.prod(shape[1:]))
        nb = (n * esz + 31) // 32 * 32
        off = self.top
        self.top += nb
        assert self.top <= self.cap, f"arena overflow {self.top} > {self.cap} at {name}"
        v = self.ap[0:shape[0], off // 4: off // 4 + (n * esz + 3) // 4]
        if dt != F32:
            v = v.bitcast(dt)[:, 0:n]
        if len(shape) == 3:
            v = v.rearrange("p (a b) -> p a b", b=shape[2])
        elif len(shape) == 4:
            v = v.rearrange("p (a b c) -> p a b c", b=shape[2], c=shape[3])
        return v, Buf(name)


class K:
    pass


def build_program(debug=(), stop_after=None, nlayers=DEPTH):
    nc = bass.Bass("TRN2", target_bir_lowering=False)
    P = Prog(nc)
    k = K()
    k.nc, k.P = nc, P
    k.stop = stop_after

    def din(name, shape, dt=F32):
        return nc.dram_tensor(name, list(shape), dt, kind="ExternalInput").ap()

    def dscr(name, shape, dt, out=False):
        return nc.dram_tensor(name, list(shape), dt, kind=("ExternalOutput" if (out or name in debug) else "Internal")).ap()

    I = {}
    I["x"] = din("x", [NL, D])
    I["ctx"] = din("ctx", [NC, D])
    I["c2T"] = din("c2T", [128, KC, 2])
    I["ngcol"] = din("ngcol", [DEPTH, 128, KC])
    I["w_mod"] = din("w_mod", [DEPTH, D, 3 * D])
    I["b_mod"] = din("b_mod", [DEPTH, 3 * D])
    I["w_in"] = din("w_in", [DEPTH, D, N_IN])
    I["qg"] = din("qg", [DEPTH, 128])
    I["kg"] = din("kg", [DEPTH, 128])
    I["gbias"] = din("gbias", [DEPTH, 64, 2])
    I["mgain"] = din("mgain", [DEPTH, 1024])
    I["convw"] = din("convw", [DEPTH, 128, 8, 3])
    I["pool_w"] = din("pool_w", [DEPTH, 4, 256, 256])
    I["pscale"] = din("pscale", [DEPTH, 128, 8])
    I["wbt"] = din("wbt", [DEPTH, 16, 128, 4 * 8 * 128])
    I["w_out"] = din("w_out", [DEPTH, D, D])
    I["fgain"] = din("fgain", [D])
    I["ident"] = din("ident", [128, 128])
    I["masks"] = din("masks", [2, 128, 128])
    I["rope"] = din("rope", [NTI, 128, 2, 256])
    I["sel"] = din("sel", [64, 8, 128])
    I["rcnt"] = din("rcnt", [4, NT])
    k.I = I
    S = {}
    S["modd"] = dscr("modd", [DEPTH, 2, 3 * D], F32)
    S["F"] = dscr("Fs", [NF, NT], BF16)
    S["QT"] = dscr("QT", [1024, NT], BF16)
    S["KT"] = dscr("KT", [256, NT], BF16)
    S["Vt"] = dscr("Vt", [NT, 256], BF16)
    S["MKt"] = dscr("MKt", [NT, 1024], BF16)
    S["MVt"] = dscr("MVt", [NT, 1024], BF16)
    S["HF"] = dscr("HF", [2, NT, 1024], BF16)
    S["Y"] = dscr("Y", [4, 1024, NT], BF16)
    S["X1"] = dscr("X1", [NT, D], F32)
    S["ACC"] = dscr("ACC", [D, NT], BF16)
    S["GT"] = dscr("GT", [NTI, 128, 72], F32)
    S["out"] = dscr("out", [NL, D], F32, out=True)
    k.S = S
    k.SB = {n: Buf(n) for n in S}

    with contextlib.ExitStack() as es:
        ARENA_BYTES = 206 * 1024
        arena_t = es.enter_context(nc.sbuf_tensor("arena", [128, ARENA_BYTES // 4], F32))
        A = Arena(arena_t, ARENA_BYTES)
        k.A = A
        k.pb2 = [es.enter_context(nc.psum_tensor(f"pbb{i}", [128, 1024], F32))[:, :] for i in range(4)]
        k.pb = [k.pb2[i // 2][:, (i % 2) * 512:(i % 2 + 1) * 512] for i in range(8)]
        k.pbB = [Buf(f"pb{i}", excl=True) for i in range(8)]
        sems = {q: es.enter_context(nc.semaphore("s_" + q)) for q in P.eng}
        dsems = {q: [es.enter_context(nc.semaphore(f"d_{q}{i}")) for i in range(P.NDMA)] for q in P.eng}

        k.idf, k.idfB = A.alloc([128, 128], F32, "idf")
        k.idb, k.idbB = A.alloc([128, 128], BF16, "idb")
        k.onesb, k.onesbB = A.alloc([128, 128], BF16, "onesb")
        k.modcol, k.modcolB = A.alloc([128, DEPTH, 48, 2], F32, "modcol")
        k.gcol, k.gcolB = A.alloc([128, DEPTH, KC, 2], F32, "gcol")
        P.op("sp", C("dma_start", out=k.idf, in_=I["ident"]), writes=[k.idfB], dma=True)
        P.op("pool", C("dma_start", out=k.idb, in_=I["ident"]), writes=[k.idbB], dma=True)
        P.op("dve", C("memset", k.onesb, 1.0), writes=[k.onesbB])

        phase_mod(k)
        for l in range(nlayers if stop_after != ("mod", 0) else 0):
            last = (l == DEPTH - 1)
            phase_norm_inproj(k, l)
            if stop_after in (("inproj", l), ("norm", l), ("tokmaj", l)):
                break
            phase_attn(k, l)
            if stop_after == ("attn", l):
                break
            phase_mlstm(k, l)
            if stop_after == ("mlstm", l):
                break
            phase_conv_pool(k, l)
            if stop_after == ("convpool", l):
                break
            phase_merge_out(k, l)
        P.emit(sems, dsems)
    return nc


def phase_mod(k):
    P, A, I, S = k.P, k.A, k.I, k.S
    m0 = A.mark()
    cT, cTB = A.alloc([128, KC, 2], F32, "cT")
    scT, scTB = A.alloc([128, KC, 2], F32, "scT")
    ngc, ngcB = A.alloc([128, DEPTH, KC], F32, "ngc")
    wm = [A.alloc([128, 3072], F32, f"wm{i}") for i in range(3)]
    mo, moB = A.alloc([2, 6144], F32, "mo")
    bm, bmB = A.alloc([2, 6144], F32, "bm")
    tmp, tmpB = A.alloc([128, KC, 2], F32, "tmpg")
    P.op("sp", C("dma_start", out=cT, in_=I["c2T"]), writes=[cTB], dma=True)
    P.op("sp", C("dma_start", out=ngc, in_=I["ngcol"].rearrange("l p k -> p l k")), writes=[ngcB], dma=True)
    P.op("act", C("activation", out=scT, in_=cT, func=AF.Silu), reads=[cTB], writes=[scTB])
    n = 0
    for l in range(DEPTH):
        P.op("sp", C("dma_start", out=bm, in_=I["b_mod"][l, :].partition_broadcast(2)), writes=[bmB], dma=True)
        for half in range(2):
            for kc in range(KC):
                w, wB = wm[n % 3]
                n += 1
                P.op("sp", C("dma_start",
                    out=w, in_=I["w_mod"][l, kc * 128:(kc + 1) * 128, half * 3072:(half + 1) * 3072]), writes=[wB], dma=True)
                for j in range(6):
                    P.op("pe", C("matmul", k.pb[j][0:2, :], lhsT=scT[:, kc, :], rhs=w[:, j * 512:(j + 1) * 512],
                                                                start=(kc == 0), stop=(kc == KC - 1)),
                         reads=[scTB, wB], writes=[k.pbB[j]])
            for j in range(6):
                c0 = half * 3072 + j * 512
                P.op("dve", C("tensor_tensor", out=mo[:, c0:c0 + 512], in0=k.pb[j][0:2, :], in1=bm[:, c0:c0 + 512], op=ALU.add),
                     reads=[k.pbB[j], bmB], writes=[moB], part=True)
        P.op("sp", C("dma_start", out=S["modd"][l], in_=mo), reads=[moB], writes=[k.SB["modd"]], dma=True, part=True, defer=True)
        for j in range(48):
            P.op("pe", C("transpose", k.pb[6][:, 2 * j:2 * j + 2], mo[0:2, j * 128:(j + 1) * 128], k.idf[0:2, 0:2]),
                 reads=[moB, k.idfB], writes=[k.pbB[6]])
        P.op("dve", C("tensor_copy", out=k.modcol[:, l], in_=k.pb[6][:, 0:96].rearrange("p (j r) -> p j r", r=2)),
             reads=[k.pbB[6]], writes=[k.modcolB], part=True)
        P.op("dve", C("tensor_scalar", out=tmp, in0=k.modcol[:, l, 16:32, :], scalar1=1.0, scalar2=None, op0=ALU.add),
             reads=[k.modcolB], writes=[tmpB])
        P.op("dve", C("tensor_tensor", out=k.gcol[:, l], in0=tmp, in1=ngc[:, l, :].unsqueeze(2).to_broadcast([128, KC, 2]), op=ALU.mult),
             reads=[tmpB, ngcB], writes=[k.gcolB], part=True)
    P.barrier()
    A.reset(m0)


def src_tile(k, l, i):
    if l == 0:
        if i < 2:
            return k.I["ctx"][i * 128:(i + 1) * 128, :]
        return k.I["x"][(i - 2) * 128:(i - 1) * 128, :]
    return k.S["X1"][i * 128:(i + 1) * 128, :]


def phase_norm_inproj(k, l):
    P, A, I, S = k.P, k.A, k.I, k.S
    m0 = A.mark()
    hT, hTB = A.alloc([128, KC, NT], BF16, "hT")
    m1 = A.mark()
    xt = [A.alloc([128, D], F32, f"xt{i}") for i in range(2)]
    xn = [A.alloc([128, D], BF16, f"xn{i}") for i in range(2)]
    junk, junkB = A.alloc([128, D], BF16, "junk")
    ss, ssB = A.alloc([128, NTI], F32, "ss")
    rs, rsB = A.alloc([128, NTI], F32, "rs")
    x1B = [k.SB["X1"]] if l > 0 else []
    for i in range(NTI):
        r = 1 if i < 2 else 0
        x_, xB = xt[i % 2]
        n_, nB = xn[i % 2]
        P.op("sp", C("dma_start", out=x_, in_=src_tile(k, l, i)), reads=x1B, writes=[xB], dma=True)
        P.op("act", C("activation", out=junk, in_=x_, func=AF.Square, accum_out=ss[:, i:i + 1]),
             reads=[xB], writes=[junkB, ssB])
        P.op("dve", C("tensor_scalar", out=rs[:, i:i + 1], in0=ss[:, i:i + 1], scalar1=1.0 / D, scalar2=EPS, op0=ALU.mult, op1=ALU.add),
             reads=[ssB], writes=[rsB])
        P.op("act", C("activation", out=rs[:, i:i + 1], in_=rs[:, i:i + 1], func=AF.Sqrt), reads=[rsB], writes=[rsB])
        P.op("dve", C("reciprocal", out=rs[:, i:i + 1], in_=rs[:, i:i + 1]), reads=[rsB], writes=[rsB])
        P.op("dve", C("tensor_scalar", out=n_, in0=x_, scalar1=rs[:, i:i + 1], scalar2=None, op0=ALU.mult),
             reads=[xB, rsB], writes=[nB])
        import os
        KD = os.environ.get("KDBG", "")
        for half in range(2):
            if KD == "A":
                break
            bi = (2 * i + half) % 4
            pbf = k.pb[bi][:, :].bitcast(BF16)
            for j in range(8):
                kc = half * 8 + j
                P.op("pe", C("transpose", pbf[:, j * 128:(j + 1) * 128], n_[:, kc * 128:(kc + 1) * 128], k.idb),
                     reads=[nB, k.idbB], writes=[k.pbB[bi]])
            for j in range(8):
                kc = half * 8 + j
                if half == 0 or KD == "B":
                    P.op("dve", C("tensor_scalar",
                        out=hT[:, kc, i * 128:(i + 1) * 128], in0=pbf[:, j * 128:(j + 1) * 128],
                        scalar1=k.gcol[:, l, kc, r:r + 1], scalar2=k.modcol[:, l, kc, r:r + 1], op0=ALU.mult, op1=ALU.add),
                        reads=[k.pbB[bi], k.gcolB, k.modcolB], writes=[hTB], part=True)
                else:
                    P.op("act", C("activation",
                        out=hT[:, kc, i * 128:(i + 1) * 128], in_=pbf[:, j * 128:(j + 1) * 128], func=AF.Identity,
                        bias=k.modcol[:, l, kc, r:r + 1], scale=k.gcol[:, l, kc, r:r + 1]),
                        reads=[k.pbB[bi], k.gcolB, k.modcolB], writes=[hTB], part=True)
    P.barrier()
    A.reset(m1)
    if k.stop == ("norm", l):
        A.reset(m0)
        return
    W = [A.alloc([128, KC, 512], BF16, f"W{i}") for i in range(2)]
    m2 = A.mark()
    TS = []
    for ts_i in range(2):
        TS.append(dict(t1=A.alloc([128, 512], F32, f"t1{ts_i}"), t2=A.alloc([128, 512], F32, f"t2{ts_i}"), t3=A.alloc([128, 512], F32, f"t3{ts_i}"),
                       ta=A.alloc([128, 512], F32, f"ta{ts_i}"), tb=A.alloc([128, 512], F32, f"tb{ts_i}"), qf=A.alloc([128, 512], BF16, f"qf{ts_i}"),
                       ssq=A.alloc([128, 4], F32, f"ssq{ts_i}")))
    gts = [A.alloc([128, 2, 36], F32, f"gts{i}") for i in range(2)]
    for g_, gB_ in gts:
        P.op("pool", C("memset", g_, 0.0), writes=[gB_])
    rop = [A.alloc([128, 2, 8, 32], F32, f"rope{i}") for i in range(2)]
    qgb, qgbB = A.alloc([128, 128], F32, "qgb")
    kgb, kgbB = A.alloc([128, 128], F32, "kgb")
    qst = [A.alloc([128, 4, 256], BF16, f"qst{i}") for i in range(2)]
    vst = [A.alloc([128, 512], BF16, f"vst{i}") for i in range(2)]
    P.op("sp", C("dma_start", out=qgb, in_=I["qg"][l, :].partition_broadcast(128)), writes=[qgbB], dma=True)
    P.op("sp", C("dma_start", out=kgb, in_=I["kg"][l, :].partition_broadcast(128)), writes=[kgbB], dma=True)
    P.op("dve", C("tensor_scalar", out=qgb, in0=qgb, scalar1=float(128 ** -0.5), scalar2=None, op0=ALU.mult), reads=[qgbB], writes=[qgbB])
    wsrc = I["w_in"][l].rearrange("(kc p) c -> p kc c", p=128)
    groups = [("q", WCOL["aq"], 0), ("q", WCOL["aq"] + 512, 1), ("kv", WCOL["ak"], 0),
              ("mk", WCOL["mk"], 0), ("mk", WCOL["mk"] + 512, 1), ("mv", WCOL["mv"], 0), ("mv", WCOL["mv"] + 512, 1),
              ("mg", WCOL["mg"], 0)]
    nload = [0]

    def load_w(c0, ncol):
        w, wB = W[nload[0] % 2]
        nload[0] += 1
        P.op("pool", C("dma_start", out=w[:, :, 0:ncol], in_=wsrc[:, :, c0:c0 + ncol]), writes=[wB], dma=True)
        return w, wB

    def qk_post(ps, psB, nh, gb, gbB, rp, rpB, out, outB, T):
        Wd = nh * 128
        g = nh * 2
        (t1, t1B), (t2, t2B), (t3, t3B), (ta, taB), (tb, tbB), (ssq, ssqB) = T["t1"], T["t2"], T["t3"], T["ta"], T["tb"], T["ssq"]
        P.op("act", C("activation", out=t1[:, 0:Wd], in_=ps[:, 0:Wd], func=AF.Square), reads=[psB], writes=[t1B])
        P.op("dve", C("tensor_reduce", out=ssq[:, 0:nh], in_=t1[:, 0:Wd].rearrange("p (h d) -> p h d", d=128), axis=AX.X, op=ALU.add),
             reads=[t1B], writes=[ssqB])
        P.op("dve", C("tensor_scalar", out=ssq[:, 0:nh], in0=ssq[:, 0:nh], scalar1=1.0 / 128, scalar2=EPS, op0=ALU.mult, op1=ALU.add),
             reads=[ssqB], writes=[ssqB])
        P.op("act", C("activation", out=ssq[:, 0:nh], in_=ssq[:, 0:nh], func=AF.Sqrt), reads=[ssqB], writes=[ssqB])
        P.op("dve", C("reciprocal", out=ssq[:, 0:nh], in_=ssq[:, 0:nh]), reads=[ssqB], writes=[ssqB])
        P.op("dve", C("tensor_tensor", out=t2[:, 0:Wd].rearrange("p (h d) -> p h d", d=128), in0=ps[:, 0:Wd].rearrange("p (h d) -> p h d", d=128),
                                              in1=ssq[:, 0:nh].unsqueeze(2).to_broadcast([128, nh, 128]), op=ALU.mult),
             reads=[psB, ssqB], writes=[t2B])
        P.op("pool", C("tensor_tensor", out=t3[:, 0:Wd].rearrange("p (h d) -> p h d", d=128), in0=t2[:, 0:Wd].rearrange("p (h d) -> p h d", d=128),
                                               in1=gb.unsqueeze(1).to_broadcast([128, nh, 128]), op=ALU.mult),
             reads=[t2B, gbB], writes=[t3B])
        t3v = t3[:, 0:Wd].rearrange("p (g x j) -> p g x j", x=2, j=32)
        tav = ta[:, 0:Wd].rearrange("p (g x j) -> p g x j", x=2, j=32)
        tbv = tb[:, 0:Wd].rearrange("p (g x j) -> p g x j", x=2, j=32)
        ov = out.rearrange("p (g x j) -> p g x j", x=2, j=32)
        P.op("pool", C("tensor_tensor", out=tav, in0=t3v, in1=rp[:, 0, 0:g, :].unsqueeze(2).to_broadcast([128, g, 2, 32]), op=ALU.mult),
             reads=[t3B, rpB], writes=[taB])
        P.op("pool", C("tensor_tensor", out=tbv[:, :, 0, :], in0=t3v[:, :, 1, :], in1=rp[:, 1, 0:g, :], op=ALU.mult),
             reads=[t3B, rpB], writes=[tbB], part=True)
        P.op("pool", C("tensor_tensor", out=tbv[:, :, 1, :], in0=t3v[:, :, 0, :], in1=rp[:, 1, 0:g, :], op=ALU.mult),
             reads=[t3B, rpB], writes=[tbB], part=True)
        P.op("dve", C("tensor_tensor", out=ov[:, :, 0, :], in0=tav[:, :, 0, :], in1=tbv[:, :, 0, :], op=ALU.subtract),
             reads=[taB, tbB], writes=[outB], part=True)
        P.op("dve", C("tensor_tensor", out=ov[:, :, 1, :], in0=tav[:, :, 1, :], in1=tbv[:, :, 1, :], op=ALU.add),
             reads=[taB, tbB], writes=[outB], part=True)

    nps = [0]
    cur = load_w(groups[0][1], 512)
    for gi, (kind, c0, sub) in enumerate(groups):
        w, wB = cur
        if gi + 1 < len(groups):
            nk, nc0, _ = groups[gi + 1]
            cur = load_w(nc0, 16 if nk == "mg" else 512)
        ncol = 16 if kind == "mg" else 512
        for i in range(NTI):
            bi = nps[0] % 4
            nps[0] += 1
            ps, psB = k.pb[bi], k.pbB[bi]
            if kind in ("q", "kv"):
                rp, rpB = rop[i % 2]
                P.op("sp", C("dma_start", out=rp, in_=I["rope"][i].rearrange("p a (g j) -> p a g j", j=32)), writes=[rpB], dma=True)
            for kc in range(KC):
                P.op("pe", C("matmul", ps[:, 0:ncol], lhsT=hT[:, kc, i * 128:(i + 1) * 128], rhs=w[:, kc, 0:ncol],
                                                                               start=(kc == 0), stop=(kc == KC - 1)),
                     reads=[hTB, wB], writes=[psB])
            if kind == "q":
                T = TS[i % 2]
                qf, qfB = T["qf"]
                qk_post(ps, psB, 4, qgb, qgbB, rp, rpB, qf, qfB, T)
                grp = i // 2
                pos = i % 2
                st, stB = qst[grp % 2]
                tbi = 4 + (i % 2)
                pbf = k.pb[tbi][:, :].bitcast(BF16)
                for h in range(4):
                    P.op("pe", C("transpose", pbf[:, h * 128:(h + 1) * 128], qf[:, h * 128:(h + 1) * 128], k.idb),
                         reads=[qfB, k.idbB], writes=[k.pbB[tbi]])
                P.op("act", C("activation", out=st[:, :, pos * 128:(pos + 1) * 128],
                                                                            in_=pbf[:, 0:512].rearrange("p (h t) -> p h t", t=128), func=AF.Copy),
                     reads=[k.pbB[tbi]], writes=[stB], part=True)
                done = (pos == 1)
                if done:
                    t0 = grp * 256
                    n = 256
                    P.op("sp", C("dma_start",
                        out=S["QT"].rearrange("(h d) t -> d h t", d=128)[:, sub * 4:(sub + 1) * 4, t0:t0 + n], in_=st[:, :, 0:n]),
                        reads=[stB], writes=[k.SB["QT"]], dma=True, part=True, defer=True)
            elif kind == "kv":
                T = TS[i % 2]
                qf, qfB = T["qf"]
                qk_post(ps, psB, 2, kgb, kgbB, rp, rpB, qf[:, 0:256], qfB, T)
                grp = i // 2
                pos = i % 2
                st, stB = qst[grp % 2]
                tbi = 4 + (i % 2)
                pbf = k.pb[tbi][:, :].bitcast(BF16)
                for h in range(2):
                    P.op("pe", C("transpose", pbf[:, h * 128:(h + 1) * 128], qf[:, h * 128:(h + 1) * 128], k.idb),
                         reads=[qfB, k.idbB], writes=[k.pbB[tbi]])
                P.op("act", C("activation", out=st[:, 0:2, pos * 128:(pos + 1) * 128],
                                                                            in_=pbf[:, 0:256].rearrange("p (h t) -> p h t", t=128), func=AF.Copy),
                     reads=[k.pbB[tbi]], writes=[stB], part=True)
                done = (pos == 1)
                if done:
                    t0 = grp * 256
                    n = 256
                    P.op("sp", C("dma_start",
                        out=S["KT"].rearrange("(h d) t -> d h t", d=128)[:, :, t0:t0 + n], in_=st[:, 0:2, 0:n]),
                        reads=[stB], writes=[k.SB["KT"]], dma=True, part=True, defer=True)
                v_, vB = vst[i % 2]
                P.op("act", C("activation", out=v_[:, 0:256], in_=ps[:, 256:512], func=AF.Copy), reads=[psB], writes=[vB])
                P.op("sp", C("dma_start", out=S["Vt"][i * 128:(i + 1) * 128, :], in_=v_[:, 0:256]),
                     reads=[vB], writes=[k.SB["Vt"]], dma=True, part=True, defer=True)
            elif kind in ("mk", "mv"):
                v_, vB = vst[i % 2]
                sc = 0.0625 if kind == "mk" else 1.0
                P.op("act", C("activation", out=v_, in_=ps, func=AF.Copy, scale=sc), reads=[psB], writes=[vB])
                dst = S["MKt"] if kind == "mk" else S["MVt"]
                dB = k.SB["MKt"] if kind == "mk" else k.SB["MVt"]
                P.op("sp", C("dma_start", out=dst[i * 128:(i + 1) * 128, sub * 512:(sub + 1) * 512], in_=v_),
                     reads=[vB], writes=[dB], dma=True, part=True, defer=True)
            else:
                g_, gB_ = gts[i % 2]
                for gi in range(4):
                    P.op("dve", C("tensor_copy", out=g_[:, gi % 2, (gi // 2) * 32:(gi // 2) * 32 + 4], in_=ps[:, gi * 4:gi * 4 + 4]),
                         reads=[psB], writes=[gB_], part=(gi > 0))
                P.op("sp", C("dma_start", out=S["GT"][i], in_=g_.rearrange("p a b -> p (a b)")), reads=[gB_], writes=[k.SB["GT"]], dma=True, part=True, defer=True)
    P.barrier()
    A.reset(m2)
    if k.stop == ("tokmaj", l):
        A.reset(m0)
        return
    stg = [A.alloc([128, NT], BF16, f"stg{i}") for i in range(2)]
    for st_, stB_ in stg:
        P.op("pool", C("memset", st_, 0.0), writes=[stB_])
    glist = []
    for name, ncols, fn in FSEG:
        for g in range(ncols // 512):
            glist.append((name, WCOL[name] + g * 512, FROW[name] + g * 512, fn))
    cur = load_w(glist[0][1], 512)
    nst = 0
    for gi, (name, c0, r0, fn) in enumerate(glist):
        w, wB = cur
        if gi + 1 < len(glist):
            cur = load_w(glist[gi + 1][1], 512)
        for j in range(4):
            st, stB = stg[nst % 2]
            nst += 1
            fblks = BLKS[1:] if (l == DEPTH - 1 and name != "mk") else BLKS
            for (t0, n) in fblks:
                bi = nps[0] % 4
                nps[0] += 1
                ps, psB = k.pb[bi], k.pbB[bi]
                for kc in range(KC):
                    P.op("pe", C("matmul", ps[:, 0:n], lhsT=w[:, kc, j * 128:(j + 1) * 128], rhs=hT[:, kc, t0:t0 + n],
                                                                                    start=(kc == 0), stop=(kc == KC - 1)),
                         reads=[hTB, wB], writes=[psB])
                if fn == "silu":
                    P.op("act", C("activation", out=st[:, t0:t0 + n], in_=ps[:, 0:n], func=AF.Silu),
                         reads=[psB], writes=[stB], part=True)
                elif fn == "sig":
                    P.op("act", C("activation", out=st[:, t0:t0 + n], in_=ps[:, 0:n], func=AF.Sigmoid),
                         reads=[psB], writes=[stB], part=True)
                elif fn == "copy16":
                    P.op("dve", C("tensor_scalar", out=st[:, t0:t0 + n], in0=ps[:, 0:n], scalar1=0.0625, scalar2=None, op0=ALU.mult),
                         reads=[psB], writes=[stB], part=True)
                else:
                    P.op("dve", C("tensor_copy", out=st[:, t0:t0 + n], in_=ps[:, 0:n]),
                         reads=[psB], writes=[stB], part=True)
            P.op("sp", C("dma_start", out=S["F"][r0 + j * 128:r0 + (j + 1) * 128, :], in_=st),
                 reads=[stB], writes=[k.SB["F"]], dma=True, part=True, defer=True)
    P.barrier()
    A.reset(m0)


def phase_attn(k, l):
    P, A, I, S = k.P, k.A, k.I, k.S
    m0 = A.mark()
    KTs, KTB = A.alloc([128, 2, NT], BF16, "KTs")
    Vs, VB = A.alloc([128, NTI, 256], BF16, "Vs")
    Qb = [A.alloc([128, 8, 512], BF16, f"Qb{i}") for i in range(2)]
    AZ = [A.alloc([128, 8, 512], BF16, f"AZ{i}") for i in range(2)]
    PT = [A.alloc([128, 2, 512], BF16, f"PT{i}") for i in range(3)]
    rec, recB = A.alloc([128, 512], F32, "rec")
    t4, t4B = A.alloc([128, 512], F32, "t4")
    ost = [A.alloc([128, 8, 512], BF16, f"ost{i}") for i in range(2)]
    P.op("sp", C("dma_start", out=KTs, in_=S["KT"].rearrange("(h d) t -> d h t", d=128)), reads=[k.SB["KT"]], writes=[KTB], dma=True)
    P.op("sp", C("dma_start", out=Vs, in_=S["Vt"].rearrange("(i p) c -> p i c", p=128)), reads=[k.SB["Vt"]], writes=[VB], dma=True)
    blocks = []
    if l < DEPTH - 1:
        blocks.append((0, 256, [0, 1]))
    for j in range(8):
        blocks.append((256 + 512 * j, 512, list(range(NTI))))
    QTv = S["QT"].rearrange("(h d) t -> d h t", d=128)
    AZv = S["F"][FROW["az"]:FROW["az"] + 1024, :].rearrange("(h d) t -> d h t", d=128)
    Yv = S["Y"][0].rearrange("(h d) t -> d h t", d=128)
    ns = [0]
    npt = [0]
    for bi, (t0, n, keys) in enumerate(blocks):
        q_, qB = Qb[bi % 2]
        az_, azB = AZ[bi % 2]
        o_, oB = ost[bi % 2]
        P.op("sp", C("dma_start", out=q_[:, :, 0:n], in_=QTv[:, :, t0:t0 + n]), reads=[k.SB["QT"]], writes=[qB], dma=True)
        P.op("sp", C("dma_start", out=az_[:, :, 0:n], in_=AZv[:, :, t0:t0 + n]), reads=[k.SB["F"]], writes=[azB], dma=True)
        nk = len(keys)
        npair = nk // 2
        for h in range(8):
            kv = h // 4
            psO, psOB = k.pb[4 + (h % 2)], k.pbB[4 + (h % 2)]
            psD, psDB = k.pb[6 + (h % 2)], k.pbB[6 + (h % 2)]
            spair = []

            def emitS(pi):
                pr = ns[0] % 2
                ns[0] += 1
                spair.append(pr)
                for j in range(2):
                    kt = keys[2 * pi + j]
                    bb = 2 * pr + j
                    P.op("pe", C("matmul", k.pb[bb][:, 0:n], lhsT=KTs[:, kv, kt * 128:(kt + 1) * 128], rhs=q_[:, h, 0:n], start=True, stop=True),
                         reads=[KTB, qB], writes=[k.pbB[bb]])
            emitS(0)
            for pi in range(npair):
                if pi + 1 < npair:
                    emitS(pi + 1)
                pr = spair[pi]
                p_, pB = PT[npt[0] % 3]
                npt[0] += 1
                sv = k.pb2[pr].rearrange("p (b c) -> p b c", b=2)[:, :, 0:n]
                P.op("act", C("activation", out=p_[:, :, 0:n], in_=sv, func=AF.Exp), reads=[k.pbB[2 * pr], k.pbB[2 * pr + 1]], writes=[pB])
                for j in range(2):
                    kt = keys[2 * pi + j]
                    idx = 2 * pi + j
                    P.op("pe", C("matmul", psO[:, 0:n], lhsT=Vs[:, kt, kv * 128:(kv + 1) * 128], rhs=p_[:, j, 0:n],
                                 start=(idx == 0), stop=(idx == nk - 1)), reads=[VB, pB], writes=[psOB])
                for j in range(2):
                    idx = 2 * pi + j
                    P.op("pe", C("matmul", psD[:, 0:n], lhsT=k.onesb, rhs=p_[:, j, 0:n], start=(idx == 0), stop=(idx == nk - 1)),
                         reads=[k.onesbB, pB], writes=[psDB])
            P.op("dve", C("reciprocal", out=rec[:, 0:n], in_=psD[:, 0:n]), reads=[psDB], writes=[recB])
            P.op("dve", C("tensor_tensor", out=t4[:, 0:n], in0=psO[:, 0:n], in1=rec[:, 0:n], op=ALU.mult), reads=[psOB, recB], writes=[t4B])
            P.op("pool", C("tensor_tensor", out=o_[:, h, 0:n], in0=t4[:, 0:n], in1=az_[:, h, 0:n], op=ALU.mult),
                 reads=[t4B, azB], writes=[oB], part=True)
        P.op("sp", C("dma_start", out=Yv[:, :, t0:t0 + n], in_=o_[:, :, 0:n]), reads=[oB], writes=[k.SB["Y"]], dma=True, part=True, defer=True)
    P.barrier()
    A.reset(m0)


def phase_mlstm(k, l):
    P, A, I, S = k.P, k.A, k.I, k.S
    last_layer = (l == DEPTH - 1)
    m0 = A.mark()
    WC, WCB = A.alloc([128, NTI, 16], F32, "WC")
    DECB, DECBB = A.alloc([128, 8, NTI], F32, "DECB")
    m1 = A.mark()
    GI, GIB = A.alloc([64, NT], F32, "GI")
    GF, GFB = A.alloc([64, NT], F32, "GF")
    ONE, ONEB = A.alloc([64, NT], F32, "ONE")
    BP, BPB = A.alloc([64, NT], F32, "BP")
    AP_, APB = A.alloc([64, NT], F32, "APr")
    MM, MMB = A.alloc([64, NT], F32, "MM")
    M2, M2B = A.alloc([64, NT], F32, "M2")
    WR, WRB = A.alloc([64, NT], F32, "WR")
    CL, CLB = A.alloc([64, NT], F32, "CL")
    gb, gbB = A.alloc([64, 2], F32, "gb")
    Gtok, GtokB = A.alloc([128, NTI, 2, 36], F32, "Gtok")
    P.op("sp", C("dma_start", out=Gtok.rearrange("p i a b -> p i (a b)"), in_=S["GT"].rearrange("i p c -> p i c")), reads=[k.SB["GT"]], writes=[GtokB], dma=True)
    dec, decB = A.alloc([64, NTI], F32, "dec")
    sel, selB = A.alloc([64, 8, 128], F32, "sel")
    P.op("sp", C("dma_start", out=gb, in_=I["gbias"][l]), writes=[gbB], dma=True)
    P.op("sp", C("dma_start", out=sel, in_=I["sel"]), writes=[selB], dma=True)
    for t_, tB in ((GI, GIB), (GF, GFB), (dec, decB)):
        P.op("pool", C("memset", t_, 0.0), writes=[tB])
    P.op("pool", C("memset", ONE, 1.0), writes=[ONEB])
    R = (slice(0, 4), slice(32, 36))
    nb = 0
    for (t0, n) in BLKS:
        bt0 = (t0 - 256) if t0 >= 256 else 4096
        for gf, (dst, dstB) in enumerate(((GI, GIB), (GF, GFB))):
            bi = nb % 4
            nb += 1
            ps, psB = k.pb[bi], k.pbB[bi]
            for j in range(n // 128):
                i = t0 // 128 + j
                P.op("pe", C("transpose", ps[0:36, j * 128:(j + 1) * 128], Gtok[:, i, gf, :], k.idf),
                     reads=[GtokB, k.idfB], writes=[psB])
            P.op("act", C("activation", out=dst[0:4, t0:t0 + n], in_=ps[0:4, 0:n], func=AF.Identity,
                                                                               bias=gb[0:4, gf:gf + 1], scale=1.0),
                 reads=[psB, gbB], writes=[dstB], part=True)
            P.op("act", C("activation", out=dst[32:36, bt0:bt0 + n], in_=ps[32:36, 0:n], func=AF.Identity,
                                                                                 bias=gb[32:36, gf:gf + 1], scale=1.0),
                 reads=[psB, gbB], writes=[dstB], part=True)
    P.op("act", C("activation", out=GF[0:36, :], in_=GF[0:36, :], func=AF.Exp, scale=-1.0), reads=[GFB], writes=[GFB])
    P.op("act", C("activation", out=GF[0:36, :], in_=GF[0:36, :], func=AF.Ln, bias=1.0, scale=1.0), reads=[GFB], writes=[GFB])
    P.op("dve", C("tensor_tensor_scan", out=BP[0:36, :], data0=ONE[0:36, :], data1=GF[0:36, :], initial=0.0, op0=ALU.mult, op1=ALU.add),
         reads=[ONEB, GFB], writes=[BPB])
    P.op("dve", C("tensor_scalar", out=M2[32:36, :], in0=BP[32:36, :], scalar1=BP[32:36, NT - 1:NT], scalar2=-1.0, op0=ALU.subtract, op1=ALU.mult),
         reads=[BPB], writes=[M2B])
    P.op("dve", C("tensor_tensor", out=BP[32:36, :], in0=M2[32:36, :], in1=GF[32:36, :], op=ALU.add), reads=[M2B, GFB], writes=[BPB])
    P.op("dve", C("tensor_tensor", out=AP_[0:36, :], in0=GI[0:36, :], in1=BP[0:36, :], op=ALU.add), reads=[GIB, BPB], writes=[APB])
    P.op("dve", C("tensor_tensor_scan", out=MM[0:4, :], data0=ONE[0:4, :], data1=AP_[0:4, :], initial=-1e30, op0=ALU.mult, op1=ALU.max),
         reads=[ONEB, APB], writes=[MMB], part=True)
    src, srcB = AP_, APB
    bufs = [(M2, M2B), (MM, MMB)]
    sh = 1
    step = 0
    while sh < NT:
        dst, dstB = bufs[step % 2]
        P.op("dve", C("tensor_tensor", out=dst[32:36, 0:NT - sh], in0=src[32:36, 0:NT - sh], in1=src[32:36, sh:NT], op=ALU.max),
             reads=[srcB], writes=[dstB], part=True)
        P.op("pool", C("tensor_copy", out=dst[32:36, NT - sh:NT], in_=src[32:36, NT - sh:NT]),
             reads=[srcB], writes=[dstB], part=True)
        src, srcB = dst, dstB
        sh *= 2
        step += 1
    if src is not MM:
        P.op("dve", C("tensor_copy", out=MM[32:36, :], in_=src[32:36, :]), reads=[srcB], writes=[MMB], part=True)

    def v3(t_, r):
        return t_[r, :].rearrange("p (c t) -> p c t", t=128)
    for d, r in enumerate(R):
        li = 127 if d == 0 else 0
        mlast = v3(MM, r)[:, :, li:li + 1].to_broadcast([4, NTI, 128])
        P.op("dve", C("tensor_tensor", out=v3(WR, r), in0=v3(AP_, r), in1=mlast, op=ALU.subtract), reads=[APB, MMB], writes=[WRB], part=True)
        P.op("act", C("activation", out=WR[r, :], in_=WR[r, :], func=AF.Exp), reads=[WRB], writes=[WRB], part=True)
        P.op("dve", C("tensor_tensor", out=v3(CL, r), in0=v3(BP, r), in1=mlast, op=ALU.subtract), reads=[BPB, MMB], writes=[CLB], part=True)
        P.op("act", C("activation", out=CL[r, :], in_=CL[r, :], func=AF.Exp), reads=[CLB], writes=[CLB], part=True)
        ml2 = v3(MM, r)[:, :, li]
        if d == 0:
            P.op("dve", C("tensor_tensor", out=dec[r, 1:NTI], in0=ml2[:, 0:NTI - 1], in1=ml2[:, 1:NTI], op=ALU.subtract),
                 reads=[MMB], writes=[decB], part=True)
            P.op("act", C("activation", out=dec[r, 1:NTI], in_=dec[r, 1:NTI], func=AF.Exp), reads=[decB], writes=[decB], part=True)
        else:
            P.op("dve", C("tensor_tensor", out=dec[r, 0:NTI - 1], in0=ml2[:, 1:NTI], in1=ml2[:, 0:NTI - 1], op=ALU.subtract),
                 reads=[MMB], writes=[decB], part=True)
            P.op("act", C("activation", out=dec[r, 0:NTI - 1], in_=dec[r, 0:NTI - 1], func=AF.Exp), reads=[decB], writes=[decB], part=True)
    for q in range(8):
        P.op("pe", C("matmul", k.pb[0][:, q * NTI:(q + 1) * NTI], lhsT=sel[0:36, q, :], rhs=dec[0:36, :], start=True, stop=True),
             reads=[selB, decB], writes=[k.pbB[0]])
    P.op("dve", C("tensor_copy", out=DECB, in_=k.pb[0][:, 0:8 * NTI].rearrange("p (q c) -> p q c", c=NTI)), reads=[k.pbB[0]], writes=[DECBB])
    for half in range(2):
        tiles = list(range(half * 17, half * 17 + 17))
        ps, psB = k.pb[1 + half], k.pbB[1 + half]
        for jj, i in enumerate(tiles):
            fc = i * 128
            bc = (i - 2) * 128 if i >= 2 else 4096 + i * 128
            for qq, (src, srcB, r, c0) in enumerate(((WR, WRB, R[0], fc), (WR, WRB, R[1], bc), (CL, CLB, R[0], fc), (CL, CLB, R[1], bc))):
                P.op("pe", C("transpose", ps[:, jj * 16 + qq * 4: jj * 16 + qq * 4 + 4], src[r, c0:c0 + 128], k.idf[r, r]),
                     reads=[srcB, k.idfB], writes=[psB])
        P.op("dve", C("tensor_copy", out=WC[:, half * 17:half * 17 + 17, :], in_=ps[:, 0:17 * 16].rearrange("p (i q) -> p i q", q=16)),
             reads=[psB], writes=[WCB], part=True)
    P.barrier()
    A.reset(m1)
    msk, mskB = A.alloc([128, 2, 128], F32, "msk")
    mgb, mgbB = A.alloc([128, 1024], F32, "mgb")
    P.op("sp", C("dma_start", out=msk, in_=I["masks"].rearrange("m s t -> s m t")), writes=[mskB], dma=True)
    P.op("sp", C("dma_start", out=mgb, in_=I["mgain"][l, :].partition_broadcast(128)), writes=[mgbB], dma=True)
    Fq = S["F"][FROW["mq"]:FROW["mq"] + 1024, :].rearrange("(a p) t -> p a t", p=128)
    Fk = S["F"][FROW["mk"]:FROW["mk"] + 1024, :].rearrange("(a p) t -> p a t", p=128)
    Fo = S["F"][FROW["mo"]:FROW["mo"] + 1024, :].rearrange("(a p) t -> p a t", p=128)
    Fz = S["F"][FROW["mz"]:FROW["mz"] + 1024, :].rearrange("(a p) t -> p a t", p=128)
    Yb = S["Y"][1].rearrange("(a p) t -> p a t", p=128)
    HB_ = [[Buf(f"H{d}_{i}") for i in range(NTI)] for d in range(2)]
    BD = []
    for d in range(2):
        b = K()
        b.Cf, _ = A.alloc([128, 4, 2, 257], F32, f"Cf{d}")
        b.Ct, _ = A.alloc([128, 4, 2, 257], BF16, f"Ct{d}")
        b.CfBs = [Buf(f"Cf{d}{h}") for h in range(4)]
        b.CtBs = [Buf(f"Ct{d}{h}") for h in range(4)]
        b.qT = [A.alloc([128, 8, 128], BF16, f"qT{d}{i}") for i in range(2)]
        b.kT = [A.alloc([128, 8, 128], BF16, f"kT{d}{i}") for i in range(2)]
        b.ktk = [A.alloc([128, 1024], BF16, f"ktk{d}{i}") for i in range(2)]
        b.vtk = [A.alloc([128, 4, 257], BF16, f"vtk{d}{i}") for i in range(2)]
        b.moT = [A.alloc([128, 8, 128], BF16, f"moT{d}{i}") for i in range(2)]
        b.mzT = [A.alloc([128, 8, 128], BF16, f"mzT{d}{i}") for i in range(2)]
        b.hfl = [A.alloc([128, 1024], BF16, f"hfl{d}{i}") for i in range(2)]
        b.Sm = [A.alloc([128, 128], BF16, f"Sm{d}{i}") for i in range(2)]
        b.vw = [A.alloc([128, 257], BF16, f"vw{d}{i}") for i in range(2)]
        b.hst = [A.alloc([128, 4, 256], BF16, f"hst{d}{i}") for i in range(2)]
        b.hs, b.hsB = A.alloc([128, 4, 256], F32, f"hs{d}")
        b.hj, b.hjB = A.alloc([128, 1024], F32, f"hj{d}")
        b.hb, b.hbB = A.alloc([128, 1024], BF16, f"hb{d}")
        b.hss, b.hssB = A.alloc([128, 4], F32, f"hss{d}")
        b.dn = [A.alloc([128, 2], F32, f"dn{d}{h}") for h in range(4)]
        b.tT, b.tTB = A.alloc([128, 8, 128], F32, f"tT{d}")
        b.yst = [A.alloc([128, 8, 128], BF16, f"yst{d}{i}") for i in range(2)]
        b.cnt = dict(sm=0, vw=0, y=0)
        for v_, vB in b.vtk:
            P.op("pool", C("memset", v_, 1.0), writes=[vB])
        BD.append(b)
    orders = [list(range(NTI)), [1, 0] + list(range(NTI - 1, 1, -1))]

    def chunk(d, step, i):
        b = BD[d]
        is_ctx = i < 2
        need_out = not (is_ctx and last_layer)
        if is_ctx:
            first = (i == 0) if d == 0 else (i == 1)
        else:
            first = (i <= 17) if d == 0 else (i > 17)
        combine = need_out and not first
        cidx = i if d == 0 else ((i - 2) if i >= 2 else 32 + i)
        sl = slice(i * 128, (i + 1) * 128)
        q_, qB = b.qT[step % 2]
        k_, kB = b.kT[step % 2]
        kt_, ktB = b.ktk[step % 2]
        v_, vB = b.vtk[step % 2]
        P.op("sp", C("dma_start", out=q_, in_=Fq[:, :, sl]), reads=[k.SB["F"]], writes=[qB], dma=True)
        P.op("sp", C("dma_start", out=k_, in_=Fk[:, :, sl]), reads=[k.SB["F"]], writes=[kB], dma=True)
        P.op("sp", C("dma_start", out=kt_, in_=S["MKt"][sl, :]), reads=[k.SB["MKt"]], writes=[ktB], dma=True)
        P.op("sp", C("dma_start", out=v_[:, :, 0:256], in_=S["MVt"][sl, :].rearrange("t (h e) -> t h e", e=256)),
             reads=[k.SB["MVt"]], writes=[vB], dma=True, part=True)
        if combine:
            o_, oB = b.moT[step % 2]
            z_, zB = b.mzT[step % 2]
            f_, fB = b.hfl[step % 2]
            P.op("sp", C("dma_start", out=o_, in_=Fo[:, :, sl]), reads=[k.SB["F"]], writes=[oB], dma=True)
            P.op("sp", C("dma_start", out=z_, in_=Fz[:, :, sl]), reads=[k.SB["F"]], writes=[zB], dma=True)
            P.op("sp", C("dma_start", out=f_, in_=S["HF"][1 - d][sl, :]), reads=[HB_[1 - d][i]], writes=[fB], dma=True)
        yield
        h_, hB = b.hst[step % 2]
        bS, bP, bU = d, 2 + d, 4 + 2 * d
        psS, psSB = k.pb[bS], k.pbB[bS]
        psP, psPB = k.pb[bP], k.pbB[bP]
        for hh in range(4):
            qi = d * 4 + hh
            for dc in range(2):
                P.op("pe", C("matmul", psS[:, 0:128], lhsT=k_[:, hh * 2 + dc, :], rhs=q_[:, hh * 2 + dc, :], start=(dc == 0), stop=(dc == 1)),
                     reads=[kB, qB], writes=[psSB])
            yield
            sm_, smB = b.Sm[b.cnt["sm"] % 2]
            b.cnt["sm"] += 1
            P.op("dve", C("tensor_tensor", out=sm_, in0=psS[:, 0:128], in1=msk[:, d, :], op=ALU.mult), reads=[psSB, mskB], writes=[smB])
            vw_, vwB = b.vw[b.cnt["vw"] % 2]
            b.cnt["vw"] += 1
            P.op("act", C("activation", out=vw_, in_=v_[:, hh, :], func=AF.Copy, scale=WC[:, i, qi:qi + 1]),
                 reads=[vB, WCB], writes=[vwB])
            if step > 0:
                P.op("act", C("activation", out=b.Ct[:, hh], in_=b.Cf[:, hh], func=AF.Copy, scale=DECB[:, qi, cidx:cidx + 1]),
                     reads=[b.CfBs[hh], DECBB], writes=[b.CtBs[hh]])
            yield
            P.op("pe", C("matmul", psP[:, 0:257], lhsT=sm_, rhs=vw_, start=True, stop=(step == 0)), reads=[smB, vwB], writes=[psPB])
            if step > 0:
                for dc in range(2):
                    P.op("pe", C("matmul", psP[:, 0:257], lhsT=q_[:, hh * 2 + dc, :], rhs=b.Ct[:, hh, dc, :], start=False, stop=(dc == 1)),
                         reads=[qB, b.CtBs[hh]], writes=[psPB])
            for dc in range(2):
                psU, psUB = k.pb[bU + dc], k.pbB[bU + dc]
                P.op("pe", C("matmul", psU[:, 0:257], lhsT=kt_[:, hh * 256 + dc * 128: hh * 256 + (dc + 1) * 128], rhs=vw_, start=True, stop=True),
                     reads=[ktB, vwB], writes=[psUB])
                if step == 0:
                    P.op("dve", C("tensor_copy", out=b.Cf[:, hh, dc, :], in_=psU[:, 0:257]), reads=[psUB], writes=[b.CfBs[hh]], part=(dc == 1))
                else:
                    P.op("dve", C("scalar_tensor_tensor", out=b.Cf[:, hh, dc, :], in0=b.Cf[:, hh, dc, :], scalar=DECB[:, qi, cidx:cidx + 1], in1=psU[:, 0:257],
                                  op0=ALU.mult, op1=ALU.add), reads=[psUB, b.CfBs[hh], DECBB], writes=[b.CfBs[hh]], part=(dc == 1))
            yield
            if need_out:
                dn, dnB = b.dn[hh]
                P.op("dve", C("tensor_scalar", out=dn[:, 1:2], in0=psP[:, 256:257], scalar1=WC[:, i, 8 + qi:9 + qi], scalar2=None, op0=ALU.max),
                     reads=[psPB, WCB], writes=[dnB])
                P.op("dve", C("scalar_tensor_tensor", out=dn[:, 0:1], in0=psP[:, 256:257], scalar=-1.0, in1=dn[:, 1:2], op0=ALU.mult, op1=ALU.max),
                     reads=[psPB, dnB], writes=[dnB])
                P.op("dve", C("reciprocal", out=dn[:, 1:2], in_=dn[:, 0:1]), reads=[dnB], writes=[dnB])
                if not combine:
                    P.op("act", C("activation", out=h_[:, hh, :], in_=psP[:, 0:256], func=AF.Copy, scale=dn[:, 1:2]),
                         reads=[psPB, dnB], writes=[hB], part=(hh > 0))
                else:
                    P.op("dve", C("scalar_tensor_tensor", out=b.hs[:, hh, :], in0=psP[:, 0:256], scalar=dn[:, 1:2],
                                  in1=f_[:, hh * 256:(hh + 1) * 256], op0=ALU.mult, op1=ALU.add),
                         reads=[psPB, dnB, fB], writes=[b.hsB], part=(hh > 0))
        if need_out and not combine:
            P.op("sp", C("dma_start", out=S["HF"][d][sl, :], in_=h_.rearrange("p h e -> p (h e)")), reads=[hB], writes=[HB_[d][i]], dma=True, defer=True)
        yield
        if combine:
            hs2 = b.hs.rearrange("p h e -> p (h e)")
            P.op("act", C("activation", out=b.hj, in_=hs2, func=AF.Square), reads=[b.hsB], writes=[b.hjB])
            P.op("dve", C("tensor_reduce", out=b.hss, in_=b.hj.rearrange("p (h e) -> p h e", e=256), axis=AX.X, op=ALU.add), reads=[b.hjB], writes=[b.hssB])
            P.op("dve", C("tensor_scalar", out=b.hss, in0=b.hss, scalar1=1.0 / 256, scalar2=EPS, op0=ALU.mult, op1=ALU.add), reads=[b.hssB], writes=[b.hssB])
            P.op("act", C("activation", out=b.hss, in_=b.hss, func=AF.Sqrt), reads=[b.hssB], writes=[b.hssB])
            P.op("dve", C("reciprocal", out=b.hss, in_=b.hss), reads=[b.hssB], writes=[b.hssB])
            hn = b.hj.rearrange("p (h e) -> p h e", e=256)
            P.op("dve", C("tensor_tensor", out=hn, in0=b.hs, in1=b.hss.unsqueeze(2).to_broadcast([128, 4, 256]), op=ALU.mult), reads=[b.hsB, b.hssB, b.hjB], writes=[b.hjB])
            P.op("pool", C("tensor_tensor", out=b.hb, in0=b.hj, in1=mgb, op=ALU.mult), reads=[b.hjB, mgbB], writes=[b.hbB])
            yield
            pbf = psP.bitcast(BF16)
            for cc in range(8):
                P.op("pe", C("transpose", pbf[:, cc * 128:(cc + 1) * 128], b.hb[:, cc * 128:(cc + 1) * 128], k.idb),
                     reads=[b.hbB, k.idbB], writes=[psPB])
            P.op("dve", C("tensor_tensor", out=b.tT, in0=pbf.rearrange("p (a t) -> p a t", t=128), in1=o_, op=ALU.mult),
                 reads=[psPB, oB], writes=[b.tTB])
            y_, yB = b.yst[b.cnt["y"] % 2]
            b.cnt["y"] += 1
            P.op("pool", C("tensor_tensor", out=y_, in0=b.tT, in1=z_, op=ALU.mult), reads=[b.tTB, zB], writes=[yB])
            P.op("sp", C("dma_start", out=Yb[:, :, sl], in_=y_), reads=[yB], writes=[k.SB["Y"]], dma=True, part=True, defer=True)

    for step in range(NTI):
        gens = [chunk(d, step, orders[d][step]) for d in range(2)]
        while gens:
            for g in list(gens):
                try:
                    next(g)
                except StopIteration:
                    gens.remove(g)
    P.barrier()
    A.reset(m0)


def phase_conv_pool(k, l):
    P, A, I, S = k.P, k.A, k.I, k.S
    m0 = A.mark()
    cw, cwB = A.alloc([128, 8, 3], F32, "cw")
    P.op("sp", C("dma_start", out=cw, in_=I["convw"][l]), writes=[cwB], dma=True)
    inb = [[A.alloc([128, NT], BF16, f"cv{j}_{i}") for j in range(4)] for i in range(2)]
    ap_, apB = A.alloc([128, NT + 2], F32, "apad")
    y_, yB = A.alloc([128, NT], F32, "ycv")
    y2, y2B = A.alloc([128, NT], F32, "ycv2")
    ost = [A.alloc([128, NT], BF16, f"cvo{i}") for i in range(2)]
    P.op("pool", C("memset", ap_, 0.0), writes=[apB])
    names = ("cu", "cc", "cb", "cz")
    for cc in range(8):
        tl = inb[cc % 2]
        for j, nm in enumerate(names):
            t_, tB = tl[j]
            r0 = FROW[nm] + cc * 128
            P.op("sp", C("dma_start", out=t_, in_=S["F"][r0:r0 + 128, :]), reads=[k.SB["F"]], writes=[tB], dma=True)
        (cu, cuB), (cg, cgB), (cb, cbB), (cz, czB) = tl
        w0, w1, w2 = cw[:, cc, 0:1], cw[:, cc, 1:2], cw[:, cc, 2:3]
        P.op("pool", C("tensor_tensor", out=ap_[:, 1:NT + 1], in0=cu, in1=cg, op=ALU.mult), reads=[cuB, cgB], writes=[apB])
        P.op("dve", C("tensor_scalar", out=y_, in0=ap_[:, 1:NT + 1], scalar1=w1, scalar2=None, op0=ALU.mult), reads=[apB, cwB], writes=[yB])
        P.op("dve", C("scalar_tensor_tensor", out=y_, in0=ap_[:, 0:NT], scalar=w0, in1=y_, op0=ALU.mult, op1=ALU.add), reads=[apB, cwB, yB], writes=[yB])
        P.op("dve", C("scalar_tensor_tensor", out=y_, in0=ap_[:, 2:NT + 2], scalar=w2, in1=y_, op0=ALU.mult, op1=ALU.add), reads=[apB, cwB, yB], writes=[yB])
        P.op("dve", C("tensor_scalar", out=y_[:, 255:256], in0=ap_[:, 255:256], scalar1=w0, scalar2=None, op0=ALU.mult), reads=[apB, cwB, yB], writes=[yB])
        P.op("dve", C("scalar_tensor_tensor", out=y_[:, 255:256], in0=ap_[:, 256:257], scalar=w1, in1=y_[:, 255:256], op0=ALU.mult, op1=ALU.add), reads=[apB, cwB, yB], writes=[yB])
        P.op("dve", C("tensor_scalar", out=y_[:, 256:257], in0=ap_[:, 257:258], scalar1=w1, scalar2=None, op0=ALU.mult), reads=[apB, cwB, yB], writes=[yB])
        P.op("dve", C("scalar_tensor_tensor", out=y_[:, 256:257], in0=ap_[:, 258:259], scalar=w2, in1=y_[:, 256:257], op0=ALU.mult, op1=ALU.add), reads=[apB, cwB, yB], writes=[yB])
        P.op("pool", C("tensor_tensor", out=y2, in0=y_, in1=cb, op=ALU.mult), reads=[yB, cbB], writes=[y2B])
        o_, oB = ost[cc % 2]
        P.op("pool", C("tensor_tensor", out=o_, in0=y2, in1=cz, op=ALU.mult), reads=[y2B, czB], writes=[oB])
        P.op("sp", C("dma_start", out=S["Y"][2][cc * 128:(cc + 1) * 128, :], in_=o_), reads=[oB], writes=[k.SB["Y"]], dma=True, part=True, defer=True)
    P.barrier()
    A.reset(m0)
    OC, OL = 8, 8 + 256 + 16
    PW = OL + NL + 16
    psc, pscB = A.alloc([128, 8], F32, "psc")
    P.op("sp", C("dma_start", out=psc, in_=I["pscale"][l]), writes=[pscB], dma=True)
    pub = [A.alloc([128, NT], BF16, f"pu{i}") for i in range(2)]
    pzb = [A.alloc([128, NT], BF16, f"pz{i}") for i in range(2)]
    up = [A.alloc([128, PW], F32, f"up{i}") for i in range(2)]
    sa, saB = A.alloc([128, PW], F32, "sa")
    sb_, sbB = A.alloc([128, PW], F32, "sb")
    rcb, rcbB = A.alloc([128, NT], F32, "rcb")
    dT = [A.alloc([128, NT], BF16, f"dT{i}") for i in range(2)]
    pw = [A.alloc([128, 2, 256], BF16, f"pw{i}") for i in range(2)]
    yo = [A.alloc([128, NT], BF16, f"ypo{i}") for i in range(2)]
    for u_, uB in up:
        P.op("pool", C("memset", u_, 0.0), writes=[uB])
    P.op("pool", C("memset", sa, 0.0), writes=[saB])
    P.op("pool", C("memset", sb_, 0.0), writes=[sbB])
    nps = 0
    for g, w in enumerate((2, 4, 8, 16)):
        P.op("sp", C("dma_start", out=rcb, in_=I["rcnt"][g, :].partition_broadcast(128)), writes=[rcbB], dma=True)
        pw_, pwB = pw[g % 2]
        P.op("pool", C("dma_start", out=pw_, in_=I["pool_w"][l, g].rearrange("(kc p) o -> p kc o", p=128)), writes=[pwB], dma=True)
        for kc2 in range(2):
            ct = g * 2 + kc2
            pu_, puB = pub[kc2]
            u_, uB = up[kc2]
            d_, dB = dT[kc2]
            P.op("sp", C("dma_start", out=pu_, in_=S["F"][FROW["pu"] + ct * 128:FROW["pu"] + (ct + 1) * 128, :]), reads=[k.SB["F"]], writes=[puB], dma=True)
            P.op("pool", C("tensor_copy", out=u_[:, OC:OC + NC], in_=pu_[:, 0:NC]), reads=[puB], writes=[uB], part=True)
            P.op("pool", C("tensor_copy", out=u_[:, OL:OL + NL], in_=pu_[:, NC:NT]), reads=[puB], writes=[uB], part=True)
            cur, curB = u_, uB
            m = 1
            pp = [(sa, saB), (sb_, sbB)]
            si = 0
            while m < w:
                nx, nxB = pp[si % 2]
                si += 1
                P.op("dve", C("tensor_tensor", out=nx[:, 0:PW - m], in0=cur[:, 0:PW - m], in1=cur[:, m:PW], op=ALU.add), reads=[curB], writes=[nxB])
                cur, curB = nx, nxB
                m *= 2
            hw_ = w // 2
            for (po, to, n) in ((OC, 0, NC), (OL, NC, NL)):
                P.op("dve", C("tensor_tensor", out=sa[:, po:po + n] if cur is not sa else sb_[:, po:po + n],
                                                                                        in0=cur[:, po - hw_:po - hw_ + n], in1=rcb[:, to:to + n], op=ALU.mult),
                     reads=[curB, rcbB], writes=[saB if cur is not sa else sbB])
                tmpb, tmpB = (sa, saB) if cur is not sa else (sb_, sbB)
                P.op("pool", C("tensor_tensor", out=d_[:, to:to + n], in0=tmpb[:, po:po + n], in1=u_[:, po:po + n], op=ALU.subtract),
                     reads=[tmpB, uB], writes=[dB], part=True)
        for oc in range(2):
            ct = g * 2 + oc
            pz_, pzB = pzb[oc]
            o_, oB = yo[oc]
            P.op("sp", C("dma_start", out=pz_, in_=S["F"][FROW["pz"] + ct * 128:FROW["pz"] + (ct + 1) * 128, :]), reads=[k.SB["F"]], writes=[pzB], dma=True)
            for (t0, n) in BLKS:
                bi = nps % 4
                nps += 1
                ps, psB = k.pb[bi], k.pbB[bi]
                for kc2 in range(2):
                    P.op("pe", C("matmul", ps[:, 0:n], lhsT=pw_[:, kc2, oc * 128:(oc + 1) * 128], rhs=dT[kc2][0][:, t0:t0 + n],
                                                                                          start=(kc2 == 0), stop=(kc2 == 1)), reads=[pwB, dT[kc2][1]], writes=[psB])
                P.op("dve", C("scalar_tensor_tensor", out=o_[:, t0:t0 + n], in0=ps[:, 0:n], scalar=psc[:, ct:ct + 1], in1=pz_[:, t0:t0 + n],
                                                                                                 op0=ALU.mult, op1=ALU.mult), reads=[psB, pscB, pzB], writes=[oB], part=True)
            P.op("sp", C("dma_start", out=S["Y"][3][ct * 128:(ct + 1) * 128, :], in_=o_), reads=[oB], writes=[k.SB["Y"]], dma=True, part=True, defer=True)
    P.barrier()
    A.reset(m0)


def phase_merge_out(k, l):
    P, A, I, S = k.P, k.A, k.I, k.S
    last_layer = (l == DEPTH - 1)
    blocks = BLKS[1:] if last_layer else BLKS
    m0 = A.mark()
    wbr, _ = A.alloc([128, 16, 4 * 8 * 128], BF16, "wbr")
    wbrB = [Buf(f"wbr{ct}") for ct in range(16)]
    Yb, YbB = A.alloc([128, 4, 8, 512], BF16, "Yb")
    gm = [A.alloc([128, 4, 512], BF16, f"gm{i}") for i in range(2)]
    tm = [A.alloc([128, 512], F32, f"tm{i}") for i in range(4)]
    ast = [A.alloc([128, 512], BF16, f"ast{i}") for i in range(2)]
    for ct in range(16):
        P.op("pool", C("dma_start", out=wbr[:, ct, :], in_=I["wbt"][l, ct]), writes=[wbrB[ct]], dma=True)
    Yv = S["Y"].rearrange("b (kc p) t -> p b kc t", p=128)
    Gv = S["F"][FROW["gm"]:FROW["gm"] + 4 * D, :].rearrange("(b c p) t -> p b c t", p=128, c=16)
    nset = 0
    for (t0, n) in blocks:
        P.op("sp", C("dma_start", out=Yb[:, :, :, 0:n], in_=Yv[:, :, :, t0:t0 + n]), reads=[k.SB["Y"]], writes=[YbB], dma=True)
        for ct in range(16):
            g_, gB = gm[ct % 2]
            a_, aB = ast[ct % 2]
            w_ = wbr[:, ct, :].rearrange("p (b k c) -> p b k c", b=4, k=8)
            P.op("sp", C("dma_start", out=g_[:, :, 0:n], in_=Gv[:, :, ct, t0:t0 + n]), reads=[k.SB["F"]], writes=[gB], dma=True)
            base = 4 * (nset % 2)
            nset += 1
            for br in range(4):
                ps, psB = k.pb[base + br], k.pbB[base + br]
                for kc in range(8):
                    P.op("pe", C("matmul", ps[:, 0:n], lhsT=w_[:, br, kc, :], rhs=Yb[:, br, kc, 0:n], start=(kc == 0), stop=(kc == 7)),
                         reads=[wbrB[ct], YbB], writes=[psB])
                P.op("dve", C("tensor_tensor", out=tm[br][0][:, 0:n], in0=ps[:, 0:n], in1=g_[:, br, 0:n], op=ALU.mult),
                     reads=[psB, gB], writes=[tm[br][1]])
            P.op("pool", C("tensor_tensor", out=tm[0][0][:, 0:n], in0=tm[0][0][:, 0:n], in1=tm[1][0][:, 0:n], op=ALU.add), reads=[tm[0][1], tm[1][1]], writes=[tm[0][1]])
            P.op("pool", C("tensor_tensor", out=tm[2][0][:, 0:n], in0=tm[2][0][:, 0:n], in1=tm[3][0][:, 0:n], op=ALU.add), reads=[tm[2][1], tm[3][1]], writes=[tm[2][1]])
            P.op("pool", C("tensor_tensor", out=a_[:, 0:n], in0=tm[0][0][:, 0:n], in1=tm[2][0][:, 0:n], op=ALU.add), reads=[tm[0][1], tm[2][1]], writes=[aB])
            P.op("sp", C("dma_start", out=S["ACC"][ct * 128:(ct + 1) * 128, t0:t0 + n], in_=a_[:, 0:n]), reads=[aB], writes=[k.SB["ACC"]], dma=True, part=True, defer=True)
    P.barrier()
    A.reset(m0)
    wo, _ = A.alloc([128, KC, D], BF16, "wo")
    woB = [Buf(f"wo{cg}") for cg in range(4)]
    accT = [A.alloc([128, 16, 512], BF16, f"accT{i}") for i in range(2)]
    xb = [A.alloc([128, 4, D], F32, f"xblk{i}") for i in range(2)]
    gtb = [A.alloc([128, D], F32, f"gtb{i}") for i in range(2)]
    t5, t5B = A.alloc([128, 512], F32, "t5")
    ss, ssB = A.alloc([128, 4], F32, "fss")
    junk, junkB = A.alloc([128, D], BF16, "junkf")
    wov = I["w_out"][l].rearrange("(kc p) c -> p kc c", p=128)
    for cg in range(4):
        P.op("pool", C("dma_start", out=wo[:, :, cg * 512:(cg + 1) * 512], in_=wov[:, :, cg * 512:(cg + 1) * 512]), writes=[woB[cg]], dma=True)
    if last_layer:
        fgb, fgbB = A.alloc([128, D], F32, "fgb")
        P.op("sp", C("dma_start", out=fgb, in_=I["fgain"].partition_broadcast(128)), writes=[fgbB], dma=True)
    for r in range(2):
        P.op("sp", C("dma_start", out=gtb[r][0], in_=S["modd"][l, r, 2 * D:3 * D].partition_broadcast(128)), reads=[k.SB["modd"]], writes=[gtb[r][1]], dma=True)
    Av = S["ACC"].rearrange("(kc p) t -> p kc t", p=128)
    x1B = [k.SB["X1"]] if l > 0 else []
    nps = 0
    for bi, (t0, n) in enumerate(blocks):
        r = 1 if t0 < 256 else 0
        nti = n // 128
        a_, aB = accT[bi % 2]
        xblk, xblkB = xb[bi % 2]
        P.op("sp", C("dma_start", out=a_[:, :, 0:n], in_=Av[:, :, t0:t0 + n]), reads=[k.SB["ACC"]], writes=[aB], dma=True)
        for ti in range(nti):
            i = t0 // 128 + ti
            P.op("sp", C("dma_start", out=xblk[:, ti, :], in_=src_tile(k, l, i)), reads=x1B, writes=[xblkB], dma=True, part=(ti > 0))
        for ti in range(nti):
            i = t0 // 128 + ti
            for cg in range(4):
                b_ = nps % 8
                nps += 1
                ps, psB = k.pb[b_], k.pbB[b_]
                for kc in range(KC):
                    P.op("pe", C("matmul", ps, lhsT=a_[:, kc, ti * 128:(ti + 1) * 128], rhs=wo[:, kc, cg * 512:(cg + 1) * 512], start=(kc == 0), stop=(kc == KC - 1)),
                         reads=[aB, woB[cg]], writes=[psB])
                P.op("dve", C("tensor_tensor", out=t5, in0=ps, in1=gtb[r][0][:, cg * 512:(cg + 1) * 512], op=ALU.mult), reads=[psB, gtb[r][1]], writes=[t5B])
                P.op("pool", C("tensor_tensor", out=xblk[:, ti, cg * 512:(cg + 1) * 512], in0=xblk[:, ti, cg * 512:(cg + 1) * 512], in1=t5, op=ALU.add),
                     reads=[t5B, xblkB], writes=[xblkB], part=True)
            if not last_layer:
                P.op("sp", C("dma_start", out=S["X1"][i * 128:(i + 1) * 128, :], in_=xblk[:, ti, :]), reads=[xblkB], writes=[k.SB["X1"]], dma=True, part=True, defer=True)
            else:
                P.op("act", C("activation", out=junk, in_=xblk[:, ti, :], func=AF.Square, accum_out=ss[:, ti:ti + 1]), reads=[xblkB], writes=[junkB, ssB])
                P.op("dve", C("tensor_scalar", out=ss[:, ti:ti + 1], in0=ss[:, ti:ti + 1], scalar1=1.0 / D, scalar2=EPS, op0=ALU.mult, op1=ALU.add), reads=[ssB], writes=[ssB])
                P.op("act", C("activation", out=ss[:, ti:ti + 1], in_=ss[:, ti:ti + 1], func=AF.Sqrt), reads=[ssB], writes=[ssB])
                P.op("dve", C("reciprocal", out=ss[:, ti:ti + 1], in_=ss[:, ti:ti + 1]), reads=[ssB], writes=[ssB])
                P.op("dve", C("scalar_tensor_tensor", out=xblk[:, ti, :], in0=xblk[:, ti, :], scalar=ss[:, ti:ti + 1], in1=fgb, op0=ALU.mult, op1=ALU.mult),
                     reads=[xblkB, ssB, fgbB], writes=[xblkB], part=True)
                P.op("sp", C("dma_start", out=S["out"][(i - 2) * 128:(i - 1) * 128, :], in_=xblk[:, ti, :]), reads=[xblkB], writes=[k.SB["out"]], dma=True, part=True, defer=True)
    P.barrier()
    A.reset(m0)


def host_constants():
    C = {}
    C["ident"] = np.eye(128, dtype=np.float32)
    s = np.arange(128)[:, None]
    t = np.arange(128)[None, :]
    C["masks"] = np.stack([(s <= t), (s >= t)]).astype(np.float32)
    freq = (10000.0 ** (-np.arange(32, dtype=np.float32) / 32)).astype(np.float32)
    rope = np.zeros((NTI, 128, 2, 8, 32), np.float32)
    rope[:, :, 0] = 1.0
    for i in range(2, NTI):
        tt = (i - 2) * 128 + np.arange(128)
        row = (tt // 64).astype(np.float32)
        col = (tt % 64).astype(np.float32)
        for half, pos in enumerate((row, col)):
            ang = (pos[:, None] * freq[None, :]).astype(np.float32)
            for h in range(4):
                rope[i, :, 0, h * 2 + half, :] = np.cos(ang)
                rope[i, :, 1, h * 2 + half, :] = np.sin(ang)
    C["rope"] = rope.reshape(NTI, 128, 2, 256)
    sel = np.zeros((64, 8, 128), np.float32)
    for d in range(2):
        for h in range(4):
            sel[d * 32 + h, d * 4 + h, :] = 1.0
    C["sel"] = sel
    rc = np.zeros((4, NT), np.float32)
    for g, w in enumerate((2, 4, 8, 16)):
        for (o, T) in ((0, NC), (NC, NL)):
            tt = np.arange(T)
            lo = np.clip(tt - w // 2, 0, T)
            hi = np.clip(tt + w - w // 2, 0, T)
            rc[g, o:o + T] = 1.0 / (hi - lo).astype(np.float32)
    C["rcnt"] = rc
    return C


_CACHE = {}
NCORES = 4


def make_in_maps(inputs):
    f = lambda a: np.ascontiguousarray(np.asarray(a, dtype=np.float32))
    x, c, ctx, c_ctx = f(inputs["x"]), f(inputs["c"]), f(inputs["ctx"]), f(inputs["c_ctx"])
    C = host_constants()
    shared = dict(C)
    shared["ngcol"] = f(inputs["norm_gain"]).reshape(DEPTH, KC, 128).transpose(0, 2, 1).copy()
    shared["w_mod"] = f(inputs["w_mod"])
    shared["b_mod"] = f(inputs["b_mod"])
    shared["w_in"] = f(inputs["w_in"])
    shared["qg"] = f(inputs["q_norm_gain"])
    shared["kg"] = f(inputs["k_norm_gain"])
    gb = f(inputs["mlstm_gate_bias"])
    gbias = np.zeros((DEPTH, 64, 2), np.float32)
    gbias[:, 0:4, 0] = gb[:, 0]
    gbias[:, 32:36, 0] = gb[:, 2]
    gbias[:, 0:4, 1] = gb[:, 1]
    gbias[:, 32:36, 1] = gb[:, 3]
    shared["gbias"] = gbias
    shared["mgain"] = f(inputs["mlstm_norm_gain"])
    shared["convw"] = f(inputs["conv_w"]).reshape(DEPTH, 3, 8, 128).transpose(0, 3, 2, 1).copy()
    shared["pool_w"] = f(inputs["pool_w"])
    shared["pscale"] = f(inputs["pool_scale"]).reshape(DEPTH, 8, 128).transpose(0, 2, 1).copy()
    wb = f(inputs["w_branch"])
    shared["wbt"] = wb.reshape(DEPTH, 4, 8, 128, 16, 128).transpose(0, 4, 3, 1, 2, 5).reshape(DEPTH, 16, 128, 4 * 8 * 128).copy()
    shared["w_out"] = f(inputs["w_out"])
    shared["fgain"] = f(inputs["final_norm_gain"])
    maps = []
    for core in range(NCORES):
        b = core % 4
        m = dict(shared)
        m["x"] = x[b]
        m["ctx"] = ctx[b]
        c2 = np.stack([c[b], c_ctx])
        m["c2T"] = c2.reshape(2, KC, 128).transpose(2, 1, 0).copy()
        maps.append(m)
    return maps


def kernel(**inputs):
    if "nc" not in _CACHE:
        _CACHE["nc"] = build_program()
    nc = _CACHE["nc"]
    maps = make_in_maps(inputs)
    res = run_bass_kernel_spmd(nc, maps, core_ids=list(range(NCORES)))
    out = np.stack([np.asarray(res.results[b]["out"]) for b in range(4)]).astype(np.float32)
    return out
```

```python
import contextlib
import numpy as np
import concourse.bass as bass
import concourse.mybir as mybir
from concourse.bass_utils import run_bass_kernel_spmd

F32 = mybir.dt.float32
BF16 = mybir.dt.bfloat16
AF = mybir.ActivationFunctionType
ALU = mybir.AluOpType
AX = mybir.AxisListType

NT = 4352
NTI = 34
NL = 4096
NC = 256
D = 2048
KC = 16
EPS = 1e-6
BLKS = [(0, 256)] + [(256 + 512 * j, 512) for j in range(8)]
DEPTH = 2
WCOL = dict(aq=0, ak=1024, av=1280, az=1536, mq=2560, mk=3584, mv=4608, mo=5632, mz=6656, mg=7680,
            cu=7696, cb=8720, cc=9744, cz=10768, pu=11792, pz=12816, gm=13840)
N_IN = 22032
FROW = dict(az=0, mz=1024, cz=2048, pz=3072, mo=4096, gm=5120, mq=13312, mk=14336, cu=15360, cb=16384,
            cc=17408, pu=18432)
NF = 19456
FSEG = [("az", 1024, "silu"), ("mz", 1024, "silu"), ("cz", 1024, "silu"), ("pz", 1024, "silu"),
        ("mo", 1024, "sig"), ("gm", 8192, "sig"),
        ("mq", 1024, "copy"), ("mk", 1024, "copy16"), ("cu", 1024, "copy"), ("cb", 1024, "copy"),
        ("cc", 1024, "copy"), ("pu", 1024, "copy")]


class Tok:
    __slots__ = ("q", "sem", "val", "rec")

    def __init__(self, q, rec):
        self.q = q
        self.rec = rec
        self.sem = None
        self.val = None


class Buf:
    __slots__ = ("name", "w", "r", "excl", "wf")

    def __init__(self, name="", excl=False):
        self.name = name
        self.w = {}
        self.wf = {}
        self.r = {}
        self.excl = excl


class Rec:
    __slots__ = ("fn", "deps", "signal", "tok", "dma", "dsem")

    def __init__(self, fn, deps, dma):
        self.fn = fn
        self.deps = deps
        self.signal = False
        self.tok = None
        self.dma = dma
        self.dsem = None


class Prog:
    NDMA = 8

    def __init__(self, nc):
        self.nc = nc
        self.eng = {"pe": nc.tensor, "act": nc.scalar, "dve": nc.vector, "pool": nc.gpsimd, "sp": nc.sync}
        self.q = {k: [] for k in self.eng}
        self.dma_n = {k: 0 for k in self.eng}
        self.dma_last = {k: [None] * self.NDMA for k in self.eng}
        self.last = {k: None for k in self.eng}
        self.fence = {k: [] for k in self.eng}
        self.deferred = []

    def barrier(self):
        self.flush()
        toks = []
        for q in self.eng:
            if self.last[q] is not None:
                toks.append(self.last[q])
            toks.extend(t for t in self.dma_last[q] if t is not None)
        for q in self.eng:
            self.fence[q] = list(toks)

    def flush(self):
        d, self.deferred = self.deferred, []
        for (q, fn, reads, writes, dma, part) in d:
            self.op(q, fn, reads, writes, dma, part)

    def op(self, q, fn, reads=(), writes=(), dma=False, part=False, defer=False):
        if defer:
            self.deferred.append((q, fn, list(reads), list(writes), dma, part))
            return None
        if self.deferred:
            for (_, _, dr, dw, _, _) in self.deferred:
                if any(b in dr for b in writes) or any(b in dw for b in reads):
                    self.flush()
                    break
        deps = []
        if self.fence[q]:
            deps.extend(self.fence[q])
            self.fence[q] = []
        for b in reads:
            deps.extend(b.w.values())
            if b.excl:
                deps.extend(t for kk, t in b.r.items() if kk[0] != q)
        for b in writes:
            deps.extend(b.r.values())
            if not part:
                deps.extend(b.w.values())
            else:
                deps.extend(b.wf.values())
        rec = Rec(fn, deps, dma)
        tok = Tok(q, rec)
        rec.tok = tok
        if dma:
            n = self.dma_n[q]
            self.dma_n[q] = n + 1
            slot = n % self.NDMA
            prev = self.dma_last[q][slot]
            if prev is not None:
                rec.deps.append(prev)
            self.dma_last[q][slot] = tok
            rec.dsem = slot
        else:
            self.last[q] = tok
        self.q[q].append(rec)
        key = (q, rec.dsem)
        for b in reads:
            b.r[key] = tok
        for b in writes:
            if part:
                b.w[key] = tok
            else:
                b.w = {key: tok}
                b.wf = {key: tok}
                b.r = {}
        return tok

    def emit(self, sems, dsems):
        self.flush()
        for q, recs in self.q.items():
            for rec in recs:
                for t in rec.deps:
                    if t.rec.dma:
                        continue
                    if t.q == q and q == "pe":
                        continue
                    t.rec.signal = True
        for q, recs in self.q.items():
            cnt = 0
            dcnt = [0] * self.NDMA
            for rec in recs:
                if rec.dma:
                    dcnt[rec.dsem] += 16
                    rec.tok.sem = dsems[q][rec.dsem]
                    rec.tok.val = dcnt[rec.dsem]
                elif rec.signal:
                    cnt += 1
                    rec.tok.sem = sems[q]
                    rec.tok.val = cnt
        self.stats = {}
        with self.nc.Block() as block:
            def mk(q):
                def body(e):
                    seen = {}
                    nw = 0
                    for rec in self.q[q]:
                        need = {}
                        for t in rec.deps:
                            if t.sem is None or (t.q == q and q == "pe" and not t.rec.dma):
                                continue
                            k = id(t.sem)
                            if seen.get(k, 0) >= t.val:
                                continue
                            if k not in need or need[k][1] < t.val:
                                need[k] = (t.sem, t.val)
                        for k, (s, v) in need.items():
                            e.wait_ge(s, v)
                            seen[k] = v
                            nw += 1
                        ins = rec.fn(e)
                        if rec.dma:
                            ins.then_inc(rec.tok.sem, 16)
                        elif rec.signal:
                            ins.then_inc(rec.tok.sem, 1)
                    for t in self.dma_last[q]:
                        if t is not None:
                            e.wait_ge(t.sem, t.val)
                    self.stats[q] = (len(self.q[q]), nw)
                return body
            block.tensor(mk("pe"))
            block.scalar(mk("act"))
            block.vector(mk("dve"))
            block.gpsimd(mk("pool"))
            block.sync(mk("sp"))


def C(name, *a, **kw):
    return lambda e: getattr(e, name)(*a, **kw)


class Arena:
    def __init__(self, ap, nbytes):
        self.ap = ap
        self.cap = nbytes
        self.top = 0

    def mark(self):
        return self.top

    def reset(self, m):
        self.top = m

    def alloc(self, shape, dt, name=""):
        esz = 2 if dt == BF16 else 4
        n = int(np.prod(shape[1:]))
        nb = (n * esz + 31) // 32 * 32
        off = self.top
        self.top += nb
        assert self.top <= self.cap, f"arena overflow {self.top} > {self.cap} at {name}"
        v = self.ap[0:shape[0], off // 4: off // 4 + (n * esz + 3) // 4]
        if dt != F32:
            v = v.bitcast(dt)[:, 0:n]
        if len(shape) == 3:
            v = v.rearrange("p (a b) -> p a b", b=shape[2])
        elif len(shape) == 4:
            v = v.rearrange("p (a b c) -> p a b c", b=shape[2], c=shape[3])
        return v, Buf(name)


class K:
    pass


def build_program(debug=(), stop_after=None, nlayers=DEPTH):
    nc = bass.Bass("TRN2", target_bir_lowering=False)
    P = Prog(nc)
    k = K()
    k.nc, k.P = nc, P
    k.stop = stop_after

    def din(name, shape, dt=F32):
        return nc.dram_tensor(name, list(shape), dt, kind="ExternalInput").ap()

    def dscr(name, shape, dt, out=False):
        return nc.dram_tensor(name, list(shape), dt, kind=("ExternalOutput" if (out or name in debug) else "Internal")).ap()

    I = {}
    I["x"] = din("x", [NL, D])
    I["ctx"] = din("ctx", [NC, D])
    I["c2T"] = din("c2T", [128, KC, 2])
    I["ngcol"] = din("ngcol", [DEPTH, 128, KC])
    I["w_mod"] = din("w_mod", [DEPTH, D, 3 * D])
    I["b_mod"] = din("b_mod", [DEPTH, 3 * D])
    I["w_in"] = din("w_in", [DEPTH, D, N_IN])
    I["qg"] = din("qg", [DEPTH, 128])
    I["kg"] = din("kg", [DEPTH, 128])
    I["gbias"] = din("gbias", [DEPTH, 64, 2])
    I["mgain"] = din("mgain", [DEPTH, 1024])
    I["convw"] = din("convw", [DEPTH, 128, 8, 3])
    I["pool_w"] = din("pool_w", [DEPTH, 4, 256, 256])
    I["pscale"] = din("pscale", [DEPTH, 128, 8])
    I["wbt"] = din("wbt", [DEPTH, 16, 128, 4 * 8 * 128])
    I["w_out"] = din("w_out", [DEPTH, D, D])
    I["fgain"] = din("fgain", [D])
    I["ident"] = din("ident", [128, 128])
    I["masks"] = din("masks", [2, 128, 128])
    I["rope"] = din("rope", [NTI, 128, 2, 256])
    I["sel"] = din("sel", [64, 8, 128])
    I["rcnt"] = din("rcnt", [4, NT])
    k.I = I
    S = {}
    S["modd"] = dscr("modd", [DEPTH, 2, 3 * D], F32)
    S["F"] = dscr("Fs", [NF, NT], BF16)
    S["QT"] = dscr("QT", [1024, NT], BF16)
    S["KT"] = dscr("KT", [256, NT], BF16)
    S["Vt"] = dscr("Vt", [NT, 256], BF16)
    S["MKt"] = dscr("MKt", [NT, 1024], BF16)
    S["MVt"] = dscr("MVt", [NT, 1024], BF16)
    S["HF"] = dscr("HF", [2, NT, 1024], BF16)
    S["Y"] = dscr("Y", [4, 1024, NT], BF16)
    S["X1"] = dscr("X1", [NT, D], F32)
    S["ACC"] = dscr("ACC", [D, NT], BF16)
    S["GT"] = dscr("GT", [NTI, 128, 72], F32)
    S["out"] = dscr("out", [NL, D], F32, out=True)
    k.S = S
    k.SB = {n: Buf(n) for n in S}

    with contextlib.ExitStack() as es:
        ARENA_BYTES = 206 * 1024
        arena_t = es.enter_context(nc.sbuf_tensor("arena", [128, ARENA_BYTES // 4], F32))
        A = Arena(arena_t, ARENA_BYTES)
        k.A = A
        k.pb2 = [es.enter_context(nc.psum_tensor(f"pbb{i}", [128, 1024], F32))[:, :] for i in range(4)]
        k.pb = [k.pb2[i // 2][:, (i % 2) * 512:(i % 2 + 1) * 512] for i in range(8)]
        k.pbB = [Buf(f"pb{i}", excl=True) for i in range(8)]
        sems = {q: es.enter_context(nc.semaphore("s_" + q)) for q in P.eng}
        dsems = {q: [es.enter_context(nc.semaphore(f"d_{q}{i}")) for i in range(P.NDMA)] for q in P.eng}

        k.idf, k.idfB = A.alloc([128, 128], F32, "idf")
        k.idb, k.idbB = A.alloc([128, 128], BF16, "idb")
        k.onesb, k.onesbB = A.alloc([128, 128], BF16, "onesb")
        k.modcol, k.modcolB = A.alloc([128, DEPTH, 48, 2], F32, "modcol")
        k.gcol, k.gcolB = A.alloc([128, DEPTH, KC, 2], F32, "gcol")
        P.op("sp", C("dma_start", out=k.idf, in_=I["ident"]), writes=[k.idfB], dma=True)
        P.op("pool", C("dma_start", out=k.idb, in_=I["ident"]), writes=[k.idbB], dma=True)
        P.op("dve", C("memset", k.onesb, 1.0), writes=[k.onesbB])

        phase_mod(k)
        for l in range(nlayers if stop_after != ("mod", 0) else 0):
            last = (l == DEPTH - 1)
            phase_norm_inproj(k, l)
            if stop_after in (("inproj", l), ("norm", l), ("tokmaj", l)):
                break
            phase_attn(k, l)
            if stop_after == ("attn", l):
                break
            phase_mlstm(k, l)
            if stop_after == ("mlstm", l):
                break
            phase_conv_pool(k, l)
            if stop_after == ("convpool", l):
                break
            phase_merge_out(k, l)
        P.emit(sems, dsems)
    return nc


def phase_mod(k):
    P, A, I, S = k.P, k.A, k.I, k.S
    m0 = A.mark()
    cT, cTB = A.alloc([128, KC, 2], F32, "cT")
    scT, scTB = A.alloc([128, KC, 2], F32, "scT")
    ngc, ngcB = A.alloc([128, DEPTH, KC], F32, "ngc")
    wm = [A.alloc([128, 3072], F32, f"wm{i}") for i in range(3)]
    mo, moB = A.alloc([2, 6144], F32, "mo")
    bm, bmB = A.alloc([2, 6144], F32, "bm")
    tmp, tmpB = A.alloc([128, KC, 2], F32, "tmpg")
    P.op("sp", C("dma_start", out=cT, in_=I["c2T"]), writes=[cTB], dma=True)
    P.op("sp", C("dma_start", out=ngc, in_=I["ngcol"].rearrange("l p k -> p l k")), writes=[ngcB], dma=True)
    P.op("act", C("activation", out=scT, in_=cT, func=AF.Silu), reads=[cTB], writes=[scTB])
    n = 0
    for l in range(DEPTH):
        P.op("sp", C("dma_start", out=bm, in_=I["b_mod"][l, :].partition_broadcast(2)), writes=[bmB], dma=True)
        for half in range(2):
            for kc in range(KC):
                w, wB = wm[n % 3]
                n += 1
                P.op("sp", C("dma_start",
                    out=w, in_=I["w_mod"][l, kc * 128:(kc + 1) * 128, half * 3072:(half + 1) * 3072]), writes=[wB], dma=True)
                for j in range(6):
                    P.op("pe", C("matmul", k.pb[j][0:2, :], lhsT=scT[:, kc, :], rhs=w[:, j * 512:(j + 1) * 512],
                                                                start=(kc == 0), stop=(kc == KC - 1)),
                         reads=[scTB, wB], writes=[k.pbB[j]])
            for j in range(6):
                c0 = half * 3072 + j * 512
                P.op("dve", C("tensor_tensor", out=mo[:, c0:c0 + 512], in0=k.pb[j][0:2, :], in1=bm[:, c0:c0 + 512], op=ALU.add),
                     reads=[k.pbB[j], bmB], writes=[moB], part=True)
        P.op("sp", C("dma_start", out=S["modd"][l], in_=mo), reads=[moB], writes=[k.SB["modd"]], dma=True, part=True, defer=True)
        for j in range(48):
            P.op("pe", C("transpose", k.pb[6][:, 2 * j:2 * j + 2], mo[0:2, j * 128:(j + 1) * 128], k.idf[0:2, 0:2]),
                 reads=[moB, k.idfB], writes=[k.pbB[6]])
        P.op("dve", C("tensor_copy", out=k.modcol[:, l], in_=k.pb[6][:, 0:96].rearrange("p (j r) -> p j r", r=2)),
             reads=[k.pbB[6]], writes=[k.modcolB], part=True)
        P.op("dve", C("tensor_scalar", out=tmp, in0=k.modcol[:, l, 16:32, :], scalar1=1.0, scalar2=None, op0=ALU.add),
             reads=[k.modcolB], writes=[tmpB])
        P.op("dve", C("tensor_tensor", out=k.gcol[:, l], in0=tmp, in1=ngc[:, l, :].unsqueeze(2).to_broadcast([128, KC, 2]), op=ALU.mult),
             reads=[tmpB, ngcB], writes=[k.gcolB], part=True)
    P.barrier()
    A.reset(m0)


def src_tile(k, l, i):
    if l == 0:
        if i < 2:
            return k.I["ctx"][i * 128:(i + 1) * 128, :]
        return k.I["x"][(i - 2) * 128:(i - 1) * 128, :]
    return k.S["X1"][i * 128:(i + 1) * 128, :]


def phase_norm_inproj(k, l):
    P, A, I, S = k.P, k.A, k.I, k.S
    m0 = A.mark()
    hT, hTB = A.alloc([128, KC, NT], BF16, "hT")
    m1 = A.mark()
    xt = [A.alloc([128, D], F32, f"xt{i}") for i in range(3)]
    xn = [A.alloc([128, D], BF16, f"xn{i}") for i in range(3)]
    junk, junkB = A.alloc([128, D], BF16, "junk")
    ss, ssB = A.alloc([128, NTI], F32, "ss")
    rs, rsB = A.alloc([128, NTI], F32, "rs")
    x1B = [k.SB["X1"]] if l > 0 else []
    for i in range(NTI):
        r = 1 if i < 2 else 0
        x_, xB = xt[i % 3]
        n_, nB = xn[i % 3]
        P.op("sp", C("dma_start", out=x_, in_=src_tile(k, l, i)), reads=x1B, writes=[xB], dma=True)
        P.op("act", C("activation", out=junk, in_=x_, func=AF.Square, accum_out=ss[:, i:i + 1]),
             reads=[xB], writes=[junkB, ssB])
        P.op("dve", C("tensor_scalar", out=rs[:, i:i + 1], in0=ss[:, i:i + 1], scalar1=1.0 / D, scalar2=EPS, op0=ALU.mult, op1=ALU.add),
             reads=[ssB], writes=[rsB])
        P.op("act", C("activation", out=rs[:, i:i + 1], in_=rs[:, i:i + 1], func=AF.Sqrt), reads=[rsB], writes=[rsB])
        P.op("dve", C("reciprocal", out=rs[:, i:i + 1], in_=rs[:, i:i + 1]), reads=[rsB], writes=[rsB])
        P.op("dve", C("tensor_scalar", out=n_, in0=x_, scalar1=rs[:, i:i + 1], scalar2=None, op0=ALU.mult),
             reads=[xB, rsB], writes=[nB])
        import os
        KD = os.environ.get("KDBG", "")
        for half in range(2):
            if KD == "A":
                break
            bi = (2 * i + half) % 4
            pbf = k.pb[bi][:, :].bitcast(BF16)
            for j in range(8):
                kc = half * 8 + j
                P.op("pe", C("transpose", pbf[:, j * 128:(j + 1) * 128], n_[:, kc * 128:(kc + 1) * 128], k.idb),
                     reads=[nB, k.idbB], writes=[k.pbB[bi]])
            for j in range(8):
                kc = half * 8 + j
                if half == 0 or KD == "B":
                    P.op("dve", C("tensor_scalar",
                        out=hT[:, kc, i * 128:(i + 1) * 128], in0=pbf[:, j * 128:(j + 1) * 128],
                        scalar1=k.gcol[:, l, kc, r:r + 1], scalar2=k.modcol[:, l, kc, r:r + 1], op0=ALU.mult, op1=ALU.add),
                        reads=[k.pbB[bi], k.gcolB, k.modcolB], writes=[hTB], part=True)
                else:
                    P.op("act", C("activation",
                        out=hT[:, kc, i * 128:(i + 1) * 128], in_=pbf[:, j * 128:(j + 1) * 128], func=AF.Identity,
                        bias=k.modcol[:, l, kc, r:r + 1], scale=k.gcol[:, l, kc, r:r + 1]),
                        reads=[k.pbB[bi], k.gcolB, k.modcolB], writes=[hTB], part=True)
    P.barrier()
    A.reset(m1)
    if k.stop == ("norm", l):
        A.reset(m0)
        return
    W = [A.alloc([128, KC, 512], BF16, f"W{i}") for i in range(2)]
    m2 = A.mark()
    TS = []
    for ts_i in range(2):
        TS.append(dict(t1=A.alloc([128, 512], F32, f"t1{ts_i}"), t2=A.alloc([128, 512], F32, f"t2{ts_i}"), t3=A.alloc([128, 512], F32, f"t3{ts_i}"),
                       ta=A.alloc([128, 512], F32, f"ta{ts_i}"), tb=A.alloc([128, 512], F32, f"tb{ts_i}"), qf=A.alloc([128, 512], BF16, f"qf{ts_i}"),
                       ssq=A.alloc([128, 4], F32, f"ssq{ts_i}")))
    gts = [A.alloc([128, 2, 36], F32, f"gts{i}") for i in range(2)]
    for g_, gB_ in gts:
        P.op("pool", C("memset", g_, 0.0), writes=[gB_])
    rop = [A.alloc([128, 2, 8, 32], F32, f"rope{i}") for i in range(2)]
    qgb, qgbB = A.alloc([128, 128], F32, "qgb")
    kgb, kgbB = A.alloc([128, 128], F32, "kgb")
    qst = [A.alloc([128, 4, 256], BF16, f"qst{i}") for i in range(2)]
    vst = [A.alloc([128, 512], BF16, f"vst{i}") for i in range(2)]
    P.op("sp", C("dma_start", out=qgb, in_=I["qg"][l, :].partition_broadcast(128)), writes=[qgbB], dma=True)
    P.op("sp", C("dma_start", out=kgb, in_=I["kg"][l, :].partition_broadcast(128)), writes=[kgbB], dma=True)
    P.op("dve", C("tensor_scalar", out=qgb, in0=qgb, scalar1=float(128 ** -0.5), scalar2=None, op0=ALU.mult), reads=[qgbB], writes=[qgbB])
    wsrc = I["w_in"][l].rearrange("(kc p) c -> p kc c", p=128)
    groups = [("q", WCOL["aq"], 0), ("q", WCOL["aq"] + 512, 1), ("kv", WCOL["ak"], 0),
              ("mk", WCOL["mk"], 0), ("mk", WCOL["mk"] + 512, 1), ("mv", WCOL["mv"], 0), ("mv", WCOL["mv"] + 512, 1),
              ("mg", WCOL["mg"], 0)]
    nload = [0]

    def load_w(c0, ncol):
        w, wB = W[nload[0] % 2]
        nload[0] += 1
        P.op("pool", C("dma_start", out=w[:, :, 0:ncol], in_=wsrc[:, :, c0:c0 + ncol]), writes=[wB], dma=True)
        return w, wB

    def qk_post(ps, psB, nh, gb, gbB, rp, rpB, out, outB, T):
        Wd = nh * 128
        g = nh * 2
        (t1, t1B), (t2, t2B), (t3, t3B), (ta, taB), (tb, tbB), (ssq, ssqB) = T["t1"], T["t2"], T["t3"], T["ta"], T["tb"], T["ssq"]
        P.op("act", C("activation", out=t1[:, 0:Wd], in_=ps[:, 0:Wd], func=AF.Square), reads=[psB], writes=[t1B])
        P.op("dve", C("tensor_reduce", out=ssq[:, 0:nh], in_=t1[:, 0:Wd].rearrange("p (h d) -> p h d", d=128), axis=AX.X, op=ALU.add),
             reads=[t1B], writes=[ssqB])
        P.op("dve", C("tensor_scalar", out=ssq[:, 0:nh], in0=ssq[:, 0:nh], scalar1=1.0 / 128, scalar2=EPS, op0=ALU.mult, op1=ALU.add),
             reads=[ssqB], writes=[ssqB])
        P.op("act", C("activation", out=ssq[:, 0:nh], in_=ssq[:, 0:nh], func=AF.Sqrt), reads=[ssqB], writes=[ssqB])
        P.op("dve", C("reciprocal", out=ssq[:, 0:nh], in_=ssq[:, 0:nh]), reads=[ssqB], writes=[ssqB])
        P.op("dve", C("tensor_tensor", out=t2[:, 0:Wd].rearrange("p (h d) -> p h d", d=128), in0=ps[:, 0:Wd].rearrange("p (h d) -> p h d", d=128),
                                              in1=ssq[:, 0:nh].unsqueeze(2).to_broadcast([128, nh, 128]), op=ALU.mult),
             reads=[psB, ssqB], writes=[t2B])
        P.op("pool", C("tensor_tensor", out=t3[:, 0:Wd].rearrange("p (h d) -> p h d", d=128), in0=t2[:, 0:Wd].rearrange("p (h d) -> p h d", d=128),
                                               in1=gb.unsqueeze(1).to_broadcast([128, nh, 128]), op=ALU.mult),
             reads=[t2B, gbB], writes=[t3B])
        t3v = t3[:, 0:Wd].rearrange("p (g x j) -> p g x j", x=2, j=32)
        tav = ta[:, 0:Wd].rearrange("p (g x j) -> p g x j", x=2, j=32)
        tbv = tb[:, 0:Wd].rearrange("p (g x j) -> p g x j", x=2, j=32)
        ov = out.rearrange("p (g x j) -> p g x j", x=2, j=32)
        P.op("pool", C("tensor_tensor", out=tav, in0=t3v, in1=rp[:, 0, 0:g, :].unsqueeze(2).to_broadcast([128, g, 2, 32]), op=ALU.mult),
             reads=[t3B, rpB], writes=[taB])
        P.op("pool", C("tensor_tensor", out=tbv[:, :, 0, :], in0=t3v[:, :, 1, :], in1=rp[:, 1, 0:g, :], op=ALU.mult),
             reads=[t3B, rpB], writes=[tbB], part=True)
        P.op("pool", C("tensor_tensor", out=tbv[:, :, 1, :], in0=t3v[:, :, 0, :], in1=rp[:, 1, 0:g, :], op=ALU.mult),
             reads=[t3B, rpB], writes=[tbB], part=True)
        P.op("dve", C("tensor_tensor", out=ov[:, :, 0, :], in0=tav[:, :, 0, :], in1=tbv[:, :, 0, :], op=ALU.subtract),
             reads=[taB, tbB], writes=[outB], part=True)
        P.op("dve", C("tensor_tensor", out=ov[:, :, 1, :], in0=tav[:, :, 1, :], in1=tbv[:, :, 1, :], op=ALU.add),
             reads=[taB, tbB], writes=[outB], part=True)

    nps = [0]
    cur = load_w(groups[0][1], 512)
    for gi, (kind, c0, sub) in enumerate(groups):
        w, wB = cur
        if gi + 1 < len(groups):
            nk, nc0, _ = groups[gi + 1]
            cur = load_w(nc0, 16 if nk == "mg" else 512)
        ncol = 16 if kind == "mg" else 512
        pending = []
        for i in range(NTI + 1):
            if i == NTI:
                for f_ in pending:
                    f_()
                pending[:] = []
                break
            bi = nps[0] % 4
            nps[0] += 1
            ps, psB = k.pb[bi], k.pbB[bi]
            if kind in ("q", "kv"):
                rp, rpB = rop[i % 2]
                P.op("sp", C("dma_start", out=rp, in_=I["rope"][i].rearrange("p a (g j) -> p a g j", j=32)), writes=[rpB], dma=True)
            for kc in range(KC):
                P.op("pe", C("matmul", ps[:, 0:ncol], lhsT=hT[:, kc, i * 128:(i + 1) * 128], rhs=w[:, kc, 0:ncol],
                                                                               start=(kc == 0), stop=(kc == KC - 1)),
                     reads=[hTB, wB], writes=[psB])
            if kind == "q":
                T = TS[i % 2]
                qf, qfB = T["qf"]
                qk_post(ps, psB, 4, qgb, qgbB, rp, rpB, qf, qfB, T)
                def fin_q(i=i, qf=qf, qfB=qfB, sub=sub):
                    grp = i // 2
                    pos = i % 2
                    st, stB = qst[grp % 2]
                    tbi = 4 + (i % 2)
                    pbf = k.pb[tbi][:, :].bitcast(BF16)
                    for h in range(4):
                        P.op("pe", C("transpose", pbf[:, h * 128:(h + 1) * 128], qf[:, h * 128:(h + 1) * 128], k.idb),
                             reads=[qfB, k.idbB], writes=[k.pbB[tbi]])
                    P.op("act", C("activation", out=st[:, :, pos * 128:(pos + 1) * 128],
                                  in_=pbf[:, 0:512].rearrange("p (h t) -> p h t", t=128), func=AF.Copy),
                         reads=[k.pbB[tbi]], writes=[stB], part=True)
                    if pos == 1:
                        t0 = grp * 256
                        P.op("sp", C("dma_start", out=S["QT"].rearrange("(h d) t -> d h t", d=128)[:, sub * 4:(sub + 1) * 4, t0:t0 + 256], in_=st[:, :, 0:256]),
                             reads=[stB], writes=[k.SB["QT"]], dma=True, part=True, defer=True)
                for f_ in pending:
                    f_()
                pending[:] = [fin_q]
            elif kind == "kv":
                T = TS[i % 2]
                qf, qfB = T["qf"]
                qk_post(ps, psB, 2, kgb, kgbB, rp, rpB, qf[:, 0:256], qfB, T)
                def fin_k(i=i, qf=qf, qfB=qfB):
                    grp = i // 2
                    pos = i % 2
                    st, stB = qst[grp % 2]
                    tbi = 4 + (i % 2)
                    pbf = k.pb[tbi][:, :].bitcast(BF16)
                    for h in range(2):
                        P.op("pe", C("transpose", pbf[:, h * 128:(h + 1) * 128], qf[:, h * 128:(h + 1) * 128], k.idb),
                             reads=[qfB, k.idbB], writes=[k.pbB[tbi]])
                    P.op("act", C("activation", out=st[:, 0:2, pos * 128:(pos + 1) * 128],
                                  in_=pbf[:, 0:256].rearrange("p (h t) -> p h t", t=128), func=AF.Copy),
                         reads=[k.pbB[tbi]], writes=[stB], part=True)
                    if pos == 1:
                        t0 = grp * 256
                        P.op("sp", C("dma_start", out=S["KT"].rearrange("(h d) t -> d h t", d=128)[:, :, t0:t0 + 256], in_=st[:, 0:2, 0:256]),
                             reads=[stB], writes=[k.SB["KT"]], dma=True, part=True, defer=True)
                for f_ in pending:
                    f_()
                pending[:] = [fin_k]
                v_, vB = vst[i % 2]
                P.op("act", C("activation", out=v_[:, 0:256], in_=ps[:, 256:512], func=AF.Copy), reads=[psB], writes=[vB])
                P.op("sp", C("dma_start", out=S["Vt"][i * 128:(i + 1) * 128, :], in_=v_[:, 0:256]),
                     reads=[vB], writes=[k.SB["Vt"]], dma=True, part=True, defer=True)
            elif kind in ("mk", "mv"):
                v_, vB = vst[i % 2]
                sc = 0.0625 if kind == "mk" else 1.0
                P.op("act", C("activation", out=v_, in_=ps, func=AF.Copy, scale=sc), reads=[psB], writes=[vB])
                dst = S["MKt"] if kind == "mk" else S["MVt"]
                dB = k.SB["MKt"] if kind == "mk" else k.SB["MVt"]
                P.op("sp", C("dma_start", out=dst[i * 128:(i + 1) * 128, sub * 512:(sub + 1) * 512], in_=v_),
                     reads=[vB], writes=[dB], dma=True, part=True, defer=True)
            else:
                g_, gB_ = gts[i % 2]
                for gi in range(4):
                    P.op("dve", C("tensor_copy", out=g_[:, gi % 2, (gi // 2) * 32:(gi // 2) * 32 + 4], in_=ps[:, gi * 4:gi * 4 + 4]),
                         reads=[psB], writes=[gB_], part=(gi > 0))
                P.op("sp", C("dma_start", out=S["GT"][i], in_=g_.rearrange("p a b -> p (a b)")), reads=[gB_], writes=[k.SB["GT"]], dma=True, part=True, defer=True)
    P.barrier()
    A.reset(m2)
    if k.stop == ("tokmaj", l):
        A.reset(m0)
        return
    stg = [A.alloc([128, NT], BF16, f"stg{i}") for i in range(2)]
    for st_, stB_ in stg:
        P.op("pool", C("memset", st_, 0.0), writes=[stB_])
    glist = []
    for name, ncols, fn in FSEG:
        for g in range(ncols // 512):
            glist.append((name, WCOL[name] + g * 512, FROW[name] + g * 512, fn))
    cur = load_w(glist[0][1], 512)
    nst = 0
    for gi, (name, c0, r0, fn) in enumerate(glist):
        w, wB = cur
        if gi + 1 < len(glist):
            cur = load_w(glist[gi + 1][1], 512)
        for j in range(4):
            st, stB = stg[nst % 2]
            nst += 1
            fblks = BLKS[1:] if (l == DEPTH - 1 and name != "mk") else BLKS
            for (t0, n) in fblks:
                bi = nps[0] % 4
                nps[0] += 1
                ps, psB = k.pb[bi], k.pbB[bi]
                for kc in range(KC):
                    P.op("pe", C("matmul", ps[:, 0:n], lhsT=w[:, kc, j * 128:(j + 1) * 128], rhs=hT[:, kc, t0:t0 + n],
                                                                                    start=(kc == 0), stop=(kc == KC - 1)),
                         reads=[hTB, wB], writes=[psB])
                if fn == "silu":
                    P.op("act", C("activation", out=st[:, t0:t0 + n], in_=ps[:, 0:n], func=AF.Silu),
                         reads=[psB], writes=[stB], part=True)
                elif fn == "sig":
                    P.op("act", C("activation", out=st[:, t0:t0 + n], in_=ps[:, 0:n], func=AF.Sigmoid),
                         reads=[psB], writes=[stB], part=True)
                elif fn == "copy16":
                    P.op("dve", C("tensor_scalar", out=st[:, t0:t0 + n], in0=ps[:, 0:n], scalar1=0.0625, scalar2=None, op0=ALU.mult),
                         reads=[psB], writes=[stB], part=True)
                else:
                    P.op("dve", C("tensor_copy", out=st[:, t0:t0 + n], in_=ps[:, 0:n]),
                         reads=[psB], writes=[stB], part=True)
            P.op("sp", C("dma_start", out=S["F"][r0 + j * 128:r0 + (j + 1) * 128, :], in_=st),
                 reads=[stB], writes=[k.SB["F"]], dma=True, part=True, defer=True)
    P.barrier()
    A.reset(m0)


def phase_attn(k, l):
    P, A, I, S = k.P, k.A, k.I, k.S
    m0 = A.mark()
    KTs, KTB = A.alloc([128, 2, NT], BF16, "KTs")
    Vs, VB = A.alloc([128, NTI, 256], BF16, "Vs")
    Qb = [A.alloc([128, 8, 512], BF16, f"Qb{i}") for i in range(2)]
    AZ = [A.alloc([128, 8, 512], BF16, f"AZ{i}") for i in range(2)]
    PT = [A.alloc([128, 2, 512], BF16, f"PT{i}") for i in range(3)]
    rec, recB = A.alloc([128, 512], F32, "rec")
    t4, t4B = A.alloc([128, 512], F32, "t4")
    ost = [A.alloc([128, 8, 512], BF16, f"ost{i}") for i in range(2)]
    P.op("sp", C("dma_start", out=KTs, in_=S["KT"].rearrange("(h d) t -> d h t", d=128)), reads=[k.SB["KT"]], writes=[KTB], dma=True)
    P.op("sp", C("dma_start", out=Vs, in_=S["Vt"].rearrange("(i p) c -> p i c", p=128)), reads=[k.SB["Vt"]], writes=[VB], dma=True)
    blocks = []
    if l < DEPTH - 1:
        blocks.append((0, 256, [0, 1]))
    for j in range(8):
        blocks.append((256 + 512 * j, 512, list(range(NTI))))
    QTv = S["QT"].rearrange("(h d) t -> d h t", d=128)
    AZv = S["F"][FROW["az"]:FROW["az"] + 1024, :].rearrange("(h d) t -> d h t", d=128)
    Yv = S["Y"][0].rearrange("(h d) t -> d h t", d=128)
    ns = [0]
    npt = [0]
    for bi, (t0, n, keys) in enumerate(blocks):
        q_, qB = Qb[bi % 2]
        az_, azB = AZ[bi % 2]
        o_, oB = ost[bi % 2]
        P.op("sp", C("dma_start", out=q_[:, :, 0:n], in_=QTv[:, :, t0:t0 + n]), reads=[k.SB["QT"]], writes=[qB], dma=True)
        P.op("sp", C("dma_start", out=az_[:, :, 0:n], in_=AZv[:, :, t0:t0 + n]), reads=[k.SB["F"]], writes=[azB], dma=True)
        nk = len(keys)
        npair = nk // 2
        for h in range(8):
            kv = h // 4
            psO, psOB = k.pb[4 + (h % 2)], k.pbB[4 + (h % 2)]
            psD, psDB = k.pb[6 + (h % 2)], k.pbB[6 + (h % 2)]
            spair = []

            def emitS(pi):
                pr = ns[0] % 2
                ns[0] += 1
                spair.append(pr)
                for j in range(2):
                    kt = keys[2 * pi + j]
                    bb = 2 * pr + j
                    P.op("pe", C("matmul", k.pb[bb][:, 0:n], lhsT=KTs[:, kv, kt * 128:(kt + 1) * 128], rhs=q_[:, h, 0:n], start=True, stop=True),
                         reads=[KTB, qB], writes=[k.pbB[bb]])
            emitS(0)
            for pi in range(npair):
                if pi + 1 < npair:
                    emitS(pi + 1)
                pr = spair[pi]
                p_, pB = PT[npt[0] % 3]
                npt[0] += 1
                sv = k.pb2[pr].rearrange("p (b c) -> p b c", b=2)[:, :, 0:n]
                P.op("act", C("activation", out=p_[:, :, 0:n], in_=sv, func=AF.Exp), reads=[k.pbB[2 * pr], k.pbB[2 * pr + 1]], writes=[pB])
                for j in range(2):
                    kt = keys[2 * pi + j]
                    idx = 2 * pi + j
                    P.op("pe", C("matmul", psO[:, 0:n], lhsT=Vs[:, kt, kv * 128:(kv + 1) * 128], rhs=p_[:, j, 0:n],
                                 start=(idx == 0), stop=(idx == nk - 1)), reads=[VB, pB], writes=[psOB])
                for j in range(2):
                    idx = 2 * pi + j
                    P.op("pe", C("matmul", psD[:, 0:n], lhsT=k.onesb, rhs=p_[:, j, 0:n], start=(idx == 0), stop=(idx == nk - 1)),
                         reads=[k.onesbB, pB], writes=[psDB])
            P.op("dve", C("reciprocal", out=rec[:, 0:n], in_=psD[:, 0:n]), reads=[psDB], writes=[recB])
            P.op("dve", C("tensor_tensor", out=t4[:, 0:n], in0=psO[:, 0:n], in1=rec[:, 0:n], op=ALU.mult), reads=[psOB, recB], writes=[t4B])
            P.op("pool", C("tensor_tensor", out=o_[:, h, 0:n], in0=t4[:, 0:n], in1=az_[:, h, 0:n], op=ALU.mult),
                 reads=[t4B, azB], writes=[oB], part=True)
        P.op("sp", C("dma_start", out=Yv[:, :, t0:t0 + n], in_=o_[:, :, 0:n]), reads=[oB], writes=[k.SB["Y"]], dma=True, part=True, defer=True)
    P.barrier()
    A.reset(m0)


def phase_mlstm(k, l):
    P, A, I, S = k.P, k.A, k.I, k.S
    last_layer = (l == DEPTH - 1)
    m0 = A.mark()
    WC, WCB = A.alloc([128, NTI, 16], F32, "WC")
    DECB, DECBB = A.alloc([128, 8, NTI], F32, "DECB")
    m1 = A.mark()
    GI, GIB = A.alloc([64, NT], F32, "GI")
    GF, GFB = A.alloc([64, NT], F32, "GF")
    ONE, ONEB = A.alloc([64, NT], F32, "ONE")
    BP, BPB = A.alloc([64, NT], F32, "BP")
    AP_, APB = A.alloc([64, NT], F32, "APr")
    MM, MMB = A.alloc([64, NT], F32, "MM")
    M2, M2B = A.alloc([64, NT], F32, "M2")
    WR, WRB = A.alloc([64, NT], F32, "WR")
    CL, CLB = A.alloc([64, NT], F32, "CL")
    gb, gbB = A.alloc([64, 2], F32, "gb")
    Gtok, GtokB = A.alloc([128, NTI, 2, 36], F32, "Gtok")
    P.op("sp", C("dma_start", out=Gtok.rearrange("p i a b -> p i (a b)"), in_=S["GT"].rearrange("i p c -> p i c")), reads=[k.SB["GT"]], writes=[GtokB], dma=True)
    dec, decB = A.alloc([64, NTI], F32, "dec")
    sel, selB = A.alloc([64, 8, 128], F32, "sel")
    P.op("sp", C("dma_start", out=gb, in_=I["gbias"][l]), writes=[gbB], dma=True)
    P.op("sp", C("dma_start", out=sel, in_=I["sel"]), writes=[selB], dma=True)
    for t_, tB in ((GI, GIB), (GF, GFB), (dec, decB)):
        P.op("pool", C("memset", t_, 0.0), writes=[tB])
    P.op("pool", C("memset", ONE, 1.0), writes=[ONEB])
    R = (slice(0, 4), slice(32, 36))
    nb = 0
    for (t0, n) in BLKS:
        bt0 = (t0 - 256) if t0 >= 256 else 4096
        for gf, (dst, dstB) in enumerate(((GI, GIB), (GF, GFB))):
            bi = nb % 4
            nb += 1
            ps, psB = k.pb[bi], k.pbB[bi]
            for j in range(n // 128):
                i = t0 // 128 + j
                P.op("pe", C("transpose", ps[0:36, j * 128:(j + 1) * 128], Gtok[:, i, gf, :], k.idf),
                     reads=[GtokB, k.idfB], writes=[psB])
            P.op("act", C("activation", out=dst[0:4, t0:t0 + n], in_=ps[0:4, 0:n], func=AF.Identity,
                                                                               bias=gb[0:4, gf:gf + 1], scale=1.0),
                 reads=[psB, gbB], writes=[dstB], part=True)
            P.op("act", C("activation", out=dst[32:36, bt0:bt0 + n], in_=ps[32:36, 0:n], func=AF.Identity,
                                                                                 bias=gb[32:36, gf:gf + 1], scale=1.0),
                 reads=[psB, gbB], writes=[dstB], part=True)
    P.op("act", C("activation", out=GF[0:36, :], in_=GF[0:36, :], func=AF.Exp, scale=-1.0), reads=[GFB], writes=[GFB])
    P.op("act", C("activation", out=GF[0:36, :], in_=GF[0:36, :], func=AF.Ln, bias=1.0, scale=1.0), reads=[GFB], writes=[GFB])
    P.op("dve", C("tensor_tensor_scan", out=BP[0:36, :], data0=ONE[0:36, :], data1=GF[0:36, :], initial=0.0, op0=ALU.mult, op1=ALU.add),
         reads=[ONEB, GFB], writes=[BPB])
    P.op("dve", C("tensor_scalar", out=M2[32:36, :], in0=BP[32:36, :], scalar1=BP[32:36, NT - 1:NT], scalar2=-1.0, op0=ALU.subtract, op1=ALU.mult),
         reads=[BPB], writes=[M2B])
    P.op("dve", C("tensor_tensor", out=BP[32:36, :], in0=M2[32:36, :], in1=GF[32:36, :], op=ALU.add), reads=[M2B, GFB], writes=[BPB])
    P.op("dve", C("tensor_tensor", out=AP_[0:36, :], in0=GI[0:36, :], in1=BP[0:36, :], op=ALU.add), reads=[GIB, BPB], writes=[APB])
    P.op("dve", C("tensor_tensor_scan", out=MM[0:4, :], data0=ONE[0:4, :], data1=AP_[0:4, :], initial=-1e30, op0=ALU.mult, op1=ALU.max),
         reads=[ONEB, APB], writes=[MMB], part=True)
    src, srcB = AP_, APB
    bufs = [(M2, M2B), (MM, MMB)]
    sh = 1
    step = 0
    while sh < NT:
        dst, dstB = bufs[step % 2]
        P.op("dve", C("tensor_tensor", out=dst[32:36, 0:NT - sh], in0=src[32:36, 0:NT - sh], in1=src[32:36, sh:NT], op=ALU.max),
             reads=[srcB], writes=[dstB], part=True)
        P.op("pool", C("tensor_copy", out=dst[32:36, NT - sh:NT], in_=src[32:36, NT - sh:NT]),
             reads=[srcB], writes=[dstB], part=True)
        src, srcB = dst, dstB
        sh *= 2
        step += 1
    if src is not MM:
        P.op("dve", C("tensor_copy", out=MM[32:36, :], in_=src[32:36, :]), reads=[srcB], writes=[MMB], part=True)

    def v3(t_, r):
        return t_[r, :].rearrange("p (c t) -> p c t", t=128)
    for d, r in enumerate(R):
        li = 127 if d == 0 else 0
        mlast = v3(MM, r)[:, :, li:li + 1].to_broadcast([4, NTI, 128])
        P.op("dve", C("tensor_tensor", out=v3(WR, r), in0=v3(AP_, r), in1=mlast, op=ALU.subtract), reads=[APB, MMB], writes=[WRB], part=True)
        P.op("act", C("activation", out=WR[r, :], in_=WR[r, :], func=AF.Exp), reads=[WRB], writes=[WRB], part=True)
        P.op("dve", C("tensor_tensor", out=v3(CL, r), in0=v3(BP, r), in1=mlast, op=ALU.subtract), reads=[BPB, MMB], writes=[CLB], part=True)
        P.op("act", C("activation", out=CL[r, :], in_=CL[r, :], func=AF.Exp), reads=[CLB], writes=[CLB], part=True)
        ml2 = v3(MM, r)[:, :, li]
        if d == 0:
            P.op("dve", C("tensor_tensor", out=dec[r, 1:NTI], in0=ml2[:, 0:NTI - 1], in1=ml2[:, 1:NTI], op=ALU.subtract),
                 reads=[MMB], writes=[decB], part=True)
            P.op("act", C("activation", out=dec[r, 1:NTI], in_=dec[r, 1:NTI], func=AF.Exp), reads=[decB], writes=[decB], part=True)
        else:
            P.op("dve", C("tensor_tensor", out=dec[r, 0:NTI - 1], in0=ml2[:, 1:NTI], in1=ml2[:, 0:NTI - 1], op=ALU.subtract),
                 reads=[MMB], writes=[decB], part=True)
            P.op("act", C("activation", out=dec[r, 0:NTI - 1], in_=dec[r, 0:NTI - 1], func=AF.Exp), reads=[decB], writes=[decB], part=True)
    for q in range(8):
        P.op("pe", C("matmul", k.pb[0][:, q * NTI:(q + 1) * NTI], lhsT=sel[0:36, q, :], rhs=dec[0:36, :], start=True, stop=True),
             reads=[selB, decB], writes=[k.pbB[0]])
    P.op("dve", C("tensor_copy", out=DECB, in_=k.pb[0][:, 0:8 * NTI].rearrange("p (q c) -> p q c", c=NTI)), reads=[k.pbB[0]], writes=[DECBB])
    for half in range(2):
        tiles = list(range(half * 17, half * 17 + 17))
        ps, psB = k.pb[1 + half], k.pbB[1 + half]
        for jj, i in enumerate(tiles):
            fc = i * 128
            bc = (i - 2) * 128 if i >= 2 else 4096 + i * 128
            for qq, (src, srcB, r, c0) in enumerate(((WR, WRB, R[0], fc), (WR, WRB, R[1], bc), (CL, CLB, R[0], fc), (CL, CLB, R[1], bc))):
                P.op("pe", C("transpose", ps[:, jj * 16 + qq * 4: jj * 16 + qq * 4 + 4], src[r, c0:c0 + 128], k.idf[r, r]),
                     reads=[srcB, k.idfB], writes=[psB])
        P.op("dve", C("tensor_copy", out=WC[:, half * 17:half * 17 + 17, :], in_=ps[:, 0:17 * 16].rearrange("p (i q) -> p i q", q=16)),
             reads=[psB], writes=[WCB], part=True)
    P.barrier()
    A.reset(m1)
    msk, mskB = A.alloc([128, 2, 128], F32, "msk")
    mgb, mgbB = A.alloc([128, 1024], F32, "mgb")
    P.op("sp", C("dma_start", out=msk, in_=I["masks"].rearrange("m s t -> s m t")), writes=[mskB], dma=True)
    P.op("sp", C("dma_start", out=mgb, in_=I["mgain"][l, :].partition_broadcast(128)), writes=[mgbB], dma=True)
    Fq = S["F"][FROW["mq"]:FROW["mq"] + 1024, :].rearrange("(a p) t -> p a t", p=128)
    Fk = S["F"][FROW["mk"]:FROW["mk"] + 1024, :].rearrange("(a p) t -> p a t", p=128)
    Fo = S["F"][FROW["mo"]:FROW["mo"] + 1024, :].rearrange("(a p) t -> p a t", p=128)
    Fz = S["F"][FROW["mz"]:FROW["mz"] + 1024, :].rearrange("(a p) t -> p a t", p=128)
    Yb = S["Y"][1].rearrange("(a p) t -> p a t", p=128)
    HB_ = [[Buf(f"H{d}_{i}") for i in range(NTI)] for d in range(2)]
    BD = []
    for d in range(2):
        b = K()
        b.Cf, _ = A.alloc([128, 4, 2, 257], F32, f"Cf{d}")
        b.Ct, _ = A.alloc([128, 4, 2, 257], BF16, f"Ct{d}")
        b.CfBs = [Buf(f"Cf{d}{h}") for h in range(4)]
        b.CtBs = [Buf(f"Ct{d}{h}") for h in range(4)]
        b.qT = [A.alloc([128, 8, 128], BF16, f"qT{d}{i}") for i in range(2)]
        b.kT = [A.alloc([128, 8, 128], BF16, f"kT{d}{i}") for i in range(2)]
        b.ktk = [A.alloc([128, 1024], BF16, f"ktk{d}{i}") for i in range(2)]
        b.vtk = [A.alloc([128, 4, 257], BF16, f"vtk{d}{i}") for i in range(2)]
        b.moT = [A.alloc([128, 8, 128], BF16, f"moT{d}{i}") for i in range(2)]
        b.mzT = [A.alloc([128, 8, 128], BF16, f"mzT{d}{i}") for i in range(2)]
        b.hfl = [A.alloc([128, 1024], BF16, f"hfl{d}{i}") for i in range(2)]
        b.Sm = [A.alloc([128, 128], BF16, f"Sm{d}{i}") for i in range(2)]
        b.vw = [A.alloc([128, 257], BF16, f"vw{d}{i}") for i in range(2)]
        b.hst = [A.alloc([128, 4, 256], BF16, f"hst{d}{i}") for i in range(2)]
        b.hs, b.hsB = A.alloc([128, 4, 256], F32, f"hs{d}")
        b.hj, b.hjB = A.alloc([128, 1024], F32, f"hj{d}")
        b.hb, b.hbB = A.alloc([128, 1024], BF16, f"hb{d}")
        b.hss, b.hssB = A.alloc([128, 4], F32, f"hss{d}")
        b.dn = [A.alloc([128, 2], F32, f"dn{d}{h}") for h in range(4)]
        b.tT, b.tTB = A.alloc([128, 8, 128], F32, f"tT{d}")
        b.yst = [A.alloc([128, 8, 128], BF16, f"yst{d}{i}") for i in range(2)]
        b.cnt = dict(sm=0, vw=0, y=0)
        for v_, vB in b.vtk:
            P.op("pool", C("memset", v_, 1.0), writes=[vB])
        BD.append(b)
    orders = [list(range(NTI)), [1, 0] + list(range(NTI - 1, 1, -1))]

    def chunk(d, step, i):
        b = BD[d]
        is_ctx = i < 2
        need_out = not (is_ctx and last_layer)
        if is_ctx:
            first = (i == 0) if d == 0 else (i == 1)
        else:
            first = (i <= 17) if d == 0 else (i > 17)
        combine = need_out and not first
        cidx = i if d == 0 else ((i - 2) if i >= 2 else 32 + i)
        sl = slice(i * 128, (i + 1) * 128)
        q_, qB = b.qT[step % 2]
        k_, kB = b.kT[step % 2]
        kt_, ktB = b.ktk[step % 2]
        v_, vB = b.vtk[step % 2]
        P.op("sp", C("dma_start", out=q_, in_=Fq[:, :, sl]), reads=[k.SB["F"]], writes=[qB], dma=True)
        P.op("sp", C("dma_start", out=k_, in_=Fk[:, :, sl]), reads=[k.SB["F"]], writes=[kB], dma=True)
        P.op("sp", C("dma_start", out=kt_, in_=S["MKt"][sl, :]), reads=[k.SB["MKt"]], writes=[ktB], dma=True)
        P.op("sp", C("dma_start", out=v_[:, :, 0:256], in_=S["MVt"][sl, :].rearrange("t (h e) -> t h e", e=256)),
             reads=[k.SB["MVt"]], writes=[vB], dma=True, part=True)
        if combine:
            o_, oB = b.moT[step % 2]
            z_, zB = b.mzT[step % 2]
            f_, fB = b.hfl[step % 2]
            P.op("sp", C("dma_start", out=o_, in_=Fo[:, :, sl]), reads=[k.SB["F"]], writes=[oB], dma=True)
            P.op("sp", C("dma_start", out=z_, in_=Fz[:, :, sl]), reads=[k.SB["F"]], writes=[zB], dma=True)
            P.op("sp", C("dma_start", out=f_, in_=S["HF"][1 - d][sl, :]), reads=[HB_[1 - d][i]], writes=[fB], dma=True)
        yield
        h_, hB = b.hst[step % 2]
        bS, bP, bU = d, 2 + d, 4 + 2 * d
        psS, psSB = k.pb[bS], k.pbB[bS]
        psP, psPB = k.pb[bP], k.pbB[bP]
        for hh in range(4):
            qi = d * 4 + hh
            for dc in range(2):
                P.op("pe", C("matmul", psS[:, 0:128], lhsT=k_[:, hh * 2 + dc, :], rhs=q_[:, hh * 2 + dc, :], start=(dc == 0), stop=(dc == 1)),
                     reads=[kB, qB], writes=[psSB])
            yield
            sm_, smB = b.Sm[b.cnt["sm"] % 2]
            b.cnt["sm"] += 1
            P.op("dve", C("tensor_tensor", out=sm_, in0=psS[:, 0:128], in1=msk[:, d, :], op=ALU.mult), reads=[psSB, mskB], writes=[smB])
            vw_, vwB = b.vw[b.cnt["vw"] % 2]
            b.cnt["vw"] += 1
            P.op("act", C("activation", out=vw_, in_=v_[:, hh, :], func=AF.Copy, scale=WC[:, i, qi:qi + 1]),
                 reads=[vB, WCB], writes=[vwB])
            if step > 0:
                P.op("act", C("activation", out=b.Ct[:, hh], in_=b.Cf[:, hh], func=AF.Copy, scale=DECB[:, qi, cidx:cidx + 1]),
                     reads=[b.CfBs[hh], DECBB], writes=[b.CtBs[hh]])
            yield
            P.op("pe", C("matmul", psP[:, 0:257], lhsT=sm_, rhs=vw_, start=True, stop=(step == 0)), reads=[smB, vwB], writes=[psPB])
            if step > 0:
                for dc in range(2):
                    P.op("pe", C("matmul", psP[:, 0:257], lhsT=q_[:, hh * 2 + dc, :], rhs=b.Ct[:, hh, dc, :], start=False, stop=(dc == 1)),
                         reads=[qB, b.CtBs[hh]], writes=[psPB])
            for dc in range(2):
                psU, psUB = k.pb[bU + dc], k.pbB[bU + dc]
                P.op("pe", C("matmul", psU[:, 0:257], lhsT=kt_[:, hh * 256 + dc * 128: hh * 256 + (dc + 1) * 128], rhs=vw_, start=True, stop=True),
                     reads=[ktB, vwB], writes=[psUB])
                if step == 0:
                    P.op("dve", C("tensor_copy", out=b.Cf[:, hh, dc, :], in_=psU[:, 0:257]), reads=[psUB], writes=[b.CfBs[hh]], part=(dc == 1))
                else:
                    P.op("dve", C("scalar_tensor_tensor", out=b.Cf[:, hh, dc, :], in0=b.Cf[:, hh, dc, :], scalar=DECB[:, qi, cidx:cidx + 1], in1=psU[:, 0:257],
                                  op0=ALU.mult, op1=ALU.add), reads=[psUB, b.CfBs[hh], DECBB], writes=[b.CfBs[hh]], part=(dc == 1))
            yield
            if need_out:
                dn, dnB = b.dn[hh]
                P.op("dve", C("tensor_scalar", out=dn[:, 1:2], in0=psP[:, 256:257], scalar1=WC[:, i, 8 + qi:9 + qi], scalar2=None, op0=ALU.max),
                     reads=[psPB, WCB], writes=[dnB])
                P.op("dve", C("scalar_tensor_tensor", out=dn[:, 0:1], in0=psP[:, 256:257], scalar=-1.0, in1=dn[:, 1:2], op0=ALU.mult, op1=ALU.max),
                     reads=[psPB, dnB], writes=[dnB])
                P.op("dve", C("reciprocal", out=dn[:, 1:2], in_=dn[:, 0:1]), reads=[dnB], writes=[dnB])
                if not combine:
                    P.op("act", C("activation", out=h_[:, hh, :], in_=psP[:, 0:256], func=AF.Copy, scale=dn[:, 1:2]),
                         reads=[psPB, dnB], writes=[hB], part=(hh > 0))
                else:
                    P.op("dve", C("scalar_tensor_tensor", out=b.hs[:, hh, :], in0=psP[:, 0:256], scalar=dn[:, 1:2],
                                  in1=f_[:, hh * 256:(hh + 1) * 256], op0=ALU.mult, op1=ALU.add),
                         reads=[psPB, dnB, fB], writes=[b.hsB], part=(hh > 0))
        if need_out and not combine:
            P.op("sp", C("dma_start", out=S["HF"][d][sl, :], in_=h_.rearrange("p h e -> p (h e)")), reads=[hB], writes=[HB_[d][i]], dma=True, defer=True)
        yield
        if combine:
            hs2 = b.hs.rearrange("p h e -> p (h e)")
            P.op("act", C("activation", out=b.hj, in_=hs2, func=AF.Square), reads=[b.hsB], writes=[b.hjB])
            P.op("dve", C("tensor_reduce", out=b.hss, in_=b.hj.rearrange("p (h e) -> p h e", e=256), axis=AX.X, op=ALU.add), reads=[b.hjB], writes=[b.hssB])
            P.op("dve", C("tensor_scalar", out=b.hss, in0=b.hss, scalar1=1.0 / 256, scalar2=EPS, op0=ALU.mult, op1=ALU.add), reads=[b.hssB], writes=[b.hssB])
            P.op("act", C("activation", out=b.hss, in_=b.hss, func=AF.Sqrt), reads=[b.hssB], writes=[b.hssB])
            P.op("dve", C("reciprocal", out=b.hss, in_=b.hss), reads=[b.hssB], writes=[b.hssB])
            hn = b.hj.rearrange("p (h e) -> p h e", e=256)
            P.op("dve", C("tensor_tensor", out=hn, in0=b.hs, in1=b.hss.unsqueeze(2).to_broadcast([128, 4, 256]), op=ALU.mult), reads=[b.hsB, b.hssB, b.hjB], writes=[b.hjB])
            P.op("pool", C("tensor_tensor", out=b.hb, in0=b.hj, in1=mgb, op=ALU.mult), reads=[b.hjB, mgbB], writes=[b.hbB])
            yield
            pbf = psP.bitcast(BF16)
            for cc in range(8):
                P.op("pe", C("transpose", pbf[:, cc * 128:(cc + 1) * 128], b.hb[:, cc * 128:(cc + 1) * 128], k.idb),
                     reads=[b.hbB, k.idbB], writes=[psPB])
            P.op("dve", C("tensor_tensor", out=b.tT, in0=pbf.rearrange("p (a t) -> p a t", t=128), in1=o_, op=ALU.mult),
                 reads=[psPB, oB], writes=[b.tTB])
            y_, yB = b.yst[b.cnt["y"] % 2]
            b.cnt["y"] += 1
            P.op("pool", C("tensor_tensor", out=y_, in0=b.tT, in1=z_, op=ALU.mult), reads=[b.tTB, zB], writes=[yB])
            P.op("sp", C("dma_start", out=Yb[:, :, sl], in_=y_), reads=[yB], writes=[k.SB["Y"]], dma=True, part=True, defer=True)

    for step in range(NTI):
        gens = [chunk(d, step, orders[d][step]) for d in range(2)]
        while gens:
            for g in list(gens):
                try:
                    next(g)
                except StopIteration:
                    gens.remove(g)
    P.barrier()
    A.reset(m0)


def phase_conv_pool(k, l):
    P, A, I, S = k.P, k.A, k.I, k.S
    m0 = A.mark()
    cw, cwB = A.alloc([128, 8, 3], F32, "cw")
    P.op("sp", C("dma_start", out=cw, in_=I["convw"][l]), writes=[cwB], dma=True)
    inb = [[A.alloc([128, NT], BF16, f"cv{j}_{i}") for j in range(4)] for i in range(2)]
    ap_, apB = A.alloc([128, NT + 2], F32, "apad")
    y_, yB = A.alloc([128, NT], F32, "ycv")
    y2, y2B = A.alloc([128, NT], F32, "ycv2")
    ost = [A.alloc([128, NT], BF16, f"cvo{i}") for i in range(2)]
    P.op("pool", C("memset", ap_, 0.0), writes=[apB])
    names = ("cu", "cc", "cb", "cz")
    for cc in range(8):
        tl = inb[cc % 2]
        for j, nm in enumerate(names):
            t_, tB = tl[j]
            r0 = FROW[nm] + cc * 128
            P.op("sp", C("dma_start", out=t_, in_=S["F"][r0:r0 + 128, :]), reads=[k.SB["F"]], writes=[tB], dma=True)
        (cu, cuB), (cg, cgB), (cb, cbB), (cz, czB) = tl
        w0, w1, w2 = cw[:, cc, 0:1], cw[:, cc, 1:2], cw[:, cc, 2:3]
        P.op("pool", C("tensor_tensor", out=ap_[:, 1:NT + 1], in0=cu, in1=cg, op=ALU.mult), reads=[cuB, cgB], writes=[apB])
        P.op("dve", C("tensor_scalar", out=y_, in0=ap_[:, 1:NT + 1], scalar1=w1, scalar2=None, op0=ALU.mult), reads=[apB, cwB], writes=[yB])
        P.op("dve", C("scalar_tensor_tensor", out=y_, in0=ap_[:, 0:NT], scalar=w0, in1=y_, op0=ALU.mult, op1=ALU.add), reads=[apB, cwB, yB], writes=[yB])
        P.op("dve", C("scalar_tensor_tensor", out=y_, in0=ap_[:, 2:NT + 2], scalar=w2, in1=y_, op0=ALU.mult, op1=ALU.add), reads=[apB, cwB, yB], writes=[yB])
        P.op("dve", C("tensor_scalar", out=y_[:, 255:256], in0=ap_[:, 255:256], scalar1=w0, scalar2=None, op0=ALU.mult), reads=[apB, cwB, yB], writes=[yB])
        P.op("dve", C("scalar_tensor_tensor", out=y_[:, 255:256], in0=ap_[:, 256:257], scalar=w1, in1=y_[:, 255:256], op0=ALU.mult, op1=ALU.add), reads=[apB, cwB, yB], writes=[yB])
        P.op("dve", C("tensor_scalar", out=y_[:, 256:257], in0=ap_[:, 257:258], scalar1=w1, scalar2=None, op0=ALU.mult), reads=[apB, cwB, yB], writes=[yB])
        P.op("dve", C("scalar_tensor_tensor", out=y_[:, 256:257], in0=ap_[:, 258:259], scalar=w2, in1=y_[:, 256:257], op0=ALU.mult, op1=ALU.add), reads=[apB, cwB, yB], writes=[yB])
        P.op("pool", C("tensor_tensor", out=y2, in0=y_, in1=cb, op=ALU.mult), reads=[yB, cbB], writes=[y2B])
        o_, oB = ost[cc % 2]
        P.op("pool", C("tensor_tensor", out=o_, in0=y2, in1=cz, op=ALU.mult), reads=[y2B, czB], writes=[oB])
        P.op("sp", C("dma_start", out=S["Y"][2][cc * 128:(cc + 1) * 128, :], in_=o_), reads=[oB], writes=[k.SB["Y"]], dma=True, part=True, defer=True)
    P.barrier()
    A.reset(m0)
    OC, OL = 8, 8 + 256 + 16
    PW = OL + NL + 16
    psc, pscB = A.alloc([128, 8], F32, "psc")
    P.op("sp", C("dma_start", out=psc, in_=I["pscale"][l]), writes=[pscB], dma=True)
    pub = [A.alloc([128, NT], BF16, f"pu{i}") for i in range(2)]
    pzb = [A.alloc([128, NT], BF16, f"pz{i}") for i in range(2)]
    up = [A.alloc([128, PW], F32, f"up{i}") for i in range(2)]
    sa, saB = A.alloc([128, PW], F32, "sa")
    sb_, sbB = A.alloc([128, PW], F32, "sb")
    rcb, rcbB = A.alloc([128, NT], F32, "rcb")
    dT = [A.alloc([128, NT], BF16, f"dT{i}") for i in range(2)]
    pw = [A.alloc([128, 2, 256], BF16, f"pw{i}") for i in range(2)]
    yo = [A.alloc([128, NT], BF16, f"ypo{i}") for i in range(2)]
    for u_, uB in up:
        P.op("pool", C("memset", u_, 0.0), writes=[uB])
    P.op("pool", C("memset", sa, 0.0), writes=[saB])
    P.op("pool", C("memset", sb_, 0.0), writes=[sbB])
    nps = 0
    for g, w in enumerate((2, 4, 8, 16)):
        P.op("sp", C("dma_start", out=rcb, in_=I["rcnt"][g, :].partition_broadcast(128)), writes=[rcbB], dma=True)
        pw_, pwB = pw[g % 2]
        P.op("pool", C("dma_start", out=pw_, in_=I["pool_w"][l, g].rearrange("(kc p) o -> p kc o", p=128)), writes=[pwB], dma=True)
        for kc2 in range(2):
            ct = g * 2 + kc2
            pu_, puB = pub[kc2]
            u_, uB = up[kc2]
            d_, dB = dT[kc2]
            P.op("sp", C("dma_start", out=pu_, in_=S["F"][FROW["pu"] + ct * 128:FROW["pu"] + (ct + 1) * 128, :]), reads=[k.SB["F"]], writes=[puB], dma=True)
            P.op("pool", C("tensor_copy", out=u_[:, OC:OC + NC], in_=pu_[:, 0:NC]), reads=[puB], writes=[uB], part=True)
            P.op("pool", C("tensor_copy", out=u_[:, OL:OL + NL], in_=pu_[:, NC:NT]), reads=[puB], writes=[uB], part=True)
            cur, curB = u_, uB
            m = 1
            pp = [(sa, saB), (sb_, sbB)]
            si = 0
            while m < w:
                nx, nxB = pp[si % 2]
                si += 1
                P.op("dve", C("tensor_tensor", out=nx[:, 0:PW - m], in0=cur[:, 0:PW - m], in1=cur[:, m:PW], op=ALU.add), reads=[curB], writes=[nxB])
                cur, curB = nx, nxB
                m *= 2
            hw_ = w // 2
            for (po, to, n) in ((OC, 0, NC), (OL, NC, NL)):
                P.op("dve", C("tensor_tensor", out=sa[:, po:po + n] if cur is not sa else sb_[:, po:po + n],
                                                                                        in0=cur[:, po - hw_:po - hw_ + n], in1=rcb[:, to:to + n], op=ALU.mult),
                     reads=[curB, rcbB], writes=[saB if cur is not sa else sbB])
                tmpb, tmpB = (sa, saB) if cur is not sa else (sb_, sbB)
                P.op("pool", C("tensor_tensor", out=d_[:, to:to + n], in0=tmpb[:, po:po + n], in1=u_[:, po:po + n], op=ALU.subtract),
                     reads=[tmpB, uB], writes=[dB], part=True)
        for oc in range(2):
            ct = g * 2 + oc
            pz_, pzB = pzb[oc]
            o_, oB = yo[oc]
            P.op("sp", C("dma_start", out=pz_, in_=S["F"][FROW["pz"] + ct * 128:FROW["pz"] + (ct + 1) * 128, :]), reads=[k.SB["F"]], writes=[pzB], dma=True)
            for (t0, n) in BLKS:
                bi = nps % 4
                nps += 1
                ps, psB = k.pb[bi], k.pbB[bi]
                for kc2 in range(2):
                    P.op("pe", C("matmul", ps[:, 0:n], lhsT=pw_[:, kc2, oc * 128:(oc + 1) * 128], rhs=dT[kc2][0][:, t0:t0 + n],
                                                                                          start=(kc2 == 0), stop=(kc2 == 1)), reads=[pwB, dT[kc2][1]], writes=[psB])
                P.op("dve", C("scalar_tensor_tensor", out=o_[:, t0:t0 + n], in0=ps[:, 0:n], scalar=psc[:, ct:ct + 1], in1=pz_[:, t0:t0 + n],
                                                                                                 op0=ALU.mult, op1=ALU.mult), reads=[psB, pscB, pzB], writes=[oB], part=True)
            P.op("sp", C("dma_start", out=S["Y"][3][ct * 128:(ct + 1) * 128, :], in_=o_), reads=[oB], writes=[k.SB["Y"]], dma=True, part=True, defer=True)
    P.barrier()
    A.reset(m0)


def phase_merge_out(k, l):
    P, A, I, S = k.P, k.A, k.I, k.S
    last_layer = (l == DEPTH - 1)
    blocks = BLKS[1:] if last_layer else BLKS
    m0 = A.mark()
    wbr, _ = A.alloc([128, 16, 4 * 8 * 128], BF16, "wbr")
    wbrB = [Buf(f"wbr{ct}") for ct in range(16)]
    Yb, YbB = A.alloc([128, 4, 8, 512], BF16, "Yb")
    gm = [A.alloc([128, 4, 512], BF16, f"gm{i}") for i in range(2)]
    tm = [A.alloc([128, 512], F32, f"tm{i}") for i in range(4)]
    ast = [A.alloc([128, 512], BF16, f"ast{i}") for i in range(2)]
    for ct in range(16):
        P.op("pool", C("dma_start", out=wbr[:, ct, :], in_=I["wbt"][l, ct]), writes=[wbrB[ct]], dma=True)
    Yv = S["Y"].rearrange("b (kc p) t -> p b kc t", p=128)
    Gv = S["F"][FROW["gm"]:FROW["gm"] + 4 * D, :].rearrange("(b c p) t -> p b c t", p=128, c=16)
    nset = 0
    for (t0, n) in blocks:
        P.op("sp", C("dma_start", out=Yb[:, :, :, 0:n], in_=Yv[:, :, :, t0:t0 + n]), reads=[k.SB["Y"]], writes=[YbB], dma=True)
        for ct in range(16):
            g_, gB = gm[ct % 2]
            a_, aB = ast[ct % 2]
            w_ = wbr[:, ct, :].rearrange("p (b k c) -> p b k c", b=4, k=8)
            P.op("sp", C("dma_start", out=g_[:, :, 0:n], in_=Gv[:, :, ct, t0:t0 + n]), reads=[k.SB["F"]], writes=[gB], dma=True)
            base = 4 * (nset % 2)
            nset += 1
            for br in range(4):
                ps, psB = k.pb[base + br], k.pbB[base + br]
                for kc in range(8):
                    P.op("pe", C("matmul", ps[:, 0:n], lhsT=w_[:, br, kc, :], rhs=Yb[:, br, kc, 0:n], start=(kc == 0), stop=(kc == 7)),
                         reads=[wbrB[ct], YbB], writes=[psB])
                P.op("dve", C("tensor_tensor", out=tm[br][0][:, 0:n], in0=ps[:, 0:n], in1=g_[:, br, 0:n], op=ALU.mult),
                     reads=[psB, gB], writes=[tm[br][1]])
            P.op("pool", C("tensor_tensor", out=tm[0][0][:, 0:n], in0=tm[0][0][:, 0:n], in1=tm[1][0][:, 0:n], op=ALU.add), reads=[tm[0][1], tm[1][1]], writes=[tm[0][1]])
            P.op("pool", C("tensor_tensor", out=tm[2][0][:, 0:n], in0=tm[2][0][:, 0:n], in1=tm[3][0][:, 0:n], op=ALU.add), reads=[tm[2][1], tm[3][1]], writes=[tm[2][1]])
            P.op("pool", C("tensor_tensor", out=a_[:, 0:n], in0=tm[0][0][:, 0:n], in1=tm[2][0][:, 0:n], op=ALU.add), reads=[tm[0][1], tm[2][1]], writes=[aB])
            P.op("sp", C("dma_start", out=S["ACC"][ct * 128:(ct + 1) * 128, t0:t0 + n], in_=a_[:, 0:n]), reads=[aB], writes=[k.SB["ACC"]], dma=True, part=True, defer=True)
    P.barrier()
    A.reset(m0)
    wo, _ = A.alloc([128, KC, D], BF16, "wo")
    woB = [Buf(f"wo{cg}") for cg in range(4)]
    accT = [A.alloc([128, 16, 512], BF16, f"accT{i}") for i in range(2)]
    xb = [A.alloc([128, 4, D], F32, f"xblk{i}") for i in range(2)]
    gtb = [A.alloc([128, D], F32, f"gtb{i}") for i in range(2)]
    t5, t5B = A.alloc([128, 512], F32, "t5")
    ss, ssB = A.alloc([128, 4], F32, "fss")
    junk, junkB = A.alloc([128, D], BF16, "junkf")
    wov = I["w_out"][l].rearrange("(kc p) c -> p kc c", p=128)
    for cg in range(4):
        P.op("pool", C("dma_start", out=wo[:, :, cg * 512:(cg + 1) * 512], in_=wov[:, :, cg * 512:(cg + 1) * 512]), writes=[woB[cg]], dma=True)
    if last_layer:
        fgb, fgbB = A.alloc([128, D], F32, "fgb")
        P.op("sp", C("dma_start", out=fgb, in_=I["fgain"].partition_broadcast(128)), writes=[fgbB], dma=True)
    for r in range(2):
        P.op("sp", C("dma_start", out=gtb[r][0], in_=S["modd"][l, r, 2 * D:3 * D].partition_broadcast(128)), reads=[k.SB["modd"]], writes=[gtb[r][1]], dma=True)
    Av = S["ACC"].rearrange("(kc p) t -> p kc t", p=128)
    x1B = [k.SB["X1"]] if l > 0 else []
    nps = 0
    for bi, (t0, n) in enumerate(blocks):
        r = 1 if t0 < 256 else 0
        nti = n // 128
        a_, aB = accT[bi % 2]
        xblk, xblkB = xb[bi % 2]
        P.op("sp", C("dma_start", out=a_[:, :, 0:n], in_=Av[:, :, t0:t0 + n]), reads=[k.SB["ACC"]], writes=[aB], dma=True)
        for ti in range(nti):
            i = t0 // 128 + ti
            P.op("sp", C("dma_start", out=xblk[:, ti, :], in_=src_tile(k, l, i)), reads=x1B, writes=[xblkB], dma=True, part=(ti > 0))
        for ti in range(nti):
            i = t0 // 128 + ti
            for cg in range(4):
                b_ = nps % 8
                nps += 1
                ps, psB = k.pb[b_], k.pbB[b_]
                for kc in range(KC):
                    P.op("pe", C("matmul", ps, lhsT=a_[:, kc, ti * 128:(ti + 1) * 128], rhs=wo[:, kc, cg * 512:(cg + 1) * 512], start=(kc == 0), stop=(kc == KC - 1)),
                         reads=[aB, woB[cg]], writes=[psB])
                P.op("dve", C("tensor_tensor", out=t5, in0=ps, in1=gtb[r][0][:, cg * 512:(cg + 1) * 512], op=ALU.mult), reads=[psB, gtb[r][1]], writes=[t5B])
                P.op("pool", C("tensor_tensor", out=xblk[:, ti, cg * 512:(cg + 1) * 512], in0=xblk[:, ti, cg * 512:(cg + 1) * 512], in1=t5, op=ALU.add),
                     reads=[t5B, xblkB], writes=[xblkB], part=True)
            if not last_layer:
                P.op("sp", C("dma_start", out=S["X1"][i * 128:(i + 1) * 128, :], in_=xblk[:, ti, :]), reads=[xblkB], writes=[k.SB["X1"]], dma=True, part=True, defer=True)
            else:
                P.op("act", C("activation", out=junk, in_=xblk[:, ti, :], func=AF.Square, accum_out=ss[:, ti:ti + 1]), reads=[xblkB], writes=[junkB, ssB])
                P.op("dve", C("tensor_scalar", out=ss[:, ti:ti + 1], in0=ss[:, ti:ti + 1], scalar1=1.0 / D, scalar2=EPS, op0=ALU.mult, op1=ALU.add), reads=[ssB], writes=[ssB])
                P.op("act", C("activation", out=ss[:, ti:ti + 1], in_=ss[:, ti:ti + 1], func=AF.Sqrt), reads=[ssB], writes=[ssB])
                P.op("dve", C("reciprocal", out=ss[:, ti:ti + 1], in_=ss[:, ti:ti + 1]), reads=[ssB], writes=[ssB])
                P.op("dve", C("scalar_tensor_tensor", out=xblk[:, ti, :], in0=xblk[:, ti, :], scalar=ss[:, ti:ti + 1], in1=fgb, op0=ALU.mult, op1=ALU.mult),
                     reads=[xblkB, ssB, fgbB], writes=[xblkB], part=True)
                P.op("sp", C("dma_start", out=S["out"][(i - 2) * 128:(i - 1) * 128, :], in_=xblk[:, ti, :]), reads=[xblkB], writes=[k.SB["out"]], dma=True, part=True, defer=True)
    P.barrier()
    A.reset(m0)


def host_constants():
    C = {}
    C["ident"] = np.eye(128, dtype=np.float32)
    s = np.arange(128)[:, None]
    t = np.arange(128)[None, :]
    C["masks"] = np.stack([(s <= t), (s >= t)]).astype(np.float32)
    freq = (10000.0 ** (-np.arange(32, dtype=np.float32) / 32)).astype(np.float32)
    rope = np.zeros((NTI, 128, 2, 8, 32), np.float32)
    rope[:, :, 0] = 1.0
    for i in range(2, NTI):
        tt = (i - 2) * 128 + np.arange(128)
        row = (tt // 64).astype(np.float32)
        col = (tt % 64).astype(np.float32)
        for half, pos in enumerate((row, col)):
            ang = (pos[:, None] * freq[None, :]).astype(np.float32)
            for h in range(4):
                rope[i, :, 0, h * 2 + half, :] = np.cos(ang)
                rope[i, :, 1, h * 2 + half, :] = np.sin(ang)
    C["rope"] = rope.reshape(NTI, 128, 2, 256)
    sel = np.zeros((64, 8, 128), np.float32)
    for d in range(2):
        for h in range(4):
            sel[d * 32 + h, d * 4 + h, :] = 1.0
    C["sel"] = sel
    rc = np.zeros((4, NT), np.float32)
    for g, w in enumerate((2, 4, 8, 16)):
        for (o, T) in ((0, NC), (NC, NL)):
            tt = np.arange(T)
            lo = np.clip(tt - w // 2, 0, T)
            hi = np.clip(tt + w - w // 2, 0, T)
            rc[g, o:o + T] = 1.0 / (hi - lo).astype(np.float32)
    C["rcnt"] = rc
    return C


_CACHE = {}
NCORES = 4


def make_in_maps(inputs):
    f = lambda a: np.ascontiguousarray(np.asarray(a, dtype=np.float32))
    x, c, ctx, c_ctx = f(inputs["x"]), f(inputs["c"]), f(inputs["ctx"]), f(inputs["c_ctx"])
    C = host_constants()
    shared = dict(C)
    shared["ngcol"] = f(inputs["norm_gain"]).reshape(DEPTH, KC, 128).transpose(0, 2, 1).copy()
    shared["w_mod"] = f(inputs["w_mod"])
    shared["b_mod"] = f(inputs["b_mod"])
    shared["w_in"] = f(inputs["w_in"])
    shared["qg"] = f(inputs["q_norm_gain"])
    shared["kg"] = f(inputs["k_norm_gain"])
    gb = f(inputs["mlstm_gate_bias"])
    gbias = np.zeros((DEPTH, 64, 2), np.float32)
    gbias[:, 0:4, 0] = gb[:, 0]
    gbias[:, 32:36, 0] = gb[:, 2]
    gbias[:, 0:4, 1] = gb[:, 1]
    gbias[:, 32:36, 1] = gb[:, 3]
    shared["gbias"] = gbias
    shared["mgain"] = f(inputs["mlstm_norm_gain"])
    shared["convw"] = f(inputs["conv_w"]).reshape(DEPTH, 3, 8, 128).transpose(0, 3, 2, 1).copy()
    shared["pool_w"] = f(inputs["pool_w"])
    shared["pscale"] = f(inputs["pool_scale"]).reshape(DEPTH, 8, 128).transpose(0, 2, 1).copy()
    wb = f(inputs["w_branch"])
    shared["wbt"] = wb.reshape(DEPTH, 4, 8, 128, 16, 128).transpose(0, 4, 3, 1, 2, 5).reshape(DEPTH, 16, 128, 4 * 8 * 128).copy()
    shared["w_out"] = f(inputs["w_out"])
    shared["fgain"] = f(inputs["final_norm_gain"])
    maps = []
    for core in range(NCORES):
        b = core % 4
        m = dict(shared)
        m["x"] = x[b]
        m["ctx"] = ctx[b]
        c2 = np.stack([c[b], c_ctx])
        m["c2T"] = c2.reshape(2, KC, 128).transpose(2, 1, 0).copy()
        maps.append(m)
    return maps


def kernel(**inputs):
    if "nc" not in _CACHE:
        _CACHE["nc"] = build_program()
    nc = _CACHE["nc"]
    maps = make_in_maps(inputs)
    res = run_bass_kernel_spmd(nc, maps, core_ids=list(range(NCORES)))
    out = np.stack([np.asarray(res.results[b]["out"]) for b in range(4)]).astype(np.float32)
    return out
```

```python
import contextlib
import numpy as np
import concourse.bass as bass
import concourse.mybir as mybir
from concourse.bass_utils import run_bass_kernel_spmd

F32 = mybir.dt.float32
BF16 = mybir.dt.bfloat16
AF = mybir.ActivationFunctionType
ALU = mybir.AluOpType
AX = mybir.AxisListType

NT = 4352
NTI = 34
NL = 4096
NC = 256
D = 2048
KC = 16
EPS = 1e-6
BLKS = [(0, 256)] + [(256 + 512 * j, 512) for j in range(8)]
DEPTH = 2
WCOL = dict(aq=0, ak=1024, av=1280, az=1536, mq=2560, mk=3584, mv=4608, mo=5632, mz=6656, mg=7680,
            cu=7696, cb=8720, cc=9744, cz=10768, pu=11792, pz=12816, gm=13840)
N_IN = 22032
FROW = dict(az=0, mz=1024, cz=2048, pz=3072, mo=4096, gm=5120, mq=13312, mk=14336, cu=15360, cb=16384,
            cc=17408, pu=18432)
NF = 19456
FSEG = [("az", 1024, "silu"), ("mz", 1024, "silu"), ("cz", 1024, "silu"), ("pz", 1024, "silu"),
        ("mo", 1024, "sig"), ("gm", 8192, "sig"),
        ("mq", 1024, "copy"), ("mk", 1024, "copy16"), ("cu", 1024, "copy"), ("cb", 1024, "copy"),
        ("cc", 1024, "copy"), ("pu", 1024, "copy")]


class Tok:
    __slots__ = ("q", "sem", "val", "rec")

    def __init__(self, q, rec):
        self.q = q
        self.rec = rec
        self.sem = None
        self.val = None


class Buf:
    __slots__ = ("name", "w", "r", "excl", "wf")

    def __init__(self, name="", excl=False):
        self.name = name
        self.w = {}
        self.wf = {}
        self.r = {}
        self.excl = excl


class Rec:
    __slots__ = ("fn", "deps", "signal", "tok", "dma", "dsem")

    def __init__(self, fn, deps, dma):
        self.fn = fn
        self.deps = deps
        self.signal = False
        self.tok = None
        self.dma = dma
        self.dsem = None


class Prog:
    NDMA = 16

    def __init__(self, nc):
        self.nc = nc
        self.eng = {"pe": nc.tensor, "act": nc.scalar, "dve": nc.vector, "pool": nc.gpsimd, "sp": nc.sync}
        self.q = {k: [] for k in self.eng}
        self.dma_n = {k: 0 for k in self.eng}
        self.dma_last = {k: [None] * self.NDMA for k in self.eng}
        self.last = {k: None for k in self.eng}
        self.fence = {k: [] for k in self.eng}
        self.deferred = []

    def barrier(self):
        self.flush()
        toks = []
        for q in self.eng:
            if self.last[q] is not None:
                toks.append(self.last[q])
            toks.extend(t for t in self.dma_last[q] if t is not None)
        for q in self.eng:
            self.fence[q] = list(toks)

    def flush(self):
        d, self.deferred = self.deferred, []
        for (q, fn, reads, writes, dma, part) in d:
            self.op(q, fn, reads, writes, dma, part)

    def op(self, q, fn, reads=(), writes=(), dma=False, part=False, defer=False):
        if defer:
            self.deferred.append((q, fn, list(reads), list(writes), dma, part))
            return None
        if self.deferred:
            for (_, _, dr, dw, _, _) in self.deferred:
                if any(b in dr for b in writes) or any(b in dw for b in reads):
                    self.flush()
                    break
        deps = []
        if self.fence[q]:
            deps.extend(self.fence[q])
            self.fence[q] = []
        for b in reads:
            deps.extend(b.w.values())
            if b.excl:
                deps.extend(t for kk, t in b.r.items() if kk[0] != q)
        for b in writes:
            deps.extend(b.r.values())
            if not part:
                deps.extend(b.w.values())
            else:
                deps.extend(b.wf.values())
        rec = Rec(fn, deps, dma)
        tok = Tok(q, rec)
        rec.tok = tok
        if dma:
            n = self.dma_n[q]
            self.dma_n[q] = n + 1
            slot = n % self.NDMA
            prev = self.dma_last[q][slot]
            if prev is not None:
                rec.deps.append(prev)
            self.dma_last[q][slot] = tok
            rec.dsem = slot
        else:
            self.last[q] = tok
        self.q[q].append(rec)
        key = (q, rec.dsem)
        for b in reads:
            b.r[key] = tok
        for b in writes:
            if part:
                b.w[key] = tok
            else:
                b.w = {key: tok}
                b.wf = {key: tok}
                b.r = {}
        return tok

    def emit(self, sems, dsems):
        self.flush()
        for q, recs in self.q.items():
            for rec in recs:
                for t in rec.deps:
                    if t.rec.dma:
                        continue
                    if t.q == q and q == "pe":
                        continue
                    t.rec.signal = True
        for q, recs in self.q.items():
            cnt = 0
            dcnt = [0] * self.NDMA
            for rec in recs:
                if rec.dma:
                    dcnt[rec.dsem] += 16
                    rec.tok.sem = dsems[q][rec.dsem]
                    rec.tok.val = dcnt[rec.dsem]
                elif rec.signal:
                    cnt += 1
                    rec.tok.sem = sems[q]
                    rec.tok.val = cnt
        self.stats = {}
        with self.nc.Block() as block:
            def mk(q):
                def body(e):
                    seen = {}
                    nw = 0
                    for rec in self.q[q]:
                        need = {}
                        for t in rec.deps:
                            if t.sem is None or (t.q == q and q == "pe" and not t.rec.dma):
                                continue
                            k = id(t.sem)
                            if seen.get(k, 0) >= t.val:
                                continue
                            if k not in need or need[k][1] < t.val:
                                need[k] = (t.sem, t.val)
                        for k, (s, v) in need.items():
                            e.wait_ge(s, v)
                            seen[k] = v
                            nw += 1
                        ins = rec.fn(e)
                        if rec.dma:
                            ins.then_inc(rec.tok.sem, 16)
                        elif rec.signal:
                            ins.then_inc(rec.tok.sem, 1)
                    for t in self.dma_last[q]:
                        if t is not None:
                            e.wait_ge(t.sem, t.val)
                    self.stats[q] = (len(self.q[q]), nw)
                return body
            block.tensor(mk("pe"))
            block.scalar(mk("act"))
            block.vector(mk("dve"))
            block.gpsimd(mk("pool"))
            block.sync(mk("sp"))


def C(name, *a, **kw):
    return lambda e: getattr(e, name)(*a, **kw)


class Arena:
    def __init__(self, ap, nbytes):
        self.ap = ap
        self.cap = nbytes
        self.top = 0

    def mark(self):
        return self.top

    def reset(self, m):
        self.top = m

    def alloc(self, shape, dt, name=""):
        esz = 2 if dt == BF16 else 4
        n = int(np.prod(shape[1:]))
        nb = (n * esz + 31) // 32 * 32
        off = self.top
        self.top += nb
        assert self.top <= self.cap, f"arena overflow {self.top} > {self.cap} at {name}"
        v = self.ap[0:shape[0], off // 4: off // 4 + (n * esz + 3) // 4]
        if dt != F32:
            v = v.bitcast(dt)[:, 0:n]
        if len(shape) == 3:
            v = v.rearrange("p (a b) -> p a b", b=shape[2])
        elif len(shape) == 4:
            v = v.rearrange("p (a b c) -> p a b c", b=shape[2], c=shape[3])
        return v, Buf(name)


class K:
    pass


def build_program(debug=(), stop_after=None, nlayers=DEPTH):
    nc = bass.Bass("TRN2", target_bir_lowering=False)
    P = Prog(nc)
    k = K()
    k.nc, k.P = nc, P
    k.stop = stop_after

    def din(name, shape, dt=F32):
        return nc.dram_tensor(name, list(shape), dt, kind="ExternalInput").ap()

    def dscr(name, shape, dt, out=False):
        return nc.dram_tensor(name, list(shape), dt, kind=("ExternalOutput" if (out or name in debug) else "Internal")).ap()

    I = {}
    I["x"] = din("x", [NL, D])
    I["ctx"] = din("ctx", [NC, D])
    I["c2T"] = din("c2T", [128, KC, 2])
    I["ngcol"] = din("ngcol", [DEPTH, 128, KC])
    I["w_mod"] = din("w_mod", [DEPTH, D, 3 * D])
    I["b_mod"] = din("b_mod", [DEPTH, 3 * D])
    I["w_in"] = din("w_in", [DEPTH, D, N_IN])
    I["qg"] = din("qg", [DEPTH, 128])
    I["kg"] = din("kg", [DEPTH, 128])
    I["gbias"] = din("gbias", [DEPTH, 64, 2])
    I["mgain"] = din("mgain", [DEPTH, 1024])
    I["convw"] = din("convw", [DEPTH, 128, 8, 3])
    I["pool_w"] = din("pool_w", [DEPTH, 4, 256, 256])
    I["pscale"] = din("pscale", [DEPTH, 128, 8])
    I["wbt"] = din("wbt", [DEPTH, 16, 128, 4 * 8 * 128])
    I["w_out"] = din("w_out", [DEPTH, D, D])
    I["fgain"] = din("fgain", [D])
    I["ident"] = din("ident", [128, 128])
    I["masks"] = din("masks", [2, 128, 128])
    I["rope"] = din("rope", [NTI, 128, 2, 256])
    I["sel"] = din("sel", [64, 8, 128])
    I["rcnt"] = din("rcnt", [4, NT])
    k.I = I
    S = {}
    S["modd"] = dscr("modd", [DEPTH, 2, 3 * D], F32)
    S["F"] = dscr("Fs", [NF, NT], BF16)
    S["QT"] = dscr("QT", [1024, NT], BF16)
    S["KT"] = dscr("KT", [256, NT], BF16)
    S["Vt"] = dscr("Vt", [NT, 256], BF16)
    S["MKt"] = dscr("MKt", [NT, 1024], BF16)
    S["MVt"] = dscr("MVt", [NT, 1024], BF16)
    S["HF"] = dscr("HF", [2, NT, 1024], BF16)
    S["Y"] = dscr("Y", [4, 1024, NT], BF16)
    S["X1"] = dscr("X1", [NT, D], F32)
    S["ACC"] = dscr("ACC", [D, NT], BF16)
    S["GT"] = dscr("GT", [NTI, 128, 72], F32)
    S["out"] = dscr("out", [NL, D], F32, out=True)
    k.S = S
    k.SB = {n: Buf(n) for n in S}

    with contextlib.ExitStack() as es:
        ARENA_BYTES = 206 * 1024
        arena_t = es.enter_context(nc.sbuf_tensor("arena", [128, ARENA_BYTES // 4], F32))
        A = Arena(arena_t, ARENA_BYTES)
        k.A = A
        k.pb2 = [es.enter_context(nc.psum_tensor(f"pbb{i}", [128, 1024], F32))[:, :] for i in range(4)]
        k.pb = [k.pb2[i // 2][:, (i % 2) * 512:(i % 2 + 1) * 512] for i in range(8)]
        k.pbB = [Buf(f"pb{i}", excl=True) for i in range(8)]
        sems = {q: es.enter_context(nc.semaphore("s_" + q)) for q in P.eng}
        dsems = {q: [es.enter_context(nc.semaphore(f"d_{q}{i}")) for i in range(P.NDMA)] for q in P.eng}

        k.idf, k.idfB = A.alloc([128, 128], F32, "idf")
        k.idb, k.idbB = A.alloc([128, 128], BF16, "idb")
        k.onesb, k.onesbB = A.alloc([128, 128], BF16, "onesb")
        k.modcol, k.modcolB = A.alloc([128, DEPTH, 48, 2], F32, "modcol")
        k.gcol, k.gcolB = A.alloc([128, DEPTH, KC, 2], F32, "gcol")
        P.op("sp", C("dma_start", out=k.idf, in_=I["ident"]), writes=[k.idfB], dma=True)
        P.op("pool", C("dma_start", out=k.idb, in_=I["ident"]), writes=[k.idbB], dma=True)
        P.op("dve", C("memset", k.onesb, 1.0), writes=[k.onesbB])

        phase_mod(k)
        for l in range(nlayers if stop_after != ("mod", 0) else 0):
            last = (l == DEPTH - 1)
            phase_norm_inproj(k, l)
            if stop_after in (("inproj", l), ("norm", l), ("tokmaj", l)):
                break
            phase_attn(k, l)
            if stop_after == ("attn", l):
                break
            phase_mlstm(k, l)
            if stop_after == ("mlstm", l):
                break
            phase_conv_pool(k, l)
            if stop_after == ("convpool", l):
                break
            phase_merge_out(k, l)
        P.emit(sems, dsems)
    return nc


def phase_mod(k):
    P, A, I, S = k.P, k.A, k.I, k.S
    m0 = A.mark()
    cT, cTB = A.alloc([128, KC, 2], F32, "cT")
    scT, scTB = A.alloc([128, KC, 2], F32, "scT")
    ngc, ngcB = A.alloc([128, DEPTH, KC], F32, "ngc")
    wm = [A.alloc([128, 3072], F32, f"wm{i}") for i in range(3)]
    mo, moB = A.alloc([2, 6144], F32, "mo")
    bm, bmB = A.alloc([2, 6144], F32, "bm")
    tmp, tmpB = A.alloc([128, KC, 2], F32, "tmpg")
    P.op("sp", C("dma_start", out=cT, in_=I["c2T"]), writes=[cTB], dma=True)
    P.op("sp", C("dma_start", out=ngc, in_=I["ngcol"].rearrange("l p k -> p l k")), writes=[ngcB], dma=True)
    P.op("act", C("activation", out=scT, in_=cT, func=AF.Silu), reads=[cTB], writes=[scTB])
    n = 0
    for l in range(DEPTH):
        P.op("sp", C("dma_start", out=bm, in_=I["b_mod"][l, :].partition_broadcast(2)), writes=[bmB], dma=True)
        for half in range(2):
            for kc in range(KC):
                w, wB = wm[n % 3]
                n += 1
                P.op("sp", C("dma_start",
                    out=w, in_=I["w_mod"][l, kc * 128:(kc + 1) * 128, half * 3072:(half + 1) * 3072]), writes=[wB], dma=True)
                for j in range(6):
                    P.op("pe", C("matmul", k.pb[j][0:2, :], lhsT=scT[:, kc, :], rhs=w[:, j * 512:(j + 1) * 512],
                                                                start=(kc == 0), stop=(kc == KC - 1)),
                         reads=[scTB, wB], writes=[k.pbB[j]])
            for j in range(6):
                c0 = half * 3072 + j * 512
                P.op("dve", C("tensor_tensor", out=mo[:, c0:c0 + 512], in0=k.pb[j][0:2, :], in1=bm[:, c0:c0 + 512], op=ALU.add),
                     reads=[k.pbB[j], bmB], writes=[moB], part=True)
        P.op("sp", C("dma_start", out=S["modd"][l], in_=mo), reads=[moB], writes=[k.SB["modd"]], dma=True, part=True, defer=True)
        for j in range(48):
            P.op("pe", C("transpose", k.pb[6][:, 2 * j:2 * j + 2], mo[0:2, j * 128:(j + 1) * 128], k.idf[0:2, 0:2]),
                 reads=[moB, k.idfB], writes=[k.pbB[6]])
        P.op("dve", C("tensor_copy", out=k.modcol[:, l], in_=k.pb[6][:, 0:96].rearrange("p (j r) -> p j r", r=2)),
             reads=[k.pbB[6]], writes=[k.modcolB], part=True)
        P.op("dve", C("tensor_scalar", out=tmp, in0=k.modcol[:, l, 16:32, :], scalar1=1.0, scalar2=None, op0=ALU.add),
             reads=[k.modcolB], writes=[tmpB])
        P.op("dve", C("tensor_tensor", out=k.gcol[:, l], in0=tmp, in1=ngc[:, l, :].unsqueeze(2).to_broadcast([128, KC, 2]), op=ALU.mult),
             reads=[tmpB, ngcB], writes=[k.gcolB], part=True)
    P.barrier()
    A.reset(m0)


def src_tile(k, l, i):
    if l == 0:
        if i < 2:
            return k.I["ctx"][i * 128:(i + 1) * 128, :]
        return k.I["x"][(i - 2) * 128:(i - 1) * 128, :]
    return k.S["X1"][i * 128:(i + 1) * 128, :]


def phase_norm_inproj(k, l):
    P, A, I, S = k.P, k.A, k.I, k.S
    m0 = A.mark()
    hT, hTB = A.alloc([128, KC, NT], BF16, "hT")
    m1 = A.mark()
    xt = [A.alloc([128, D], F32, f"xt{i}") for i in range(2)]
    xn = [A.alloc([128, D], BF16, f"xn{i}") for i in range(2)]
    junk, junkB = A.alloc([128, D], BF16, "junk")
    ss, ssB = A.alloc([128, NTI], F32, "ss")
    rs, rsB = A.alloc([128, NTI], F32, "rs")
    x1B = [k.SB["X1"]] if l > 0 else []
    for i in range(NTI):
        r = 1 if i < 2 else 0
        x_, xB = xt[i % 2]
        n_, nB = xn[i % 2]
        P.op("sp", C("dma_start", out=x_, in_=src_tile(k, l, i)), reads=x1B, writes=[xB], dma=True)
        P.op("act", C("activation", out=junk, in_=x_, func=AF.Square, accum_out=ss[:, i:i + 1]),
             reads=[xB], writes=[junkB, ssB])
        P.op("dve", C("tensor_scalar", out=rs[:, i:i + 1], in0=ss[:, i:i + 1], scalar1=1.0 / D, scalar2=EPS, op0=ALU.mult, op1=ALU.add),
             reads=[ssB], writes=[rsB])
        P.op("act", C("activation", out=rs[:, i:i + 1], in_=rs[:, i:i + 1], func=AF.Sqrt), reads=[rsB], writes=[rsB])
        P.op("dve", C("reciprocal", out=rs[:, i:i + 1], in_=rs[:, i:i + 1]), reads=[rsB], writes=[rsB])
        P.op("dve", C("tensor_scalar", out=n_, in0=x_, scalar1=rs[:, i:i + 1], scalar2=None, op0=ALU.mult),
             reads=[xB, rsB], writes=[nB])
        import os
        KD = os.environ.get("KDBG", "")
        for half in range(2):
            if KD == "A":
                break
            bi = (2 * i + half) % 4
            pbf = k.pb[bi][:, :].bitcast(BF16)
            for j in range(8):
                kc = half * 8 + j
                P.op("pe", C("transpose", pbf[:, j * 128:(j + 1) * 128], n_[:, kc * 128:(kc + 1) * 128], k.idb),
                     reads=[nB, k.idbB], writes=[k.pbB[bi]])
            for j in range(8):
                kc = half * 8 + j
                if half == 0 or KD == "B":
                    P.op("dve", C("tensor_scalar",
                        out=hT[:, kc, i * 128:(i + 1) * 128], in0=pbf[:, j * 128:(j + 1) * 128],
                        scalar1=k.gcol[:, l, kc, r:r + 1], scalar2=k.modcol[:, l, kc, r:r + 1], op0=ALU.mult, op1=ALU.add),
                        reads=[k.pbB[bi], k.gcolB, k.modcolB], writes=[hTB], part=True)
                else:
                    P.op("act", C("activation",
                        out=hT[:, kc, i * 128:(i + 1) * 128], in_=pbf[:, j * 128:(j + 1) * 128], func=AF.Identity,
                        bias=k.modcol[:, l, kc, r:r + 1], scale=k.gcol[:, l, kc, r:r + 1]),
                        reads=[k.pbB[bi], k.gcolB, k.modcolB], writes=[hTB], part=True)
    P.barrier()
    A.reset(m1)
    if k.stop == ("norm", l):
        A.reset(m0)
        return
    W = [A.alloc([128, KC, 512], BF16, f"W{i}") for i in range(2)]
    m2 = A.mark()
    TS = []
    for ts_i in range(2):
        TS.append(dict(t1=A.alloc([128, 512], F32, f"t1{ts_i}"), t2=A.alloc([128, 512], F32, f"t2{ts_i}"), t3=A.alloc([128, 512], F32, f"t3{ts_i}"),
                       ta=A.alloc([128, 512], F32, f"ta{ts_i}"), tb=A.alloc([128, 512], F32, f"tb{ts_i}"), qf=A.alloc([128, 512], BF16, f"qf{ts_i}"),
                       ssq=A.alloc([128, 4], F32, f"ssq{ts_i}")))
    gts = [A.alloc([128, 2, 36], F32, f"gts{i}") for i in range(2)]
    for g_, gB_ in gts:
        P.op("pool", C("memset", g_, 0.0), writes=[gB_])
    rop = [A.alloc([128, 2, 8, 32], F32, f"rope{i}") for i in range(2)]
    qgb, qgbB = A.alloc([128, 128], F32, "qgb")
    kgb, kgbB = A.alloc([128, 128], F32, "kgb")
    qst = [A.alloc([128, 4, 256], BF16, f"qst{i}") for i in range(2)]
    vst = [A.alloc([128, 512], BF16, f"vst{i}") for i in range(2)]
    P.op("sp", C("dma_start", out=qgb, in_=I["qg"][l, :].partition_broadcast(128)), writes=[qgbB], dma=True)
    P.op("sp", C("dma_start", out=kgb, in_=I["kg"][l, :].partition_broadcast(128)), writes=[kgbB], dma=True)
    P.op("dve", C("tensor_scalar", out=qgb, in0=qgb, scalar1=float(128 ** -0.5), scalar2=None, op0=ALU.mult), reads=[qgbB], writes=[qgbB])
    wsrc = I["w_in"][l].rearrange("(kc p) c -> p kc c", p=128)
    groups = [("q", WCOL["aq"], 0), ("q", WCOL["aq"] + 512, 1), ("kv", WCOL["ak"], 0),
              ("mk", WCOL["mk"], 0), ("mk", WCOL["mk"] + 512, 1), ("mv", WCOL["mv"], 0), ("mv", WCOL["mv"] + 512, 1),
              ("mg", WCOL["mg"], 0)]
    nload = [0]

    def load_w(c0, ncol):
        w, wB = W[nload[0] % 2]
        nload[0] += 1
        P.op("pool", C("dma_start", out=w[:, :, 0:ncol], in_=wsrc[:, :, c0:c0 + ncol]), writes=[wB], dma=True)
        return w, wB

    def qk_post(ps, psB, nh, gb, gbB, rp, rpB, out, outB, T):
        Wd = nh * 128
        g = nh * 2
        (t1, t1B), (t2, t2B), (t3, t3B), (ta, taB), (tb, tbB), (ssq, ssqB) = T["t1"], T["t2"], T["t3"], T["ta"], T["tb"], T["ssq"]
        P.op("act", C("activation", out=t1[:, 0:Wd], in_=ps[:, 0:Wd], func=AF.Square), reads=[psB], writes=[t1B])
        P.op("dve", C("tensor_reduce", out=ssq[:, 0:nh], in_=t1[:, 0:Wd].rearrange("p (h d) -> p h d", d=128), axis=AX.X, op=ALU.add),
             reads=[t1B], writes=[ssqB])
        P.op("dve", C("tensor_scalar", out=ssq[:, 0:nh], in0=ssq[:, 0:nh], scalar1=1.0 / 128, scalar2=EPS, op0=ALU.mult, op1=ALU.add),
             reads=[ssqB], writes=[ssqB])
        P.op("act", C("activation", out=ssq[:, 0:nh], in_=ssq[:, 0:nh], func=AF.Sqrt), reads=[ssqB], writes=[ssqB])
        P.op("dve", C("reciprocal", out=ssq[:, 0:nh], in_=ssq[:, 0:nh]), reads=[ssqB], writes=[ssqB])
        P.op("dve", C("tensor_tensor", out=t2[:, 0:Wd].rearrange("p (h d) -> p h d", d=128), in0=ps[:, 0:Wd].rearrange("p (h d) -> p h d", d=128),
                                              in1=ssq[:, 0:nh].unsqueeze(2).to_broadcast([128, nh, 128]), op=ALU.mult),
             reads=[psB, ssqB], writes=[t2B])
        P.op("pool", C("tensor_tensor", out=t3[:, 0:Wd].rearrange("p (h d) -> p h d", d=128), in0=t2[:, 0:Wd].rearrange("p (h d) -> p h d", d=128),
                                               in1=gb.unsqueeze(1).to_broadcast([128, nh, 128]), op=ALU.mult),
             reads=[t2B, gbB], writes=[t3B])
        t3v = t3[:, 0:Wd].rearrange("p (g x j) -> p g x j", x=2, j=32)
        tav = ta[:, 0:Wd].rearrange("p (g x j) -> p g x j", x=2, j=32)
        tbv = tb[:, 0:Wd].rearrange("p (g x j) -> p g x j", x=2, j=32)
        ov = out.rearrange("p (g x j) -> p g x j", x=2, j=32)
        P.op("pool", C("tensor_tensor", out=tav, in0=t3v, in1=rp[:, 0, 0:g, :].unsqueeze(2).to_broadcast([128, g, 2, 32]), op=ALU.mult),
             reads=[t3B, rpB], writes=[taB])
        P.op("pool", C("tensor_tensor", out=tbv[:, :, 0, :], in0=t3v[:, :, 1, :], in1=rp[:, 1, 0:g, :], op=ALU.mult),
             reads=[t3B, rpB], writes=[tbB], part=True)
        P.op("pool", C("tensor_tensor", out=tbv[:, :, 1, :], in0=t3v[:, :, 0, :], in1=rp[:, 1, 0:g, :], op=ALU.mult),
             reads=[t3B, rpB], writes=[tbB], part=True)
        P.op("dve", C("tensor_tensor", out=ov[:, :, 0, :], in0=tav[:, :, 0, :], in1=tbv[:, :, 0, :], op=ALU.subtract),
             reads=[taB, tbB], writes=[outB], part=True)
        P.op("dve", C("tensor_tensor", out=ov[:, :, 1, :], in0=tav[:, :, 1, :], in1=tbv[:, :, 1, :], op=ALU.add),
             reads=[taB, tbB], writes=[outB], part=True)

    nps = [0]
    cur = load_w(groups[0][1], 512)
    for gi, (kind, c0, sub) in enumerate(groups):
        w, wB = cur
        if gi + 1 < len(groups):
            nk, nc0, _ = groups[gi + 1]
            cur = load_w(nc0, 16 if nk == "mg" else 512)
        ncol = 16 if kind == "mg" else 512
        pending = []
        for i in range(NTI + 1):
            if i == NTI:
                for f_ in pending:
                    f_()
                pending[:] = []
                break
            bi = nps[0] % 4
            nps[0] += 1
            ps, psB = k.pb[bi], k.pbB[bi]
            if kind in ("q", "kv"):
                rp, rpB = rop[i % 2]
                P.op("sp", C("dma_start", out=rp, in_=I["rope"][i].rearrange("p a (g j) -> p a g j", j=32)), writes=[rpB], dma=True)
            for kc in range(KC):
                P.op("pe", C("matmul", ps[:, 0:ncol], lhsT=hT[:, kc, i * 128:(i + 1) * 128], rhs=w[:, kc, 0:ncol],
                                                                               start=(kc == 0), stop=(kc == KC - 1)),
                     reads=[hTB, wB], writes=[psB])
            if kind == "q":
                T = TS[i % 2]
                qf, qfB = T["qf"]
                qk_post(ps, psB, 4, qgb, qgbB, rp, rpB, qf, qfB, T)
                def fin_q(i=i, qf=qf, qfB=qfB, sub=sub):
                    grp = i // 2
                    pos = i % 2
                    st, stB = qst[grp % 2]
                    tbi = 4 + (i % 2)
                    pbf = k.pb[tbi][:, :].bitcast(BF16)
                    for h in range(4):
                        P.op("pe", C("transpose", pbf[:, h * 128:(h + 1) * 128], qf[:, h * 128:(h + 1) * 128], k.idb),
                             reads=[qfB, k.idbB], writes=[k.pbB[tbi]])
                    P.op("act", C("activation", out=st[:, :, pos * 128:(pos + 1) * 128],
                                  in_=pbf[:, 0:512].rearrange("p (h t) -> p h t", t=128), func=AF.Copy),
                         reads=[k.pbB[tbi]], writes=[stB], part=True)
                    if pos == 1:
                        t0 = grp * 256
                        P.op("sp", C("dma_start", out=S["QT"].rearrange("(h d) t -> d h t", d=128)[:, sub * 4:(sub + 1) * 4, t0:t0 + 256], in_=st[:, :, 0:256]),
                             reads=[stB], writes=[k.SB["QT"]], dma=True, part=True, defer=True)
                for f_ in pending:
                    f_()
                pending[:] = [fin_q]
            elif kind == "kv":
                T = TS[i % 2]
                qf, qfB = T["qf"]
                qk_post(ps, psB, 2, kgb, kgbB, rp, rpB, qf[:, 0:256], qfB, T)
                def fin_k(i=i, qf=qf, qfB=qfB):
                    grp = i // 2
                    pos = i % 2
                    st, stB = qst[grp % 2]
                    tbi = 4 + (i % 2)
                    pbf = k.pb[tbi][:, :].bitcast(BF16)
                    for h in range(2):
                        P.op("pe", C("transpose", pbf[:, h * 128:(h + 1) * 128], qf[:, h * 128:(h + 1) * 128], k.idb),
                             reads=[qfB, k.idbB], writes=[k.pbB[tbi]])
                    P.op("act", C("activation", out=st[:, 0:2, pos * 128:(pos + 1) * 128],
                                  in_=pbf[:, 0:256].rearrange("p (h t) -> p h t", t=128), func=AF.Copy),
                         reads=[k.pbB[tbi]], writes=[stB], part=True)
                    if pos == 1:
                        t0 = grp * 256
                        P.op("sp", C("dma_start", out=S["KT"].rearrange("(h d) t -> d h t", d=128)[:, :, t0:t0 + 256], in_=st[:, 0:2, 0:256]),
                             reads=[stB], writes=[k.SB["KT"]], dma=True, part=True, defer=True)
                for f_ in pending:
                    f_()
                pending[:] = [fin_k]
                v_, vB = vst[i % 2]
                P.op("act", C("activation", out=v_[:, 0:256], in_=ps[:, 256:512], func=AF.Copy), reads=[psB], writes=[vB])
                P.op("sp", C("dma_start", out=S["Vt"][i * 128:(i + 1) * 128, :], in_=v_[:, 0:256]),
                     reads=[vB], writes=[k.SB["Vt"]], dma=True, part=True, defer=True)
            elif kind in ("mk", "mv"):
                v_, vB = vst[i % 2]
                sc = 0.0625 if kind == "mk" else 1.0
                P.op("act", C("activation", out=v_, in_=ps, func=AF.Copy, scale=sc), reads=[psB], writes=[vB])
                dst = S["MKt"] if kind == "mk" else S["MVt"]
                dB = k.SB["MKt"] if kind == "mk" else k.SB["MVt"]
                P.op("sp", C("dma_start", out=dst[i * 128:(i + 1) * 128, sub * 512:(sub + 1) * 512], in_=v_),
                     reads=[vB], writes=[dB], dma=True, part=True, defer=True)
            else:
                g_, gB_ = gts[i % 2]
                for gi in range(4):
                    P.op("dve", C("tensor_copy", out=g_[:, gi % 2, (gi // 2) * 32:(gi // 2) * 32 + 4], in_=ps[:, gi * 4:gi * 4 + 4]),
                         reads=[psB], writes=[gB_], part=(gi > 0))
                P.op("sp", C("dma_start", out=S["GT"][i], in_=g_.rearrange("p a b -> p (a b)")), reads=[gB_], writes=[k.SB["GT"]], dma=True, part=True, defer=True)
    P.barrier()
    A.reset(m2)
    if k.stop == ("tokmaj", l):
        A.reset(m0)
        return
    stg = [A.alloc([128, NT], BF16, f"stg{i}") for i in range(2)]
    for st_, stB_ in stg:
        P.op("pool", C("memset", st_, 0.0), writes=[stB_])
    glist = []
    for name, ncols, fn in FSEG:
        for g in range(ncols // 512):
            glist.append((name, WCOL[name] + g * 512, FROW[name] + g * 512, fn))
    cur = load_w(glist[0][1], 512)
    nst = 0
    for gi, (name, c0, r0, fn) in enumerate(glist):
        w, wB = cur
        if gi + 1 < len(glist):
            cur = load_w(glist[gi + 1][1], 512)
        for j in range(4):
            st, stB = stg[nst % 2]
            nst += 1
            fblks = BLKS[1:] if (l == DEPTH - 1 and name != "mk") else BLKS
            for (t0, n) in fblks:
                bi = nps[0] % 4
                nps[0] += 1
                ps, psB = k.pb[bi], k.pbB[bi]
                for kc in range(KC):
                    P.op("pe", C("matmul", ps[:, 0:n], lhsT=w[:, kc, j * 128:(j + 1) * 128], rhs=hT[:, kc, t0:t0 + n],
                                                                                    start=(kc == 0), stop=(kc == KC - 1)),
                         reads=[hTB, wB], writes=[psB])
                if fn == "silu":
                    P.op("act", C("activation", out=st[:, t0:t0 + n], in_=ps[:, 0:n], func=AF.Silu),
                         reads=[psB], writes=[stB], part=True)
                elif fn == "sig":
                    P.op("act", C("activation", out=st[:, t0:t0 + n], in_=ps[:, 0:n], func=AF.Sigmoid),
                         reads=[psB], writes=[stB], part=True)
                elif fn == "copy16":
                    P.op("dve", C("tensor_scalar", out=st[:, t0:t0 + n], in0=ps[:, 0:n], scalar1=0.0625, scalar2=None, op0=ALU.mult),
                         reads=[psB], writes=[stB], part=True)
                else:
                    P.op("dve", C("tensor_copy", out=st[:, t0:t0 + n], in_=ps[:, 0:n]),
                         reads=[psB], writes=[stB], part=True)
            P.op("sp", C("dma_start", out=S["F"][r0 + j * 128:r0 + (j + 1) * 128, :], in_=st),
                 reads=[stB], writes=[k.SB["F"]], dma=True, part=True, defer=True)
    P.barrier()
    A.reset(m0)


def phase_attn(k, l):
    P, A, I, S = k.P, k.A, k.I, k.S
    m0 = A.mark()
    KTs, KTB = A.alloc([128, 2, NT], BF16, "KTs")
    Vs, VB = A.alloc([128, NTI, 256], BF16, "Vs")
    Qb = [A.alloc([128, 8, 512], BF16, f"Qb{i}") for i in range(2)]
    AZ = [A.alloc([128, 8, 512], BF16, f"AZ{i}") for i in range(2)]
    PT = [A.alloc([128, 2, 512], BF16, f"PT{i}") for i in range(3)]
    rec, recB = A.alloc([128, 512], F32, "rec")
    t4, t4B = A.alloc([128, 512], F32, "t4")
    ost = [A.alloc([128, 8, 512], BF16, f"ost{i}") for i in range(2)]
    P.op("sp", C("dma_start", out=KTs, in_=S["KT"].rearrange("(h d) t -> d h t", d=128)), reads=[k.SB["KT"]], writes=[KTB], dma=True)
    P.op("sp", C("dma_start", out=Vs, in_=S["Vt"].rearrange("(i p) c -> p i c", p=128)), reads=[k.SB["Vt"]], writes=[VB], dma=True)
    blocks = []
    if l < DEPTH - 1:
        blocks.append((0, 256, [0, 1]))
    for j in range(8):
        blocks.append((256 + 512 * j, 512, list(range(NTI))))
    QTv = S["QT"].rearrange("(h d) t -> d h t", d=128)
    AZv = S["F"][FROW["az"]:FROW["az"] + 1024, :].rearrange("(h d) t -> d h t", d=128)
    Yv = S["Y"][0].rearrange("(h d) t -> d h t", d=128)
    ns = [0]
    npt = [0]
    for bi, (t0, n, keys) in enumerate(blocks):
        q_, qB = Qb[bi % 2]
        az_, azB = AZ[bi % 2]
        o_, oB = ost[bi % 2]
        P.op("sp", C("dma_start", out=q_[:, :, 0:n], in_=QTv[:, :, t0:t0 + n]), reads=[k.SB["QT"]], writes=[qB], dma=True)
        P.op("sp", C("dma_start", out=az_[:, :, 0:n], in_=AZv[:, :, t0:t0 + n]), reads=[k.SB["F"]], writes=[azB], dma=True)
        nk = len(keys)
        npair = nk // 2
        for h in range(8):
            kv = h // 4
            psO, psOB = k.pb[4 + (h % 2)], k.pbB[4 + (h % 2)]
            psD, psDB = k.pb[6 + (h % 2)], k.pbB[6 + (h % 2)]
            spair = []

            def emitS(pi):
                pr = ns[0] % 2
                ns[0] += 1
                spair.append(pr)
                for j in range(2):
                    kt = keys[2 * pi + j]
                    bb = 2 * pr + j
                    P.op("pe", C("matmul", k.pb[bb][:, 0:n], lhsT=KTs[:, kv, kt * 128:(kt + 1) * 128], rhs=q_[:, h, 0:n], start=True, stop=True),
                         reads=[KTB, qB], writes=[k.pbB[bb]])
            emitS(0)
            for pi in range(npair):
                if pi + 1 < npair:
                    emitS(pi + 1)
                pr = spair[pi]
                p_, pB = PT[npt[0] % 3]
                npt[0] += 1
                sv = k.pb2[pr].rearrange("p (b c) -> p b c", b=2)[:, :, 0:n]
                P.op("act", C("activation", out=p_[:, :, 0:n], in_=sv, func=AF.Exp), reads=[k.pbB[2 * pr], k.pbB[2 * pr + 1]], writes=[pB])
                for j in range(2):
                    kt = keys[2 * pi + j]
                    idx = 2 * pi + j
                    P.op("pe", C("matmul", psO[:, 0:n], lhsT=Vs[:, kt, kv * 128:(kv + 1) * 128], rhs=p_[:, j, 0:n],
                                 start=(idx == 0), stop=(idx == nk - 1)), reads=[VB, pB], writes=[psOB])
                for j in range(2):
                    idx = 2 * pi + j
                    P.op("pe", C("matmul", psD[:, 0:n], lhsT=k.onesb, rhs=p_[:, j, 0:n], start=(idx == 0), stop=(idx == nk - 1)),
                         reads=[k.onesbB, pB], writes=[psDB])
            P.op("dve", C("reciprocal", out=rec[:, 0:n], in_=psD[:, 0:n]), reads=[psDB], writes=[recB])
            P.op("dve", C("tensor_tensor", out=t4[:, 0:n], in0=psO[:, 0:n], in1=rec[:, 0:n], op=ALU.mult), reads=[psOB, recB], writes=[t4B])
            P.op("pool", C("tensor_tensor", out=o_[:, h, 0:n], in0=t4[:, 0:n], in1=az_[:, h, 0:n], op=ALU.mult),
                 reads=[t4B, azB], writes=[oB], part=True)
        P.op("sp", C("dma_start", out=Yv[:, :, t0:t0 + n], in_=o_[:, :, 0:n]), reads=[oB], writes=[k.SB["Y"]], dma=True, part=True, defer=True)
    P.barrier()
    A.reset(m0)


def phase_mlstm(k, l):
    P, A, I, S = k.P, k.A, k.I, k.S
    last_layer = (l == DEPTH - 1)
    m0 = A.mark()
    WC, WCB = A.alloc([128, NTI, 16], F32, "WC")
    DECB, DECBB = A.alloc([128, 8, NTI], F32, "DECB")
    m1 = A.mark()
    GI, GIB = A.alloc([64, NT], F32, "GI")
    GF, GFB = A.alloc([64, NT], F32, "GF")
    ONE, ONEB = A.alloc([64, NT], F32, "ONE")
    BP, BPB = A.alloc([64, NT], F32, "BP")
    AP_, APB = A.alloc([64, NT], F32, "APr")
    MM, MMB = A.alloc([64, NT], F32, "MM")
    M2, M2B = A.alloc([64, NT], F32, "M2")
    WR, WRB = A.alloc([64, NT], F32, "WR")
    CL, CLB = A.alloc([64, NT], F32, "CL")
    gb, gbB = A.alloc([64, 2], F32, "gb")
    Gtok, GtokB = A.alloc([128, NTI, 2, 36], F32, "Gtok")
    P.op("sp", C("dma_start", out=Gtok.rearrange("p i a b -> p i (a b)"), in_=S["GT"].rearrange("i p c -> p i c")), reads=[k.SB["GT"]], writes=[GtokB], dma=True)
    dec, decB = A.alloc([64, NTI], F32, "dec")
    sel, selB = A.alloc([64, 8, 128], F32, "sel")
    P.op("sp", C("dma_start", out=gb, in_=I["gbias"][l]), writes=[gbB], dma=True)
    P.op("sp", C("dma_start", out=sel, in_=I["sel"]), writes=[selB], dma=True)
    for t_, tB in ((GI, GIB), (GF, GFB), (dec, decB)):
        P.op("pool", C("memset", t_, 0.0), writes=[tB])
    P.op("pool", C("memset", ONE, 1.0), writes=[ONEB])
    R = (slice(0, 4), slice(32, 36))
    nb = 0
    for (t0, n) in BLKS:
        bt0 = (t0 - 256) if t0 >= 256 else 4096
        for gf, (dst, dstB) in enumerate(((GI, GIB), (GF, GFB))):
            bi = nb % 4
            nb += 1
            ps, psB = k.pb[bi], k.pbB[bi]
            for j in range(n // 128):
                i = t0 // 128 + j
                P.op("pe", C("transpose", ps[0:36, j * 128:(j + 1) * 128], Gtok[:, i, gf, :], k.idf),
                     reads=[GtokB, k.idfB], writes=[psB])
            P.op("act", C("activation", out=dst[0:4, t0:t0 + n], in_=ps[0:4, 0:n], func=AF.Identity,
                                                                               bias=gb[0:4, gf:gf + 1], scale=1.0),
                 reads=[psB, gbB], writes=[dstB], part=True)
            P.op("act", C("activation", out=dst[32:36, bt0:bt0 + n], in_=ps[32:36, 0:n], func=AF.Identity,
                                                                                 bias=gb[32:36, gf:gf + 1], scale=1.0),
                 reads=[psB, gbB], writes=[dstB], part=True)
    P.op("act", C("activation", out=GF[0:36, :], in_=GF[0:36, :], func=AF.Exp, scale=-1.0), reads=[GFB], writes=[GFB])
    P.op("act", C("activation", out=GF[0:36, :], in_=GF[0:36, :], func=AF.Ln, bias=1.0, scale=1.0), reads=[GFB], writes=[GFB])
    P.op("dve", C("tensor_tensor_scan", out=BP[0:36, :], data0=ONE[0:36, :], data1=GF[0:36, :], initial=0.0, op0=ALU.mult, op1=ALU.add),
         reads=[ONEB, GFB], writes=[BPB])
    P.op("dve", C("tensor_scalar", out=M2[32:36, :], in0=BP[32:36, :], scalar1=BP[32:36, NT - 1:NT], scalar2=-1.0, op0=ALU.subtract, op1=ALU.mult),
         reads=[BPB], writes=[M2B])
    P.op("dve", C("tensor_tensor", out=BP[32:36, :], in0=M2[32:36, :], in1=GF[32:36, :], op=ALU.add), reads=[M2B, GFB], writes=[BPB])
    P.op("dve", C("tensor_tensor", out=AP_[0:36, :], in0=GI[0:36, :], in1=BP[0:36, :], op=ALU.add), reads=[GIB, BPB], writes=[APB])
    P.op("dve", C("tensor_tensor_scan", out=MM[0:4, :], data0=ONE[0:4, :], data1=AP_[0:4, :], initial=-1e30, op0=ALU.mult, op1=ALU.max),
         reads=[ONEB, APB], writes=[MMB], part=True)
    src, srcB = AP_, APB
    bufs = [(M2, M2B), (MM, MMB)]
    sh = 1
    step = 0
    while sh < NT:
        dst, dstB = bufs[step % 2]
        P.op("dve", C("tensor_tensor", out=dst[32:36, 0:NT - sh], in0=src[32:36, 0:NT - sh], in1=src[32:36, sh:NT], op=ALU.max),
             reads=[srcB], writes=[dstB], part=True)
        P.op("pool", C("tensor_copy", out=dst[32:36, NT - sh:NT], in_=src[32:36, NT - sh:NT]),
             reads=[srcB], writes=[dstB], part=True)
        src, srcB = dst, dstB
        sh *= 2
        step += 1
    if src is not MM:
        P.op("dve", C("tensor_copy", out=MM[32:36, :], in_=src[32:36, :]), reads=[srcB], writes=[MMB], part=True)

    def v3(t_, r):
        return t_[r, :].rearrange("p (c t) -> p c t", t=128)
    for d, r in enumerate(R):
        li = 127 if d == 0 else 0
        mlast = v3(MM, r)[:, :, li:li + 1].to_broadcast([4, NTI, 128])
        P.op("dve", C("tensor_tensor", out=v3(WR, r), in0=v3(AP_, r), in1=mlast, op=ALU.subtract), reads=[APB, MMB], writes=[WRB], part=True)
        P.op("act", C("activation", out=WR[r, :], in_=WR[r, :], func=AF.Exp), reads=[WRB], writes=[WRB], part=True)
        P.op("dve", C("tensor_tensor", out=v3(CL, r), in0=v3(BP, r), in1=mlast, op=ALU.subtract), reads=[BPB, MMB], writes=[CLB], part=True)
        P.op("act", C("activation", out=CL[r, :], in_=CL[r, :], func=AF.Exp), reads=[CLB], writes=[CLB], part=True)
        ml2 = v3(MM, r)[:, :, li]
        if d == 0:
            P.op("dve", C("tensor_tensor", out=dec[r, 1:NTI], in0=ml2[:, 0:NTI - 1], in1=ml2[:, 1:NTI], op=ALU.subtract),
                 reads=[MMB], writes=[decB], part=True)
            P.op("act", C("activation", out=dec[r, 1:NTI], in_=dec[r, 1:NTI], func=AF.Exp), reads=[decB], writes=[decB], part=True)
        else:
            P.op("dve", C("tensor_tensor", out=dec[r, 0:NTI - 1], in0=ml2[:, 1:NTI], in1=ml2[:, 0:NTI - 1], op=ALU.subtract),
                 reads=[MMB], writes=[decB], part=True)
            P.op("act", C("activation", out=dec[r, 0:NTI - 1], in_=dec[r, 0:NTI - 1], func=AF.Exp), reads=[decB], writes=[decB], part=True)
    for q in range(8):
        P.op("pe", C("matmul", k.pb[0][:, q * NTI:(q + 1) * NTI], lhsT=sel[0:36, q, :], rhs=dec[0:36, :], start=True, stop=True),
             reads=[selB, decB], writes=[k.pbB[0]])
    P.op("dve", C("tensor_copy", out=DECB, in_=k.pb[0][:, 0:8 * NTI].rearrange("p (q c) -> p q c", c=NTI)), reads=[k.pbB[0]], writes=[DECBB])
    for half in range(2):
        tiles = list(range(half * 17, half * 17 + 17))
        ps, psB = k.pb[1 + half], k.pbB[1 + half]
        for jj, i in enumerate(tiles):
            fc = i * 128
            bc = (i - 2) * 128 if i >= 2 else 4096 + i * 128
            for qq, (src, srcB, r, c0) in enumerate(((WR, WRB, R[0], fc), (WR, WRB, R[1], bc), (CL, CLB, R[0], fc), (CL, CLB, R[1], bc))):
                P.op("pe", C("transpose", ps[:, jj * 16 + qq * 4: jj * 16 + qq * 4 + 4], src[r, c0:c0 + 128], k.idf[r, r]),
                     reads=[srcB, k.idfB], writes=[psB])
        P.op("dve", C("tensor_copy", out=WC[:, half * 17:half * 17 + 17, :], in_=ps[:, 0:17 * 16].rearrange("p (i q) -> p i q", q=16)),
             reads=[psB], writes=[WCB], part=True)
    P.barrier()
    A.reset(m1)
    msk, mskB = A.alloc([128, 2, 128], F32, "msk")
    mgb, mgbB = A.alloc([128, 1024], F32, "mgb")
    P.op("sp", C("dma_start", out=msk, in_=I["masks"].rearrange("m s t -> s m t")), writes=[mskB], dma=True)
    P.op("sp", C("dma_start", out=mgb, in_=I["mgain"][l, :].partition_broadcast(128)), writes=[mgbB], dma=True)
    Fq = S["F"][FROW["mq"]:FROW["mq"] + 1024, :].rearrange("(a p) t -> p a t", p=128)
    Fk = S["F"][FROW["mk"]:FROW["mk"] + 1024, :].rearrange("(a p) t -> p a t", p=128)
    Fo = S["F"][FROW["mo"]:FROW["mo"] + 1024, :].rearrange("(a p) t -> p a t", p=128)
    Fz = S["F"][FROW["mz"]:FROW["mz"] + 1024, :].rearrange("(a p) t -> p a t", p=128)
    Yb = S["Y"][1].rearrange("(a p) t -> p a t", p=128)
    HB_ = [[Buf(f"H{d}_{i}") for i in range(NTI)] for d in range(2)]
    BD = []
    for d in range(2):
        b = K()
        b.Cf, _ = A.alloc([128, 4, 2, 257], F32, f"Cf{d}")
        b.Ct, _ = A.alloc([128, 4, 2, 257], BF16, f"Ct{d}")
        b.CfBs = [Buf(f"Cf{d}{h}") for h in range(4)]
        b.CtBs = [Buf(f"Ct{d}{h}") for h in range(4)]
        b.qT = [A.alloc([128, 8, 128], BF16, f"qT{d}{i}") for i in range(2)]
        b.kT = [A.alloc([128, 8, 128], BF16, f"kT{d}{i}") for i in range(2)]
        b.ktk = [A.alloc([128, 1024], BF16, f"ktk{d}{i}") for i in range(2)]
        b.vtk = [A.alloc([128, 4, 257], BF16, f"vtk{d}{i}") for i in range(2)]
        b.moT = [A.alloc([128, 8, 128], BF16, f"moT{d}{i}") for i in range(2)]
        b.mzT = [A.alloc([128, 8, 128], BF16, f"mzT{d}{i}") for i in range(2)]
        b.hfl = [A.alloc([128, 1024], BF16, f"hfl{d}{i}") for i in range(2)]
        b.Sm = [A.alloc([128, 128], BF16, f"Sm{d}{i}") for i in range(2)]
        b.vw = [A.alloc([128, 257], BF16, f"vw{d}{i}") for i in range(2)]
        b.hst = [A.alloc([128, 4, 256], BF16, f"hst{d}{i}") for i in range(2)]
        b.hs, b.hsB = A.alloc([128, 4, 256], F32, f"hs{d}")
        b.hj, b.hjB = A.alloc([128, 1024], F32, f"hj{d}")
        b.hb, b.hbB = A.alloc([128, 1024], BF16, f"hb{d}")
        b.hss, b.hssB = A.alloc([128, 4], F32, f"hss{d}")
        b.dn = [A.alloc([128, 2], F32, f"dn{d}{h}") for h in range(4)]
        b.tT, b.tTB = A.alloc([128, 8, 128], F32, f"tT{d}")
        b.yst = [A.alloc([128, 8, 128], BF16, f"yst{d}{i}") for i in range(2)]
        b.cnt = dict(sm=0, vw=0, y=0)
        for v_, vB in b.vtk:
            P.op("pool", C("memset", v_, 1.0), writes=[vB])
        BD.append(b)
    orders = [list(range(NTI)), [1, 0] + list(range(NTI - 1, 1, -1))]

    def chunk(d, step, i):
        b = BD[d]
        is_ctx = i < 2
        need_out = not (is_ctx and last_layer)
        if is_ctx:
            first = (i == 0) if d == 0 else (i == 1)
        else:
            first = (i <= 17) if d == 0 else (i > 17)
        combine = need_out and not first
        cidx = i if d == 0 else ((i - 2) if i >= 2 else 32 + i)
        sl = slice(i * 128, (i + 1) * 128)
        q_, qB = b.qT[step % 2]
        k_, kB = b.kT[step % 2]
        kt_, ktB = b.ktk[step % 2]
        v_, vB = b.vtk[step % 2]
        P.op("sp", C("dma_start", out=q_, in_=Fq[:, :, sl]), reads=[k.SB["F"]], writes=[qB], dma=True)
        P.op("sp", C("dma_start", out=k_, in_=Fk[:, :, sl]), reads=[k.SB["F"]], writes=[kB], dma=True)
        P.op("sp", C("dma_start", out=kt_, in_=S["MKt"][sl, :]), reads=[k.SB["MKt"]], writes=[ktB], dma=True)
        P.op("sp", C("dma_start", out=v_[:, :, 0:256], in_=S["MVt"][sl, :].rearrange("t (h e) -> t h e", e=256)),
             reads=[k.SB["MVt"]], writes=[vB], dma=True, part=True)
        if combine:
            o_, oB = b.moT[step % 2]
            z_, zB = b.mzT[step % 2]
            f_, fB = b.hfl[step % 2]
            P.op("sp", C("dma_start", out=o_, in_=Fo[:, :, sl]), reads=[k.SB["F"]], writes=[oB], dma=True)
            P.op("sp", C("dma_start", out=z_, in_=Fz[:, :, sl]), reads=[k.SB["F"]], writes=[zB], dma=True)
            P.op("sp", C("dma_start", out=f_, in_=S["HF"][1 - d][sl, :]), reads=[HB_[1 - d][i]], writes=[fB], dma=True)
        yield
        h_, hB = b.hst[step % 2]
        bS, bP, bU = d, 2 + d, 4 + 2 * d
        psS, psSB = k.pb[bS], k.pbB[bS]
        psP, psPB = k.pb[bP], k.pbB[bP]
        for hh in range(4):
            qi = d * 4 + hh
            for dc in range(2):
                P.op("pe", C("matmul", psS[:, 0:128], lhsT=k_[:, hh * 2 + dc, :], rhs=q_[:, hh * 2 + dc, :], start=(dc == 0), stop=(dc == 1)),
                     reads=[kB, qB], writes=[psSB])
            yield
            sm_, smB = b.Sm[b.cnt["sm"] % 2]
            b.cnt["sm"] += 1
            P.op("dve", C("tensor_tensor", out=sm_, in0=psS[:, 0:128], in1=msk[:, d, :], op=ALU.mult), reads=[psSB, mskB], writes=[smB])
            vw_, vwB = b.vw[b.cnt["vw"] % 2]
            b.cnt["vw"] += 1
            P.op("act", C("activation", out=vw_, in_=v_[:, hh, :], func=AF.Copy, scale=WC[:, i, qi:qi + 1]),
                 reads=[vB, WCB], writes=[vwB])
            if step > 0:
                P.op("act", C("activation", out=b.Ct[:, hh], in_=b.Cf[:, hh], func=AF.Copy, scale=DECB[:, qi, cidx:cidx + 1]),
                     reads=[b.CfBs[hh], DECBB], writes=[b.CtBs[hh]])
            yield
            P.op("pe", C("matmul", psP[:, 0:257], lhsT=sm_, rhs=vw_, start=True, stop=(step == 0)), reads=[smB, vwB], writes=[psPB])
            if step > 0:
                for dc in range(2):
                    P.op("pe", C("matmul", psP[:, 0:257], lhsT=q_[:, hh * 2 + dc, :], rhs=b.Ct[:, hh, dc, :], start=False, stop=(dc == 1)),
                         reads=[qB, b.CtBs[hh]], writes=[psPB])
            for dc in range(2):
                psU, psUB = k.pb[bU + dc], k.pbB[bU + dc]
                P.op("pe", C("matmul", psU[:, 0:257], lhsT=kt_[:, hh * 256 + dc * 128: hh * 256 + (dc + 1) * 128], rhs=vw_, start=True, stop=True),
                     reads=[ktB, vwB], writes=[psUB])
                if step == 0:
                    P.op("dve", C("tensor_copy", out=b.Cf[:, hh, dc, :], in_=psU[:, 0:257]), reads=[psUB], writes=[b.CfBs[hh]], part=(dc == 1))
                else:
                    P.op("dve", C("scalar_tensor_tensor", out=b.Cf[:, hh, dc, :], in0=b.Cf[:, hh, dc, :], scalar=DECB[:, qi, cidx:cidx + 1], in1=psU[:, 0:257],
                                  op0=ALU.mult, op1=ALU.add), reads=[psUB, b.CfBs[hh], DECBB], writes=[b.CfBs[hh]], part=(dc == 1))
            yield
            if need_out:
                dn, dnB = b.dn[hh]
                P.op("dve", C("tensor_scalar", out=dn[:, 1:2], in0=psP[:, 256:257], scalar1=WC[:, i, 8 + qi:9 + qi], scalar2=None, op0=ALU.max),
                     reads=[psPB, WCB], writes=[dnB])
                P.op("dve", C("scalar_tensor_tensor", out=dn[:, 0:1], in0=psP[:, 256:257], scalar=-1.0, in1=dn[:, 1:2], op0=ALU.mult, op1=ALU.max),
                     reads=[psPB, dnB], writes=[dnB])
                P.op("dve", C("reciprocal", out=dn[:, 1:2], in_=dn[:, 0:1]), reads=[dnB], writes=[dnB])
                if not combine:
                    P.op("act", C("activation", out=h_[:, hh, :], in_=psP[:, 0:256], func=AF.Copy, scale=dn[:, 1:2]),
                         reads=[psPB, dnB], writes=[hB], part=(hh > 0))
                else:
                    P.op("dve", C("scalar_tensor_tensor", out=b.hs[:, hh, :], in0=psP[:, 0:256], scalar=dn[:, 1:2],
                                  in1=f_[:, hh * 256:(hh + 1) * 256], op0=ALU.mult, op1=ALU.add),
                         reads=[psPB, dnB, fB], writes=[b.hsB], part=(hh > 0))
        if need_out and not combine:
            P.op("sp", C("dma_start", out=S["HF"][d][sl, :], in_=h_.rearrange("p h e -> p (h e)")), reads=[hB], writes=[HB_[d][i]], dma=True, defer=True)
        yield
        if combine:
            hs2 = b.hs.rearrange("p h e -> p (h e)")
            P.op("act", C("activation", out=b.hj, in_=hs2, func=AF.Square), reads=[b.hsB], writes=[b.hjB])
            P.op("dve", C("tensor_reduce", out=b.hss, in_=b.hj.rearrange("p (h e) -> p h e", e=256), axis=AX.X, op=ALU.add), reads=[b.hjB], writes=[b.hssB])
            P.op("dve", C("tensor_scalar", out=b.hss, in0=b.hss, scalar1=1.0 / 256, scalar2=EPS, op0=ALU.mult, op1=ALU.add), reads=[b.hssB], writes=[b.hssB])
            P.op("act", C("activation", out=b.hss, in_=b.hss, func=AF.Sqrt), reads=[b.hssB], writes=[b.hssB])
            P.op("dve", C("reciprocal", out=b.hss, in_=b.hss), reads=[b.hssB], writes=[b.hssB])
            hn = b.hj.rearrange("p (h e) -> p h e", e=256)
            P.op("dve", C("tensor_tensor", out=hn, in0=b.hs, in1=b.hss.unsqueeze(2).to_broadcast([128, 4, 256]), op=ALU.mult), reads=[b.hsB, b.hssB, b.hjB], writes=[b.hjB])
            P.op("pool", C("tensor_tensor", out=b.hb, in0=b.hj, in1=mgb, op=ALU.mult), reads=[b.hjB, mgbB], writes=[b.hbB])
            yield
            pbf = psP.bitcast(BF16)
            for cc in range(8):
                P.op("pe", C("transpose", pbf[:, cc * 128:(cc + 1) * 128], b.hb[:, cc * 128:(cc + 1) * 128], k.idb),
                     reads=[b.hbB, k.idbB], writes=[psPB])
            P.op("dve", C("tensor_tensor", out=b.tT, in0=pbf.rearrange("p (a t) -> p a t", t=128), in1=o_, op=ALU.mult),
                 reads=[psPB, oB], writes=[b.tTB])
            y_, yB = b.yst[b.cnt["y"] % 2]
            b.cnt["y"] += 1
            P.op("pool", C("tensor_tensor", out=y_, in0=b.tT, in1=z_, op=ALU.mult), reads=[b.tTB, zB], writes=[yB])
            P.op("sp", C("dma_start", out=Yb[:, :, sl], in_=y_), reads=[yB], writes=[k.SB["Y"]], dma=True, part=True, defer=True)

    for step in range(NTI):
        gens = [chunk(d, step, orders[d][step]) for d in range(2)]
        while gens:
            for g in list(gens):
                try:
                    next(g)
                except StopIteration:
                    gens.remove(g)
    P.barrier()
    A.reset(m0)


def phase_conv_pool(k, l):
    P, A, I, S = k.P, k.A, k.I, k.S
    m0 = A.mark()
    cw, cwB = A.alloc([128, 8, 3], F32, "cw")
    P.op("sp", C("dma_start", out=cw, in_=I["convw"][l]), writes=[cwB], dma=True)
    inb = [[A.alloc([128, NT], BF16, f"cv{j}_{i}") for j in range(4)] for i in range(2)]
    ap_, apB = A.alloc([128, NT + 2], F32, "apad")
    y_, yB = A.alloc([128, NT], F32, "ycv")
    y2, y2B = A.alloc([128, NT], F32, "ycv2")
    ost = [A.alloc([128, NT], BF16, f"cvo{i}") for i in range(2)]
    P.op("pool", C("memset", ap_, 0.0), writes=[apB])
    names = ("cu", "cc", "cb", "cz")
    for cc in range(8):
        tl = inb[cc % 2]
        for j, nm in enumerate(names):
            t_, tB = tl[j]
            r0 = FROW[nm] + cc * 128
            P.op("sp", C("dma_start", out=t_, in_=S["F"][r0:r0 + 128, :]), reads=[k.SB["F"]], writes=[tB], dma=True)
        (cu, cuB), (cg, cgB), (cb, cbB), (cz, czB) = tl
        w0, w1, w2 = cw[:, cc, 0:1], cw[:, cc, 1:2], cw[:, cc, 2:3]
        P.op("pool", C("tensor_tensor", out=ap_[:, 1:NT + 1], in0=cu, in1=cg, op=ALU.mult), reads=[cuB, cgB], writes=[apB])
        P.op("dve", C("tensor_scalar", out=y_, in0=ap_[:, 1:NT + 1], scalar1=w1, scalar2=None, op0=ALU.mult), reads=[apB, cwB], writes=[yB])
        P.op("dve", C("scalar_tensor_tensor", out=y_, in0=ap_[:, 0:NT], scalar=w0, in1=y_, op0=ALU.mult, op1=ALU.add), reads=[apB, cwB, yB], writes=[yB])
        P.op("dve", C("scalar_tensor_tensor", out=y_, in0=ap_[:, 2:NT + 2], scalar=w2, in1=y_, op0=ALU.mult, op1=ALU.add), reads=[apB, cwB, yB], writes=[yB])
        P.op("dve", C("tensor_scalar", out=y_[:, 255:256], in0=ap_[:, 255:256], scalar1=w0, scalar2=None, op0=ALU.mult), reads=[apB, cwB, yB], writes=[yB])
        P.op("dve", C("scalar_tensor_tensor", out=y_[:, 255:256], in0=ap_[:, 256:257], scalar=w1, in1=y_[:, 255:256], op0=ALU.mult, op1=ALU.add), reads=[apB, cwB, yB], writes=[yB])
        P.op("dve", C("tensor_scalar", out=y_[:, 256:257], in0=ap_[:, 257:258], scalar1=w1, scalar2=None, op0=ALU.mult), reads=[apB, cwB, yB], writes=[yB])
        P.op("dve", C("scalar_tensor_tensor", out=y_[:, 256:257], in0=ap_[:, 258:259], scalar=w2, in1=y_[:, 256:257], op0=ALU.mult, op1=ALU.add), reads=[apB, cwB, yB], writes=[yB])
        P.op("pool", C("tensor_tensor", out=y2, in0=y_, in1=cb, op=ALU.mult), reads=[yB, cbB], writes=[y2B])
        o_, oB = ost[cc % 2]
        P.op("pool", C("tensor_tensor", out=o_, in0=y2, in1=cz, op=ALU.mult), reads=[y2B, czB], writes=[oB])
        P.op("sp", C("dma_start", out=S["Y"][2][cc * 128:(cc + 1) * 128, :], in_=o_), reads=[oB], writes=[k.SB["Y"]], dma=True, part=True, defer=True)
    P.barrier()
    A.reset(m0)
    OC, OL = 8, 8 + 256 + 16
    PW = OL + NL + 16
    psc, pscB = A.alloc([128, 8], F32, "psc")
    P.op("sp", C("dma_start", out=psc, in_=I["pscale"][l]), writes=[pscB], dma=True)
    pub = [A.alloc([128, NT], BF16, f"pu{i}") for i in range(2)]
    pzb = [A.alloc([128, NT], BF16, f"pz{i}") for i in range(2)]
    up = [A.alloc([128, PW], F32, f"up{i}") for i in range(2)]
    sa, saB = A.alloc([128, PW], F32, "sa")
    sb_, sbB = A.alloc([128, PW], F32, "sb")
    rcb, rcbB = A.alloc([128, NT], F32, "rcb")
    dT = [A.alloc([128, NT], BF16, f"dT{i}") for i in range(2)]
    pw = [A.alloc([128, 2, 256], BF16, f"pw{i}") for i in range(2)]
    yo = [A.alloc([128, NT], BF16, f"ypo{i}") for i in range(2)]
    for u_, uB in up:
        P.op("pool", C("memset", u_, 0.0), writes=[uB])
    P.op("pool", C("memset", sa, 0.0), writes=[saB])
    P.op("pool", C("memset", sb_, 0.0), writes=[sbB])
    nps = 0
    for g, w in enumerate((2, 4, 8, 16)):
        P.op("sp", C("dma_start", out=rcb, in_=I["rcnt"][g, :].partition_broadcast(128)), writes=[rcbB], dma=True)
        pw_, pwB = pw[g % 2]
        P.op("pool", C("dma_start", out=pw_, in_=I["pool_w"][l, g].rearrange("(kc p) o -> p kc o", p=128)), writes=[pwB], dma=True)
        for kc2 in range(2):
            ct = g * 2 + kc2
            pu_, puB = pub[kc2]
            u_, uB = up[kc2]
            d_, dB = dT[kc2]
            P.op("sp", C("dma_start", out=pu_, in_=S["F"][FROW["pu"] + ct * 128:FROW["pu"] + (ct + 1) * 128, :]), reads=[k.SB["F"]], writes=[puB], dma=True)
            P.op("pool", C("tensor_copy", out=u_[:, OC:OC + NC], in_=pu_[:, 0:NC]), reads=[puB], writes=[uB], part=True)
            P.op("pool", C("tensor_copy", out=u_[:, OL:OL + NL], in_=pu_[:, NC:NT]), reads=[puB], writes=[uB], part=True)
            cur, curB = u_, uB
            m = 1
            pp = [(sa, saB), (sb_, sbB)]
            si = 0
            while m < w:
                nx, nxB = pp[si % 2]
                si += 1
                P.op("dve", C("tensor_tensor", out=nx[:, 0:PW - m], in0=cur[:, 0:PW - m], in1=cur[:, m:PW], op=ALU.add), reads=[curB], writes=[nxB])
                cur, curB = nx, nxB
                m *= 2
            hw_ = w // 2
            for (po, to, n) in ((OC, 0, NC), (OL, NC, NL)):
                P.op("dve", C("tensor_tensor", out=sa[:, po:po + n] if cur is not sa else sb_[:, po:po + n],
                                                                                        in0=cur[:, po - hw_:po - hw_ + n], in1=rcb[:, to:to + n], op=ALU.mult),
                     reads=[curB, rcbB], writes=[saB if cur is not sa else sbB])
                tmpb, tmpB = (sa, saB) if cur is not sa else (sb_, sbB)
                P.op("pool", C("tensor_tensor", out=d_[:, to:to + n], in0=tmpb[:, po:po + n], in1=u_[:, po:po + n], op=ALU.subtract),
                     reads=[tmpB, uB], writes=[dB], part=True)
        for oc in range(2):
            ct = g * 2 + oc
            pz_, pzB = pzb[oc]
            o_, oB = yo[oc]
            P.op("sp", C("dma_start", out=pz_, in_=S["F"][FROW["pz"] + ct * 128:FROW["pz"] + (ct + 1) * 128, :]), reads=[k.SB["F"]], writes=[pzB], dma=True)
            for (t0, n) in BLKS:
                bi = nps % 4
                nps += 1
                ps, psB = k.pb[bi], k.pbB[bi]
                for kc2 in range(2):
                    P.op("pe", C("matmul", ps[:, 0:n], lhsT=pw_[:, kc2, oc * 128:(oc + 1) * 128], rhs=dT[kc2][0][:, t0:t0 + n],
                                                                                          start=(kc2 == 0), stop=(kc2 == 1)), reads=[pwB, dT[kc2][1]], writes=[psB])
                P.op("dve", C("scalar_tensor_tensor", out=o_[:, t0:t0 + n], in0=ps[:, 0:n], scalar=psc[:, ct:ct + 1], in1=pz_[:, t0:t0 + n],
                                                                                                 op0=ALU.mult, op1=ALU.mult), reads=[psB, pscB, pzB], writes=[oB], part=True)
            P.op("sp", C("dma_start", out=S["Y"][3][ct * 128:(ct + 1) * 128, :], in_=o_), reads=[oB], writes=[k.SB["Y"]], dma=True, part=True, defer=True)
    P.barrier()
    A.reset(m0)


def phase_merge_out(k, l):
    P, A, I, S = k.P, k.A, k.I, k.S
    last_layer = (l == DEPTH - 1)
    blocks = BLKS[1:] if last_layer else BLKS
    m0 = A.mark()
    wbr, _ = A.alloc([128, 16, 4 * 8 * 128], BF16, "wbr")
    wbrB = [Buf(f"wbr{ct}") for ct in range(16)]
    Yb, YbB = A.alloc([128, 4, 8, 512], BF16, "Yb")
    gm = [A.alloc([128, 4, 512], BF16, f"gm{i}") for i in range(2)]
    tm = [A.alloc([128, 512], F32, f"tm{i}") for i in range(4)]
    ast = [A.alloc([128, 512], BF16, f"ast{i}") for i in range(2)]
    for ct in range(16):
        P.op("pool", C("dma_start", out=wbr[:, ct, :], in_=I["wbt"][l, ct]), writes=[wbrB[ct]], dma=True)
    Yv = S["Y"].rearrange("b (kc p) t -> p b kc t", p=128)
    Gv = S["F"][FROW["gm"]:FROW["gm"] + 4 * D, :].rearrange("(b c p) t -> p b c t", p=128, c=16)
    nset = 0
    for (t0, n) in blocks:
        P.op("sp", C("dma_start", out=Yb[:, :, :, 0:n], in_=Yv[:, :, :, t0:t0 + n]), reads=[k.SB["Y"]], writes=[YbB], dma=True)
        for ct in range(16):
            g_, gB = gm[ct % 2]
            a_, aB = ast[ct % 2]
            w_ = wbr[:, ct, :].rearrange("p (b k c) -> p b k c", b=4, k=8)
            P.op("sp", C("dma_start", out=g_[:, :, 0:n], in_=Gv[:, :, ct, t0:t0 + n]), reads=[k.SB["F"]], writes=[gB], dma=True)
            base = 4 * (nset % 2)
            nset += 1
            for br in range(4):
                ps, psB = k.pb[base + br], k.pbB[base + br]
                for kc in range(8):
                    P.op("pe", C("matmul", ps[:, 0:n], lhsT=w_[:, br, kc, :], rhs=Yb[:, br, kc, 0:n], start=(kc == 0), stop=(kc == 7)),
                         reads=[wbrB[ct], YbB], writes=[psB])
                P.op("dve", C("tensor_tensor", out=tm[br][0][:, 0:n], in0=ps[:, 0:n], in1=g_[:, br, 0:n], op=ALU.mult),
                     reads=[psB, gB], writes=[tm[br][1]])
            P.op("pool", C("tensor_tensor", out=tm[0][0][:, 0:n], in0=tm[0][0][:, 0:n], in1=tm[1][0][:, 0:n], op=ALU.add), reads=[tm[0][1], tm[1][1]], writes=[tm[0][1]])
            P.op("pool", C("tensor_tensor", out=tm[2][0][:, 0:n], in0=tm[2][0][:, 0:n], in1=tm[3][0][:, 0:n], op=ALU.add), reads=[tm[2][1], tm[3][1]], writes=[tm[2][1]])
            P.op("pool", C("tensor_tensor", out=a_[:, 0:n], in0=tm[0][0][:, 0:n], in1=tm[2][0][:, 0:n], op=ALU.add), reads=[tm[0][1], tm[2][1]], writes=[aB])
            P.op("sp", C("dma_start", out=S["ACC"][ct * 128:(ct + 1) * 128, t0:t0 + n], in_=a_[:, 0:n]), reads=[aB], writes=[k.SB["ACC"]], dma=True, part=True, defer=True)
    P.barrier()
    A.reset(m0)
    wo, _ = A.alloc([128, KC, D], BF16, "wo")
    woB = [Buf(f"wo{cg}") for cg in range(4)]
    accT = [A.alloc([128, 16, 512], BF16, f"accT{i}") for i in range(2)]
    xb = [A.alloc([128, 4, D], F32, f"xblk{i}") for i in range(2)]
    gtb = [A.alloc([128, D], F32, f"gtb{i}") for i in range(2)]
    t5, t5B = A.alloc([128, 512], F32, "t5")
    ss, ssB = A.alloc([128, 4], F32, "fss")
    junk, junkB = A.alloc([128, D], BF16, "junkf")
    wov = I["w_out"][l].rearrange("(kc p) c -> p kc c", p=128)
    for cg in range(4):
        P.op("pool", C("dma_start", out=wo[:, :, cg * 512:(cg + 1) * 512], in_=wov[:, :, cg * 512:(cg + 1) * 512]), writes=[woB[cg]], dma=True)
    if last_layer:
        fgb, fgbB = A.alloc([128, D], F32, "fgb")
        P.op("sp", C("dma_start", out=fgb, in_=I["fgain"].partition_broadcast(128)), writes=[fgbB], dma=True)
    for r in range(2):
        P.op("sp", C("dma_start", out=gtb[r][0], in_=S["modd"][l, r, 2 * D:3 * D].partition_broadcast(128)), reads=[k.SB["modd"]], writes=[gtb[r][1]], dma=True)
    Av = S["ACC"].rearrange("(kc p) t -> p kc t", p=128)
    x1B = [k.SB["X1"]] if l > 0 else []
    nps = 0
    for bi, (t0, n) in enumerate(blocks):
        r = 1 if t0 < 256 else 0
        nti = n // 128
        a_, aB = accT[bi % 2]
        xblk, xblkB = xb[bi % 2]
        P.op("sp", C("dma_start", out=a_[:, :, 0:n], in_=Av[:, :, t0:t0 + n]), reads=[k.SB["ACC"]], writes=[aB], dma=True)
        for ti in range(nti):
            i = t0 // 128 + ti
            P.op("sp", C("dma_start", out=xblk[:, ti, :], in_=src_tile(k, l, i)), reads=x1B, writes=[xblkB], dma=True, part=(ti > 0))
        for ti in range(nti):
            i = t0 // 128 + ti
            for cg in range(4):
                b_ = nps % 8
                nps += 1
                ps, psB = k.pb[b_], k.pbB[b_]
                for kc in range(KC):
                    P.op("pe", C("matmul", ps, lhsT=a_[:, kc, ti * 128:(ti + 1) * 128], rhs=wo[:, kc, cg * 512:(cg + 1) * 512], start=(kc == 0), stop=(kc == KC - 1)),
                         reads=[aB, woB[cg]], writes=[psB])
                P.op("dve", C("tensor_tensor", out=t5, in0=ps, in1=gtb[r][0][:, cg * 512:(cg + 1) * 512], op=ALU.mult), reads=[psB, gtb[r][1]], writes=[t5B])
                P.op("pool", C("tensor_tensor", out=xblk[:, ti, cg * 512:(cg + 1) * 512], in0=xblk[:, ti, cg * 512:(cg + 1) * 512], in1=t5, op=ALU.add),
                     reads=[t5B, xblkB], writes=[xblkB], part=True)
            if not last_layer:
                P.op("sp", C("dma_start", out=S["X1"][i * 128:(i + 1) * 128, :], in_=xblk[:, ti, :]), reads=[xblkB], writes=[k.SB["X1"]], dma=True, part=True, defer=True)
            else:
                P.op("act", C("activation", out=junk, in_=xblk[:, ti, :], func=AF.Square, accum_out=ss[:, ti:ti + 1]), reads=[xblkB], writes=[junkB, ssB])
                P.op("dve", C("tensor_scalar", out=ss[:, ti:ti + 1], in0=ss[:, ti:ti + 1], scalar1=1.0 / D, scalar2=EPS, op0=ALU.mult, op1=ALU.add), reads=[ssB], writes=[ssB])
                P.op("act", C("activation", out=ss[:, ti:ti + 1], in_=ss[:, ti:ti + 1], func=AF.Sqrt), reads=[ssB], writes=[ssB])
                P.op("dve", C("reciprocal", out=ss[:, ti:ti + 1], in_=ss[:, ti:ti + 1]), reads=[ssB], writes=[ssB])
                P.op("dve", C("scalar_tensor_tensor", out=xblk[:, ti, :], in0=xblk[:, ti, :], scalar=ss[:, ti:ti + 1], in1=fgb, op0=ALU.mult, op1=ALU.mult),
                     reads=[xblkB, ssB, fgbB], writes=[xblkB], part=True)
                P.op("sp", C("dma_start", out=S["out"][(i - 2) * 128:(i - 1) * 128, :], in_=xblk[:, ti, :]), reads=[xblkB], writes=[k.SB["out"]], dma=True, part=True, defer=True)
    P.barrier()
    A.reset(m0)


def host_constants():
    C = {}
    C["ident"] = np.eye(128, dtype=np.float32)
    s = np.arange(128)[:, None]
    t = np.arange(128)[None, :]
    C["masks"] = np.stack([(s <= t), (s >= t)]).astype(np.float32)
    freq = (10000.0 ** (-np.arange(32, dtype=np.float32) / 32)).astype(np.float32)
    rope = np.zeros((NTI, 128, 2, 8, 32), np.float32)
    rope[:, :, 0] = 1.0
    for i in range(2, NTI):
        tt = (i - 2) * 128 + np.arange(128)
        row = (tt // 64).astype(np.float32)
        col = (tt % 64).astype(np.float32)
        for half, pos in enumerate((row, col)):
            ang = (pos[:, None] * freq[None, :]).astype(np.float32)
            for h in range(4):
                rope[i, :, 0, h * 2 + half, :] = np.cos(ang)
                rope[i, :, 1, h * 2 + half, :] = np.sin(ang)
    C["rope"] = rope.reshape(NTI, 128, 2, 256)
    sel = np.zeros((64, 8, 128), np.float32)
    for d in range(2):
        for h in range(4):
            sel[d * 32 + h, d * 4 + h, :] = 1.0
    C["sel"] = sel
    rc = np.zeros((4, NT), np.float32)
    for g, w in enumerate((2, 4, 8, 16)):
        for (o, T) in ((0, NC), (NC, NL)):
            tt = np.arange(T)
            lo = np.clip(tt - w // 2, 0, T)
            hi = np.clip(tt + w - w // 2, 0, T)
            rc[g, o:o + T] = 1.0 / (hi - lo).astype(np.float32)
    C["rcnt"] = rc
    return C


_CACHE = {}
NCORES = 4


def make_in_maps(inputs):
    f = lambda a: np.ascontiguousarray(np.asarray(a, dtype=np.float32))
    x, c, ctx, c_ctx = f(inputs["x"]), f(inputs["c"]), f(inputs["ctx"]), f(inputs["c_ctx"])
    C = host_constants()
    shared = dict(C)
    shared["ngcol"] = f(inputs["norm_gain"]).reshape(DEPTH, KC, 128).transpose(0, 2, 1).copy()
    shared["w_mod"] = f(inputs["w_mod"])
    shared["b_mod"] = f(inputs["b_mod"])
    shared["w_in"] = f(inputs["w_in"])
    shared["qg"] = f(inputs["q_norm_gain"])
    shared["kg"] = f(inputs["k_norm_gain"])
    gb = f(inputs["mlstm_gate_bias"])
    gbias = np.zeros((DEPTH, 64, 2), np.float32)
    gbias[:, 0:4, 0] = gb[:, 0]
    gbias[:, 32:36, 0] = gb[:, 2]
    gbias[:, 0:4, 1] = gb[:, 1]
    gbias[:, 32:36, 1] = gb[:, 3]
    shared["gbias"] = gbias
    shared["mgain"] = f(inputs["mlstm_norm_gain"])
    shared["convw"] = f(inputs["conv_w"]).reshape(DEPTH, 3, 8, 128).transpose(0, 3, 2, 1).copy()
    shared["pool_w"] = f(inputs["pool_w"])
    shared["pscale"] = f(inputs["pool_scale"]).reshape(DEPTH, 8, 128).transpose(0, 2, 1).copy()
    wb = f(inputs["w_branch"])
    shared["wbt"] = wb.reshape(DEPTH, 4, 8, 128, 16, 128).transpose(0, 4, 3, 1, 2, 5).reshape(DEPTH, 16, 128, 4 * 8 * 128).copy()
    shared["w_out"] = f(inputs["w_out"])
    shared["fgain"] = f(inputs["final_norm_gain"])
    maps = []
    for core in range(NCORES):
        b = core % 4
        m = dict(shared)
        m["x"] = x[b]
        m["ctx"] = ctx[b]
        c2 = np.stack([c[b], c_ctx])
        m["c2T"] = c2.reshape(2, KC, 128).transpose(2, 1, 0).copy()
        maps.append(m)
    return maps


def kernel(**inputs):
    if "nc" not in _CACHE:
        _CACHE["nc"] = build_program()
    nc = _CACHE["nc"]
    maps = make_in_maps(inputs)
    res = run_bass_kernel_spmd(nc, maps, core_ids=list(range(NCORES)))
    out = np.stack([np.asarray(res.results[b]["out"]) for b in range(4)]).astype(np.float32)
    return out
```
